# Optimizing a Trainium2 kernel written in Bass

```python
import jax, jax.numpy as jnp
from jax import lax
import numpy as np

D_MODEL = 1024
BATCH = 1
SEQ = 16384
DEPTH = 2

HEAD_DIM = 64
N_HEADS = D_MODEL // HEAD_DIM
N_SB_HEADS = N_HEADS // 2
N_MOBA_HEADS = N_HEADS - N_SB_HEADS
N_FOX_HEADS = N_HEADS
Q_BLOCK = 128
MOBA_BLOCK = 256
MOBA_TOPK = 3
ROPE_THETA = 10000.0
MEM_LEN = 256
XA_HEADS = 4
XA_HEAD_DIM = 64
XA_DIM = XA_HEADS * XA_HEAD_DIM
D_FF = 2816
CONV_WIDTH = 3
RMS_EPS = 1e-6
NEG_INF = -1e9
N_EVEN = (DEPTH + 1) // 2
N_ODD = DEPTH // 2

kernel_name = "hybrid_stickbreak_moba_fox_convffn"


def rmsnorm(x, g):
    xf = x.astype(jnp.float32)
    y = xf * lax.rsqrt(jnp.mean(xf * xf, axis=-1, keepdims=True) + RMS_EPS)
    return (y * g.astype(jnp.float32)).astype(x.dtype)


def rope(x, positions):
    half = HEAD_DIM // 2
    inv_freq = ROPE_THETA ** (-jnp.arange(half, dtype=jnp.float32) / half)
    ang = positions.astype(jnp.float32)[:, None, :, None] * inv_freq
    cos, sin = jnp.cos(ang), jnp.sin(ang)
    xf = x.astype(jnp.float32)
    x1, x2 = xf[..., :half], xf[..., half:]
    return jnp.concatenate([x1 * cos - x2 * sin, x2 * cos + x1 * sin], axis=-1).astype(x.dtype)


def to_chunks(a):
    B, H, S = a.shape[:3]
    a = a.reshape((B, H, S // Q_BLOCK, Q_BLOCK) + a.shape[3:])
    return jnp.moveaxis(a, 2, 0)


def from_chunks(o):
    n, B, H, qb, dh = o.shape
    return jnp.moveaxis(o, 0, 2).reshape(B, H, n * qb, dh)


def stick_breaking_attention(q, k, v):
    S, dh = q.shape[2], q.shape[3]
    scale = dh ** -0.5
    key_pos = jnp.arange(S)

    def block(args):
        qb, start = args
        z = jnp.einsum('bhtd,bhsd->bhts', qb, k).astype(jnp.float32) * scale
        qpos = start + jnp.arange(Q_BLOCK)
        past = key_pos[None, :] < qpos[:, None]
        log_beta = jax.nn.log_sigmoid(z)
        log_1mb = jnp.where(past, log_beta - z, 0.0)
        tail = lax.cumsum(log_1mb, axis=3, reverse=True) - log_1mb
        a = jnp.where(past, jnp.exp(log_beta + tail), 0.0)
        return jnp.einsum('bhts,bhsd->bhtd', a.astype(v.dtype), v)

    starts = jnp.arange(S // Q_BLOCK) * Q_BLOCK
    return from_chunks(lax.map(block, (to_chunks(q), starts)))


def moba_attention(q, k, v):
    B, H, S, dh = q.shape
    scale = dh ** -0.5
    nb = -(-S // MOBA_BLOCK)
    pad = nb * MOBA_BLOCK - S
    kb = jnp.pad(k, ((0, 0), (0, 0), (0, pad), (0, 0))).reshape(B, H, nb, MOBA_BLOCK, dh)
    vb = jnp.pad(v, ((0, 0), (0, 0), (0, pad), (0, 0))).reshape(B, H, nb, MOBA_BLOCK, dh)
    k_mean = jnp.mean(kb.astype(jnp.float32), axis=3)
    n_sel = min(MOBA_TOPK, nb)
    bi = jnp.arange(B)[:, None, None, None]
    hi = jnp.arange(H)[None, :, None, None]
    blk_ids = jnp.arange(nb)

    def block(args):
        qb, start = args
        own = start // MOBA_BLOCK
        qpos = start + jnp.arange(Q_BLOCK)
        gate = jnp.einsum('bhtd,bhnd->bhtn', qb.astype(jnp.float32), k_mean)
        gate = jnp.where(blk_ids < own, gate, NEG_INF)
        _, idx = lax.top_k(gate, n_sel)
        valid = jnp.arange(n_sel) < own
        kg = kb[bi, hi, idx]
        vg = vb[bi, hi, idx]
        s_sel = jnp.einsum('bhtd,bhtnkd->bhtnk', qb, kg).astype(jnp.float32) * scale
        s_sel = jnp.where(valid[:, None], s_sel, NEG_INF).reshape(B, H, Q_BLOCK, n_sel * MOBA_BLOCK)
        k_own = lax.dynamic_index_in_dim(kb, own, axis=2, keepdims=False)
        v_own = lax.dynamic_index_in_dim(vb, own, axis=2, keepdims=False)
        s_own = jnp.einsum('bhtd,bhkd->bhtk', qb, k_own).astype(jnp.float32) * scale
        own_pos = own * MOBA_BLOCK + jnp.arange(MOBA_BLOCK)
        s_own = jnp.where(own_pos[None, :] <= qpos[:, None], s_own, NEG_INF)
        p = jax.nn.softmax(jnp.concatenate([s_sel, s_own], axis=-1), axis=-1).astype(v.dtype)
        p_sel = p[..., :n_sel * MOBA_BLOCK].reshape(B, H, Q_BLOCK, n_sel, MOBA_BLOCK)
        p_own = p[..., n_sel * MOBA_BLOCK:]
        return (jnp.einsum('bhtnk,bhtnkd->bhtd', p_sel, vg)
                + jnp.einsum('bhtk,bhkd->bhtd', p_own, v_own))

    starts = jnp.arange(S // Q_BLOCK) * Q_BLOCK
    return from_chunks(lax.map(block, (to_chunks(q), starts)))


def forgetting_attention(q, k, v, log_f):
    S, dh = q.shape[2], q.shape[3]
    scale = dh ** -0.5
    key_pos = jnp.arange(S)
    c = lax.cumsum(log_f, axis=2)

    def block(args):
        qb, cb, start = args
        qpos = start + jnp.arange(Q_BLOCK)
        s = (jnp.einsum('bhtd,bhsd->bhts', qb, k).astype(jnp.float32) * scale
             + cb[..., None] - c[:, :, None, :])
        s = jnp.where(key_pos[None, :] <= qpos[:, None], s, NEG_INF)
        p = jax.nn.softmax(s, axis=-1).astype(v.dtype)
        return jnp.einsum('bhts,bhsd->bhtd', p, v)

    starts = jnp.arange(S // Q_BLOCK) * Q_BLOCK
    return from_chunks(lax.map(block, (to_chunks(q), to_chunks(c), starts)))


def split_heads(a, n_heads):
    B, S, _ = a.shape
    return a.reshape(B, S, n_heads, HEAD_DIM).transpose(0, 2, 1, 3)


def merge_heads(o):
    B, H, S, dh = o.shape
    return o.transpose(0, 2, 1, 3).reshape(B, S, H * dh)


def sb_moba_mixer(h, positions, w_in, w_out):
    qkv = h @ w_in
    q = split_heads(qkv[..., :D_MODEL], N_HEADS)
    k = split_heads(qkv[..., D_MODEL:2 * D_MODEL], N_HEADS)
    v = split_heads(qkv[..., 2 * D_MODEL:], N_HEADS)
    o_sb = stick_breaking_attention(q[:, :N_SB_HEADS], k[:, :N_SB_HEADS], v[:, :N_SB_HEADS])
    o_moba = moba_attention(rope(q[:, N_SB_HEADS:], positions),
                            rope(k[:, N_SB_HEADS:], positions), v[:, N_SB_HEADS:])
    return merge_heads(jnp.concatenate([o_sb, o_moba], axis=1)) @ w_out


def fox_mixer(h, w_in, b_f, w_out):
    proj = h @ w_in
    q = split_heads(proj[..., :D_MODEL], N_FOX_HEADS)
    k = split_heads(proj[..., D_MODEL:2 * D_MODEL], N_FOX_HEADS)
    v = split_heads(proj[..., 2 * D_MODEL:3 * D_MODEL], N_FOX_HEADS)
    f_logit = (proj[..., 3 * D_MODEL:] + b_f).astype(jnp.float32)
    log_f = jax.nn.log_sigmoid(f_logit).transpose(0, 2, 1)
    return merge_heads(forgetting_attention(q, k, v, log_f)) @ w_out


def memory_cross_attention(h, mem_h, w_q, w_kv, w_out):
    B, S, _ = h.shape
    M = mem_h.shape[1]
    q = (h @ w_q).reshape(B, S, XA_HEADS, XA_HEAD_DIM)
    kv = (mem_h @ w_kv).reshape(B, M, 2, XA_HEADS, XA_HEAD_DIM)
    k, v = kv[:, :, 0], kv[:, :, 1]
    s = jnp.einsum('bshd,bmhd->bhsm', q, k).astype(jnp.float32) * (XA_HEAD_DIM ** -0.5)
    p = jax.nn.softmax(s, axis=-1).astype(h.dtype)
    o = jnp.einsum('bhsm,bmhd->bshd', p, v).reshape(B, S, XA_DIM)
    return o @ w_out


def conv_ffn(h, w_up, conv_w, conv_b, w_down):
    S = h.shape[1]
    u = h @ w_up
    up = jnp.pad(u, ((0, 0), (CONV_WIDTH - 1, 0), (0, 0)))
    c = conv_b
    for i in range(CONV_WIDTH):
        c = c + up[:, i:i + S] * conv_w[i]
    gate, val = c[..., :D_FF], c[..., D_FF:]
    return (jax.nn.silu(gate) * val) @ w_down


def setup_inputs(seed: int = 0) -> dict:
    key = jax.random.key(seed)
    ks = jax.random.split(key, 24)
    nrm = lambda k, shape, s: jax.random.normal(k, shape, jnp.float32) * s
    D = D_MODEL
    positions = jnp.broadcast_to(jnp.arange(SEQ, dtype=jnp.int32)[None, :], (BATCH, SEQ))
    fox_b_f = (jnp.broadcast_to(jnp.linspace(1.0, 4.0, N_FOX_HEADS, dtype=jnp.float32), (N_ODD, N_FOX_HEADS))
               + nrm(ks[9], (N_ODD, N_FOX_HEADS), 0.1))
    return {
        "x": nrm(ks[0], (BATCH, SEQ, D), 1.0),
        "mem": nrm(ks[1], (BATCH, MEM_LEN, D), 1.0),
        "positions": positions,
        "norm_mix_g": 1.0 + nrm(ks[2], (DEPTH, D), 0.02),
        "norm_xa_g": 1.0 + nrm(ks[3], (DEPTH, D), 0.02),
        "norm_mem_g": 1.0 + nrm(ks[4], (DEPTH, D), 0.02),
        "norm_ffn_g": 1.0 + nrm(ks[5], (DEPTH, D), 0.02),
        "ab_w_in": nrm(ks[6], (N_EVEN, D, 3 * D), D ** -0.5),
        "ab_w_out": nrm(ks[7], (N_EVEN, D, D), D ** -0.5),
        "fox_w_in": nrm(ks[8], (N_ODD, D, 3 * D + N_FOX_HEADS), D ** -0.5),
        "fox_b_f": fox_b_f,
        "fox_w_out": nrm(ks[10], (N_ODD, D, D), D ** -0.5),
        "xa_w_q": nrm(ks[11], (DEPTH, D, XA_DIM), D ** -0.5),
        "xa_w_kv": nrm(ks[12], (DEPTH, D, 2 * XA_DIM), D ** -0.5),
        "xa_w_out": nrm(ks[13], (DEPTH, XA_DIM, D), XA_DIM ** -0.5),
        "ffn_w_up": nrm(ks[14], (DEPTH, D, 2 * D_FF), D ** -0.5),
        "ffn_conv_w": nrm(ks[15], (DEPTH, CONV_WIDTH, 2 * D_FF), CONV_WIDTH ** -0.5),
        "ffn_conv_b": nrm(ks[16], (DEPTH, 2 * D_FF), 0.02),
        "ffn_w_down": nrm(ks[17], (DEPTH, D_FF, D), D_FF ** -0.5),
        "final_norm_g": 1.0 + nrm(ks[18], (D,), 0.02),
    }


def reference(x, mem, positions, norm_mix_g, norm_xa_g, norm_mem_g, norm_ffn_g,
              ab_w_in, ab_w_out, fox_w_in, fox_b_f, fox_w_out,
              xa_w_q, xa_w_kv, xa_w_out, ffn_w_up, ffn_conv_w, ffn_conv_b, ffn_w_down,
              final_norm_g):
    h = x
    for layer in range(DEPTH):
        n = layer // 2
        hn = rmsnorm(h, norm_mix_g[layer])
        if layer % 2 == 0:
            h = h + sb_moba_mixer(hn, positions, ab_w_in[n], ab_w_out[n])
        else:
            h = h + fox_mixer(hn, fox_w_in[n], fox_b_f[n], fox_w_out[n])
        h = h + memory_cross_attention(rmsnorm(h, norm_xa_g[layer]), rmsnorm(mem, norm_mem_g[layer]),
                                       xa_w_q[layer], xa_w_kv[layer], xa_w_out[layer])
        h = h + conv_ffn(rmsnorm(h, norm_ffn_g[layer]), ffn_w_up[layer], ffn_conv_w[layer],
                         ffn_conv_b[layer], ffn_w_down[layer])
    return rmsnorm(h, final_norm_g)
```

```python
import contextlib
import numpy as np
import ml_dtypes
import concourse.bass as bass
import concourse.mybir as mybir
from concourse.bass_utils import run_bass_kernel_spmd

F32 = mybir.dt.float32
BF16 = mybir.dt.bfloat16
I32 = mybir.dt.int32
AF = mybir.ActivationFunctionType
ALU = mybir.AluOpType
AX = mybir.AxisListType

NCORES = 8
D = 1024
SEQ = 16384
NT = SEQ // NCORES
DFF = 2816
NPAIR = DFF // 128
MEM = 256
EPS = 1e-6
ENGS = ("pe", "act", "dve", "pool", "sp")
SEM_EPOCH = 24000
DEBUG = False
LAST = None


class Op:
    __slots__ = ("eng", "fn", "deps", "inc", "cnt", "dma", "ndma", "dcnt")

    def __init__(self, eng, fn, dma=None, ndma=1):
        self.eng = eng
        self.fn = fn
        self.deps = set()
        self.inc = False
        self.cnt = 0
        self.dma = dma
        self.ndma = ndma
        self.dcnt = 0


class Sched:
    def __init__(self, nc, same_engine_sync=("act", "dve", "pool")):
        self.nc = nc
        self.ops = {e: [] for e in ENGS}
        self.last_w = {}
        self.readers = {}
        self.streams = {}
        self.same = set(same_engine_sync)

    def _add(self, op, reads, writes):
        for b in reads:
            w = self.last_w.get(b)
            if w is not None:
                op.deps.add(w)
        for b in writes:
            w = self.last_w.get(b)
            if w is not None:
                op.deps.add(w)
            for r in self.readers.get(b, ()):
                op.deps.add(r)
        op.deps.discard(op)
        for b in reads:
            self.readers.setdefault(b, []).append(op)
        for b in writes:
            self.last_w[b] = op
            self.readers[b] = []
        self.ops[op.eng].append(op)
        return op

    def op(self, eng, fn, reads=(), writes=()):
        return self._add(Op(eng, fn), reads, writes)

    def dma(self, stream, fn, reads=(), writes=(), eng="sp", n=1):
        op = Op(eng, fn, dma=stream, ndma=n)
        self.streams.setdefault(stream, []).append(op)
        return self._add(op, reads, writes)

    def emit(self, final_wait_streams=()):
        nc = self.nc
        for e in ENGS:
            for op in self.ops[e]:
                for d in list(op.deps):
                    if d.dma is not None:
                        continue
                    if d.eng == op.eng and op.dma is None and d.eng not in self.same:
                        op.deps.discard(d)
                        continue
                    d.inc = True
        nep = {}
        for e in ENGS:
            c = 0
            for op in self.ops[e]:
                if op.dma is None and op.inc:
                    c += 1
                op.cnt = c
            nep[e] = max(1, -(-c // SEM_EPOCH))
        total = {}
        for s, lst in self.streams.items():
            c = 0
            for op in lst:
                c += 16 * op.ndma
                op.dcnt = c
            total[s] = c
        with contextlib.ExitStack() as st:
            esem = {e: [st.enter_context(nc.semaphore("s_%s%d" % (e, k))) for k in range(nep[e])] for e in ENGS}
            ssem = {s: st.enter_context(nc.semaphore("d_" + s)) for s in self.streams}
            block = st.enter_context(nc.Block())

            def run(e, eng_obj):
                seen = {}
                for op in self.ops[e]:
                    need = {}
                    for d in op.deps:
                        if d.dma is not None:
                            key, val = ("d", d.dma), d.dcnt
                        else:
                            key, val = ("e", d.eng), d.cnt
                        if val > need.get(key, 0):
                            need[key] = val
                    for key, val in need.items():
                        if seen.get(key, 0) >= val:
                            continue
                        seen[key] = val
                        if key[0] == "d":
                            eng_obj.wait_ge(ssem[key[1]], val)
                        else:
                            k = (val - 1) // SEM_EPOCH
                            eng_obj.wait_ge(esem[key[1]][k], val - k * SEM_EPOCH)
                    ins = op.fn(eng_obj)
                    if op.dma is not None:
                        if not isinstance(ins, (list, tuple)):
                            ins = [ins]
                        assert len(ins) == op.ndma, (len(ins), op.ndma)
                        for i_ in ins:
                            i_.then_inc(ssem[op.dma], 16)
                    elif op.inc:
                        ins.then_inc(esem[e][(op.cnt - 1) // SEM_EPOCH], 1)
                if e == "sp":
                    for s in final_wait_streams:
                        eng_obj.wait_ge(ssem[s], total[s])

            @block.tensor
            def _(eng):
                run("pe", eng)

            @block.scalar
            def _(eng):
                run("act", eng)

            @block.vector
            def _(eng):
                run("dve", eng)

            @block.gpsimd
            def _(eng):
                run("pool", eng)

            @block.sync
            def _(eng):
                run("sp", eng)


class Ctx:
    def __init__(self, nc, st):
        self.nc = nc
        self.st = st
        self.S = Sched(nc)
        self._rot = 0

    def sb(self, name, shape, dt):
        return self.st.enter_context(self.nc.sbuf_tensor("sb_" + name, list(shape), dt))

    def ps(self, name, shape, dt):
        return self.st.enter_context(self.nc.psum_tensor("ps_" + name, list(shape), dt))

    def din(self, name, shape, dt):
        return self.nc.dram_tensor(name, list(shape), dt, kind="ExternalInput").ap()

    def dout(self, name, shape, dt):
        return self.nc.dram_tensor(name, list(shape), dt, kind="ExternalOutput").ap()

    def dint(self, name, shape, dt):
        return self.nc.dram_tensor("di_" + name, list(shape), dt, kind="Internal").ap()


def emit_rest(cx, io, final):
    nc, S = cx.nc, cx.S
    sb, ps = cx.sb, cx.ps
    NSUB = NT // 128
    NTT = NT // 512
    ident = sb("ident", [128, 128], BF16)
    ones_f = sb("ones_f", [128, 64], F32)
    gxa = sb("gxa", [128, D], F32)
    gffn = sb("gffn", [128, D], F32)
    gmem = sb("gmem", [128, D], F32)
    gfin = sb("gfin", [128, D], F32) if final else None
    cw = sb("cw", [128, 44, 3], F32)
    cb = sb("cb", [128, 44], F32)
    flag = sb("flag", [128, 1], F32)
    wo_s = sb("wo_s", [128, 8, 1024], BF16)
    wq_s = sb("wq_s", [128, 8, 256], BF16)
    wkv_s = sb("wkv_s", [128, 8, 512], BF16)
    wxo_s = sb("wxo_s", [64, 4, 1024], BF16)
    kxT = sb("kxT", [64, 4, MEM], BF16)
    vxa = sb("vxa", [128, 2, 4, 65], BF16)
    ucarry = sb("ucarry", [128, 44, 2], F32)
    wup_b = cx.dint("wup_b", [NPAIR, 128, 8 * 256], BF16)
    wdn_b = cx.dint("wdn_b", [NPAIR, 2, 128, 512], BF16)
    NUP = 5
    NDN = 8
    wup_r = [sb("wup_r%d" % i, [128, 8, 256], BF16) for i in range(NUP)]
    wdn_r = [sb("wdn_r%d" % i, [128, 512], BF16) for i in range(NDN)]
    hT = [sb("hT%d" % i, [128, 4, D], F32) for i in range(2)]
    oTs = [sb("oTs%d" % i, [128, 8, 512], BF16) for i in range(2)]
    hn_s = [sb("hn_s%d" % i, [128, D], BF16) for i in range(2)]
    junk = sb("junk", [128, D], BF16)
    stat = sb("stat", [128, 8], F32)
    hnT = sb("hnT", [128, 8, 512], BF16)
    qxT = sb("qxT", [64, 4, 512], BF16)
    pxT = [sb("pxT%d" % i, [128, 512], BF16) for i in range(4)]
    nm = sb("nm", [128, 512], F32)
    rd = sb("rd", [128, 512], F32)
    oxT = sb("oxT", [64, 4, 512], BF16)
    Yg = [sb("Yg%d" % i, [128, 512], F32) for i in range(2)]
    Yv = [sb("Yv%d" % i, [128, 512], F32) for i in range(2)]
    gT = sb("gT", [128, NPAIR, 512], BF16)
    trp = [ps("trp%d" % i, [128, 1024], BF16) for i in range(2)]
    acc = [ps("acc%d" % i, [128, 512], F32) for i in range(6)]
    rot = {"acc": 0, "tr": 0, "px": 0}

    def nacc():
        rot["acc"] = (rot["acc"] + 1) % 6
        return rot["acc"]

    def ntr():
        rot["tr"] = (rot["tr"] + 1) % 2
        return rot["tr"]

    def pdma(stream, out, in_, writes, reads=()):
        S.dma(stream, lambda e: nc.gpsimd.dma_start(out=out, in_=in_), reads=reads, writes=writes, eng="pool")

    pdma("c_id", ident[:], io["ident"], ["ident"])
    pdma("c_wo", wo_s[:], io["w_out"], ["wo_s"])
    pdma("c_wq", wq_s[:], io["wq"], ["wq_s"])
    pdma("c_wkv", wkv_s[:], io["wkv"], ["wkv_s"])
    pdma("c_wxo", wxo_s[:], io["wxo"], ["wxo_s"])
    S.dma("c_g", lambda e: [e.dma_start(out=gxa[:], in_=io["g_xa"].partition_broadcast(128)),
                            e.dma_start(out=gffn[:], in_=io["g_ffn"].partition_broadcast(128)),
                            e.dma_start(out=gmem[:], in_=io["g_mem"].partition_broadcast(128)),
                            e.dma_start(out=cw[:], in_=io["cw"]),
                            e.dma_start(out=cb[:], in_=io["cb"]),
                            e.dma_start(out=flag[:], in_=io["flag"])],
          writes=["gxa", "gffn", "gmem", "cw", "cb", "flag"], n=6)
    if final:
        S.dma("c_gf", lambda e: e.dma_start(out=gfin[:], in_=io["g_fin"].partition_broadcast(128)), writes=["gfin"])
    for j in range(NPAIR):
        S.dma("c_up%d" % (j % 4), lambda e, j=j: nc.gpsimd.dma_start(out=wup_b[j], in_=io["wup"][j]),
              writes=[("wup_b", j)], eng="pool")
    for j in range(NPAIR):
        S.dma("c_dn%d" % (j % 4), lambda e, j=j: nc.gpsimd.dma_start(out=wdn_b[j], in_=io["wdn"][j]),
              writes=[("wdn_b", j)], eng="pool")
    S.op("dve", lambda e: e.memset(ones_f[:], 1.0), writes=["ones_f"])
    S.op("dve", lambda e: e.memset(vxa[:], 1.0), writes=["vxa"])

    def rmsnorm_to_T(src_ap, src_key, g_tile, g_key, dstT, dst_key, col0, ncol=128, nrows=128):
        b = ntr()
        hb = hn_s[b]
        S.op("act", lambda e: e.activation(out=junk[:nrows, :], in_=src_ap, func=AF.Square, accum_out=stat[:nrows, 0:1]),
             reads=[src_key], writes=["junk", "stat"])
        S.op("act", lambda e: e.activation(out=stat[:nrows, 1:2], in_=stat[:nrows, 0:1], func=AF.Ln, bias=EPS, scale=1.0 / D),
             reads=["stat"], writes=["stat"])
        S.op("act", lambda e: e.activation(out=stat[:nrows, 2:3], in_=stat[:nrows, 1:2], func=AF.Exp, scale=-0.5),
             reads=["stat"], writes=["stat"])
        S.op("dve", lambda e: e.scalar_tensor_tensor(out=hb[:nrows, :], in0=src_ap, scalar=stat[:nrows, 2:3], in1=g_tile[:nrows, :],
                                                     op0=ALU.mult, op1=ALU.mult),
             reads=[src_key, "stat", g_key], writes=[("hn_s", b)])
        for c in range(8):
            S.op("pe", lambda e, c=c: e.transpose(trp[b][:, c * 128:c * 128 + nrows], hb[:nrows, c * 128:(c + 1) * 128], ident[:nrows, :nrows]),
                 reads=[("hn_s", b), "ident"], writes=[("trp", b)])
        S.op("act", lambda e: e.copy(out=dstT[:, :, col0:col0 + nrows],
                                     in_=trp[b][:].rearrange("p (c t) -> p c t", c=8)[:, :, 0:nrows]),
             reads=[("trp", b)], writes=[dst_key])

    memT = hnT
    mem_s = hT[1]
    S.dma("ld_mem", lambda e: e.dma_start(out=mem_s[:, 0:2, :], in_=io["mem"].rearrange("(s p) d -> p s d", p=128)),
          writes=[("hT", 1)])
    for s in range(2):
        rmsnorm_to_T(mem_s[:, s, :], ("hT", 1), gmem, "gmem", memT, "hnT", s * 128)
    for hd in range(4):
        a = nacc()
        for c in range(8):
            S.op("pe", lambda e, a=a, c=c, hd=hd: e.matmul(acc[a][0:64, 0:MEM], lhsT=wkv_s[:, c, hd * 64:(hd + 1) * 64], rhs=memT[:, c, 0:MEM],
                                                            start=(c == 0), stop=(c == 7)),
                 reads=["wkv_s", "hnT"], writes=[("acc", a)])
        S.op("act", lambda e, a=a, hd=hd: e.copy(out=kxT[:, hd, :], in_=acc[a][0:64, 0:MEM]), reads=[("acc", a)], writes=["kxT"])
    for mc in range(2):
        a = nacc()
        for c in range(8):
            S.op("pe", lambda e, a=a, c=c, mc=mc: e.matmul(acc[a][:, 0:256], lhsT=memT[:, c, mc * 128:(mc + 1) * 128], rhs=wkv_s[:, c, 256:512],
                                                            start=(c == 0), stop=(c == 7)),
                 reads=["wkv_s", "hnT"], writes=[("acc", a)])
        S.op("act", lambda e, a=a, mc=mc: e.copy(out=vxa[:, mc, :, 0:64], in_=acc[a][:, 0:256].rearrange("p (h d) -> p h d", h=4)),
             reads=[("acc", a)], writes=["vxa"])

    upq = {"n": 0}
    dnq = {"n": 0}

    def load_up(j):
        slot = upq["n"] % NUP
        upq["n"] += 1
        S.dma("r_up%d" % slot, lambda e: e.dma_start(out=wup_r[slot][:], in_=wup_b[j].rearrange("p (c n) -> p c n", c=8)),
              reads=[("wup_b", j)], writes=[("wup_r", slot)])
        return slot

    def load_dn(j, half):
        slot = dnq["n"] % NDN
        dnq["n"] += 1
        S.dma("r_dn%d" % slot, lambda e: e.dma_start(out=wdn_r[slot][:], in_=wdn_b[j, half]),
              reads=[("wdn_b", j)], writes=[("wdn_r", slot)])
        return slot

    def front(buf, nsub, row0):
        ntok = nsub * 128
        hb = hT[buf]
        S.dma("ld_h%d" % buf, lambda e: e.dma_start(out=hb[:, 0:nsub, :], in_=io["hin"][row0:row0 + ntok, :].rearrange("(s p) d -> p s d", p=128)),
              writes=[("hT", buf)])
        S.dma("ld_o%d" % buf, lambda e: e.dma_start(out=oTs[buf][:, :, 0:ntok], in_=io["oT"][:, row0:row0 + ntok].rearrange("(c p) t -> p c t", p=128)),
              writes=[("oTs", buf)])
        for s in range(nsub):
            for half in range(2):
                a = nacc()
                for c in range(8):
                    S.op("pe", lambda e, a=a, c=c, s=s, half=half: e.matmul(acc[a][:, :], lhsT=oTs[buf][:, c, s * 128:(s + 1) * 128],
                                                                            rhs=wo_s[:, c, half * 512:(half + 1) * 512], start=(c == 0), stop=(c == 7)),
                         reads=[("oTs", buf), "wo_s"], writes=[("acc", a)])
                S.op("dve", lambda e, a=a, s=s, half=half: e.tensor_tensor(out=hb[:, s, half * 512:(half + 1) * 512], in0=hb[:, s, half * 512:(half + 1) * 512],
                                                                           in1=acc[a][:, :], op=ALU.add),
                     reads=[("acc", a), ("hT", buf)], writes=[("hT", buf)])
        for s in range(nsub):
            rmsnorm_to_T(hb[:, s, :], ("hT", buf), gxa, "gxa", hnT, "hnT", s * 128)
        for hd in range(4):
            a = nacc()
            for c in range(8):
                S.op("pe", lambda e, a=a, c=c, hd=hd: e.matmul(acc[a][0:64, 0:ntok], lhsT=wq_s[:, c, hd * 64:(hd + 1) * 64], rhs=hnT[:, c, 0:ntok],
                                                                start=(c == 0), stop=(c == 7)),
                     reads=["wq_s", "hnT"], writes=[("acc", a)])
            S.op("act", lambda e, a=a, hd=hd: e.copy(out=qxT[:, hd, 0:ntok], in_=acc[a][0:64, 0:ntok]), reads=[("acc", a)], writes=[("qxT", hd)])
        for hd in range(4):
            pk = []
            for mc in range(2):
                a = nacc()
                S.op("pe", lambda e, a=a, mc=mc, hd=hd: e.matmul(acc[a][:, 0:ntok], lhsT=kxT[:, hd, mc * 128:(mc + 1) * 128], rhs=qxT[:, hd, 0:ntok],
                                                                  start=True, stop=True),
                     reads=["kxT", ("qxT", hd)], writes=[("acc", a)])
                p = rot["px"] = (rot["px"] + 1) % 4
                S.op("act", lambda e, a=a, p=p: e.activation(out=pxT[p][:, 0:ntok], in_=acc[a][:, 0:ntok], func=AF.Exp, scale=0.125),
                     reads=[("acc", a)], writes=[("pxT", p)])
                pk.append(p)
            a = nacc()
            for mc in range(2):
                S.op("pe", lambda e, a=a, mc=mc, hd=hd, p=pk[mc]: e.matmul(acc[a][0:65, 0:ntok], lhsT=vxa[:, mc, hd, :], rhs=pxT[p][:, 0:ntok],
                                                                            start=(mc == 0), stop=(mc == 1)),
                     reads=["vxa", ("pxT", pk[mc])], writes=[("acc", a)])
            S.op("dve", lambda e, a=a: e.reciprocal(out=rd[64:65, 0:ntok], in_=acc[a][64:65, 0:ntok]), reads=[("acc", a)], writes=["rd"])
            S.op("act", lambda e, a=a: e.copy(out=nm[0:64, 0:ntok], in_=acc[a][0:64, 0:ntok]), reads=[("acc", a)], writes=["nm"])
            a2 = nacc()
            S.op("pe", lambda e, a2=a2: e.matmul(acc[a2][0:64, 0:ntok], lhsT=ones_f[64:65, 0:64], rhs=rd[64:65, 0:ntok], start=True, stop=True),
                 reads=["ones_f", "rd"], writes=[("acc", a2)])
            S.op("dve", lambda e, a2=a2, hd=hd: e.tensor_tensor(out=oxT[:, hd, 0:ntok], in0=nm[0:64, 0:ntok], in1=acc[a2][0:64, 0:ntok], op=ALU.mult),
                 reads=["nm", ("acc", a2)], writes=[("oxT", hd)])
        for s in range(nsub):
            for half in range(2):
                a = nacc()
                for hd in range(4):
                    S.op("pe", lambda e, a=a, hd=hd, s=s, half=half: e.matmul(acc[a][:, :], lhsT=oxT[:, hd, s * 128:(s + 1) * 128],
                                                                              rhs=wxo_s[:, hd, half * 512:(half + 1) * 512], start=(hd == 0), stop=(hd == 3)),
                         reads=[("oxT", hd), "wxo_s"], writes=[("acc", a)])
                S.op("dve", lambda e, a=a, s=s, half=half: e.tensor_tensor(out=hb[:, s, half * 512:(half + 1) * 512], in0=hb[:, s, half * 512:(half + 1) * 512],
                                                                           in1=acc[a][:, :], op=ALU.add),
                     reads=[("acc", a), ("hT", buf)], writes=[("hT", buf)])
        for s in range(nsub):
            rmsnorm_to_T(hb[:, s, :], ("hT", buf), gffn, "gffn", hnT, "hnT", s * 128)

    front(1, 1, 0)
    for j in range(NPAIR):
        slot = load_up(j)
        for gv in range(2):
            grp = j + gv * NPAIR
            a = nacc()
            for c in range(8):
                S.op("pe", lambda e, a=a, c=c, gv=gv, slot=slot: e.matmul(acc[a][:, 0:2], lhsT=wup_r[slot][:, c, gv * 128:(gv + 1) * 128], rhs=hnT[:, c, 126:128],
                                                                           start=(c == 0), stop=(c == 7)),
                     reads=[("wup_r", slot), "hnT"], writes=[("acc", a)])
            S.op("dve", lambda e, a=a, grp=grp: e.tensor_scalar(out=ucarry[:, grp, :], in0=acc[a][:, 0:2], scalar1=flag[:, 0:1], scalar2=None, op0=ALU.mult),
                 reads=[("acc", a), "flag"], writes=[("ucarry", grp)])

    for tt in range(NTT):
        buf = tt % 2
        hb = hT[buf]
        front(buf, 4, 128 + tt * 512)
        for j in range(NPAIR):
            slot = load_up(j)
            yb = j % 2
            for gv in range(2):
                grp = j + gv * NPAIR
                Y = (Yg if gv == 0 else Yv)[yb]
                ykey = ("Y", gv, yb)
                a = nacc()
                for c in range(8):
                    S.op("pe", lambda e, a=a, c=c, gv=gv, slot=slot: e.matmul(acc[a][:, :], lhsT=wup_r[slot][:, c, gv * 128:(gv + 1) * 128], rhs=hnT[:, c, :],
                                                                               start=(c == 0), stop=(c == 7)),
                         reads=[("wup_r", slot), "hnT"], writes=[("acc", a)])
                S.op("act", lambda e, a=a, grp=grp, Y=Y: e.activation(out=Y[:, :], in_=acc[a][:, :], func=AF.Identity, bias=cb[:, grp:grp + 1], scale=cw[:, grp, 2:3]),
                     reads=[("acc", a), "cw", "cb"], writes=[ykey])
                S.op("dve", lambda e, a=a, grp=grp, Y=Y: e.scalar_tensor_tensor(out=Y[:, 1:512], in0=acc[a][:, 0:511], scalar=cw[:, grp, 1:2], in1=Y[:, 1:512],
                                                                                op0=ALU.mult, op1=ALU.add),
                     reads=[("acc", a), "cw", ykey], writes=[ykey])
                S.op("dve", lambda e, a=a, grp=grp, Y=Y: e.scalar_tensor_tensor(out=Y[:, 2:512], in0=acc[a][:, 0:510], scalar=cw[:, grp, 0:1], in1=Y[:, 2:512],
                                                                                op0=ALU.mult, op1=ALU.add),
                     reads=[("acc", a), "cw", ykey], writes=[ykey])
                S.op("dve", lambda e, grp=grp, Y=Y: e.scalar_tensor_tensor(out=Y[:, 0:1], in0=ucarry[:, grp, 1:2], scalar=cw[:, grp, 1:2], in1=Y[:, 0:1],
                                                                            op0=ALU.mult, op1=ALU.add),
                     reads=[("ucarry", grp), "cw", ykey], writes=[ykey])
                S.op("dve", lambda e, grp=grp, Y=Y: e.scalar_tensor_tensor(out=Y[:, 0:2], in0=ucarry[:, grp, 0:2], scalar=cw[:, grp, 0:1], in1=Y[:, 0:2],
                                                                            op0=ALU.mult, op1=ALU.add),
                     reads=[("ucarry", grp), "cw", ykey], writes=[ykey])
                S.op("dve", lambda e, a=a, grp=grp: e.tensor_copy(out=ucarry[:, grp, :], in_=acc[a][:, 510:512]),
                     reads=[("acc", a)], writes=[("ucarry", grp)])
            S.op("act", lambda e, yb=yb: e.activation(out=Yg[yb][:, :], in_=Yg[yb][:, :], func=AF.Silu),
                 reads=[("Y", 0, yb)], writes=[("Y", 0, yb)])
            S.op("dve", lambda e, yb=yb, j=j: e.tensor_tensor(out=gT[:, j, :], in0=Yg[yb][:, :], in1=Yv[yb][:, :], op=ALU.mult),
                 reads=[("Y", 0, yb), ("Y", 1, yb)], writes=[("gT", j)])
        for half in range(2):
            accs = [nacc() for _ in range(4)]
            for j in range(NPAIR):
                slot = load_dn(j, half)
                for s in range(4):
                    a = accs[s]
                    S.op("pe", lambda e, a=a, j=j, s=s, slot=slot: e.matmul(acc[a][:, :], lhsT=gT[:, j, s * 128:(s + 1) * 128], rhs=wdn_r[slot][:, :],
                                                                             start=(j == 0), stop=(j == NPAIR - 1)),
                         reads=[("gT", j), ("wdn_r", slot)], writes=[("acc", a)])
            for s in range(4):
                a = accs[s]
                S.op("dve", lambda e, a=a, s=s, half=half, hb=hb: e.tensor_tensor(out=hb[:, s, half * 512:(half + 1) * 512], in0=hb[:, s, half * 512:(half + 1) * 512],
                                                                           in1=acc[a][:, :], op=ALU.add),
                     reads=[("acc", a), ("hT", buf)], writes=[("hT", buf)])
        if final:
            for s in range(4):
                S.op("act", lambda e, s=s, hb=hb: e.activation(out=junk[:, :], in_=hb[:, s, :], func=AF.Square, accum_out=stat[:, 4:5]),
                     reads=[("hT", buf)], writes=["junk", "stat2"])
                S.op("act", lambda e: e.activation(out=stat[:, 5:6], in_=stat[:, 4:5], func=AF.Ln, bias=EPS, scale=1.0 / D),
                     reads=["stat2"], writes=["stat2"])
                S.op("act", lambda e: e.activation(out=stat[:, 6:7], in_=stat[:, 5:6], func=AF.Exp, scale=-0.5),
                     reads=["stat2"], writes=["stat2"])
                S.op("dve", lambda e, s=s, hb=hb: e.scalar_tensor_tensor(out=hb[:, s, :], in0=hb[:, s, :], scalar=stat[:, 6:7], in1=gfin[:, :],
                                                                  op0=ALU.mult, op1=ALU.mult),
                     reads=[("hT", buf), "stat2", "gfin"], writes=[("hT", buf)])
        S.dma("st_h", lambda e, tt=tt, hb=hb: e.dma_start(out=io["hout"][tt * 512:(tt + 1) * 512, :].rearrange("(s p) d -> p s d", p=128), in_=hb[:, :, :]),
              reads=[("hT", buf)], writes=[("hout", tt)])


def build_rest(final):
    nc = bass.Bass("TRN2", target_bir_lowering=False)
    with contextlib.ExitStack() as st:
        cx = Ctx(nc, st)
        io = {
            "hin": cx.din("hin", [NT + 128, D], F32),
            "oT": cx.din("oT", [D, NT + 128], BF16),
            "flag": cx.din("flag", [128, 1], F32),
            "w_out": cx.din("w_out", [128, 8, 1024], F32),
            "wq": cx.din("wq", [128, 8, 256], F32),
            "wkv": cx.din("wkv", [128, 8, 512], F32),
            "wxo": cx.din("wxo", [64, 4, 1024], F32),
            "wup": cx.din("wup", [NPAIR, 128, 8 * 256], F32),
            "wdn": cx.din("wdn", [NPAIR, 2, 128, 512], F32),
            "cw": cx.din("cw", [128, 44, 3], F32),
            "cb": cx.din("cb", [128, 44], F32),
            "g_xa": cx.din("g_xa", [D], F32),
            "g_mem": cx.din("g_mem", [D], F32),
            "g_ffn": cx.din("g_ffn", [D], F32),
            "g_fin": cx.din("g_fin", [D], F32),
            "mem": cx.din("mem", [MEM, D], F32),
            "ident": cx.din("ident", [128, 128], F32),
            "hout": cx.dout("hout", [NT, D], F32),
        }
        emit_rest(cx, io, final)
        cx.S.emit(final_wait_streams=["st_h"])
    return nc


def rest_weights(layer, w_out, xa_w_q, xa_w_kv, xa_w_out, ffn_w_up, ffn_conv_w, ffn_conv_b, ffn_w_down,
                 norm_xa_g, norm_mem_g, norm_ffn_g, final_norm_g, mem):
    f = np.float32
    c = np.ascontiguousarray
    wup = ffn_w_up[layer].reshape(8, 128, 2, NPAIR, 128).transpose(3, 1, 0, 2, 4)
    return {
        "w_out": c(w_out.reshape(8, 128, 1024).transpose(1, 0, 2), dtype=f),
        "wq": c(xa_w_q[layer].reshape(8, 128, 256).transpose(1, 0, 2), dtype=f),
        "wkv": c(xa_w_kv[layer].reshape(8, 128, 512).transpose(1, 0, 2), dtype=f),
        "wxo": c(xa_w_out[layer].reshape(4, 64, 1024).transpose(1, 0, 2), dtype=f),
        "wup": c(wup.reshape(NPAIR, 128, 8 * 256), dtype=f),
        "wdn": c(ffn_w_down[layer].reshape(NPAIR, 128, 2, 512).transpose(0, 2, 1, 3), dtype=f),
        "cw": c(ffn_conv_w[layer].reshape(3, 44, 128).transpose(2, 1, 0), dtype=f),
        "cb": c(ffn_conv_b[layer].reshape(44, 128).T, dtype=f),
        "g_xa": c(norm_xa_g[layer], dtype=f),
        "g_mem": c(norm_mem_g[layer], dtype=f),
        "g_ffn": c(norm_ffn_g[layer], dtype=f),
        "g_fin": c(final_norm_g, dtype=f),
        "mem": c(mem[0], dtype=f),
        "ident": np.eye(128, dtype=f),
    }


def run_rest(nc, h, oT, wts):
    in_maps = []
    for cid in range(NCORES):
        t0 = cid * NT
        if cid == 0:
            hin = np.concatenate([np.zeros((128, D), np.float32), h[0:NT]], axis=0)
            oin = np.concatenate([np.zeros((D, 128), oT.dtype), oT[:, 0:NT]], axis=1)
        else:
            hin = h[t0 - 128:t0 + NT]
            oin = oT[:, t0 - 128:t0 + NT]
        m = dict(wts)
        m["hin"] = np.ascontiguousarray(hin)
        m["oT"] = np.ascontiguousarray(oin)
        m["flag"] = np.full((128, 1), 0.0 if cid == 0 else 1.0, np.float32)
        in_maps.append(m)
    res = run_bass_kernel_spmd(nc, in_maps, core_ids=list(range(NCORES)))
    return np.concatenate([r["hout"] for r in res.results], axis=0)


NTILE = SEQ // 512
NBLK = SEQ // 128
SB_WIN = 3


def emit_mixer(cx, io, kind, ntile=NTILE):
    nc, S = cx.nc, cx.S
    sb, ps = cx.sb, cx.ps
    fox = kind == "fox"
    NCOL = 386 if fox else 512
    if fox:
        QA, KA, QB, KB, VV, FF = 0, 64, 128, 192, 256, 384
    else:
        QA, KA, QB, KB, VV, QP, KP = 0, 64, 128, 192, 256, 384, 448
    ident = sb("ident", [128, 128], BF16)
    ident_f = sb("ident_f", [128, 128], F32)
    ones_f = sb("ones_f", [128, 64], F32)
    gmix = sb("gmix", [128, D], F32)
    w_s = sb("w_s", [128, 8, NCOL], BF16)
    maskI = sb("maskI", [128, 4, 512], BF16)
    kT = [sb("kT%d" % h, [128, SEQ], BF16) for h in range(2)]
    vX = [sb("vX%d" % h, [128, NBLK, 65], BF16) for h in range(2)]
    qT = [[sb("qT%d_%d" % (h, i), [128, 512], BF16) for i in range(2)] for h in range(2)]
    NXT = 2 if fox else 1
    xt = [sb("xt%d" % i, [128, 4, D], F32) for i in range(NXT)]
    hn_s = [sb("hn_s%d" % i, [128, D], BF16) for i in range(2)]
    junk = sb("junk", [128, D], BF16)
    stat = sb("stat", [128, 8], F32)
    hnT = sb("hnT", [128, 8, 512], BF16)
    NPT = 6 if fox else 4
    pT = [sb("pT%d" % i, [128, 512], BF16) for i in range(NPT)]
    nm = sb("nm", [128, 512], F32)
    rd = sb("rd", [128, 512], F32)
    oTs = [sb("oTs%d" % i, [64, 512], BF16) for i in range(2)]
    trp = [ps("trp%d" % i, [128, 1024], BF16) for i in range(1)]
    pacc = [ps("pacc%d" % i, [128, 512], F32) for i in range(2)]
    NSC = 3 if fox else 2
    sc = [ps("sc%d" % i, [128, 512], F32) for i in range(NSC)]
    av = [ps("av%d" % i, [128, 512], F32) for i in range(2)]
    rot = {"pacc": 0, "sc": 0, "pt": 0, "av": 0, "hn": 0, "ot": 0}

    def nxt(k, n):
        rot[k] = (rot[k] + 1) % n
        return rot[k]

    def pdma(stream, out, in_, writes):
        S.dma(stream, lambda e: nc.gpsimd.dma_start(out=out, in_=in_), writes=writes, eng="pool")

    pdma("c_id", ident[:], io["ident"], ["ident"])
    pdma("c_w", w_s[:], io["w"], ["w_s"])
    pdma("c_mi", maskI[:], io["maskI"], ["maskI"])
    S.dma("c_g", lambda e: [e.dma_start(out=gmix[:], in_=io["g_mix"].partition_broadcast(128)),
                            e.dma_start(out=ident_f[:], in_=io["ident"])], writes=["gmix", "ident_f"], n=2)
    S.op("dve", lambda e: e.memset(ones_f[:], 1.0), writes=["ones_f"])
    for h in range(2):
        S.op("pool", lambda e, h=h: e.memset(vX[h][:], 1.0), writes=[("vX", h, i_) for i_ in range(NTILE)])
    if fox:
        nbf = sb("nbf", [2, 1], F32)
        cprev = sb("cprev", [2, 1], F32)
        fE = sb("fE", [2, 512], F32)
        cc = sb("cc", [2, 512], F32)
        rbf = sb("rbf", [2, 512], BF16)
        negc = sb("negc", [128, NBLK, 2], F32)
        S.dma("c_bf", lambda e: e.dma_start(out=nbf[:], in_=io["bf"]), writes=["nbf"])
        S.op("dve", lambda e: e.tensor_scalar(out=nbf[:], in0=nbf[:], scalar1=-1.0, scalar2=None, op0=ALU.mult), reads=["nbf"], writes=["nbf"])
        S.op("dve", lambda e: e.memset(cprev[:], 0.0), writes=["cprev"])
        for h in range(2):
            S.op("pool", lambda e, h=h: e.memset(kT[h][64:128, :], 1.0), writes=[("kT", h, i_) for i_ in range(NTILE)])


    if not fox:
        maskS = sb("maskS", [128, 4, 512], BF16)
        tri = sb("tri", [128, 256], F32)
        invf = sb("invf", [64, 1], F32)
        sgn = sb("sgn", [64, 1], F32)
        pos_i = sb("pos_i", [64, 512], I32)
        ang = sb("ang", [64, 512], F32)
        kint = sb("kint", [64, 512], I32)
        kflt = sb("kflt", [64, 512], F32)
        msk = sb("msk", [64, 512], F32)
        cosF = sb("cosF", [64, 512], F32)
        sinS = sb("sinS", [64, 512], F32)
        rt1 = sb("rt1", [64, 512], F32)
        qrot = sb("qrot", [64, 512], F32)
        krot = sb("krot", [64, 512], F32)
        kmT = sb("kmT", [64, 64], F32)
        G = sb("G", [128, 64], F32)
        top8 = sb("top8", [128, 8], F32)
        MB = sb("MB", [128, 128], BF16)
        spE = [sb("spE%d" % k, [128, 512], F32) for k in range(2)]
        spM = [sb("spM%d" % k, [128, 512], F32) for k in range(2)]
        lgA = [sb("lgA%d" % k, [128, 512], F32) for k in range(2)]
        tailP = ps("tailP", [128, 512], F32)
        pdma("c_ms", maskS[:], io["maskS"], ["maskS"])
        S.dma("c_ab", lambda e: [e.dma_start(out=tri[:], in_=io["tri"]), e.dma_start(out=invf[:], in_=io["invf"]),
                                 e.dma_start(out=sgn[:], in_=io["sgn"])], writes=["tri", "invf", "sgn"], n=3)
        S.dma("c_koh", lambda e: nc.gpsimd.dma_start(out=kT[1][64:128, :], in_=io["koh"]), writes=[("kT", 1, i_) for i_ in range(NTILE)], eng="pool")
        S.op("dve", lambda e: e.memset(kmT[:], 0.0), writes=["kmT"])
        S.op("dve", lambda e: e.memset(MB[:], 0.0), writes=["MB"])

    def rope_tables(i):
        PI = float(np.pi)
        S.dma("ld_pos", lambda e: e.dma_start(out=pos_i[:], in_=io["pos"][i * 512:(i + 1) * 512].partition_broadcast(64)), writes=["pos_i"])
        S.op("dve", lambda e: e.tensor_copy(out=ang[:], in_=pos_i[:]), reads=["pos_i"], writes=["ang"])
        S.op("dve", lambda e: e.tensor_scalar(out=ang[:], in0=ang[:], scalar1=invf[:, 0:1], scalar2=None, op0=ALU.mult), reads=["ang", "invf"], writes=["ang"])
        S.op("dve", lambda e: e.tensor_scalar(out=kint[:], in0=ang[:], scalar1=float(1.0 / (2 * np.pi)), scalar2=None, op0=ALU.mult), reads=["ang"], writes=["kint"])
        S.op("dve", lambda e: e.tensor_copy(out=kflt[:], in_=kint[:]), reads=["kint"], writes=["kflt"])
        S.op("dve", lambda e: e.scalar_tensor_tensor(out=ang[:], in0=kflt[:], scalar=-6.28125, in1=ang[:], op0=ALU.mult, op1=ALU.add), reads=["kflt", "ang"], writes=["ang"])
        S.op("dve", lambda e: e.scalar_tensor_tensor(out=ang[:], in0=kflt[:], scalar=float(-(2 * np.pi - 6.28125)), in1=ang[:], op0=ALU.mult, op1=ALU.add),
             reads=["kflt", "ang"], writes=["ang"])
        S.op("dve", lambda e: e.tensor_scalar(out=msk[:], in0=ang[:], scalar1=PI, scalar2=-2 * PI, op0=ALU.is_gt, op1=ALU.mult), reads=["ang"], writes=["msk"])
        S.op("dve", lambda e: e.tensor_tensor(out=ang[:], in0=ang[:], in1=msk[:], op=ALU.add), reads=["ang", "msk"], writes=["ang"])
        S.op("dve", lambda e: e.tensor_scalar(out=msk[:], in0=ang[:], scalar1=-PI, scalar2=2 * PI, op0=ALU.is_lt, op1=ALU.mult), reads=["ang"], writes=["msk"])
        S.op("dve", lambda e: e.tensor_tensor(out=ang[:], in0=ang[:], in1=msk[:], op=ALU.add), reads=["ang", "msk"], writes=["ang"])
        S.op("dve", lambda e: e.tensor_scalar(out=rt1[:], in0=ang[:], scalar1=PI / 2, scalar2=None, op0=ALU.add), reads=["ang"], writes=["rt1"])
        S.op("dve", lambda e: e.tensor_scalar(out=msk[:], in0=rt1[:], scalar1=PI, scalar2=-2 * PI, op0=ALU.is_gt, op1=ALU.mult), reads=["rt1"], writes=["msk"])
        S.op("dve", lambda e: e.tensor_tensor(out=rt1[:], in0=rt1[:], in1=msk[:], op=ALU.add), reads=["rt1", "msk"], writes=["rt1"])
        S.op("act", lambda e: e.activation(out=sinS[:], in_=ang[:], func=AF.Sin, scale=sgn[:, 0:1]), reads=["ang", "sgn"], writes=["sinS"])
        S.op("act", lambda e: e.activation(out=cosF[:], in_=rt1[:], func=AF.Sin), reads=["rt1"], writes=["cosF"])

    def rope_apply(a_main, a_perm, dst, dkey):
        S.op("dve", lambda e: e.tensor_tensor(out=rt1[:], in0=pacc[a_main][0:64, :], in1=cosF[:], op=ALU.mult), reads=[("pacc", a_main), "cosF"], writes=["rt1"])
        S.op("dve", lambda e: e.tensor_tensor(out=dst[:], in0=pacc[a_perm][0:64, :], in1=sinS[:], op=ALU.mult), reads=[("pacc", a_perm), "sinS"], writes=[dkey])
        S.op("dve", lambda e: e.tensor_tensor(out=dst[:], in0=dst[:], in1=rt1[:], op=ALU.add), reads=[dkey, "rt1"], writes=[dkey])

    def moba_gate(i, qbuf):
        for s in range(4):
            own = 2 * i + s // 2
            if own > 0:
                a = nxt("pacc", 2)
                S.op("pe", lambda e, a=a, s=s: e.matmul(pacc[a][:, 0:64], lhsT=qrot[0:64, s * 128:(s + 1) * 128], rhs=kmT[0:64, 0:64], start=True, stop=True),
                     reads=["qrot", "kmT"], writes=[("pacc", a)])
                S.op("dve", lambda e: e.memset(G[:], -1e9), writes=["G"])
                S.op("dve", lambda e, a=a, own=own: e.tensor_copy(out=G[:, 0:own], in_=pacc[a][:, 0:own]), reads=[("pacc", a), "G"], writes=["G"])
                S.op("dve", lambda e: e.max(out=top8[:], in_=G[:]), reads=["G"], writes=["top8"])
                S.op("dve", lambda e: e.tensor_scalar(out=top8[:, 2:3], in0=top8[:, 2:3], scalar1=-1e8, scalar2=None, op0=ALU.max), reads=["top8"], writes=["top8"])
                S.op("dve", lambda e: e.tensor_scalar(out=MB[:, 64:128], in0=G[:], scalar1=top8[:, 2:3], scalar2=-1.0, op0=ALU.is_ge, op1=ALU.add),
                     reads=["G", "top8"], writes=["MB"])
            S.op("dve", lambda e, own=own: e.memset(MB[:, 64 + own:128], 0.0), reads=["MB"], writes=["MB"])
            S.op("pe", lambda e: e.transpose(trp[0][:, 0:128], MB[:, :], ident[:, :]), reads=["MB", "ident"], writes=["trp"])
            S.op("act", lambda e, s=s: e.copy(out=qT[1][qbuf][64:128, s * 128:(s + 1) * 128], in_=trp[0][64:128, 0:128]),
                 reads=["trp"], writes=[("qT", 1, qbuf)])

    def sb_attend(i, qbuf):
        a = nxt("av", 2)
        lo = max(0, 4 * i - SB_WIN)
        blocks = list(range(4 * i + 3, lo - 1, -1))
        for n, blk in enumerate(blocks):
            j = blk - 4 * i
            s_ = nxt("sc", NSC)
            p_ = nxt("pt", NPT)
            b2 = n % 2
            S.op("pe", lambda e, s_=s_, blk=blk: e.matmul(sc[s_][:, :], lhsT=kT[0][0:64, blk * 128:(blk + 1) * 128], rhs=qT[0][qbuf][0:64, :], start=True, stop=True),
                 reads=[("kT", 0, blk // 4), ("qT", 0, qbuf)], writes=[("sc", s_)])
            S.op("act", lambda e, s_=s_, b2=b2: e.activation(out=spE[b2][:, :], in_=sc[s_][:, :], func=AF.Exp), reads=[("sc", s_)], writes=[("spE", b2)])
            S.op("act", lambda e, b2=b2: e.activation(out=spE[b2][:, :], in_=spE[b2][:, :], func=AF.Ln, bias=1.0, scale=1.0), reads=[("spE", b2)], writes=[("spE", b2)])
            if j >= 0:
                S.op("pool", lambda e, b2=b2, j=j: e.tensor_tensor(out=spM[b2][:, :], in0=spE[b2][:, :], in1=maskS[:, j, :], op=ALU.mult),
                     reads=[("spE", b2), "maskS"], writes=[("spM", b2)])
                src, skey = spM[b2], ("spM", b2)
            else:
                src, skey = spE[b2], ("spE", b2)
            S.op("pe", lambda e, src=src, n=n: e.matmul(tailP[:, :], lhsT=tri[:, 0:128], rhs=src[:, :], start=(n == 0), stop=True),
                 reads=["tri", skey], writes=["tailP"])
            S.op("dve", lambda e, s_=s_, b2=b2: e.tensor_tensor(out=lgA[b2][:, :], in0=sc[s_][:, :], in1=spE[b2][:, :], op=ALU.subtract),
                 reads=[("sc", s_), ("spE", b2)], writes=[("lgA", b2)])
            S.op("dve", lambda e, b2=b2: e.tensor_tensor(out=lgA[b2][:, :], in0=lgA[b2][:, :], in1=tailP[:, :], op=ALU.add),
                 reads=[("lgA", b2), "tailP"], writes=[("lgA", b2)])
            S.op("pe", lambda e, src=src: e.matmul(tailP[:, :], lhsT=tri[:, 128:256], rhs=src[:, :], start=False, stop=True),
                 reads=["tri", skey], writes=["tailP"])
            S.op("act", lambda e, p_=p_, b2=b2: e.activation(out=pT[p_][:, :], in_=lgA[b2][:, :], func=AF.Exp), reads=[("lgA", b2)], writes=[("pT", p_)])
            if j >= 0:
                S.op("pool", lambda e, p_=p_, j=j: e.tensor_tensor(out=pT[p_][:, :], in0=pT[p_][:, :], in1=maskS[:, j, :], op=ALU.mult),
                     reads=[("pT", p_), "maskS"], writes=[("pT", p_)])
            S.op("pe", lambda e, a=a, p_=p_, blk=blk, n=n: e.matmul(av[a][0:64, :], lhsT=vX[0][:, blk, 0:64], rhs=pT[p_][:, :], start=(n == 0), stop=(n == len(blocks) - 1)),
                 reads=[("vX", 0, blk // 4), ("pT", p_)], writes=[("av", a)])
        finalize(i, 0, a, False)

    def proj64(col, i, nrows=64):
        a = nxt("pacc", 2)
        for c in range(8):
            S.op("pe", lambda e, a=a, c=c: e.matmul(pacc[a][0:nrows, :], lhsT=w_s[:, c, col:col + nrows], rhs=hnT[:, c, :], start=(c == 0), stop=(c == 7)),
                 reads=["w_s", "hnT"], writes=[("pacc", a)])
        return a

    def attend(i, h, blocks, krows, bias_fn, qbuf):
        a = nxt("av", 2)
        last = blocks[-1]
        for blk in blocks:
            s_ = nxt("sc", NSC)
            p_ = nxt("pt", NPT)
            j = blk - 4 * i
            S.op("pe", lambda e, s_=s_, blk=blk, j=j: e.matmul(sc[s_][:, :], lhsT=kT[h][0:krows, blk * 128:(blk + 1) * 128], rhs=qT[h][qbuf][0:krows, :],
                                                               start=True, stop=(j < 0)),
                 reads=[("kT", h, blk // 4), ("qT", h, qbuf)], writes=[("sc", s_)])
            if j >= 0:
                S.op("pe", lambda e, s_=s_, j=j: e.matmul(sc[s_][:, :], lhsT=ident[:, :], rhs=maskI[:, j, :], start=False, stop=True),
                     reads=["ident", "maskI"], writes=[("sc", s_)])
            b_ap, b_key = bias_fn(blk)
            S.op("act", lambda e, s_=s_, p_=p_, b_ap=b_ap: e.activation(out=pT[p_][:, :], in_=sc[s_][:, :], func=AF.Exp, bias=b_ap, scale=1.0),
                 reads=[("sc", s_)] + b_key, writes=[("pT", p_)])
            S.op("pe", lambda e, a=a, p_=p_, blk=blk: e.matmul(av[a][0:65, :], lhsT=vX[h][:, blk, :], rhs=pT[p_][:, :], start=(blk == blocks[0]), stop=(blk == last)),
                 reads=[("vX", h, blk // 4), ("pT", p_)], writes=[("av", a)])
        finalize(i, h, a, True)

    def finalize(i, h, a, normalize):
        o_ = nxt("ot", 2)
        if normalize:
            S.op("dve", lambda e: e.reciprocal(out=rd[64:65, :], in_=av[a][64:65, :]), reads=[("av", a)], writes=["rd"])
            S.op("act", lambda e: e.copy(out=nm[0:64, :], in_=av[a][0:64, :]), reads=[("av", a)], writes=["nm"])
            a2 = nxt("pacc", 2)
            S.op("pe", lambda e: e.matmul(pacc[a2][0:64, :], lhsT=ones_f[64:65, 0:64], rhs=rd[64:65, :], start=True, stop=True),
                 reads=["ones_f", "rd"], writes=[("pacc", a2)])
            S.op("dve", lambda e: e.tensor_tensor(out=oTs[o_][:, :], in0=nm[0:64, :], in1=pacc[a2][0:64, :], op=ALU.mult),
                 reads=["nm", ("pacc", a2)], writes=[("oTs", o_)])
        else:
            S.op("act", lambda e: e.copy(out=oTs[o_][:, :], in_=av[a][0:64, :]), reads=[("av", a)], writes=[("oTs", o_)])
        S.dma("st_o%d" % o_, lambda e: e.dma_start(out=io["oT"][h * 64:(h + 1) * 64, i * 512:(i + 1) * 512], in_=oTs[o_][:, :]),
              reads=[("oTs", o_)], writes=[("oT", h, i)])

    for i in range(ntile):
        xb = i % NXT
        qbuf = i % 2
        S.dma("ld_x%d" % xb, lambda e, i=i, xb=xb: e.dma_start(out=xt[xb][:, :, :], in_=io["hfull"][i * 512:(i + 1) * 512, :].rearrange("(s p) d -> p s d", p=128)),
              writes=[("xt", xb)])
        for s in range(4):
            b = nxt("hn", 2)
            hb = hn_s[b]
            src = xt[xb][:, s, :]
            S.op("act", lambda e, src=src: e.activation(out=junk[:, :], in_=src, func=AF.Square, accum_out=stat[:, 0:1]),
                 reads=[("xt", xb)], writes=["junk", "stat"])
            S.op("act", lambda e: e.activation(out=stat[:, 1:2], in_=stat[:, 0:1], func=AF.Ln, bias=EPS, scale=1.0 / D), reads=["stat"], writes=["stat"])
            S.op("act", lambda e: e.activation(out=stat[:, 2:3], in_=stat[:, 1:2], func=AF.Exp, scale=-0.5), reads=["stat"], writes=["stat"])
            S.op("dve", lambda e, src=src, hb=hb: e.scalar_tensor_tensor(out=hb[:, :], in0=src, scalar=stat[:, 2:3], in1=gmix[:, :], op0=ALU.mult, op1=ALU.mult),
                 reads=[("xt", xb), "stat", "gmix"], writes=[("hn_s", b)])
            for c in range(8):
                S.op("pe", lambda e, c=c, hb=hb: e.transpose(trp[0][:, c * 128:(c + 1) * 128], hb[:, c * 128:(c + 1) * 128], ident[:, :]),
                     reads=[("hn_s", b), "ident"], writes=["trp"])
            S.op("dve", lambda e, s=s: e.tensor_copy(out=hnT[:, :, s * 128:(s + 1) * 128], in_=trp[0][:].rearrange("p (c t) -> p c t", c=8)),
                 reads=["trp"], writes=["hnT"])
        for s in range(4):
            a = nxt("pacc", 2)
            for c in range(8):
                S.op("pe", lambda e, a=a, c=c, s=s: e.matmul(pacc[a][:, 0:128], lhsT=hnT[:, c, s * 128:(s + 1) * 128], rhs=w_s[:, c, VV:VV + 128], start=(c == 0), stop=(c == 7)),
                     reads=["w_s", "hnT"], writes=[("pacc", a)])
            for h in range(2):
                S.op("dve", lambda e, a=a, h=h, s=s, i=i: e.tensor_copy(out=vX[h][:, 4 * i + s, 0:64], in_=pacc[a][:, h * 64:(h + 1) * 64]),
                     reads=[("pacc", a)], writes=[("vX", h, i)])
        if fox:
            a = nxt("pacc", 2)
            for c in range(8):
                S.op("pe", lambda e, a=a, c=c: e.matmul(pacc[a][0:2, :], lhsT=w_s[:, c, FF:FF + 2], rhs=hnT[:, c, :], start=(c == 0), stop=(c == 7)),
                     reads=["w_s", "hnT"], writes=[("pacc", a)])
            S.op("act", lambda e, a=a: e.activation(out=fE[:, :], in_=pacc[a][0:2, :], func=AF.Exp, bias=nbf[:, 0:1], scale=-1.0),
                 reads=[("pacc", a), "nbf"], writes=["fE"])
            S.op("act", lambda e: e.activation(out=fE[:, :], in_=fE[:, :], func=AF.Ln, bias=1.0, scale=1.0), reads=["fE"], writes=["fE"])
            S.op("dve", lambda e: e.tensor_scalar(out=fE[:, :], in0=fE[:, :], scalar1=-1.0, scalar2=None, op0=ALU.mult), reads=["fE"], writes=["fE"])
            S.op("dve", lambda e: e.tensor_tensor_scan(out=cc[:, :], data0=fE[:, :], data1=fE[:, :], initial=cprev[:, 0:1], op0=ALU.add, op1=ALU.bypass),
                 reads=["fE", "cprev"], writes=["cc"])
            S.op("dve", lambda e: e.tensor_copy(out=cprev[:, :], in_=cc[:, 511:512]), reads=["cc"], writes=["cprev"])
            S.op("dve", lambda e: e.tensor_copy(out=rbf[:, :], in_=cc[:, :]), reads=["cc"], writes=["rbf"])
            a = nxt("pacc", 2)
            for s in range(4):
                S.op("pe", lambda e, a=a, s=s: e.transpose(pacc[a][:, 2 * s:2 * s + 2], cc[0:2, s * 128:(s + 1) * 128], ident_f[0:2, 0:2]),
                     reads=["cc", "ident_f"], writes=[("pacc", a)])
            S.op("dve", lambda e, a=a, i=i: e.tensor_scalar(out=negc[:, 4 * i:4 * i + 4, :], in0=pacc[a][:, 0:8].rearrange("p (s h) -> p s h", h=2),
                                                            scalar1=-1.0, scalar2=None, op0=ALU.mult),
                 reads=[("pacc", a)], writes=[("negc", i)])
            for h in range(2):
                qc, kc = (QA, KA) if h == 0 else (QB, KB)
                a = proj64(qc, i)
                S.op("act", lambda e, a=a, h=h, qbuf=qbuf: e.activation(out=qT[h][qbuf][0:64, :], in_=pacc[a][0:64, :], func=AF.Copy, scale=0.125),
                     reads=[("pacc", a)], writes=[("qT", h, qbuf)])
                S.dma("ld_r%d" % h, lambda e, h=h, qbuf=qbuf: e.dma_start(out=qT[h][qbuf][64:65, :], in_=rbf[h:h + 1, :]), reads=["rbf", ("qT", h, qbuf)], writes=[("qT", h, qbuf)])
                a = proj64(kc, i)
                S.op("act", lambda e, a=a, h=h, i=i: e.copy(out=kT[h][0:64, i * 512:(i + 1) * 512], in_=pacc[a][0:64, :]),
                     reads=[("pacc", a)], writes=[("kT", h, i)])
            for h in range(2):
                attend(i, h, list(range(4 * i + 4)), 65, lambda blk, h=h: (negc[:, blk, h:h + 1], [("negc", blk // 4)]), qbuf)
        if not fox:
            rope_tables(i)
            a = proj64(QA, i)
            S.op("act", lambda e, a=a, qbuf=qbuf: e.activation(out=qT[0][qbuf][0:64, :], in_=pacc[a][0:64, :], func=AF.Copy, scale=0.125),
                 reads=[("pacc", a)], writes=[("qT", 0, qbuf)])
            a = proj64(KA, i)
            S.op("act", lambda e, a=a, i=i: e.copy(out=kT[0][0:64, i * 512:(i + 1) * 512], in_=pacc[a][0:64, :]), reads=[("pacc", a)], writes=[("kT", 0, i)])
            a1 = proj64(QB, i)
            a2 = proj64(QP, i)
            rope_apply(a1, a2, qrot, "qrot")
            S.op("act", lambda e, qbuf=qbuf: e.activation(out=qT[1][qbuf][0:64, :], in_=qrot[:, :], func=AF.Copy, scale=0.125),
                 reads=["qrot"], writes=[("qT", 1, qbuf)])
            a1 = proj64(KB, i)
            a2 = proj64(KP, i)
            rope_apply(a1, a2, krot, "krot")
            S.op("act", lambda e, i=i: e.copy(out=kT[1][0:64, i * 512:(i + 1) * 512], in_=krot[:, :]), reads=["krot"], writes=[("kT", 1, i)])
            S.op("dve", lambda e, i=i: e.tensor_reduce(out=kmT[:, 2 * i:2 * i + 2], in_=krot[:, :].rearrange("p (n k) -> p n k", n=2), axis=AX.X, op=ALU.add),
                 reads=["krot", "kmT"], writes=["kmT"])
            S.op("dve", lambda e, i=i: e.tensor_scalar(out=kmT[:, 2 * i:2 * i + 2], in0=kmT[:, 2 * i:2 * i + 2], scalar1=1.0 / 256, scalar2=None, op0=ALU.mult),
                 reads=["kmT"], writes=["kmT"])
            moba_gate(i, qbuf)
            sb_attend(i, qbuf)
            attend(i, 1, list(range(4 * i + 4)), 128, lambda blk: (0.0, []), qbuf)
    if "dbg_negc" in io:
        S.dma("dbg", lambda e: [e.dma_start(out=io["dbg_negc"], in_=negc[:].rearrange("p b h -> p (b h)")),
                                e.dma_start(out=io["dbg_q"], in_=qT[0][(ntile - 1) % 2][:, :]),
                                e.dma_start(out=io["dbg_k"], in_=kT[0][:, 0:1024]),
                                e.dma_start(out=io["dbg_cc"], in_=cc[:, :])],
              reads=[("negc", i_) for i_ in range(ntile)] + [("qT", 0, (ntile - 1) % 2), ("kT", 0, 0), ("kT", 0, 1), "cc"], n=4)


def _masks():
    s_ = np.arange(128)[:, None, None]
    j_ = np.arange(4)[None, :, None]
    t_ = np.arange(512)[None, None, :]
    mi = ((128 * j_ + s_) <= t_).astype(np.float32)
    ms = ((128 * j_ + s_) < t_).astype(np.float32)
    return mi, ms


def build_mixer(kind, ntile=NTILE):
    nc = bass.Bass("TRN2", target_bir_lowering=False)
    fox = kind == "fox"
    with contextlib.ExitStack() as st:
        cx = Ctx(nc, st)
        io = {
            "hfull": cx.din("hfull", [SEQ, D], F32),
            "g_mix": cx.din("g_mix", [D], F32),
            "w": cx.din("w", [128, 8, 386 if fox else 512], F32),
            "ident": cx.din("ident", [128, 128], F32),
            "maskI": cx.din("maskI", [128, 4, 512], F32),
            "oT": cx.dout("oT", [128, SEQ], BF16),
        }
        if fox:
            io["bf"] = cx.din("bf", [2, 1], F32)
            if DEBUG:
                io["dbg_negc"] = cx.dout("dbg_negc", [128, NBLK * 2], F32)
                io["dbg_q"] = cx.dout("dbg_q", [128, 512], BF16)
                io["dbg_k"] = cx.dout("dbg_k", [128, 1024], BF16)
                io["dbg_cc"] = cx.dout("dbg_cc", [2, 512], F32)
        else:
            io["maskS"] = cx.din("maskS", [128, 4, 512], F32)
            io["pos"] = cx.din("pos", [SEQ], I32)
            io["invf"] = cx.din("invf", [64, 1], F32)
            io["sgn"] = cx.din("sgn", [64, 1], F32)
            io["koh"] = cx.din("koh", [64, SEQ], F32)
            io["tri"] = cx.din("tri", [128, 256], F32)
        emit_mixer(cx, io, kind, ntile)
        cx.S.emit(final_wait_streams=["st_o0", "st_o1"] + (["dbg"] if (DEBUG and fox) else []))
    return nc


def run_fox(nc, h, g_mix, w_in, b_f):
    mi, ms = _masks()
    in_maps = []
    for cid in range(NCORES):
        hA, hB = 2 * cid, 2 * cid + 1
        cols = []
        for hh in (hA, hB):
            cols.append(w_in[:, hh * 64:(hh + 1) * 64])
            cols.append(w_in[:, D + hh * 64:D + (hh + 1) * 64])
        cols.append(w_in[:, 2 * D + hA * 64:2 * D + (hA + 1) * 64])
        cols.append(w_in[:, 2 * D + hB * 64:2 * D + (hB + 1) * 64])
        cols.append(w_in[:, 3 * D + hA:3 * D + hA + 1])
        cols.append(w_in[:, 3 * D + hB:3 * D + hB + 1])
        wc = np.concatenate(cols, axis=1)
        in_maps.append({
            "hfull": h, "g_mix": np.ascontiguousarray(g_mix, dtype=np.float32),
            "w": np.ascontiguousarray(wc.reshape(8, 128, -1).transpose(1, 0, 2), dtype=np.float32),
            "ident": np.eye(128, dtype=np.float32), "maskI": (mi - 1.0) * 30000.0,
            "bf": np.ascontiguousarray(b_f[[hA, hB]].reshape(2, 1), dtype=np.float32),
        })
    res = run_bass_kernel_spmd(nc, in_maps, core_ids=list(range(NCORES)))
    if DEBUG:
        global LAST
        LAST = res.results
    return np.concatenate([r["oT"] for r in res.results], axis=0)


def run_ab(nc, h, g_mix, w_in, positions):
    mi, ms = _masks()
    half = 32
    inv_freq = (10000.0 ** (-np.arange(half, dtype=np.float32) / half)).astype(np.float32)
    invf = np.concatenate([inv_freq, inv_freq]).reshape(64, 1).astype(np.float32)
    sgn = np.concatenate([-np.ones(32), np.ones(32)]).reshape(64, 1).astype(np.float32)
    koh = np.zeros((64, SEQ), np.float32)
    for n in range(64):
        koh[n, n * 256:(n + 1) * 256] = 30000.0
    jj = np.arange(128)[:, None]
    ss = np.arange(128)[None, :]
    tri = np.concatenate([-(jj > ss).astype(np.float32), -(jj <= ss).astype(np.float32)], axis=1)
    perm = np.concatenate([np.arange(32, 64), np.arange(0, 32)])
    in_maps = []
    for cid in range(NCORES):
        hA, hB = cid, 8 + cid
        qA = w_in[:, hA * 64:(hA + 1) * 64]
        kA = w_in[:, D + hA * 64:D + (hA + 1) * 64]
        qB = w_in[:, hB * 64:(hB + 1) * 64]
        kB = w_in[:, D + hB * 64:D + (hB + 1) * 64]
        vA = w_in[:, 2 * D + hA * 64:2 * D + (hA + 1) * 64]
        vB = w_in[:, 2 * D + hB * 64:2 * D + (hB + 1) * 64]
        wc = np.concatenate([qA, kA, qB, kB, vA, vB, qB[:, perm], kB[:, perm]], axis=1)
        in_maps.append({
            "hfull": h, "g_mix": np.ascontiguousarray(g_mix, dtype=np.float32),
            "w": np.ascontiguousarray(wc.reshape(8, 128, -1).transpose(1, 0, 2), dtype=np.float32),
            "ident": np.eye(128, dtype=np.float32), "maskI": (mi - 1.0) * 30000.0, "maskS": ms,
            "pos": np.ascontiguousarray(positions.reshape(-1), dtype=np.int32),
            "invf": invf, "sgn": sgn, "koh": koh, "tri": tri,
        })
    res = run_bass_kernel_spmd(nc, in_maps, core_ids=list(range(NCORES)))
    out = np.zeros((D, SEQ), dtype=res.results[0]["oT"].dtype)
    for cid in range(NCORES):
        r = res.results[cid]["oT"]
        out[cid * 64:(cid + 1) * 64] = r[0:64]
        out[(8 + cid) * 64:(9 + cid) * 64] = r[64:128]
    return out


_CACHE = {}


def _get(name, fn):
    if name not in _CACHE:
        _CACHE[name] = fn()
    return _CACHE[name]


def kernel(x, mem, positions, norm_mix_g, norm_xa_g, norm_mem_g, norm_ffn_g,
           ab_w_in, ab_w_out, fox_w_in, fox_b_f, fox_w_out,
           xa_w_q, xa_w_kv, xa_w_out, ffn_w_up, ffn_conv_w, ffn_conv_b, ffn_w_down,
           final_norm_g):
    a = lambda v: np.asarray(v)
    x, mem, positions = a(x), a(mem), a(positions)
    h0 = np.ascontiguousarray(x[0], dtype=np.float32)
    oT0 = run_ab(_get("ab", lambda: build_mixer("ab")), h0, a(norm_mix_g)[0], a(ab_w_in)[0], positions)
    w0 = rest_weights(0, a(ab_w_out)[0], a(xa_w_q), a(xa_w_kv), a(xa_w_out), a(ffn_w_up), a(ffn_conv_w), a(ffn_conv_b),
                      a(ffn_w_down), a(norm_xa_g), a(norm_mem_g), a(norm_ffn_g), a(final_norm_g), mem)
    h1 = run_rest(_get("r0", lambda: build_rest(False)), h0, oT0, w0)
    oT1 = run_fox(_get("fox", lambda: build_mixer("fox")), h1, a(norm_mix_g)[1], a(fox_w_in)[0], a(fox_b_f)[0])
    w1 = rest_weights(1, a(fox_w_out)[0], a(xa_w_q), a(xa_w_kv), a(xa_w_out), a(ffn_w_up), a(ffn_conv_w), a(ffn_conv_b),
                      a(ffn_w_down), a(norm_xa_g), a(norm_mem_g), a(norm_ffn_g), a(final_norm_g), mem)
    out = run_rest(_get("r1", lambda: build_rest(True)), h1, oT1, w1)
    return np.ascontiguousarray(out[None].astype(np.float32))
```

```python
import contextlib
import numpy as np
import ml_dtypes
import concourse.bass as bass
import concourse.mybir as mybir
from concourse.bass_utils import run_bass_kernel_spmd

F32 = mybir.dt.float32
BF16 = mybir.dt.bfloat16
I32 = mybir.dt.int32
AF = mybir.ActivationFunctionType
ALU = mybir.AluOpType
AX = mybir.AxisListType

NCORES = 8
D = 1024
SEQ = 16384
NT = SEQ // NCORES
DFF = 2816
NPAIR = DFF // 128
MEM = 256
EPS = 1e-6
ENGS = ("pe", "act", "dve", "pool", "sp")
SEM_EPOCH = 24000
DEBUG = False
LAST = None


class Op:
    __slots__ = ("eng", "fn", "deps", "inc", "cnt", "dma", "ndma", "dcnt")

    def __init__(self, eng, fn, dma=None, ndma=1):
        self.eng = eng
        self.fn = fn
        self.deps = set()
        self.inc = False
        self.cnt = 0
        self.dma = dma
        self.ndma = ndma
        self.dcnt = 0


class Sched:
    def __init__(self, nc, same_engine_sync=("act", "dve", "pool")):
        self.nc = nc
        self.ops = {e: [] for e in ENGS}
        self.last_w = {}
        self.readers = {}
        self.streams = {}
        self.cc_streams = set()
        self.same = set(same_engine_sync)
        self.persist_sems = False
        self.tag = ""

    def _add(self, op, reads, writes):
        for b in reads:
            w = self.last_w.get(b)
            if w is not None:
                op.deps.add(w)
        for b in writes:
            w = self.last_w.get(b)
            if w is not None:
                op.deps.add(w)
            for r in self.readers.get(b, ()):
                op.deps.add(r)
        op.deps.discard(op)
        for b in reads:
            self.readers.setdefault(b, []).append(op)
        for b in writes:
            self.last_w[b] = op
            self.readers[b] = []
        self.ops[op.eng].append(op)
        return op

    def op(self, eng, fn, reads=(), writes=()):
        return self._add(Op(eng, fn), reads, writes)

    def dma(self, stream, fn, reads=(), writes=(), eng="sp", n=1):
        op = Op(eng, fn, dma=stream, ndma=n)
        self.streams.setdefault(stream, []).append(op)
        return self._add(op, reads, writes)

    def cc(self, stream, fn, reads=(), writes=()):
        op = Op("pool", fn, dma=stream, ndma=1)
        self.cc_streams.add(stream)
        self.streams.setdefault(stream, []).append(op)
        return self._add(op, reads, writes)

    def emit(self, final_wait_streams=()):
        nc = self.nc
        for e in ENGS:
            for op in self.ops[e]:
                for d in list(op.deps):
                    if d.dma is not None:
                        continue
                    if d.eng == op.eng and op.dma is None and d.eng not in self.same:
                        op.deps.discard(d)
                        continue
                    d.inc = True
        nep = {}
        for e in ENGS:
            c = 0
            for op in self.ops[e]:
                if op.dma is None and op.inc:
                    c += 1
                op.cnt = c
            nep[e] = max(1, -(-c // SEM_EPOCH))
        total = {}
        for s, lst in self.streams.items():
            c = 0
            for op in lst:
                c += (1 if s in self.cc_streams else 16) * op.ndma
                op.dcnt = c
            total[s] = c
        with contextlib.ExitStack() as st:
            tag = self.tag
            if self.persist_sems:
                esem = {e: [nc.alloc_semaphore("s%s_%s%d" % (tag, e, k)) for k in range(nep[e])] for e in ENGS}
                ssem = {s: nc.alloc_semaphore("d%s_%s" % (tag, s)) for s in self.streams}
            else:
                esem = {e: [st.enter_context(nc.semaphore("s_%s%d" % (e, k))) for k in range(nep[e])] for e in ENGS}
                ssem = {s: st.enter_context(nc.semaphore("d_" + s)) for s in self.streams}
            block = st.enter_context(nc.Block())

            def run(e, eng_obj):
                seen = {}
                for op in self.ops[e]:
                    need = {}
                    for d in op.deps:
                        if d.dma is not None:
                            key, val = ("d", d.dma), d.dcnt
                        else:
                            key, val = ("e", d.eng), d.cnt
                        if val > need.get(key, 0):
                            need[key] = val
                    for key, val in need.items():
                        if seen.get(key, 0) >= val:
                            continue
                        seen[key] = val
                        if key[0] == "d":
                            eng_obj.wait_ge(ssem[key[1]], val)
                        else:
                            k = (val - 1) // SEM_EPOCH
                            eng_obj.wait_ge(esem[key[1]][k], val - k * SEM_EPOCH)
                    ins = op.fn(eng_obj)
                    if op.dma is not None:
                        if not isinstance(ins, (list, tuple)):
                            ins = [ins]
                        assert len(ins) == op.ndma, (len(ins), op.ndma)
                        for i_ in ins:
                            if op.dma in self.cc_streams:
                                i_.then_inc(ssem[op.dma])
                            else:
                                i_.then_inc(ssem[op.dma], 16)
                    elif op.inc:
                        ins.then_inc(esem[e][(op.cnt - 1) // SEM_EPOCH], 1)
                if e == "sp":
                    for s in final_wait_streams:
                        eng_obj.wait_ge(ssem[s], total[s])

            @block.tensor
            def _(eng):
                run("pe", eng)

            @block.scalar
            def _(eng):
                run("act", eng)

            @block.vector
            def _(eng):
                run("dve", eng)

            @block.gpsimd
            def _(eng):
                run("pool", eng)

            @block.sync
            def _(eng):
                run("sp", eng)


class Ctx:
    def __init__(self, nc, st, tag=""):
        self.nc = nc
        self.st = st
        self.S = Sched(nc)
        self.tag = tag
        if tag:
            self.S.persist_sems = True
            self.S.tag = tag

    def sb(self, name, shape, dt):
        return self.st.enter_context(self.nc.sbuf_tensor("sb%s_%s" % (self.tag, name), list(shape), dt))

    def ps(self, name, shape, dt):
        return self.st.enter_context(self.nc.psum_tensor("ps%s_%s" % (self.tag, name), list(shape), dt))

    def din(self, name, shape, dt):
        return self.nc.dram_tensor(name, list(shape), dt, kind="ExternalInput").ap()

    def dout(self, name, shape, dt):
        return self.nc.dram_tensor(name, list(shape), dt, kind="ExternalOutput").ap()

    def dint(self, name, shape, dt):
        return self.nc.dram_tensor("di%s_%s" % (self.tag, name), list(shape), dt, kind="Internal").ap()


def emit_rest(cx, io, final):
    nc, S = cx.nc, cx.S
    sb, ps = cx.sb, cx.ps
    NSUB = NT // 128
    NTT = NT // 512
    ident = sb("ident", [128, 128], BF16)
    ones_f = sb("ones_f", [128, 64], F32)
    gxa = sb("gxa", [128, D], F32)
    gffn = sb("gffn", [128, D], F32)
    gmem = sb("gmem", [128, D], F32)
    gfin = sb("gfin", [128, D], F32) if final else None
    cw = sb("cw", [128, 44, 3], F32)
    cb = sb("cb", [128, 44], F32)
    flag = sb("flag", [128, 1], F32)
    wo_s = sb("wo_s", [128, 8, 1024], BF16)
    wq_s = sb("wq_s", [128, 8, 256], BF16)
    wkv_s = sb("wkv_s", [128, 8, 512], BF16)
    wxo_s = sb("wxo_s", [64, 4, 1024], BF16)
    kxT = sb("kxT", [64, 4, MEM], BF16)
    vxa = sb("vxa", [128, 2, 4, 65], BF16)
    ucarry = sb("ucarry", [128, 44, 2], F32)
    wup_b = cx.dint("wup_b", [NPAIR, 128, 8 * 256], BF16)
    wdn_b = cx.dint("wdn_b", [NPAIR, 2, 128, 512], BF16)
    NUP = 5
    NDN = 8
    wup_r = [sb("wup_r%d" % i, [128, 8, 256], BF16) for i in range(NUP)]
    wdn_r = [sb("wdn_r%d" % i, [128, 512], BF16) for i in range(NDN)]
    hT = [sb("hT%d" % i, [128, 4, D], F32) for i in range(2)]
    oTs = [sb("oTs%d" % i, [128, 8, 512], BF16) for i in range(2)]
    hn_s = [sb("hn_s%d" % i, [128, D], BF16) for i in range(2)]
    junk = sb("junk", [128, D], BF16)
    stat = sb("stat", [128, 8], F32)
    hnT = sb("hnT", [128, 8, 512], BF16)
    qxT = sb("qxT", [64, 4, 512], BF16)
    pxT = [sb("pxT%d" % i, [128, 512], BF16) for i in range(4)]
    nm = sb("nm", [128, 512], F32)
    rd = sb("rd", [128, 512], F32)
    oxT = sb("oxT", [64, 4, 512], BF16)
    Yg = [sb("Yg%d" % i, [128, 512], F32) for i in range(2)]
    Yv = [sb("Yv%d" % i, [128, 512], F32) for i in range(2)]
    gT = sb("gT", [128, NPAIR, 512], BF16)
    trp = [ps("trp%d" % i, [128, 1024], BF16) for i in range(2)]
    acc = [ps("acc%d" % i, [128, 512], F32) for i in range(6)]
    rot = {"acc": 0, "tr": 0, "px": 0}

    def nacc():
        rot["acc"] = (rot["acc"] + 1) % 6
        return rot["acc"]

    def ntr():
        rot["tr"] = (rot["tr"] + 1) % 2
        return rot["tr"]

    def pdma(stream, out, in_, writes, reads=()):
        S.dma(stream, lambda e: nc.gpsimd.dma_start(out=out, in_=in_), reads=reads, writes=writes, eng="pool")

    pdma("c_id", ident[:], io["ident"], ["ident"])
    pdma("c_wo", wo_s[:], io["w_out"], ["wo_s"])
    pdma("c_wq", wq_s[:], io["wq"], ["wq_s"])
    pdma("c_wkv", wkv_s[:], io["wkv"], ["wkv_s"])
    pdma("c_wxo", wxo_s[:], io["wxo"], ["wxo_s"])
    S.dma("c_g", lambda e: [e.dma_start(out=gxa[:], in_=io["g_xa"].partition_broadcast(128)),
                            e.dma_start(out=gffn[:], in_=io["g_ffn"].partition_broadcast(128)),
                            e.dma_start(out=gmem[:], in_=io["g_mem"].partition_broadcast(128)),
                            e.dma_start(out=cw[:], in_=io["cw"]),
                            e.dma_start(out=cb[:], in_=io["cb"]),
                            e.dma_start(out=flag[:], in_=io["flag"])],
          writes=["gxa", "gffn", "gmem", "cw", "cb", "flag"], n=6)
    if final:
        S.dma("c_gf", lambda e: e.dma_start(out=gfin[:], in_=io["g_fin"].partition_broadcast(128)), writes=["gfin"])
    for g in range(4):
        js = list(range(g * 6, min(NPAIR, g * 6 + 6)))
        S.dma("c_up%d" % g, lambda e, js=js: [nc.gpsimd.dma_start(out=wup_b[j], in_=io["wup"][j]) for j in js],
              writes=[("wup_b", j) for j in js], eng="pool", n=len(js))
    for g in range(4):
        js = list(range(g * 6, min(NPAIR, g * 6 + 6)))
        S.dma("c_dn%d" % g, lambda e, js=js: [nc.gpsimd.dma_start(out=wdn_b[j], in_=io["wdn"][j]) for j in js],
              writes=[("wdn_b", j) for j in js], eng="pool", n=len(js))
    S.op("dve", lambda e: e.memset(ones_f[:], 1.0), writes=["ones_f"])
    S.op("dve", lambda e: e.memset(vxa[:], 1.0), writes=["vxa"])
    oidx = sb("oidx", [128, 5, 8], I32)
    hidx = sb("hidx", [128, 17], I32)
    S.dma("c_ix", lambda e: [e.dma_start(out=oidx[:], in_=io["oidx"])] + ([e.dma_start(out=hidx[:], in_=io["hidx"])] if "hidx" in io else []),
          writes=["oidx", "hidx"], n=(2 if "hidx" in io else 1))

    def rmsnorm_to_T(src_ap, src_key, g_tile, g_key, dstT, dst_key, col0, ncol=128, nrows=128):
        b = ntr()
        hb = hn_s[b]
        S.op("act", lambda e: e.activation(out=junk[:nrows, :], in_=src_ap, func=AF.Square, accum_out=stat[:nrows, 0:1]),
             reads=[src_key], writes=["junk", "stat"])
        S.op("act", lambda e: e.activation(out=stat[:nrows, 1:2], in_=stat[:nrows, 0:1], func=AF.Ln, bias=EPS, scale=1.0 / D),
             reads=["stat"], writes=["stat"])
        S.op("act", lambda e: e.activation(out=stat[:nrows, 2:3], in_=stat[:nrows, 1:2], func=AF.Exp, scale=-0.5),
             reads=["stat"], writes=["stat"])
        S.op("dve", lambda e: e.scalar_tensor_tensor(out=hb[:nrows, :], in0=src_ap, scalar=stat[:nrows, 2:3], in1=g_tile[:nrows, :],
                                                     op0=ALU.mult, op1=ALU.mult),
             reads=[src_key, "stat", g_key], writes=[("hn_s", b)])
        for c in range(8):
            S.op("pe", lambda e, c=c: e.transpose(trp[b][:, c * 128:c * 128 + nrows], hb[:nrows, c * 128:(c + 1) * 128], ident[:nrows, :nrows]),
                 reads=[("hn_s", b), "ident"], writes=[("trp", b)])
        S.op("act", lambda e: e.copy(out=dstT[:, :, col0:col0 + nrows],
                                     in_=trp[b][:].rearrange("p (c t) -> p c t", c=8)[:, :, 0:nrows]),
             reads=[("trp", b)], writes=[dst_key])

    memT = hnT
    mem_s = hT[1]
    S.dma("ld_mem", lambda e: e.dma_start(out=mem_s[:, 0:2, :], in_=io["mem"].rearrange("(s p) d -> p s d", p=128)),
          writes=[("hT", 1)])
    for s in range(2):
        rmsnorm_to_T(mem_s[:, s, :], ("hT", 1), gmem, "gmem", memT, "hnT", s * 128)
    for hd in range(4):
        a = nacc()
        for c in range(8):
            S.op("pe", lambda e, a=a, c=c, hd=hd: e.matmul(acc[a][0:64, 0:MEM], lhsT=wkv_s[:, c, hd * 64:(hd + 1) * 64], rhs=memT[:, c, 0:MEM],
                                                            start=(c == 0), stop=(c == 7)),
                 reads=["wkv_s", "hnT"], writes=[("acc", a)])
        S.op("act", lambda e, a=a, hd=hd: e.copy(out=kxT[:, hd, :], in_=acc[a][0:64, 0:MEM]), reads=[("acc", a)], writes=["kxT"])
    for mc in range(2):
        a = nacc()
        for c in range(8):
            S.op("pe", lambda e, a=a, c=c, mc=mc: e.matmul(acc[a][:, 0:256], lhsT=memT[:, c, mc * 128:(mc + 1) * 128], rhs=wkv_s[:, c, 256:512],
                                                            start=(c == 0), stop=(c == 7)),
                 reads=["wkv_s", "hnT"], writes=[("acc", a)])
        S.op("act", lambda e, a=a, mc=mc: e.copy(out=vxa[:, mc, :, 0:64], in_=acc[a][:, 0:256].rearrange("p (h d) -> p h d", h=4)),
             reads=[("acc", a)], writes=["vxa"])

    upq = {"n": 0}
    dnq = {"n": 0}

    def load_up(j):
        slot = upq["n"] % NUP
        upq["n"] += 1
        S.dma("r_up%d" % slot, lambda e: e.dma_start(out=wup_r[slot][:], in_=wup_b[j].rearrange("p (c n) -> p c n", c=8)),
              reads=[("wup_b", j)], writes=[("wup_r", slot)])
        return slot

    def load_dn(j, half):
        slot = dnq["n"] % NDN
        dnq["n"] += 1
        S.dma("r_dn%d" % slot, lambda e: e.dma_start(out=wdn_r[slot][:], in_=wdn_b[j, half]),
              reads=[("wdn_b", j)], writes=[("wdn_r", slot)])
        return slot

    def front(buf, nsub, row0):
        ntok = nsub * 128
        hb = hT[buf]
        k0 = row0 // 128
        t5 = 0 if row0 == 0 else 1 + (row0 - 128) // 512
        oc0 = 384 if row0 == 0 else 0
        if "hidx" in io:
            S.dma("ld_h%d" % buf, lambda e: [nc.gpsimd.indirect_dma_start(out=hb[:, s_, :], out_offset=None, in_=io["hin"],
                                                                        in_offset=bass.IndirectOffsetOnAxis(ap=hidx[:, k0 + s_:k0 + s_ + 1], axis=0))
                                             for s_ in range(nsub)],
                  reads=["hidx"], writes=[("hT", buf)], eng="pool", n=nsub)
        else:
            S.dma("ld_h%d" % buf, lambda e: e.dma_start(out=hb[:, 0:nsub, :], in_=io["hin"][row0:row0 + ntok, :].rearrange("(s p) d -> p s d", p=128)),
                  writes=[("hT", buf)])
        S.dma("ld_o%d" % buf, lambda e: [nc.gpsimd.indirect_dma_start(out=oTs[buf][:, r_, :], out_offset=None, in_=io["oall"],
                                                                    in_offset=bass.IndirectOffsetOnAxis(ap=oidx[:, t5, r_:r_ + 1], axis=0))
                                         for r_ in range(8)],
              reads=["oidx"], writes=[("oTs", buf)], eng="pool", n=8)
        for s in range(nsub):
            for half in range(2):
                a = nacc()
                for c in range(8):
                    S.op("pe", lambda e, a=a, c=c, s=s, half=half: e.matmul(acc[a][:, :], lhsT=oTs[buf][:, c, oc0 + s * 128:oc0 + (s + 1) * 128],
                                                                            rhs=wo_s[:, c, half * 512:(half + 1) * 512], start=(c == 0), stop=(c == 7)),
                         reads=[("oTs", buf), "wo_s"], writes=[("acc", a)])
                S.op("dve", lambda e, a=a, s=s, half=half: e.tensor_tensor(out=hb[:, s, half * 512:(half + 1) * 512], in0=hb[:, s, half * 512:(half + 1) * 512],
                                                                           in1=acc[a][:, :], op=ALU.add),
                     reads=[("acc", a), ("hT", buf)], writes=[("hT", buf)])
        for s in range(nsub):
            rmsnorm_to_T(hb[:, s, :], ("hT", buf), gxa, "gxa", hnT, "hnT", s * 128)
        for hd in range(4):
            a = nacc()
            for c in range(8):
                S.op("pe", lambda e, a=a, c=c, hd=hd: e.matmul(acc[a][0:64, 0:ntok], lhsT=wq_s[:, c, hd * 64:(hd + 1) * 64], rhs=hnT[:, c, 0:ntok],
                                                                start=(c == 0), stop=(c == 7)),
                     reads=["wq_s", "hnT"], writes=[("acc", a)])
            S.op("act", lambda e, a=a, hd=hd: e.copy(out=qxT[:, hd, 0:ntok], in_=acc[a][0:64, 0:ntok]), reads=[("acc", a)], writes=[("qxT", hd)])
        for hd in range(4):
            pk = []
            for mc in range(2):
                a = nacc()
                S.op("pe", lambda e, a=a, mc=mc, hd=hd: e.matmul(acc[a][:, 0:ntok], lhsT=kxT[:, hd, mc * 128:(mc + 1) * 128], rhs=qxT[:, hd, 0:ntok],
                                                                  start=True, stop=True),
                     reads=["kxT", ("qxT", hd)], writes=[("acc", a)])
                p = rot["px"] = (rot["px"] + 1) % 4
                S.op("act", lambda e, a=a, p=p: e.activation(out=pxT[p][:, 0:ntok], in_=acc[a][:, 0:ntok], func=AF.Exp, scale=0.125),
                     reads=[("acc", a)], writes=[("pxT", p)])
                pk.append(p)
            a = nacc()
            for mc in range(2):
                S.op("pe", lambda e, a=a, mc=mc, hd=hd, p=pk[mc]: e.matmul(acc[a][0:65, 0:ntok], lhsT=vxa[:, mc, hd, :], rhs=pxT[p][:, 0:ntok],
                                                                            start=(mc == 0), stop=(mc == 1)),
                     reads=["vxa", ("pxT", pk[mc])], writes=[("acc", a)])
            S.op("dve", lambda e, a=a: e.reciprocal(out=rd[64:65, 0:ntok], in_=acc[a][64:65, 0:ntok]), reads=[("acc", a)], writes=["rd"])
            S.op("act", lambda e, a=a: e.copy(out=nm[0:64, 0:ntok], in_=acc[a][0:64, 0:ntok]), reads=[("acc", a)], writes=["nm"])
            a2 = nacc()
            S.op("pe", lambda e, a2=a2: e.matmul(acc[a2][0:64, 0:ntok], lhsT=ones_f[64:65, 0:64], rhs=rd[64:65, 0:ntok], start=True, stop=True),
                 reads=["ones_f", "rd"], writes=[("acc", a2)])
            S.op("dve", lambda e, a2=a2, hd=hd: e.tensor_tensor(out=oxT[:, hd, 0:ntok], in0=nm[0:64, 0:ntok], in1=acc[a2][0:64, 0:ntok], op=ALU.mult),
                 reads=["nm", ("acc", a2)], writes=[("oxT", hd)])
        for s in range(nsub):
            for half in range(2):
                a = nacc()
                for hd in range(4):
                    S.op("pe", lambda e, a=a, hd=hd, s=s, half=half: e.matmul(acc[a][:, :], lhsT=oxT[:, hd, s * 128:(s + 1) * 128],
                                                                              rhs=wxo_s[:, hd, half * 512:(half + 1) * 512], start=(hd == 0), stop=(hd == 3)),
                         reads=[("oxT", hd), "wxo_s"], writes=[("acc", a)])
                S.op("dve", lambda e, a=a, s=s, half=half: e.tensor_tensor(out=hb[:, s, half * 512:(half + 1) * 512], in0=hb[:, s, half * 512:(half + 1) * 512],
                                                                           in1=acc[a][:, :], op=ALU.add),
                     reads=[("acc", a), ("hT", buf)], writes=[("hT", buf)])
        for s in range(nsub):
            rmsnorm_to_T(hb[:, s, :], ("hT", buf), gffn, "gffn", hnT, "hnT", s * 128)

    front(1, 1, 0)
    for j in range(NPAIR):
        slot = load_up(j)
        for gv in range(2):
            grp = j + gv * NPAIR
            a = nacc()
            for c in range(8):
                S.op("pe", lambda e, a=a, c=c, gv=gv, slot=slot: e.matmul(acc[a][:, 0:2], lhsT=wup_r[slot][:, c, gv * 128:(gv + 1) * 128], rhs=hnT[:, c, 126:128],
                                                                           start=(c == 0), stop=(c == 7)),
                     reads=[("wup_r", slot), "hnT"], writes=[("acc", a)])
            S.op("dve", lambda e, a=a, grp=grp: e.tensor_scalar(out=ucarry[:, grp, :], in0=acc[a][:, 0:2], scalar1=flag[:, 0:1], scalar2=None, op0=ALU.mult),
                 reads=[("acc", a), "flag"], writes=[("ucarry", grp)])

    for tt in range(NTT):
        buf = tt % 2
        hb = hT[buf]
        front(buf, 4, 128 + tt * 512)
        for j in range(NPAIR):
            slot = load_up(j)
            yb = j % 2
            for gv in range(2):
                grp = j + gv * NPAIR
                Y = (Yg if gv == 0 else Yv)[yb]
                ykey = ("Y", gv, yb)
                a = nacc()
                for c in range(8):
                    S.op("pe", lambda e, a=a, c=c, gv=gv, slot=slot: e.matmul(acc[a][:, :], lhsT=wup_r[slot][:, c, gv * 128:(gv + 1) * 128], rhs=hnT[:, c, :],
                                                                               start=(c == 0), stop=(c == 7)),
                         reads=[("wup_r", slot), "hnT"], writes=[("acc", a)])
                S.op("act", lambda e, a=a, grp=grp, Y=Y: e.activation(out=Y[:, :], in_=acc[a][:, :], func=AF.Identity, bias=cb[:, grp:grp + 1], scale=cw[:, grp, 2:3]),
                     reads=[("acc", a), "cw", "cb"], writes=[ykey])
                S.op("dve", lambda e, a=a, grp=grp, Y=Y: e.scalar_tensor_tensor(out=Y[:, 1:512], in0=acc[a][:, 0:511], scalar=cw[:, grp, 1:2], in1=Y[:, 1:512],
                                                                                op0=ALU.mult, op1=ALU.add),
                     reads=[("acc", a), "cw", ykey], writes=[ykey])
                S.op("dve", lambda e, a=a, grp=grp, Y=Y: e.scalar_tensor_tensor(out=Y[:, 2:512], in0=acc[a][:, 0:510], scalar=cw[:, grp, 0:1], in1=Y[:, 2:512],
                                                                                op0=ALU.mult, op1=ALU.add),
                     reads=[("acc", a), "cw", ykey], writes=[ykey])
                S.op("dve", lambda e, grp=grp, Y=Y: e.scalar_tensor_tensor(out=Y[:, 0:1], in0=ucarry[:, grp, 1:2], scalar=cw[:, grp, 1:2], in1=Y[:, 0:1],
                                                                            op0=ALU.mult, op1=ALU.add),
                     reads=[("ucarry", grp), "cw", ykey], writes=[ykey])
                S.op("dve", lambda e, grp=grp, Y=Y: e.scalar_tensor_tensor(out=Y[:, 0:2], in0=ucarry[:, grp, 0:2], scalar=cw[:, grp, 0:1], in1=Y[:, 0:2],
                                                                            op0=ALU.mult, op1=ALU.add),
                     reads=[("ucarry", grp), "cw", ykey], writes=[ykey])
                S.op("dve", lambda e, a=a, grp=grp: e.tensor_copy(out=ucarry[:, grp, :], in_=acc[a][:, 510:512]),
                     reads=[("acc", a)], writes=[("ucarry", grp)])
            S.op("act", lambda e, yb=yb: e.activation(out=Yg[yb][:, :], in_=Yg[yb][:, :], func=AF.Silu),
                 reads=[("Y", 0, yb)], writes=[("Y", 0, yb)])
            S.op("dve", lambda e, yb=yb, j=j: e.tensor_tensor(out=gT[:, j, :], in0=Yg[yb][:, :], in1=Yv[yb][:, :], op=ALU.mult),
                 reads=[("Y", 0, yb), ("Y", 1, yb)], writes=[("gT", j)])
        for half in range(2):
            accs = [nacc() for _ in range(4)]
            for j in range(NPAIR):
                slot = load_dn(j, half)
                for s in range(4):
                    a = accs[s]
                    S.op("pe", lambda e, a=a, j=j, s=s, slot=slot: e.matmul(acc[a][:, :], lhsT=gT[:, j, s * 128:(s + 1) * 128], rhs=wdn_r[slot][:, :],
                                                                             start=(j == 0), stop=(j == NPAIR - 1)),
                         reads=[("gT", j), ("wdn_r", slot)], writes=[("acc", a)])
            for s in range(4):
                a = accs[s]
                S.op("dve", lambda e, a=a, s=s, half=half, hb=hb: e.tensor_tensor(out=hb[:, s, half * 512:(half + 1) * 512], in0=hb[:, s, half * 512:(half + 1) * 512],
                                                                           in1=acc[a][:, :], op=ALU.add),
                     reads=[("acc", a), ("hT", buf)], writes=[("hT", buf)])
        if final:
            for s in range(4):
                S.op("act", lambda e, s=s, hb=hb: e.activation(out=junk[:, :], in_=hb[:, s, :], func=AF.Square, accum_out=stat[:, 4:5]),
                     reads=[("hT", buf)], writes=["junk", "stat2"])
                S.op("act", lambda e: e.activation(out=stat[:, 5:6], in_=stat[:, 4:5], func=AF.Ln, bias=EPS, scale=1.0 / D),
                     reads=["stat2"], writes=["stat2"])
                S.op("act", lambda e: e.activation(out=stat[:, 6:7], in_=stat[:, 5:6], func=AF.Exp, scale=-0.5),
                     reads=["stat2"], writes=["stat2"])
                S.op("dve", lambda e, s=s, hb=hb: e.scalar_tensor_tensor(out=hb[:, s, :], in0=hb[:, s, :], scalar=stat[:, 6:7], in1=gfin[:, :],
                                                                  op0=ALU.mult, op1=ALU.mult),
                     reads=[("hT", buf), "stat2", "gfin"], writes=[("hT", buf)])
        S.dma("st_h", lambda e, tt=tt, hb=hb: e.dma_start(out=io["hout"][tt * 512:(tt + 1) * 512, :].rearrange("(s p) d -> p s d", p=128), in_=hb[:, :, :]),
              reads=[("hT", buf)], writes=[("hout", tt)])


def rest_weights(layer, w_out, xa_w_q, xa_w_kv, xa_w_out, ffn_w_up, ffn_conv_w, ffn_conv_b, ffn_w_down,
                 norm_xa_g, norm_mem_g, norm_ffn_g, final_norm_g, mem):
    f = np.float32
    c = np.ascontiguousarray
    wup = ffn_w_up[layer].reshape(8, 128, 2, NPAIR, 128).transpose(3, 1, 0, 2, 4)
    return {
        "w_out": c(w_out.reshape(8, 128, 1024).transpose(1, 0, 2), dtype=f),
        "wq": c(xa_w_q[layer].reshape(8, 128, 256).transpose(1, 0, 2), dtype=f),
        "wkv": c(xa_w_kv[layer].reshape(8, 128, 512).transpose(1, 0, 2), dtype=f),
        "wxo": c(xa_w_out[layer].reshape(4, 64, 1024).transpose(1, 0, 2), dtype=f),
        "wup": c(wup.reshape(NPAIR, 128, 8 * 256), dtype=f),
        "wdn": c(ffn_w_down[layer].reshape(NPAIR, 128, 2, 512).transpose(0, 2, 1, 3), dtype=f),
        "cw": c(ffn_conv_w[layer].reshape(3, 44, 128).transpose(2, 1, 0), dtype=f),
        "cb": c(ffn_conv_b[layer].reshape(44, 128).T, dtype=f),
        "g_xa": c(norm_xa_g[layer], dtype=f),
        "g_mem": c(norm_mem_g[layer], dtype=f),
        "g_ffn": c(norm_ffn_g[layer], dtype=f),
        "g_fin": c(final_norm_g, dtype=f),
        "mem": c(mem[0], dtype=f),
        "ident": np.eye(128, dtype=f),
    }


NTILE = SEQ // 512
NBLK = SEQ // 128
SB_WIN = 3


def emit_mixer(cx, io, kind, ntile=NTILE):
    nc, S = cx.nc, cx.S
    sb, ps = cx.sb, cx.ps
    fox = kind == "fox"
    NCOL = 386 if fox else 512
    if fox:
        QA, KA, QB, KB, VV, FF = 0, 64, 128, 192, 256, 384
    else:
        QA, KA, QB, KB, VV, QP, KP = 0, 64, 128, 192, 256, 384, 448
    ident = sb("ident", [128, 128], BF16)
    ident_f = sb("ident_f", [128, 128], F32)
    ones_f = sb("ones_f", [128, 64], F32)
    gmix = sb("gmix", [128, D], F32)
    w_s = sb("w_s", [128, 8, NCOL], BF16)
    maskI = sb("maskI", [128, 4, 512], BF16)
    kT = [sb("kT%d" % h, [128, SEQ], BF16) for h in range(2)]
    vX = [sb("vX%d" % h, [128, NBLK, 65], BF16) for h in range(2)]
    qT = [[sb("qT%d_%d" % (h, i), [128, 512], BF16) for i in range(2)] for h in range(2)]
    NXT = 2 if fox else 1
    xt = [sb("xt%d" % i, [128, 4, D], F32) for i in range(NXT)]
    hn_s = [sb("hn_s%d" % i, [128, D], BF16) for i in range(2)]
    junk = sb("junk", [128, D], BF16)
    stat = sb("stat", [128, 8], F32)
    hnT = sb("hnT", [128, 8, 512], BF16)
    NPT = 6 if fox else 4
    pT = [sb("pT%d" % i, [128, 512], BF16) for i in range(NPT)]
    nm = sb("nm", [128, 512], F32)
    rd = sb("rd", [128, 512], F32)
    oTs = [sb("oTs%d" % i, [64, 512], BF16) for i in range(2)]
    trp = [ps("trp%d" % i, [128, 1024], BF16) for i in range(1)]
    pacc = [ps("pacc%d" % i, [128, 512], F32) for i in range(2)]
    NSC = 3 if fox else 2
    sc = [ps("sc%d" % i, [128, 512], F32) for i in range(NSC)]
    av = [ps("av%d" % i, [128, 512], F32) for i in range(2)]
    rot = {"pacc": 0, "sc": 0, "pt": 0, "av": 0, "hn": 0, "ot": 0}

    def nxt(k, n):
        rot[k] = (rot[k] + 1) % n
        return rot[k]

    def pdma(stream, out, in_, writes):
        S.dma(stream, lambda e: nc.gpsimd.dma_start(out=out, in_=in_), writes=writes, eng="pool")

    pdma("c_id", ident[:], io["ident"], ["ident"])
    pdma("c_w", w_s[:], io["w"], ["w_s"])
    pdma("c_mi", maskI[:], io["maskI"], ["maskI"])
    S.dma("c_g", lambda e: [e.dma_start(out=gmix[:], in_=io["g_mix"].partition_broadcast(128)),
                            e.dma_start(out=ident_f[:], in_=io["ident"])], writes=["gmix", "ident_f"], n=2)
    S.op("dve", lambda e: e.memset(ones_f[:], 1.0), writes=["ones_f"])
    for h in range(2):
        S.op("pool", lambda e, h=h: e.memset(vX[h][:], 1.0), writes=[("vX", h, i_) for i_ in range(NTILE)])
    if fox:
        nbf = sb("nbf", [2, 1], F32)
        cprev = sb("cprev", [2, 1], F32)
        fE = sb("fE", [2, 512], F32)
        cc = sb("cc", [2, 512], F32)
        rbf = sb("rbf", [2, 512], BF16)
        negc = sb("negc", [128, NBLK, 2], F32)
        S.dma("c_bf", lambda e: e.dma_start(out=nbf[:], in_=io["bf"]), writes=["nbf"])
        S.op("dve", lambda e: e.tensor_scalar(out=nbf[:], in0=nbf[:], scalar1=-1.0, scalar2=None, op0=ALU.mult), reads=["nbf"], writes=["nbf"])
        S.op("dve", lambda e: e.memset(cprev[:], 0.0), writes=["cprev"])
        for h in range(2):
            S.op("pool", lambda e, h=h: e.memset(kT[h][64:128, :], 1.0), writes=[("kT", h, i_) for i_ in range(NTILE)])


    if not fox:
        maskS = sb("maskS", [128, 4, 512], BF16)
        tri = sb("tri", [128, 256], F32)
        invf = sb("invf", [64, 1], F32)
        sgn = sb("sgn", [64, 1], F32)
        pos_i = sb("pos_i", [64, 512], I32)
        ang = sb("ang", [64, 512], F32)
        kint = sb("kint", [64, 512], I32)
        kflt = sb("kflt", [64, 512], F32)
        msk = sb("msk", [64, 512], F32)
        cosF = sb("cosF", [64, 512], F32)
        sinS = sb("sinS", [64, 512], F32)
        rt1 = sb("rt1", [64, 512], F32)
        qrot = sb("qrot", [64, 512], F32)
        krot = sb("krot", [64, 512], F32)
        kmT = sb("kmT", [64, 64], F32)
        G = sb("G", [128, 64], F32)
        top8 = sb("top8", [128, 8], F32)
        MB = sb("MB", [128, 128], BF16)
        spE = [sb("spE%d" % k, [128, 512], F32) for k in range(2)]
        spM = [sb("spM%d" % k, [128, 512], F32) for k in range(2)]
        lgA = [sb("lgA%d" % k, [128, 512], F32) for k in range(2)]
        tailP = ps("tailP", [128, 512], F32)
        pdma("c_ms", maskS[:], io["maskS"], ["maskS"])
        S.dma("c_ab", lambda e: [e.dma_start(out=tri[:], in_=io["tri"]), e.dma_start(out=invf[:], in_=io["invf"]),
                                 e.dma_start(out=sgn[:], in_=io["sgn"])], writes=["tri", "invf", "sgn"], n=3)
        S.dma("c_koh", lambda e: nc.gpsimd.dma_start(out=kT[1][64:128, :], in_=io["koh"]), writes=[("kT", 1, i_) for i_ in range(NTILE)], eng="pool")
        S.op("dve", lambda e: e.memset(kmT[:], 0.0), writes=["kmT"])
        S.op("dve", lambda e: e.memset(MB[:], 0.0), writes=["MB"])

    def rope_tables(i):
        PI = float(np.pi)
        S.dma("ld_pos", lambda e: e.dma_start(out=pos_i[:], in_=io["pos"][i * 512:(i + 1) * 512].partition_broadcast(64)), writes=["pos_i"])
        S.op("dve", lambda e: e.tensor_copy(out=ang[:], in_=pos_i[:]), reads=["pos_i"], writes=["ang"])
        S.op("dve", lambda e: e.tensor_scalar(out=ang[:], in0=ang[:], scalar1=invf[:, 0:1], scalar2=None, op0=ALU.mult), reads=["ang", "invf"], writes=["ang"])
        S.op("dve", lambda e: e.tensor_scalar(out=kint[:], in0=ang[:], scalar1=float(1.0 / (2 * np.pi)), scalar2=None, op0=ALU.mult), reads=["ang"], writes=["kint"])
        S.op("dve", lambda e: e.tensor_copy(out=kflt[:], in_=kint[:]), reads=["kint"], writes=["kflt"])
        S.op("dve", lambda e: e.scalar_tensor_tensor(out=ang[:], in0=kflt[:], scalar=-6.28125, in1=ang[:], op0=ALU.mult, op1=ALU.add), reads=["kflt", "ang"], writes=["ang"])
        S.op("dve", lambda e: e.scalar_tensor_tensor(out=ang[:], in0=kflt[:], scalar=float(-(2 * np.pi - 6.28125)), in1=ang[:], op0=ALU.mult, op1=ALU.add),
             reads=["kflt", "ang"], writes=["ang"])
        S.op("dve", lambda e: e.tensor_scalar(out=msk[:], in0=ang[:], scalar1=PI, scalar2=-2 * PI, op0=ALU.is_gt, op1=ALU.mult), reads=["ang"], writes=["msk"])
        S.op("dve", lambda e: e.tensor_tensor(out=ang[:], in0=ang[:], in1=msk[:], op=ALU.add), reads=["ang", "msk"], writes=["ang"])
        S.op("dve", lambda e: e.tensor_scalar(out=msk[:], in0=ang[:], scalar1=-PI, scalar2=2 * PI, op0=ALU.is_lt, op1=ALU.mult), reads=["ang"], writes=["msk"])
        S.op("dve", lambda e: e.tensor_tensor(out=ang[:], in0=ang[:], in1=msk[:], op=ALU.add), reads=["ang", "msk"], writes=["ang"])
        S.op("dve", lambda e: e.tensor_scalar(out=rt1[:], in0=ang[:], scalar1=PI / 2, scalar2=None, op0=ALU.add), reads=["ang"], writes=["rt1"])
        S.op("dve", lambda e: e.tensor_scalar(out=msk[:], in0=rt1[:], scalar1=PI, scalar2=-2 * PI, op0=ALU.is_gt, op1=ALU.mult), reads=["rt1"], writes=["msk"])
        S.op("dve", lambda e: e.tensor_tensor(out=rt1[:], in0=rt1[:], in1=msk[:], op=ALU.add), reads=["rt1", "msk"], writes=["rt1"])
        S.op("act", lambda e: e.activation(out=sinS[:], in_=ang[:], func=AF.Sin, scale=sgn[:, 0:1]), reads=["ang", "sgn"], writes=["sinS"])
        S.op("act", lambda e: e.activation(out=cosF[:], in_=rt1[:], func=AF.Sin), reads=["rt1"], writes=["cosF"])

    def rope_apply(a_main, a_perm, dst, dkey):
        S.op("dve", lambda e: e.tensor_tensor(out=rt1[:], in0=pacc[a_main][0:64, :], in1=cosF[:], op=ALU.mult), reads=[("pacc", a_main), "cosF"], writes=["rt1"])
        S.op("dve", lambda e: e.tensor_tensor(out=dst[:], in0=pacc[a_perm][0:64, :], in1=sinS[:], op=ALU.mult), reads=[("pacc", a_perm), "sinS"], writes=[dkey])
        S.op("dve", lambda e: e.tensor_tensor(out=dst[:], in0=dst[:], in1=rt1[:], op=ALU.add), reads=[dkey, "rt1"], writes=[dkey])

    def moba_gate(i, qbuf):
        for s in range(4):
            own = 2 * i + s // 2
            if own > 0:
                a = nxt("pacc", 2)
                S.op("pe", lambda e, a=a, s=s: e.matmul(pacc[a][:, 0:64], lhsT=qrot[0:64, s * 128:(s + 1) * 128], rhs=kmT[0:64, 0:64], start=True, stop=True),
                     reads=["qrot", "kmT"], writes=[("pacc", a)])
                S.op("dve", lambda e: e.memset(G[:], -1e9), writes=["G"])
                S.op("dve", lambda e, a=a, own=own: e.tensor_copy(out=G[:, 0:own], in_=pacc[a][:, 0:own]), reads=[("pacc", a), "G"], writes=["G"])
                S.op("dve", lambda e: e.max(out=top8[:], in_=G[:]), reads=["G"], writes=["top8"])
                S.op("dve", lambda e: e.tensor_scalar(out=top8[:, 2:3], in0=top8[:, 2:3], scalar1=-1e8, scalar2=None, op0=ALU.max), reads=["top8"], writes=["top8"])
                S.op("dve", lambda e: e.tensor_scalar(out=MB[:, 64:128], in0=G[:], scalar1=top8[:, 2:3], scalar2=-1.0, op0=ALU.is_ge, op1=ALU.add),
                     reads=["G", "top8"], writes=["MB"])
            S.op("dve", lambda e, own=own: e.memset(MB[:, 64 + own:128], 0.0), reads=["MB"], writes=["MB"])
            S.op("pe", lambda e: e.transpose(trp[0][:, 0:128], MB[:, :], ident[:, :]), reads=["MB", "ident"], writes=["trp"])
            S.op("act", lambda e, s=s: e.copy(out=qT[1][qbuf][64:128, s * 128:(s + 1) * 128], in_=trp[0][64:128, 0:128]),
                 reads=["trp"], writes=[("qT", 1, qbuf)])

    def sb_attend(i, qbuf):
        a = nxt("av", 2)
        lo = max(0, 4 * i - SB_WIN)
        blocks = list(range(4 * i + 3, lo - 1, -1))
        for n, blk in enumerate(blocks):
            j = blk - 4 * i
            s_ = nxt("sc", NSC)
            p_ = nxt("pt", NPT)
            b2 = n % 2
            S.op("pe", lambda e, s_=s_, blk=blk: e.matmul(sc[s_][:, :], lhsT=kT[0][0:64, blk * 128:(blk + 1) * 128], rhs=qT[0][qbuf][0:64, :], start=True, stop=True),
                 reads=[("kT", 0, blk // 4), ("qT", 0, qbuf)], writes=[("sc", s_)])
            S.op("act", lambda e, s_=s_, b2=b2: e.activation(out=spE[b2][:, :], in_=sc[s_][:, :], func=AF.Exp), reads=[("sc", s_)], writes=[("spE", b2)])
            S.op("act", lambda e, b2=b2: e.activation(out=spE[b2][:, :], in_=spE[b2][:, :], func=AF.Ln, bias=1.0, scale=1.0), reads=[("spE", b2)], writes=[("spE", b2)])
            if j >= 0:
                S.op("pool", lambda e, b2=b2, j=j: e.tensor_tensor(out=spM[b2][:, :], in0=spE[b2][:, :], in1=maskS[:, j, :], op=ALU.mult),
                     reads=[("spE", b2), "maskS"], writes=[("spM", b2)])
                src, skey = spM[b2], ("spM", b2)
            else:
                src, skey = spE[b2], ("spE", b2)
            S.op("pe", lambda e, src=src, n=n: e.matmul(tailP[:, :], lhsT=tri[:, 0:128], rhs=src[:, :], start=(n == 0), stop=True),
                 reads=["tri", skey], writes=["tailP"])
            S.op("dve", lambda e, s_=s_, b2=b2: e.tensor_tensor(out=lgA[b2][:, :], in0=sc[s_][:, :], in1=spE[b2][:, :], op=ALU.subtract),
                 reads=[("sc", s_), ("spE", b2)], writes=[("lgA", b2)])
            S.op("dve", lambda e, b2=b2: e.tensor_tensor(out=lgA[b2][:, :], in0=lgA[b2][:, :], in1=tailP[:, :], op=ALU.add),
                 reads=[("lgA", b2), "tailP"], writes=[("lgA", b2)])
            S.op("pe", lambda e, src=src: e.matmul(tailP[:, :], lhsT=tri[:, 128:256], rhs=src[:, :], start=False, stop=True),
                 reads=["tri", skey], writes=["tailP"])
            S.op("act", lambda e, p_=p_, b2=b2: e.activation(out=pT[p_][:, :], in_=lgA[b2][:, :], func=AF.Exp), reads=[("lgA", b2)], writes=[("pT", p_)])
            if j >= 0:
                S.op("pool", lambda e, p_=p_, j=j: e.tensor_tensor(out=pT[p_][:, :], in0=pT[p_][:, :], in1=maskS[:, j, :], op=ALU.mult),
                     reads=[("pT", p_), "maskS"], writes=[("pT", p_)])
            S.op("pe", lambda e, a=a, p_=p_, blk=blk, n=n: e.matmul(av[a][0:64, :], lhsT=vX[0][:, blk, 0:64], rhs=pT[p_][:, :], start=(n == 0), stop=(n == len(blocks) - 1)),
                 reads=[("vX", 0, blk // 4), ("pT", p_)], writes=[("av", a)])
        finalize(i, 0, a, False)

    def proj64(col, i, nrows=64):
        a = nxt("pacc", 2)
        for c in range(8):
            S.op("pe", lambda e, a=a, c=c: e.matmul(pacc[a][0:nrows, :], lhsT=w_s[:, c, col:col + nrows], rhs=hnT[:, c, :], start=(c == 0), stop=(c == 7)),
                 reads=["w_s", "hnT"], writes=[("pacc", a)])
        return a

    LOOK = 2

    def attend(i, h, blocks, krows, bias_fn, qbuf):
        a = nxt("av", 2)
        nb = len(blocks)
        pbuf = [None] * nb
        for n in range(nb + LOOK):
            if n < nb:
                blk = blocks[n]
                s_ = nxt("sc", NSC)
                p_ = nxt("pt", NPT)
                pbuf[n] = p_
                j = blk - 4 * i
                S.op("pe", lambda e, s_=s_, blk=blk, j=j: e.matmul(sc[s_][:, :], lhsT=kT[h][0:krows, blk * 128:(blk + 1) * 128], rhs=qT[h][qbuf][0:krows, :],
                                                                   start=True, stop=(j < 0)),
                     reads=[("kT", h, blk // 4), ("qT", h, qbuf)], writes=[("sc", s_)])
                if j >= 0:
                    S.op("pe", lambda e, s_=s_, j=j: e.matmul(sc[s_][:, :], lhsT=ident[:, :], rhs=maskI[:, j, :], start=False, stop=True),
                         reads=["ident", "maskI"], writes=[("sc", s_)])
                b_ap, b_key = bias_fn(blk)
                S.op("act", lambda e, s_=s_, p_=p_, b_ap=b_ap: e.activation(out=pT[p_][:, :], in_=sc[s_][:, :], func=AF.Exp, bias=b_ap, scale=1.0),
                     reads=[("sc", s_)] + b_key, writes=[("pT", p_)])
            m = n - LOOK
            if m >= 0:
                blk = blocks[m]
                p_ = pbuf[m]
                S.op("pe", lambda e, p_=p_, blk=blk, m=m: e.matmul(av[a][0:65, :], lhsT=vX[h][:, blk, :], rhs=pT[p_][:, :], start=(m == 0), stop=(m == nb - 1)),
                     reads=[("vX", h, blk // 4), ("pT", p_)], writes=[("av", a)])
        finalize(i, h, a, True)

    def finalize(i, h, a, normalize):
        o_ = nxt("ot", 2)
        if normalize:
            S.op("dve", lambda e: e.reciprocal(out=rd[64:65, :], in_=av[a][64:65, :]), reads=[("av", a)], writes=["rd"])
            S.op("act", lambda e: e.copy(out=nm[0:64, :], in_=av[a][0:64, :]), reads=[("av", a)], writes=["nm"])
            a2 = nxt("pacc", 2)
            S.op("pe", lambda e: e.matmul(pacc[a2][0:64, :], lhsT=ones_f[64:65, 0:64], rhs=rd[64:65, :], start=True, stop=True),
                 reads=["ones_f", "rd"], writes=[("pacc", a2)])
            S.op("dve", lambda e: e.tensor_tensor(out=oTs[o_][:, :], in0=nm[0:64, :], in1=pacc[a2][0:64, :], op=ALU.mult),
                 reads=["nm", ("pacc", a2)], writes=[("oTs", o_)])
        else:
            S.op("act", lambda e: e.copy(out=oTs[o_][:, :], in_=av[a][0:64, :]), reads=[("av", a)], writes=[("oTs", o_)])
        S.dma("st_o%d" % o_, lambda e: e.dma_start(out=io["oT"][i, h * 64:(h + 1) * 64, :], in_=oTs[o_][:, :]),
              reads=[("oTs", o_)], writes=[("oT", h, i)])

    for i in range(ntile):
        xb = i % NXT
        qbuf = i % 2
        S.dma("ld_x%d" % xb, lambda e, i=i, xb=xb: e.dma_start(out=xt[xb][:, :, :], in_=io["hfull"][i * 512:(i + 1) * 512, :].rearrange("(s p) d -> p s d", p=128)),
              writes=[("xt", xb)])
        for s in range(4):
            b = nxt("hn", 2)
            hb = hn_s[b]
            src = xt[xb][:, s, :]
            S.op("act", lambda e, src=src: e.activation(out=junk[:, :], in_=src, func=AF.Square, accum_out=stat[:, 0:1]),
                 reads=[("xt", xb)], writes=["junk", "stat"])
            S.op("act", lambda e: e.activation(out=stat[:, 1:2], in_=stat[:, 0:1], func=AF.Ln, bias=EPS, scale=1.0 / D), reads=["stat"], writes=["stat"])
            S.op("act", lambda e: e.activation(out=stat[:, 2:3], in_=stat[:, 1:2], func=AF.Exp, scale=-0.5), reads=["stat"], writes=["stat"])
            S.op("dve", lambda e, src=src, hb=hb: e.scalar_tensor_tensor(out=hb[:, :], in0=src, scalar=stat[:, 2:3], in1=gmix[:, :], op0=ALU.mult, op1=ALU.mult),
                 reads=[("xt", xb), "stat", "gmix"], writes=[("hn_s", b)])
            for c in range(8):
                S.op("pe", lambda e, c=c, hb=hb: e.transpose(trp[0][:, c * 128:(c + 1) * 128], hb[:, c * 128:(c + 1) * 128], ident[:, :]),
                     reads=[("hn_s", b), "ident"], writes=["trp"])
            S.op("dve", lambda e, s=s: e.tensor_copy(out=hnT[:, :, s * 128:(s + 1) * 128], in_=trp[0][:].rearrange("p (c t) -> p c t", c=8)),
                 reads=["trp"], writes=["hnT"])
        for s in range(4):
            a = nxt("pacc", 2)
            for c in range(8):
                S.op("pe", lambda e, a=a, c=c, s=s: e.matmul(pacc[a][:, 0:128], lhsT=hnT[:, c, s * 128:(s + 1) * 128], rhs=w_s[:, c, VV:VV + 128], start=(c == 0), stop=(c == 7)),
                     reads=["w_s", "hnT"], writes=[("pacc", a)])
            for h in range(2):
                S.op("dve", lambda e, a=a, h=h, s=s, i=i: e.tensor_copy(out=vX[h][:, 4 * i + s, 0:64], in_=pacc[a][:, h * 64:(h + 1) * 64]),
                     reads=[("pacc", a)], writes=[("vX", h, i)])
        if fox:
            a = nxt("pacc", 2)
            for c in range(8):
                S.op("pe", lambda e, a=a, c=c: e.matmul(pacc[a][0:2, :], lhsT=w_s[:, c, FF:FF + 2], rhs=hnT[:, c, :], start=(c == 0), stop=(c == 7)),
                     reads=["w_s", "hnT"], writes=[("pacc", a)])
            S.op("act", lambda e, a=a: e.activation(out=fE[:, :], in_=pacc[a][0:2, :], func=AF.Exp, bias=nbf[:, 0:1], scale=-1.0),
                 reads=[("pacc", a), "nbf"], writes=["fE"])
            S.op("act", lambda e: e.activation(out=fE[:, :], in_=fE[:, :], func=AF.Ln, bias=1.0, scale=1.0), reads=["fE"], writes=["fE"])
            S.op("dve", lambda e: e.tensor_scalar(out=fE[:, :], in0=fE[:, :], scalar1=-1.0, scalar2=None, op0=ALU.mult), reads=["fE"], writes=["fE"])
            S.op("dve", lambda e: e.tensor_tensor_scan(out=cc[:, :], data0=fE[:, :], data1=fE[:, :], initial=cprev[:, 0:1], op0=ALU.add, op1=ALU.bypass),
                 reads=["fE", "cprev"], writes=["cc"])
            S.op("dve", lambda e: e.tensor_copy(out=cprev[:, :], in_=cc[:, 511:512]), reads=["cc"], writes=["cprev"])
            S.op("dve", lambda e: e.tensor_copy(out=rbf[:, :], in_=cc[:, :]), reads=["cc"], writes=["rbf"])
            a = nxt("pacc", 2)
            for s in range(4):
                S.op("pe", lambda e, a=a, s=s: e.transpose(pacc[a][:, 2 * s:2 * s + 2], cc[0:2, s * 128:(s + 1) * 128], ident_f[0:2, 0:2]),
                     reads=["cc", "ident_f"], writes=[("pacc", a)])
            S.op("dve", lambda e, a=a, i=i: e.tensor_scalar(out=negc[:, 4 * i:4 * i + 4, :], in0=pacc[a][:, 0:8].rearrange("p (s h) -> p s h", h=2),
                                                            scalar1=-1.0, scalar2=None, op0=ALU.mult),
                 reads=[("pacc", a)], writes=[("negc", i)])
            for h in range(2):
                qc, kc = (QA, KA) if h == 0 else (QB, KB)
                a = proj64(qc, i)
                S.op("act", lambda e, a=a, h=h, qbuf=qbuf: e.activation(out=qT[h][qbuf][0:64, :], in_=pacc[a][0:64, :], func=AF.Copy, scale=0.125),
                     reads=[("pacc", a)], writes=[("qT", h, qbuf)])
                S.dma("ld_r%d" % h, lambda e, h=h, qbuf=qbuf: e.dma_start(out=qT[h][qbuf][64:65, :], in_=rbf[h:h + 1, :]), reads=["rbf", ("qT", h, qbuf)], writes=[("qT", h, qbuf)])
                a = proj64(kc, i)
                S.op("act", lambda e, a=a, h=h, i=i: e.copy(out=kT[h][0:64, i * 512:(i + 1) * 512], in_=pacc[a][0:64, :]),
                     reads=[("pacc", a)], writes=[("kT", h, i)])
            for h in range(2):
                attend(i, h, list(range(4 * i + 4)), 65, lambda blk, h=h: (negc[:, blk, h:h + 1], [("negc", blk // 4)]), qbuf)
        if not fox:
            rope_tables(i)
            a = proj64(QA, i)
            S.op("act", lambda e, a=a, qbuf=qbuf: e.activation(out=qT[0][qbuf][0:64, :], in_=pacc[a][0:64, :], func=AF.Copy, scale=0.125),
                 reads=[("pacc", a)], writes=[("qT", 0, qbuf)])
            a = proj64(KA, i)
            S.op("act", lambda e, a=a, i=i: e.copy(out=kT[0][0:64, i * 512:(i + 1) * 512], in_=pacc[a][0:64, :]), reads=[("pacc", a)], writes=[("kT", 0, i)])
            a1 = proj64(QB, i)
            a2 = proj64(QP, i)
            rope_apply(a1, a2, qrot, "qrot")
            S.op("act", lambda e, qbuf=qbuf: e.activation(out=qT[1][qbuf][0:64, :], in_=qrot[:, :], func=AF.Copy, scale=0.125),
                 reads=["qrot"], writes=[("qT", 1, qbuf)])
            a1 = proj64(KB, i)
            a2 = proj64(KP, i)
            rope_apply(a1, a2, krot, "krot")
            S.op("act", lambda e, i=i: e.copy(out=kT[1][0:64, i * 512:(i + 1) * 512], in_=krot[:, :]), reads=["krot"], writes=[("kT", 1, i)])
            S.op("dve", lambda e, i=i: e.tensor_reduce(out=kmT[:, 2 * i:2 * i + 2], in_=krot[:, :].rearrange("p (n k) -> p n k", n=2), axis=AX.X, op=ALU.add),
                 reads=["krot", "kmT"], writes=["kmT"])
            S.op("dve", lambda e, i=i: e.tensor_scalar(out=kmT[:, 2 * i:2 * i + 2], in0=kmT[:, 2 * i:2 * i + 2], scalar1=1.0 / 256, scalar2=None, op0=ALU.mult),
                 reads=["kmT"], writes=["kmT"])
            moba_gate(i, qbuf)
            sb_attend(i, qbuf)
            attend(i, 1, list(range(4 * i + 4)), 128, lambda blk: (0.0, []), qbuf)
    if "dbg_negc" in io:
        S.dma("dbg", lambda e: [e.dma_start(out=io["dbg_negc"], in_=negc[:].rearrange("p b h -> p (b h)")),
                                e.dma_start(out=io["dbg_q"], in_=qT[0][(ntile - 1) % 2][:, :]),
                                e.dma_start(out=io["dbg_k"], in_=kT[0][:, 0:1024]),
                                e.dma_start(out=io["dbg_cc"], in_=cc[:, :])],
              reads=[("negc", i_) for i_ in range(ntile)] + [("qT", 0, (ntile - 1) % 2), ("kT", 0, 0), ("kT", 0, 1), "cc"], n=4)


def _masks():
    s_ = np.arange(128)[:, None, None]
    j_ = np.arange(4)[None, :, None]
    t_ = np.arange(512)[None, None, :]
    mi = ((128 * j_ + s_) <= t_).astype(np.float32)
    ms = ((128 * j_ + s_) < t_).astype(np.float32)
    return mi, ms


REST_W = ("w_out", "wq", "wkv", "wxo", "wup", "wdn", "cw", "cb", "g_xa", "g_mem", "g_ffn")
REST_SHAPES = {"w_out": [128, 8, 1024], "wq": [128, 8, 256], "wkv": [128, 8, 512], "wxo": [64, 4, 1024],
               "wup": [NPAIR, 128, 8 * 256], "wdn": [NPAIR, 2, 128, 512], "cw": [128, 44, 3], "cb": [128, 44],
               "g_xa": [D], "g_mem": [D], "g_ffn": [D]}


def build_fused():
    nc = bass.Bass("TRN2", target_bir_lowering=False, num_devices=NCORES)
    din = lambda name, shape, dt: nc.dram_tensor(name, list(shape), dt, kind="ExternalInput").ap()
    dint = lambda name, shape, dt: nc.dram_tensor(name, list(shape), dt, kind="Internal").ap()
    x = din("x", [SEQ, D], F32)
    hin0 = din("hin0", [NT + 128, D], F32)
    flag = din("flag", [128, 1], F32)
    hidx = din("hidx", [128, 17], I32)
    oidx = din("oidx", [128, 5, 8], I32)
    ident = din("ident", [128, 128], F32)
    maskI = din("maskI", [128, 4, 512], F32)
    maskS = din("maskS", [128, 4, 512], F32)
    pos = din("pos", [SEQ], I32)
    invf = din("invf", [64, 1], F32)
    sgn = din("sgn", [64, 1], F32)
    koh = din("koh", [64, SEQ], F32)
    tri = din("tri", [128, 256], F32)
    mem = din("mem", [MEM, D], F32)
    a_w = din("a_w", [128, 8, 512], F32)
    a_g = din("a_g", [D], F32)
    c_w = din("c_w", [128, 8, 386], F32)
    c_g = din("c_g", [D], F32)
    c_bf = din("c_bf", [2, 1], F32)
    g_fin = din("g_fin", [D], F32)
    rw = [{k: din("r%d_%s" % (L, k), REST_SHAPES[k], F32) for k in REST_W} for L in range(2)]
    out = nc.dram_tensor("out", [NT, D], F32, kind="ExternalOutput").ap()
    o_src = [dint("o_src%d" % L, [NTILE * 128, 512], BF16) for L in range(2)]
    o_all = [dint("o_all%d" % L, [NCORES * NTILE * 128, 512], BF16) for L in range(2)]
    h1_src = dint("h1_src", [NT, D], F32)
    h1_all = dint("h1_all", [SEQ, D], F32)

    def phase(tag, body, waits):
        with nc.cleanup_on_exit():
            with contextlib.ExitStack() as st:
                cx = Ctx(nc, st, tag)
                body(cx)
                cx.S.emit(final_wait_streams=waits)
            nc.all_engine_barrier()

    def gather(tag, src, dst):
        phase(tag, lambda cx: cx.S.cc("ag", lambda e: nc.gpsimd.collective_compute(
            "AllGather", ALU.bypass, replica_groups=[list(range(NCORES))], ins=[src.opt()], outs=[dst.opt()])), ["ag"])

    def rest_io(L, hin, hout, with_hidx):
        io = dict(rw[L])
        io.update({"hin": hin, "oall": o_all[L], "oidx": oidx, "flag": flag, "mem": mem, "ident": ident, "g_fin": g_fin, "hout": hout})
        if with_hidx:
            io["hidx"] = hidx
        return io

    phase("A", lambda cx: emit_mixer(cx, {"hfull": x, "g_mix": a_g, "w": a_w, "ident": ident, "maskI": maskI, "maskS": maskS, "pos": pos,
                                          "invf": invf, "sgn": sgn, "koh": koh, "tri": tri,
                                          "oT": o_src[0].rearrange("(i f) t -> i f t", f=128)}, "ab"), ["st_o0", "st_o1"])
    gather("G0", o_src[0], o_all[0])
    phase("B", lambda cx: emit_rest(cx, rest_io(0, hin0, h1_src, False), False), ["st_h"])
    gather("G1", h1_src, h1_all)
    phase("C", lambda cx: emit_mixer(cx, {"hfull": h1_all, "g_mix": c_g, "w": c_w, "ident": ident, "maskI": maskI, "bf": c_bf,
                                          "oT": o_src[1].rearrange("(i f) t -> i f t", f=128)}, "fox"), ["st_o0", "st_o1"])
    gather("G2", o_src[1], o_all[1])
    phase("D", lambda cx: emit_rest(cx, rest_io(1, h1_all, out, True), True), ["st_h"])
    return nc


_CACHE = {}


def kernel(x, mem, positions, norm_mix_g, norm_xa_g, norm_mem_g, norm_ffn_g,
           ab_w_in, ab_w_out, fox_w_in, fox_b_f, fox_w_out,
           xa_w_q, xa_w_kv, xa_w_out, ffn_w_up, ffn_conv_w, ffn_conv_b, ffn_w_down,
           final_norm_g):
    a = lambda v: np.asarray(v)
    f32 = np.float32
    c_ = lambda v: np.ascontiguousarray(v, dtype=f32)
    x0 = c_(a(x)[0])
    mem_ = a(mem)
    w_in0, w_in1 = a(ab_w_in)[0], a(fox_w_in)[0]
    mi, ms = _masks()
    inv_freq = (10000.0 ** (-np.arange(32, dtype=f32) / 32)).astype(f32)
    invf = np.concatenate([inv_freq, inv_freq]).reshape(64, 1).astype(f32)
    sgn = np.concatenate([-np.ones(32), np.ones(32)]).reshape(64, 1).astype(f32)
    koh = np.zeros((64, SEQ), f32)
    for n in range(64):
        koh[n, n * 256:(n + 1) * 256] = 30000.0
    jj = np.arange(128)[:, None]
    ss = np.arange(128)[None, :]
    tri = np.concatenate([-(jj > ss).astype(f32), -(jj <= ss).astype(f32)], axis=1)
    perm = np.concatenate([np.arange(32, 64), np.arange(0, 32)])
    w_out0 = a(ab_w_out)[0]
    w_out0p = np.concatenate([np.concatenate([w_out0[r * 64:(r + 1) * 64], w_out0[(8 + r) * 64:(9 + r) * 64]], axis=0) for r in range(8)], axis=0)
    rws = []
    for L, wo in ((0, w_out0p), (1, a(fox_w_out)[0])):
        rws.append(rest_weights(L, wo, a(xa_w_q), a(xa_w_kv), a(xa_w_out), a(ffn_w_up), a(ffn_conv_w), a(ffn_conv_b),
                                a(ffn_w_down), a(norm_xa_g), a(norm_mem_g), a(norm_ffn_g), a(final_norm_g), mem_))
    common = {
        "x": x0, "ident": np.eye(128, dtype=f32), "maskI": c_((mi - 1.0) * 30000.0), "maskS": c_(ms),
        "pos": np.ascontiguousarray(a(positions).reshape(-1), dtype=np.int32), "invf": invf, "sgn": sgn, "koh": koh, "tri": tri,
        "mem": c_(mem_[0]), "a_g": c_(a(norm_mix_g)[0]), "c_g": c_(a(norm_mix_g)[1]), "g_fin": c_(a(final_norm_g)),
    }
    for L in range(2):
        for k in REST_W:
            common["r%d_%s" % (L, k)] = rws[L][k]
    in_maps = []
    p_ = np.arange(128)
    for cid in range(NCORES):
        m = dict(common)
        t0 = cid * NT
        m["hin0"] = np.concatenate([np.zeros((128, D), f32), x0[0:NT]], axis=0) if cid == 0 else c_(x0[t0 - 128:t0 + NT])
        m["flag"] = np.full((128, 1), 0.0 if cid == 0 else 1.0, f32)
        m["hidx"] = np.maximum(t0 - 128 + np.arange(17)[None, :] * 128 + p_[:, None], 0).astype(np.int32)
        tiles = np.array([max(4 * cid - 1, 0)] + [4 * cid + k for k in range(4)])
        m["oidx"] = (np.arange(8)[None, None, :] * (NTILE * 128) + tiles[None, :, None] * 128 + p_[:, None, None]).astype(np.int32)
        hA, hB = cid, 8 + cid
        qB = w_in0[:, hB * 64:(hB + 1) * 64]
        kB = w_in0[:, D + hB * 64:D + (hB + 1) * 64]
        wa = np.concatenate([w_in0[:, hA * 64:(hA + 1) * 64], w_in0[:, D + hA * 64:D + (hA + 1) * 64], qB, kB,
                             w_in0[:, 2 * D + hA * 64:2 * D + (hA + 1) * 64], w_in0[:, 2 * D + hB * 64:2 * D + (hB + 1) * 64],
                             qB[:, perm], kB[:, perm]], axis=1)
        m["a_w"] = c_(wa.reshape(8, 128, -1).transpose(1, 0, 2))
        hA, hB = 2 * cid, 2 * cid + 1
        cols = []
        for hh in (hA, hB):
            cols.append(w_in1[:, hh * 64:(hh + 1) * 64])
            cols.append(w_in1[:, D + hh * 64:D + (hh + 1) * 64])
        cols += [w_in1[:, 2 * D + hA * 64:2 * D + (hA + 1) * 64], w_in1[:, 2 * D + hB * 64:2 * D + (hB + 1) * 64],
                 w_in1[:, 3 * D + hA:3 * D + hA + 1], w_in1[:, 3 * D + hB:3 * D + hB + 1]]
        m["c_w"] = c_(np.concatenate(cols, axis=1).reshape(8, 128, -1).transpose(1, 0, 2))
        m["c_bf"] = c_(a(fox_b_f)[0][[hA, hB]].reshape(2, 1))
        in_maps.append(m)
    if "nc" not in _CACHE:
        _CACHE["nc"] = build_fused()
    res = run_bass_kernel_spmd(_CACHE["nc"], in_maps, core_ids=list(range(NCORES)))
    full = np.concatenate([r["out"] for r in res.results], axis=0)
    return np.ascontiguousarray(full[None].astype(np.float32))
```

```python
import contextlib
import numpy as np
import ml_dtypes
import concourse.bass as bass
import concourse.mybir as mybir
from concourse.bass_utils import run_bass_kernel_spmd

F32 = mybir.dt.float32
BF16 = mybir.dt.bfloat16
I32 = mybir.dt.int32
AF = mybir.ActivationFunctionType
ALU = mybir.AluOpType
AX = mybir.AxisListType

NCORES = 8
D = 1024
SEQ = 16384
NT = SEQ // NCORES
DFF = 2816
NPAIR = DFF // 128
MEM = 256
EPS = 1e-6
ENGS = ("pe", "act", "dve", "pool", "sp")
SEM_EPOCH = 24000
DEBUG = False
LAST = None


class Op:
    __slots__ = ("eng", "fn", "deps", "inc", "cnt", "dma", "ndma", "dcnt")

    def __init__(self, eng, fn, dma=None, ndma=1):
        self.eng = eng
        self.fn = fn
        self.deps = set()
        self.inc = False
        self.cnt = 0
        self.dma = dma
        self.ndma = ndma
        self.dcnt = 0


class Sched:
    def __init__(self, nc, same_engine_sync=("act", "dve", "pool")):
        self.nc = nc
        self.ops = {e: [] for e in ENGS}
        self.last_w = {}
        self.readers = {}
        self.streams = {}
        self.cc_streams = set()
        self.same = set(same_engine_sync)
        self.persist_sems = False
        self.tag = ""

    def _add(self, op, reads, writes):
        for b in reads:
            w = self.last_w.get(b)
            if w is not None:
                op.deps.add(w)
        for b in writes:
            w = self.last_w.get(b)
            if w is not None:
                op.deps.add(w)
            for r in self.readers.get(b, ()):
                op.deps.add(r)
        op.deps.discard(op)
        for b in reads:
            self.readers.setdefault(b, []).append(op)
        for b in writes:
            self.last_w[b] = op
            self.readers[b] = []
        self.ops[op.eng].append(op)
        return op

    def op(self, eng, fn, reads=(), writes=()):
        return self._add(Op(eng, fn), reads, writes)

    def dma(self, stream, fn, reads=(), writes=(), eng="sp", n=1):
        op = Op(eng, fn, dma=stream, ndma=n)
        self.streams.setdefault(stream, []).append(op)
        return self._add(op, reads, writes)

    def cc(self, stream, fn, reads=(), writes=()):
        op = Op("pool", fn, dma=stream, ndma=1)
        self.cc_streams.add(stream)
        self.streams.setdefault(stream, []).append(op)
        return self._add(op, reads, writes)

    def emit(self, final_wait_streams=()):
        nc = self.nc
        for e in ENGS:
            for op in self.ops[e]:
                for d in list(op.deps):
                    if d.dma is not None:
                        continue
                    if d.eng == op.eng and op.dma is None and d.eng not in self.same:
                        op.deps.discard(d)
                        continue
                    d.inc = True
        nep = {}
        for e in ENGS:
            c = 0
            for op in self.ops[e]:
                if op.dma is None and op.inc:
                    c += 1
                op.cnt = c
            nep[e] = max(1, -(-c // SEM_EPOCH))
        total = {}
        for s, lst in self.streams.items():
            c = 0
            for op in lst:
                c += (1 if s in self.cc_streams else 16) * op.ndma
                op.dcnt = c
            total[s] = c
        with contextlib.ExitStack() as st:
            tag = self.tag
            if self.persist_sems:
                esem = {e: [nc.alloc_semaphore("s%s_%s%d" % (tag, e, k)) for k in range(nep[e])] for e in ENGS}
                ssem = {s: nc.alloc_semaphore("d%s_%s" % (tag, s)) for s in self.streams}
            else:
                esem = {e: [st.enter_context(nc.semaphore("s_%s%d" % (e, k))) for k in range(nep[e])] for e in ENGS}
                ssem = {s: st.enter_context(nc.semaphore("d_" + s)) for s in self.streams}
            block = st.enter_context(nc.Block())

            def run(e, eng_obj):
                seen = {}
                for op in self.ops[e]:
                    need = {}
                    for d in op.deps:
                        if d.dma is not None:
                            key, val = ("d", d.dma), d.dcnt
                        else:
                            key, val = ("e", d.eng), d.cnt
                        if val > need.get(key, 0):
                            need[key] = val
                    for key, val in need.items():
                        if seen.get(key, 0) >= val:
                            continue
                        seen[key] = val
                        if key[0] == "d":
                            eng_obj.wait_ge(ssem[key[1]], val)
                        else:
                            k = (val - 1) // SEM_EPOCH
                            eng_obj.wait_ge(esem[key[1]][k], val - k * SEM_EPOCH)
                    ins = op.fn(eng_obj)
                    if op.dma is not None:
                        if not isinstance(ins, (list, tuple)):
                            ins = [ins]
                        assert len(ins) == op.ndma, (len(ins), op.ndma)
                        for i_ in ins:
                            if op.dma in self.cc_streams:
                                i_.then_inc(ssem[op.dma])
                            else:
                                i_.then_inc(ssem[op.dma], 16)
                    elif op.inc:
                        ins.then_inc(esem[e][(op.cnt - 1) // SEM_EPOCH], 1)
                if e == "sp":
                    for s in final_wait_streams:
                        eng_obj.wait_ge(ssem[s], total[s])

            @block.tensor
            def _(eng):
                run("pe", eng)

            @block.scalar
            def _(eng):
                run("act", eng)

            @block.vector
            def _(eng):
                run("dve", eng)

            @block.gpsimd
            def _(eng):
                run("pool", eng)

            @block.sync
            def _(eng):
                run("sp", eng)


class Ctx:
    def __init__(self, nc, st, tag=""):
        self.nc = nc
        self.st = st
        self.S = Sched(nc)
        self.tag = tag
        if tag:
            self.S.persist_sems = True
            self.S.tag = tag

    def sb(self, name, shape, dt):
        return self.st.enter_context(self.nc.sbuf_tensor("sb%s_%s" % (self.tag, name), list(shape), dt))

    def ps(self, name, shape, dt):
        return self.st.enter_context(self.nc.psum_tensor("ps%s_%s" % (self.tag, name), list(shape), dt))

    def din(self, name, shape, dt):
        return self.nc.dram_tensor(name, list(shape), dt, kind="ExternalInput").ap()

    def dout(self, name, shape, dt):
        return self.nc.dram_tensor(name, list(shape), dt, kind="ExternalOutput").ap()

    def dint(self, name, shape, dt):
        return self.nc.dram_tensor("di%s_%s" % (self.tag, name), list(shape), dt, kind="Internal").ap()


def emit_rest(cx, io, final):
    nc, S = cx.nc, cx.S
    sb, ps = cx.sb, cx.ps
    NSUB = NT // 128
    NTT = NT // 512
    ident = sb("ident", [128, 128], BF16)
    ones_f = sb("ones_f", [128, 64], F32)
    gxa = sb("gxa", [128, D], F32)
    gffn = sb("gffn", [128, D], F32)
    gmem = sb("gmem", [128, D], F32)
    gfin = sb("gfin", [128, D], F32) if final else None
    cw = sb("cw", [128, 44, 3], F32)
    cb = sb("cb", [128, 44], F32)
    flag = sb("flag", [128, 1], F32)
    wo_s = sb("wo_s", [128, 8, 1024], BF16)
    wq_s = sb("wq_s", [128, 8, 256], BF16)
    wkv_s = sb("wkv_s", [128, 8, 512], BF16)
    wxo_s = sb("wxo_s", [64, 4, 1024], BF16)
    kxT = sb("kxT", [64, 4, MEM], BF16)
    vxa = sb("vxa", [128, 2, 4, 65], BF16)
    ucarry = sb("ucarry", [128, 44, 2], F32)
    wup_b = cx.dint("wup_b", [NPAIR, 128, 8 * 256], BF16)
    wdn_b = cx.dint("wdn_b", [NPAIR, 2, 128, 512], BF16)
    NUP = 5
    NDN = 8
    wup_r = [sb("wup_r%d" % i, [128, 8, 256], BF16) for i in range(NUP)]
    wdn_r = [sb("wdn_r%d" % i, [128, 512], BF16) for i in range(NDN)]
    hT = [sb("hT%d" % i, [128, 4, D], F32) for i in range(2)]
    oTs = [sb("oTs%d" % i, [128, 8, 512], BF16) for i in range(2)]
    hn_s = [sb("hn_s%d" % i, [128, D], BF16) for i in range(2)]
    junk = sb("junk", [128, D], BF16)
    stat = sb("stat", [128, 8], F32)
    hnT = sb("hnT", [128, 8, 512], BF16)
    qxT = sb("qxT", [64, 4, 512], BF16)
    pxT = [sb("pxT%d" % i, [128, 512], BF16) for i in range(4)]
    nm = sb("nm", [128, 512], F32)
    rd = sb("rd", [128, 512], F32)
    oxT = sb("oxT", [64, 4, 512], BF16)
    Yg = [sb("Yg%d" % i, [128, 512], F32) for i in range(2)]
    Yv = [sb("Yv%d" % i, [128, 512], F32) for i in range(2)]
    gT = sb("gT", [128, NPAIR, 512], BF16)
    trp = [ps("trp%d" % i, [128, 1024], BF16) for i in range(2)]
    acc = [ps("acc%d" % i, [128, 512], F32) for i in range(6)]
    rot = {"acc": 0, "tr": 0, "px": 0}

    def nacc():
        rot["acc"] = (rot["acc"] + 1) % 6
        return rot["acc"]

    def ntr():
        rot["tr"] = (rot["tr"] + 1) % 2
        return rot["tr"]

    def pdma(stream, out, in_, writes, reads=()):
        S.dma(stream, lambda e: nc.gpsimd.dma_start(out=out, in_=in_), reads=reads, writes=writes, eng="pool")

    pdma("c_id", ident[:], io["ident"], ["ident"])
    pdma("c_wo", wo_s[:], io["w_out"], ["wo_s"])
    pdma("c_wq", wq_s[:], io["wq"], ["wq_s"])
    pdma("c_wkv", wkv_s[:], io["wkv"], ["wkv_s"])
    pdma("c_wxo", wxo_s[:], io["wxo"], ["wxo_s"])
    S.dma("c_g", lambda e: [e.dma_start(out=gxa[:], in_=io["g_xa"].partition_broadcast(128)),
                            e.dma_start(out=gffn[:], in_=io["g_ffn"].partition_broadcast(128)),
                            e.dma_start(out=gmem[:], in_=io["g_mem"].partition_broadcast(128)),
                            e.dma_start(out=cw[:], in_=io["cw"]),
                            e.dma_start(out=cb[:], in_=io["cb"]),
                            e.dma_start(out=flag[:], in_=io["flag"])],
          writes=["gxa", "gffn", "gmem", "cw", "cb", "flag"], n=6)
    if final:
        S.dma("c_gf", lambda e: e.dma_start(out=gfin[:], in_=io["g_fin"].partition_broadcast(128)), writes=["gfin"])
    for g in range(4):
        js = list(range(g * 6, min(NPAIR, g * 6 + 6)))
        S.dma("c_up%d" % g, lambda e, js=js: [nc.gpsimd.dma_start(out=wup_b[j], in_=io["wup"][j]) for j in js],
              writes=[("wup_b", j) for j in js], eng="pool", n=len(js))
    for g in range(4):
        js = list(range(g * 6, min(NPAIR, g * 6 + 6)))
        S.dma("c_dn%d" % g, lambda e, js=js: [nc.gpsimd.dma_start(out=wdn_b[j], in_=io["wdn"][j]) for j in js],
              writes=[("wdn_b", j) for j in js], eng="pool", n=len(js))
    S.op("dve", lambda e: e.memset(ones_f[:], 1.0), writes=["ones_f"])
    S.op("dve", lambda e: e.memset(vxa[:], 1.0), writes=["vxa"])
    oidx = sb("oidx", [128, 5, 8], I32)
    hidx = sb("hidx", [128, 17], I32)
    S.dma("c_ix", lambda e: [e.dma_start(out=oidx[:], in_=io["oidx"])] + ([e.dma_start(out=hidx[:], in_=io["hidx"])] if "hidx" in io else []),
          writes=["oidx", "hidx"], n=(2 if "hidx" in io else 1))

    def rmsnorm_to_T(src_ap, src_key, g_tile, g_key, dstT, dst_key, col0, ncol=128, nrows=128):
        b = ntr()
        hb = hn_s[b]
        S.op("act", lambda e: e.activation(out=junk[:nrows, :], in_=src_ap, func=AF.Square, accum_out=stat[:nrows, 0:1]),
             reads=[src_key], writes=["junk", "stat"])
        S.op("act", lambda e: e.activation(out=stat[:nrows, 1:2], in_=stat[:nrows, 0:1], func=AF.Ln, bias=EPS, scale=1.0 / D),
             reads=["stat"], writes=["stat"])
        S.op("act", lambda e: e.activation(out=stat[:nrows, 2:3], in_=stat[:nrows, 1:2], func=AF.Exp, scale=-0.5),
             reads=["stat"], writes=["stat"])
        S.op("dve", lambda e: e.scalar_tensor_tensor(out=hb[:nrows, :], in0=src_ap, scalar=stat[:nrows, 2:3], in1=g_tile[:nrows, :],
                                                     op0=ALU.mult, op1=ALU.mult),
             reads=[src_key, "stat", g_key], writes=[("hn_s", b)])
        for c in range(8):
            S.op("pe", lambda e, c=c: e.transpose(trp[b][:, c * 128:c * 128 + nrows], hb[:nrows, c * 128:(c + 1) * 128], ident[:nrows, :nrows]),
                 reads=[("hn_s", b), "ident"], writes=[("trp", b)])
        S.op("act", lambda e: e.copy(out=dstT[:, :, col0:col0 + nrows],
                                     in_=trp[b][:].rearrange("p (c t) -> p c t", c=8)[:, :, 0:nrows]),
             reads=[("trp", b)], writes=[dst_key])

    memT = hnT
    mem_s = hT[1]
    S.dma("ld_mem", lambda e: e.dma_start(out=mem_s[:, 0:2, :], in_=io["mem"].rearrange("(s p) d -> p s d", p=128)),
          writes=[("hT", 1)])
    for s in range(2):
        rmsnorm_to_T(mem_s[:, s, :], ("hT", 1), gmem, "gmem", memT, "hnT", s * 128)
    for hd in range(4):
        a = nacc()
        for c in range(8):
            S.op("pe", lambda e, a=a, c=c, hd=hd: e.matmul(acc[a][0:64, 0:MEM], lhsT=wkv_s[:, c, hd * 64:(hd + 1) * 64], rhs=memT[:, c, 0:MEM],
                                                            start=(c == 0), stop=(c == 7)),
                 reads=["wkv_s", "hnT"], writes=[("acc", a)])
        S.op("act", lambda e, a=a, hd=hd: e.copy(out=kxT[:, hd, :], in_=acc[a][0:64, 0:MEM]), reads=[("acc", a)], writes=["kxT"])
    for mc in range(2):
        a = nacc()
        for c in range(8):
            S.op("pe", lambda e, a=a, c=c, mc=mc: e.matmul(acc[a][:, 0:256], lhsT=memT[:, c, mc * 128:(mc + 1) * 128], rhs=wkv_s[:, c, 256:512],
                                                            start=(c == 0), stop=(c == 7)),
                 reads=["wkv_s", "hnT"], writes=[("acc", a)])
        S.op("act", lambda e, a=a, mc=mc: e.copy(out=vxa[:, mc, :, 0:64], in_=acc[a][:, 0:256].rearrange("p (h d) -> p h d", h=4)),
             reads=[("acc", a)], writes=["vxa"])

    upq = {"n": 0}
    dnq = {"n": 0}

    def load_up(j):
        slot = upq["n"] % NUP
        upq["n"] += 1
        S.dma("r_up%d" % slot, lambda e: e.dma_start(out=wup_r[slot][:], in_=wup_b[j].rearrange("p (c n) -> p c n", c=8)),
              reads=[("wup_b", j)], writes=[("wup_r", slot)])
        return slot

    def load_dn(j, half):
        slot = dnq["n"] % NDN
        dnq["n"] += 1
        S.dma("r_dn%d" % slot, lambda e: e.dma_start(out=wdn_r[slot][:], in_=wdn_b[j, half]),
              reads=[("wdn_b", j)], writes=[("wdn_r", slot)])
        return slot

    def front(buf, nsub, row0):
        ntok = nsub * 128
        hb = hT[buf]
        k0 = row0 // 128
        t5 = 0 if row0 == 0 else 1 + (row0 - 128) // 512
        oc0 = 384 if row0 == 0 else 0
        if "hidx" in io:
            S.dma("ld_h%d" % buf, lambda e: [nc.gpsimd.indirect_dma_start(out=hb[:, s_, :], out_offset=None, in_=io["hin"],
                                                                        in_offset=bass.IndirectOffsetOnAxis(ap=hidx[:, k0 + s_:k0 + s_ + 1], axis=0))
                                             for s_ in range(nsub)],
                  reads=["hidx"], writes=[("hT", buf)], eng="pool", n=nsub)
        else:
            S.dma("ld_h%d" % buf, lambda e: e.dma_start(out=hb[:, 0:nsub, :], in_=io["hin"][row0:row0 + ntok, :].rearrange("(s p) d -> p s d", p=128)),
                  writes=[("hT", buf)])
        S.dma("ld_o%d" % buf, lambda e: [nc.gpsimd.indirect_dma_start(out=oTs[buf][:, r_, :], out_offset=None, in_=io["oall"],
                                                                    in_offset=bass.IndirectOffsetOnAxis(ap=oidx[:, t5, r_:r_ + 1], axis=0))
                                         for r_ in range(8)],
              reads=["oidx"], writes=[("oTs", buf)], eng="pool", n=8)
        for s in range(nsub):
            for half in range(2):
                a = nacc()
                for c in range(8):
                    S.op("pe", lambda e, a=a, c=c, s=s, half=half: e.matmul(acc[a][:, :], lhsT=oTs[buf][:, c, oc0 + s * 128:oc0 + (s + 1) * 128],
                                                                            rhs=wo_s[:, c, half * 512:(half + 1) * 512], start=(c == 0), stop=(c == 7)),
                         reads=[("oTs", buf), "wo_s"], writes=[("acc", a)])
                S.op("dve", lambda e, a=a, s=s, half=half: e.tensor_tensor(out=hb[:, s, half * 512:(half + 1) * 512], in0=hb[:, s, half * 512:(half + 1) * 512],
                                                                           in1=acc[a][:, :], op=ALU.add),
                     reads=[("acc", a), ("hT", buf)], writes=[("hT", buf)])
        for s in range(nsub):
            rmsnorm_to_T(hb[:, s, :], ("hT", buf), gxa, "gxa", hnT, "hnT", s * 128)
        for hd in range(4):
            a = nacc()
            for c in range(8):
                S.op("pe", lambda e, a=a, c=c, hd=hd: e.matmul(acc[a][0:64, 0:ntok], lhsT=wq_s[:, c, hd * 64:(hd + 1) * 64], rhs=hnT[:, c, 0:ntok],
                                                                start=(c == 0), stop=(c == 7)),
                     reads=["wq_s", "hnT"], writes=[("acc", a)])
            S.op("act", lambda e, a=a, hd=hd: e.copy(out=qxT[:, hd, 0:ntok], in_=acc[a][0:64, 0:ntok]), reads=[("acc", a)], writes=[("qxT", hd)])
        for hd in range(4):
            pk = []
            for mc in range(2):
                a = nacc()
                S.op("pe", lambda e, a=a, mc=mc, hd=hd: e.matmul(acc[a][:, 0:ntok], lhsT=kxT[:, hd, mc * 128:(mc + 1) * 128], rhs=qxT[:, hd, 0:ntok],
                                                                  start=True, stop=True),
                     reads=["kxT", ("qxT", hd)], writes=[("acc", a)])
                p = rot["px"] = (rot["px"] + 1) % 4
                S.op("act", lambda e, a=a, p=p: e.activation(out=pxT[p][:, 0:ntok], in_=acc[a][:, 0:ntok], func=AF.Exp, scale=0.125),
                     reads=[("acc", a)], writes=[("pxT", p)])
                pk.append(p)
            a = nacc()
            for mc in range(2):
                S.op("pe", lambda e, a=a, mc=mc, hd=hd, p=pk[mc]: e.matmul(acc[a][0:65, 0:ntok], lhsT=vxa[:, mc, hd, :], rhs=pxT[p][:, 0:ntok],
                                                                            start=(mc == 0), stop=(mc == 1)),
                     reads=["vxa", ("pxT", pk[mc])], writes=[("acc", a)])
            S.op("dve", lambda e, a=a: e.reciprocal(out=rd[64:65, 0:ntok], in_=acc[a][64:65, 0:ntok]), reads=[("acc", a)], writes=["rd"])
            S.op("act", lambda e, a=a: e.copy(out=nm[0:64, 0:ntok], in_=acc[a][0:64, 0:ntok]), reads=[("acc", a)], writes=["nm"])
            a2 = nacc()
            S.op("pe", lambda e, a2=a2: e.matmul(acc[a2][0:64, 0:ntok], lhsT=ones_f[64:65, 0:64], rhs=rd[64:65, 0:ntok], start=True, stop=True),
                 reads=["ones_f", "rd"], writes=[("acc", a2)])
            S.op("dve", lambda e, a2=a2, hd=hd: e.tensor_tensor(out=oxT[:, hd, 0:ntok], in0=nm[0:64, 0:ntok], in1=acc[a2][0:64, 0:ntok], op=ALU.mult),
                 reads=["nm", ("acc", a2)], writes=[("oxT", hd)])
        for s in range(nsub):
            for half in range(2):
                a = nacc()
                for hd in range(4):
                    S.op("pe", lambda e, a=a, hd=hd, s=s, half=half: e.matmul(acc[a][:, :], lhsT=oxT[:, hd, s * 128:(s + 1) * 128],
                                                                              rhs=wxo_s[:, hd, half * 512:(half + 1) * 512], start=(hd == 0), stop=(hd == 3)),
                         reads=[("oxT", hd), "wxo_s"], writes=[("acc", a)])
                S.op("dve", lambda e, a=a, s=s, half=half: e.tensor_tensor(out=hb[:, s, half * 512:(half + 1) * 512], in0=hb[:, s, half * 512:(half + 1) * 512],
                                                                           in1=acc[a][:, :], op=ALU.add),
                     reads=[("acc", a), ("hT", buf)], writes=[("hT", buf)])
        for s in range(nsub):
            rmsnorm_to_T(hb[:, s, :], ("hT", buf), gffn, "gffn", hnT, "hnT", s * 128)

    front(1, 1, 0)
    for j in range(NPAIR):
        slot = load_up(j)
        for gv in range(2):
            grp = j + gv * NPAIR
            a = nacc()
            for c in range(8):
                S.op("pe", lambda e, a=a, c=c, gv=gv, slot=slot: e.matmul(acc[a][:, 0:2], lhsT=wup_r[slot][:, c, gv * 128:(gv + 1) * 128], rhs=hnT[:, c, 126:128],
                                                                           start=(c == 0), stop=(c == 7)),
                     reads=[("wup_r", slot), "hnT"], writes=[("acc", a)])
            S.op("dve", lambda e, a=a, grp=grp: e.tensor_scalar(out=ucarry[:, grp, :], in0=acc[a][:, 0:2], scalar1=flag[:, 0:1], scalar2=None, op0=ALU.mult),
                 reads=[("acc", a), "flag"], writes=[("ucarry", grp)])

    for tt in range(NTT):
        buf = tt % 2
        hb = hT[buf]
        front(buf, 4, 128 + tt * 512)
        for j in range(NPAIR):
            slot = load_up(j)
            yb = j % 2
            for gv in range(2):
                grp = j + gv * NPAIR
                Y = (Yg if gv == 0 else Yv)[yb]
                ykey = ("Y", gv, yb)
                a = nacc()
                for c in range(8):
                    S.op("pe", lambda e, a=a, c=c, gv=gv, slot=slot: e.matmul(acc[a][:, :], lhsT=wup_r[slot][:, c, gv * 128:(gv + 1) * 128], rhs=hnT[:, c, :],
                                                                               start=(c == 0), stop=(c == 7)),
                         reads=[("wup_r", slot), "hnT"], writes=[("acc", a)])
                S.op("act", lambda e, a=a, grp=grp, Y=Y: e.activation(out=Y[:, :], in_=acc[a][:, :], func=AF.Identity, bias=cb[:, grp:grp + 1], scale=cw[:, grp, 2:3]),
                     reads=[("acc", a), "cw", "cb"], writes=[ykey])
                S.op("dve", lambda e, a=a, grp=grp, Y=Y: e.scalar_tensor_tensor(out=Y[:, 1:512], in0=acc[a][:, 0:511], scalar=cw[:, grp, 1:2], in1=Y[:, 1:512],
                                                                                op0=ALU.mult, op1=ALU.add),
                     reads=[("acc", a), "cw", ykey], writes=[ykey])
                S.op("dve", lambda e, a=a, grp=grp, Y=Y: e.scalar_tensor_tensor(out=Y[:, 2:512], in0=acc[a][:, 0:510], scalar=cw[:, grp, 0:1], in1=Y[:, 2:512],
                                                                                op0=ALU.mult, op1=ALU.add),
                     reads=[("acc", a), "cw", ykey], writes=[ykey])
                S.op("dve", lambda e, grp=grp, Y=Y: e.scalar_tensor_tensor(out=Y[:, 0:1], in0=ucarry[:, grp, 1:2], scalar=cw[:, grp, 1:2], in1=Y[:, 0:1],
                                                                            op0=ALU.mult, op1=ALU.add),
                     reads=[("ucarry", grp), "cw", ykey], writes=[ykey])
                S.op("dve", lambda e, grp=grp, Y=Y: e.scalar_tensor_tensor(out=Y[:, 0:2], in0=ucarry[:, grp, 0:2], scalar=cw[:, grp, 0:1], in1=Y[:, 0:2],
                                                                            op0=ALU.mult, op1=ALU.add),
                     reads=[("ucarry", grp), "cw", ykey], writes=[ykey])
                S.op("dve", lambda e, a=a, grp=grp: e.tensor_copy(out=ucarry[:, grp, :], in_=acc[a][:, 510:512]),
                     reads=[("acc", a)], writes=[("ucarry", grp)])
            S.op("act", lambda e, yb=yb: e.activation(out=Yg[yb][:, :], in_=Yg[yb][:, :], func=AF.Silu),
                 reads=[("Y", 0, yb)], writes=[("Y", 0, yb)])
            S.op("dve", lambda e, yb=yb, j=j: e.tensor_tensor(out=gT[:, j, :], in0=Yg[yb][:, :], in1=Yv[yb][:, :], op=ALU.mult),
                 reads=[("Y", 0, yb), ("Y", 1, yb)], writes=[("gT", j)])
        for half in range(2):
            accs = [nacc() for _ in range(4)]
            for j in range(NPAIR):
                slot = load_dn(j, half)
                for s in range(4):
                    a = accs[s]
                    S.op("pe", lambda e, a=a, j=j, s=s, slot=slot: e.matmul(acc[a][:, :], lhsT=gT[:, j, s * 128:(s + 1) * 128], rhs=wdn_r[slot][:, :],
                                                                             start=(j == 0), stop=(j == NPAIR - 1)),
                         reads=[("gT", j), ("wdn_r", slot)], writes=[("acc", a)])
            for s in range(4):
                a = accs[s]
                S.op("dve", lambda e, a=a, s=s, half=half, hb=hb: e.tensor_tensor(out=hb[:, s, half * 512:(half + 1) * 512], in0=hb[:, s, half * 512:(half + 1) * 512],
                                                                           in1=acc[a][:, :], op=ALU.add),
                     reads=[("acc", a), ("hT", buf)], writes=[("hT", buf)])
        if final:
            for s in range(4):
                S.op("act", lambda e, s=s, hb=hb: e.activation(out=junk[:, :], in_=hb[:, s, :], func=AF.Square, accum_out=stat[:, 4:5]),
                     reads=[("hT", buf)], writes=["junk", "stat2"])
                S.op("act", lambda e: e.activation(out=stat[:, 5:6], in_=stat[:, 4:5], func=AF.Ln, bias=EPS, scale=1.0 / D),
                     reads=["stat2"], writes=["stat2"])
                S.op("act", lambda e: e.activation(out=stat[:, 6:7], in_=stat[:, 5:6], func=AF.Exp, scale=-0.5),
                     reads=["stat2"], writes=["stat2"])
                S.op("dve", lambda e, s=s, hb=hb: e.scalar_tensor_tensor(out=hb[:, s, :], in0=hb[:, s, :], scalar=stat[:, 6:7], in1=gfin[:, :],
                                                                  op0=ALU.mult, op1=ALU.mult),
                     reads=[("hT", buf), "stat2", "gfin"], writes=[("hT", buf)])
        S.dma("st_h", lambda e, tt=tt, hb=hb: e.dma_start(out=io["hout"][tt * 512:(tt + 1) * 512, :].rearrange("(s p) d -> p s d", p=128), in_=hb[:, :, :]),
              reads=[("hT", buf)], writes=[("hout", tt)])


def rest_weights(layer, w_out, xa_w_q, xa_w_kv, xa_w_out, ffn_w_up, ffn_conv_w, ffn_conv_b, ffn_w_down,
                 norm_xa_g, norm_mem_g, norm_ffn_g, final_norm_g, mem):
    f = np.float32
    c = np.ascontiguousarray
    wup = ffn_w_up[layer].reshape(8, 128, 2, NPAIR, 128).transpose(3, 1, 0, 2, 4)
    return {
        "w_out": c(w_out.reshape(8, 128, 1024).transpose(1, 0, 2), dtype=f),
        "wq": c(xa_w_q[layer].reshape(8, 128, 256).transpose(1, 0, 2), dtype=f),
        "wkv": c(xa_w_kv[layer].reshape(8, 128, 512).transpose(1, 0, 2), dtype=f),
        "wxo": c(xa_w_out[layer].reshape(4, 64, 1024).transpose(1, 0, 2), dtype=f),
        "wup": c(wup.reshape(NPAIR, 128, 8 * 256), dtype=f),
        "wdn": c(ffn_w_down[layer].reshape(NPAIR, 128, 2, 512).transpose(0, 2, 1, 3), dtype=f),
        "cw": c(ffn_conv_w[layer].reshape(3, 44, 128).transpose(2, 1, 0), dtype=f),
        "cb": c(ffn_conv_b[layer].reshape(44, 128).T, dtype=f),
        "g_xa": c(norm_xa_g[layer], dtype=f),
        "g_mem": c(norm_mem_g[layer], dtype=f),
        "g_ffn": c(norm_ffn_g[layer], dtype=f),
        "g_fin": c(final_norm_g, dtype=f),
        "mem": c(mem[0], dtype=f),
        "ident": np.eye(128, dtype=f),
    }


NTILE = SEQ // 512
NBLK = SEQ // 128
SB_WIN = 3


def emit_mixer(cx, io, kind, ntile=NTILE):
    nc, S = cx.nc, cx.S
    sb, ps = cx.sb, cx.ps
    fox = kind == "fox"
    NCOL = 386 if fox else 512
    if fox:
        QA, KA, QB, KB, VV, FF = 0, 64, 128, 192, 256, 384
    else:
        QA, KA, QB, KB, VV, QP, KP = 0, 64, 128, 192, 256, 384, 448
    ident = sb("ident", [128, 128], BF16)
    ident_f = sb("ident_f", [128, 128], F32)
    ones_f = sb("ones_f", [128, 64], F32)
    gmix = sb("gmix", [128, D], F32)
    w_s = sb("w_s", [128, 8, NCOL], BF16)
    maskI = sb("maskI", [128, 4, 512], BF16)
    kT = [sb("kT%d" % h, [128, SEQ], BF16) for h in range(2)]
    vX = [sb("vX%d" % h, [128, NBLK, 65], BF16) for h in range(2)]
    qT = [[sb("qT%d_%d" % (h, i), [128, 512], BF16) for i in range(2)] for h in range(2)]
    NXT = 2 if fox else 1
    xt = [sb("xt%d" % i, [128, 4, D], F32) for i in range(NXT)]
    hn_s = [sb("hn_s%d" % i, [128, D], BF16) for i in range(2)]
    junk = sb("junk", [128, D], BF16)
    stat = sb("stat", [128, 8], F32)
    hnT = sb("hnT", [128, 8, 512], BF16)
    NPT = 6 if fox else 4
    pT = [sb("pT%d" % i, [128, 512], BF16) for i in range(6)]
    nm = sb("nm", [128, 512], F32)
    rd = sb("rd", [128, 512], F32)
    oTs = [sb("oTs%d" % i, [64, 512], BF16) for i in range(2)]
    trp = [ps("trp%d" % i, [128, 1024], BF16) for i in range(1)]
    NPACC = 2 if fox else 1
    pacc = [ps("pacc%d" % i, [128, 512], F32) for i in range(NPACC)]
    NSC = 3 if fox else 2
    sc = [ps("sc%d" % i, [128, 512], F32) for i in range(NSC)]
    av = [ps("av%d" % i, [128, 512], F32) for i in range(2)]
    rot = {"pacc": 0, "sc": 0, "pt": 0, "av": 0, "hn": 0, "ot": 0}

    def nxt(k, n):
        rot[k] = (rot[k] + 1) % n
        return rot[k]

    def pdma(stream, out, in_, writes):
        S.dma(stream, lambda e: nc.gpsimd.dma_start(out=out, in_=in_), writes=writes, eng="pool")

    pdma("c_id", ident[:], io["ident"], ["ident"])
    pdma("c_w", w_s[:], io["w"], ["w_s"])
    pdma("c_mi", maskI[:], io["maskI"], ["maskI"])
    S.dma("c_g", lambda e: [e.dma_start(out=gmix[:], in_=io["g_mix"].partition_broadcast(128)),
                            e.dma_start(out=ident_f[:], in_=io["ident"])], writes=["gmix", "ident_f"], n=2)
    S.op("dve", lambda e: e.memset(ones_f[:], 1.0), writes=["ones_f"])
    for h in range(2):
        S.op("pool", lambda e, h=h: e.memset(vX[h][:], 1.0), writes=[("vX", h, i_) for i_ in range(NTILE)])
    if fox:
        nbf = sb("nbf", [2, 1], F32)
        cprev = sb("cprev", [2, 1], F32)
        fE = sb("fE", [2, 512], F32)
        cc = sb("cc", [2, 512], F32)
        rbf = sb("rbf", [2, 512], BF16)
        negc = sb("negc", [128, NBLK, 2], F32)
        S.dma("c_bf", lambda e: e.dma_start(out=nbf[:], in_=io["bf"]), writes=["nbf"])
        S.op("dve", lambda e: e.tensor_scalar(out=nbf[:], in0=nbf[:], scalar1=-1.0, scalar2=None, op0=ALU.mult), reads=["nbf"], writes=["nbf"])
        S.op("dve", lambda e: e.memset(cprev[:], 0.0), writes=["cprev"])
        for h in range(2):
            S.op("pool", lambda e, h=h: e.memset(kT[h][64:128, :], 1.0), writes=[("kT", h, i_) for i_ in range(NTILE)])


    if not fox:
        maskS = sb("maskS", [128, 4, 512], BF16)
        tri = sb("tri", [128, 256], F32)
        invf = sb("invf", [64, 1], F32)
        sgn = sb("sgn", [64, 1], F32)
        pos_i = sb("pos_i", [64, 512], I32)
        ang = sb("ang", [64, 512], F32)
        kint = sb("kint", [64, 512], I32)
        kflt = sb("kflt", [64, 512], F32)
        msk = sb("msk", [64, 512], F32)
        cosF = sb("cosF", [64, 512], F32)
        sinS = sb("sinS", [64, 512], F32)
        rt1 = sb("rt1", [64, 512], F32)
        qrot = sb("qrot", [64, 512], F32)
        krot = sb("krot", [64, 512], F32)
        kmT = sb("kmT", [64, 64], F32)
        G = sb("G", [128, 64], F32)
        top8 = sb("top8", [128, 8], F32)
        MB = sb("MB", [128, 128], BF16)
        spE = [sb("spE%d" % k, [128, 512], F32) for k in range(2)]
        spM = [sb("spM%d" % k, [128, 512], F32) for k in range(2)]
        lgA = [sb("lgA%d" % k, [128, 512], F32) for k in range(2)]
        tailP = ps("tailP", [128, 512], F32)
        scS = ps("scS", [128, 512], F32)
        pdma("c_ms", maskS[:], io["maskS"], ["maskS"])
        S.dma("c_ab", lambda e: [e.dma_start(out=tri[:], in_=io["tri"]), e.dma_start(out=invf[:], in_=io["invf"]),
                                 e.dma_start(out=sgn[:], in_=io["sgn"])], writes=["tri", "invf", "sgn"], n=3)
        S.dma("c_koh", lambda e: nc.gpsimd.dma_start(out=kT[1][64:128, :], in_=io["koh"]), writes=[("kT", 1, i_) for i_ in range(NTILE)], eng="pool")
        S.op("dve", lambda e: e.memset(kmT[:], 0.0), writes=["kmT"])
        S.op("dve", lambda e: e.memset(MB[:], 0.0), writes=["MB"])

    def rope_tables(i):
        PI = float(np.pi)
        S.dma("ld_pos", lambda e: e.dma_start(out=pos_i[:], in_=io["pos"][i * 512:(i + 1) * 512].partition_broadcast(64)), writes=["pos_i"])
        S.op("dve", lambda e: e.tensor_copy(out=ang[:], in_=pos_i[:]), reads=["pos_i"], writes=["ang"])
        S.op("dve", lambda e: e.tensor_scalar(out=ang[:], in0=ang[:], scalar1=invf[:, 0:1], scalar2=None, op0=ALU.mult), reads=["ang", "invf"], writes=["ang"])
        S.op("dve", lambda e: e.tensor_scalar(out=kint[:], in0=ang[:], scalar1=float(1.0 / (2 * np.pi)), scalar2=None, op0=ALU.mult), reads=["ang"], writes=["kint"])
        S.op("dve", lambda e: e.tensor_copy(out=kflt[:], in_=kint[:]), reads=["kint"], writes=["kflt"])
        S.op("dve", lambda e: e.scalar_tensor_tensor(out=ang[:], in0=kflt[:], scalar=-6.28125, in1=ang[:], op0=ALU.mult, op1=ALU.add), reads=["kflt", "ang"], writes=["ang"])
        S.op("dve", lambda e: e.scalar_tensor_tensor(out=ang[:], in0=kflt[:], scalar=float(-(2 * np.pi - 6.28125)), in1=ang[:], op0=ALU.mult, op1=ALU.add),
             reads=["kflt", "ang"], writes=["ang"])
        S.op("dve", lambda e: e.tensor_scalar(out=msk[:], in0=ang[:], scalar1=PI, scalar2=-2 * PI, op0=ALU.is_gt, op1=ALU.mult), reads=["ang"], writes=["msk"])
        S.op("dve", lambda e: e.tensor_tensor(out=ang[:], in0=ang[:], in1=msk[:], op=ALU.add), reads=["ang", "msk"], writes=["ang"])
        S.op("dve", lambda e: e.tensor_scalar(out=msk[:], in0=ang[:], scalar1=-PI, scalar2=2 * PI, op0=ALU.is_lt, op1=ALU.mult), reads=["ang"], writes=["msk"])
        S.op("dve", lambda e: e.tensor_tensor(out=ang[:], in0=ang[:], in1=msk[:], op=ALU.add), reads=["ang", "msk"], writes=["ang"])
        S.op("dve", lambda e: e.tensor_scalar(out=rt1[:], in0=ang[:], scalar1=PI / 2, scalar2=None, op0=ALU.add), reads=["ang"], writes=["rt1"])
        S.op("dve", lambda e: e.tensor_scalar(out=msk[:], in0=rt1[:], scalar1=PI, scalar2=-2 * PI, op0=ALU.is_gt, op1=ALU.mult), reads=["rt1"], writes=["msk"])
        S.op("dve", lambda e: e.tensor_tensor(out=rt1[:], in0=rt1[:], in1=msk[:], op=ALU.add), reads=["rt1", "msk"], writes=["rt1"])
        S.op("act", lambda e: e.activation(out=sinS[:], in_=ang[:], func=AF.Sin, scale=sgn[:, 0:1]), reads=["ang", "sgn"], writes=["sinS"])
        S.op("act", lambda e: e.activation(out=cosF[:], in_=rt1[:], func=AF.Sin), reads=["rt1"], writes=["cosF"])

    def rope_proj(col_main, col_perm, i, dst, dkey):
        a_main = proj64(col_main, i)
        S.op("dve", lambda e: e.tensor_tensor(out=rt1[:], in0=pacc[a_main][0:64, :], in1=cosF[:], op=ALU.mult), reads=[("pacc", a_main), "cosF"], writes=["rt1"])
        a_perm = proj64(col_perm, i)
        S.op("dve", lambda e: e.tensor_tensor(out=dst[:], in0=pacc[a_perm][0:64, :], in1=sinS[:], op=ALU.mult), reads=[("pacc", a_perm), "sinS"], writes=[dkey])
        S.op("dve", lambda e: e.tensor_tensor(out=dst[:], in0=dst[:], in1=rt1[:], op=ALU.add), reads=[dkey, "rt1"], writes=[dkey])

    def moba_gate(i, qbuf):
        for s in range(4):
            own = 2 * i + s // 2
            if own > 0:
                a = nxt("pacc", NPACC)
                S.op("pe", lambda e, a=a, s=s: e.matmul(pacc[a][:, 0:64], lhsT=qrot[0:64, s * 128:(s + 1) * 128], rhs=kmT[0:64, 0:64], start=True, stop=True),
                     reads=["qrot", "kmT"], writes=[("pacc", a)])
                S.op("dve", lambda e: e.memset(G[:], -1e9), writes=["G"])
                S.op("dve", lambda e, a=a, own=own: e.tensor_copy(out=G[:, 0:own], in_=pacc[a][:, 0:own]), reads=[("pacc", a), "G"], writes=["G"])
                S.op("dve", lambda e: e.max(out=top8[:], in_=G[:]), reads=["G"], writes=["top8"])
                S.op("dve", lambda e: e.tensor_scalar(out=top8[:, 2:3], in0=top8[:, 2:3], scalar1=-1e8, scalar2=None, op0=ALU.max), reads=["top8"], writes=["top8"])
                S.op("dve", lambda e: e.tensor_scalar(out=MB[:, 64:128], in0=G[:], scalar1=top8[:, 2:3], scalar2=-1.0, op0=ALU.is_ge, op1=ALU.add),
                     reads=["G", "top8"], writes=["MB"])
            S.op("dve", lambda e, own=own: e.memset(MB[:, 64 + own:128], 0.0), reads=["MB"], writes=["MB"])
            S.op("pe", lambda e: e.transpose(trp[0][:, 0:128], MB[:, :], ident[:, :]), reads=["MB", "ident"], writes=["trp"])
            S.op("act", lambda e, s=s: e.copy(out=qT[1][qbuf][64:128, s * 128:(s + 1) * 128], in_=trp[0][64:128, 0:128]),
                 reads=["trp"], writes=[("qT", 1, qbuf)])

    def sb_attend(i, qbuf):
        a = 0
        lo = max(0, 4 * i - SB_WIN)
        blocks = list(range(4 * i + 3, lo - 1, -1))
        for n, blk in enumerate(blocks):
            j = blk - 4 * i
            p_ = 4 + n % 2
            b2 = n % 2
            S.op("pe", lambda e, blk=blk: e.matmul(scS[:, :], lhsT=kT[0][0:64, blk * 128:(blk + 1) * 128], rhs=qT[0][qbuf][0:64, :], start=True, stop=True),
                 reads=[("kT", 0, blk // 4), ("qT", 0, qbuf)], writes=["scS"])
            yield
            S.op("act", lambda e, b2=b2: e.activation(out=spE[b2][:, :], in_=scS[:, :], func=AF.Exp), reads=["scS"], writes=[("spE", b2)])
            yield
            S.op("act", lambda e, b2=b2: e.activation(out=spE[b2][:, :], in_=spE[b2][:, :], func=AF.Ln, bias=1.0, scale=1.0), reads=[("spE", b2)], writes=[("spE", b2)])
            yield
            if j >= 0:
                S.op("pool", lambda e, b2=b2, j=j: e.tensor_tensor(out=spM[b2][:, :], in0=spE[b2][:, :], in1=maskS[:, j, :], op=ALU.mult),
                     reads=[("spE", b2), "maskS"], writes=[("spM", b2)])
                src, skey = spM[b2], ("spM", b2)
            else:
                src, skey = spE[b2], ("spE", b2)
            S.op("dve", lambda e, b2=b2: e.tensor_tensor(out=lgA[b2][:, :], in0=scS[:, :], in1=spE[b2][:, :], op=ALU.subtract),
                 reads=["scS", ("spE", b2)], writes=[("lgA", b2)])
            yield
            S.op("pe", lambda e, src=src, n=n: e.matmul(tailP[:, :], lhsT=tri[:, 0:128], rhs=src[:, :], start=(n == 0), stop=True),
                 reads=["tri", skey], writes=["tailP"])
            yield
            S.op("dve", lambda e, b2=b2: e.tensor_tensor(out=lgA[b2][:, :], in0=lgA[b2][:, :], in1=tailP[:, :], op=ALU.add),
                 reads=[("lgA", b2), "tailP"], writes=[("lgA", b2)])
            yield
            S.op("pe", lambda e, src=src: e.matmul(tailP[:, :], lhsT=tri[:, 128:256], rhs=src[:, :], start=False, stop=True),
                 reads=["tri", skey], writes=["tailP"])
            S.op("act", lambda e, p_=p_, b2=b2: e.activation(out=pT[p_][:, :], in_=lgA[b2][:, :], func=AF.Exp), reads=[("lgA", b2)], writes=[("pT", p_)])
            yield
            if j >= 0:
                S.op("pool", lambda e, p_=p_, j=j: e.tensor_tensor(out=pT[p_][:, :], in0=pT[p_][:, :], in1=maskS[:, j, :], op=ALU.mult),
                     reads=[("pT", p_), "maskS"], writes=[("pT", p_)])
                yield
            S.op("pe", lambda e, p_=p_, blk=blk, n=n: e.matmul(av[a][0:64, :], lhsT=vX[0][:, blk, 0:64], rhs=pT[p_][:, :], start=(n == 0), stop=(n == len(blocks) - 1)),
                 reads=[("vX", 0, blk // 4), ("pT", p_)], writes=[("av", a)])
            yield
        finalize(i, 0, a, False)

    def proj64(col, i, nrows=64):
        a = nxt("pacc", NPACC)
        for c in range(8):
            S.op("pe", lambda e, a=a, c=c: e.matmul(pacc[a][0:nrows, :], lhsT=w_s[:, c, col:col + nrows], rhs=hnT[:, c, :], start=(c == 0), stop=(c == 7)),
                 reads=["w_s", "hnT"], writes=[("pacc", a)])
        return a

    LOOK = 2

    def attend(i, h, blocks, krows, bias_fn, qbuf, a=None, side=None):
        if a is None:
            a = nxt("av", 2)
        nb = len(blocks)
        pbuf = [None] * nb
        for n in range(nb + LOOK):
            if n < nb:
                blk = blocks[n]
                s_ = nxt("sc", NSC)
                p_ = nxt("pt", NPT)
                pbuf[n] = p_
                j = blk - 4 * i
                S.op("pe", lambda e, s_=s_, blk=blk, j=j: e.matmul(sc[s_][:, :], lhsT=kT[h][0:krows, blk * 128:(blk + 1) * 128], rhs=qT[h][qbuf][0:krows, :],
                                                                   start=True, stop=(j < 0)),
                     reads=[("kT", h, blk // 4), ("qT", h, qbuf)], writes=[("sc", s_)])
                if j >= 0:
                    S.op("pe", lambda e, s_=s_, j=j: e.matmul(sc[s_][:, :], lhsT=ident[:, :], rhs=maskI[:, j, :], start=False, stop=True),
                         reads=["ident", "maskI"], writes=[("sc", s_)])
                b_ap, b_key = bias_fn(blk)
                S.op("act", lambda e, s_=s_, p_=p_, b_ap=b_ap: e.activation(out=pT[p_][:, :], in_=sc[s_][:, :], func=AF.Exp, bias=b_ap, scale=1.0),
                     reads=[("sc", s_)] + b_key, writes=[("pT", p_)])
            m = n - LOOK
            if m >= 0:
                blk = blocks[m]
                p_ = pbuf[m]
                S.op("pe", lambda e, p_=p_, blk=blk, m=m: e.matmul(av[a][0:65, :], lhsT=vX[h][:, blk, :], rhs=pT[p_][:, :], start=(m == 0), stop=(m == nb - 1)),
                     reads=[("vX", h, blk // 4), ("pT", p_)], writes=[("av", a)])
            if side is not None:
                next(side, None)
        if side is not None:
            for _ in side:
                pass
        finalize(i, h, a, True)

    def finalize(i, h, a, normalize):
        o_ = nxt("ot", 2)
        if normalize:
            S.op("dve", lambda e: e.reciprocal(out=rd[64:65, :], in_=av[a][64:65, :]), reads=[("av", a)], writes=["rd"])
            S.op("act", lambda e: e.copy(out=nm[0:64, :], in_=av[a][0:64, :]), reads=[("av", a)], writes=["nm"])
            a2 = nxt("pacc", NPACC)
            S.op("pe", lambda e: e.matmul(pacc[a2][0:64, :], lhsT=ones_f[64:65, 0:64], rhs=rd[64:65, :], start=True, stop=True),
                 reads=["ones_f", "rd"], writes=[("pacc", a2)])
            S.op("dve", lambda e: e.tensor_tensor(out=oTs[o_][:, :], in0=nm[0:64, :], in1=pacc[a2][0:64, :], op=ALU.mult),
                 reads=["nm", ("pacc", a2)], writes=[("oTs", o_)])
        else:
            S.op("act", lambda e: e.copy(out=oTs[o_][:, :], in_=av[a][0:64, :]), reads=[("av", a)], writes=[("oTs", o_)])
        S.dma("st_o%d" % o_, lambda e: e.dma_start(out=io["oT"][i, h * 64:(h + 1) * 64, :], in_=oTs[o_][:, :]),
              reads=[("oTs", o_)], writes=[("oT", h, i)])

    for i in range(ntile):
        xb = i % NXT
        qbuf = i % 2
        S.dma("ld_x%d" % xb, lambda e, i=i, xb=xb: e.dma_start(out=xt[xb][:, :, :], in_=io["hfull"][i * 512:(i + 1) * 512, :].rearrange("(s p) d -> p s d", p=128)),
              writes=[("xt", xb)])
        for s in range(4):
            b = nxt("hn", 2)
            hb = hn_s[b]
            src = xt[xb][:, s, :]
            S.op("act", lambda e, src=src: e.activation(out=junk[:, :], in_=src, func=AF.Square, accum_out=stat[:, 0:1]),
                 reads=[("xt", xb)], writes=["junk", "stat"])
            S.op("act", lambda e: e.activation(out=stat[:, 1:2], in_=stat[:, 0:1], func=AF.Ln, bias=EPS, scale=1.0 / D), reads=["stat"], writes=["stat"])
            S.op("act", lambda e: e.activation(out=stat[:, 2:3], in_=stat[:, 1:2], func=AF.Exp, scale=-0.5), reads=["stat"], writes=["stat"])
            S.op("dve", lambda e, src=src, hb=hb: e.scalar_tensor_tensor(out=hb[:, :], in0=src, scalar=stat[:, 2:3], in1=gmix[:, :], op0=ALU.mult, op1=ALU.mult),
                 reads=[("xt", xb), "stat", "gmix"], writes=[("hn_s", b)])
            for c in range(8):
                S.op("pe", lambda e, c=c, hb=hb: e.transpose(trp[0][:, c * 128:(c + 1) * 128], hb[:, c * 128:(c + 1) * 128], ident[:, :]),
                     reads=[("hn_s", b), "ident"], writes=["trp"])
            S.op("dve", lambda e, s=s: e.tensor_copy(out=hnT[:, :, s * 128:(s + 1) * 128], in_=trp[0][:].rearrange("p (c t) -> p c t", c=8)),
                 reads=["trp"], writes=["hnT"])
        for s in range(4):
            a = nxt("pacc", NPACC)
            for c in range(8):
                S.op("pe", lambda e, a=a, c=c, s=s: e.matmul(pacc[a][:, 0:128], lhsT=hnT[:, c, s * 128:(s + 1) * 128], rhs=w_s[:, c, VV:VV + 128], start=(c == 0), stop=(c == 7)),
                     reads=["w_s", "hnT"], writes=[("pacc", a)])
            for h in range(2):
                S.op("dve", lambda e, a=a, h=h, s=s, i=i: e.tensor_copy(out=vX[h][:, 4 * i + s, 0:64], in_=pacc[a][:, h * 64:(h + 1) * 64]),
                     reads=[("pacc", a)], writes=[("vX", h, i)])
        if fox:
            a = nxt("pacc", NPACC)
            for c in range(8):
                S.op("pe", lambda e, a=a, c=c: e.matmul(pacc[a][0:2, :], lhsT=w_s[:, c, FF:FF + 2], rhs=hnT[:, c, :], start=(c == 0), stop=(c == 7)),
                     reads=["w_s", "hnT"], writes=[("pacc", a)])
            S.op("act", lambda e, a=a: e.activation(out=fE[:, :], in_=pacc[a][0:2, :], func=AF.Exp, bias=nbf[:, 0:1], scale=-1.0),
                 reads=[("pacc", a), "nbf"], writes=["fE"])
            S.op("act", lambda e: e.activation(out=fE[:, :], in_=fE[:, :], func=AF.Ln, bias=1.0, scale=1.0), reads=["fE"], writes=["fE"])
            S.op("dve", lambda e: e.tensor_scalar(out=fE[:, :], in0=fE[:, :], scalar1=-1.0, scalar2=None, op0=ALU.mult), reads=["fE"], writes=["fE"])
            S.op("dve", lambda e: e.tensor_tensor_scan(out=cc[:, :], data0=fE[:, :], data1=fE[:, :], initial=cprev[:, 0:1], op0=ALU.add, op1=ALU.bypass),
                 reads=["fE", "cprev"], writes=["cc"])
            S.op("dve", lambda e: e.tensor_copy(out=cprev[:, :], in_=cc[:, 511:512]), reads=["cc"], writes=["cprev"])
            S.op("dve", lambda e: e.tensor_copy(out=rbf[:, :], in_=cc[:, :]), reads=["cc"], writes=["rbf"])
            a = nxt("pacc", NPACC)
            for s in range(4):
                S.op("pe", lambda e, a=a, s=s: e.transpose(pacc[a][:, 2 * s:2 * s + 2], cc[0:2, s * 128:(s + 1) * 128], ident_f[0:2, 0:2]),
                     reads=["cc", "ident_f"], writes=[("pacc", a)])
            S.op("dve", lambda e, a=a, i=i: e.tensor_scalar(out=negc[:, 4 * i:4 * i + 4, :], in0=pacc[a][:, 0:8].rearrange("p (s h) -> p s h", h=2),
                                                            scalar1=-1.0, scalar2=None, op0=ALU.mult),
                 reads=[("pacc", a)], writes=[("negc", i)])
            for h in range(2):
                qc, kc = (QA, KA) if h == 0 else (QB, KB)
                a = proj64(qc, i)
                S.op("act", lambda e, a=a, h=h, qbuf=qbuf: e.activation(out=qT[h][qbuf][0:64, :], in_=pacc[a][0:64, :], func=AF.Copy, scale=0.125),
                     reads=[("pacc", a)], writes=[("qT", h, qbuf)])
                S.dma("ld_r%d" % h, lambda e, h=h, qbuf=qbuf: e.dma_start(out=qT[h][qbuf][64:65, :], in_=rbf[h:h + 1, :]), reads=["rbf", ("qT", h, qbuf)], writes=[("qT", h, qbuf)])
                a = proj64(kc, i)
                S.op("act", lambda e, a=a, h=h, i=i: e.copy(out=kT[h][0:64, i * 512:(i + 1) * 512], in_=pacc[a][0:64, :]),
                     reads=[("pacc", a)], writes=[("kT", h, i)])
            for h in range(2):
                attend(i, h, list(range(4 * i + 4)), 65, lambda blk, h=h: (negc[:, blk, h:h + 1], [("negc", blk // 4)]), qbuf)
        if not fox:
            rope_tables(i)
            a = proj64(QA, i)
            S.op("act", lambda e, a=a, qbuf=qbuf: e.activation(out=qT[0][qbuf][0:64, :], in_=pacc[a][0:64, :], func=AF.Copy, scale=0.125),
                 reads=[("pacc", a)], writes=[("qT", 0, qbuf)])
            a = proj64(KA, i)
            S.op("act", lambda e, a=a, i=i: e.copy(out=kT[0][0:64, i * 512:(i + 1) * 512], in_=pacc[a][0:64, :]), reads=[("pacc", a)], writes=[("kT", 0, i)])
            rope_proj(QB, QP, i, qrot, "qrot")
            S.op("act", lambda e, qbuf=qbuf: e.activation(out=qT[1][qbuf][0:64, :], in_=qrot[:, :], func=AF.Copy, scale=0.125),
                 reads=["qrot"], writes=[("qT", 1, qbuf)])
            rope_proj(KB, KP, i, krot, "krot")
            S.op("act", lambda e, i=i: e.copy(out=kT[1][0:64, i * 512:(i + 1) * 512], in_=krot[:, :]), reads=["krot"], writes=[("kT", 1, i)])
            S.op("dve", lambda e, i=i: e.tensor_reduce(out=kmT[:, 2 * i:2 * i + 2], in_=krot[:, :].rearrange("p (n k) -> p n k", n=2), axis=AX.X, op=ALU.add),
                 reads=["krot", "kmT"], writes=["kmT"])
            S.op("dve", lambda e, i=i: e.tensor_scalar(out=kmT[:, 2 * i:2 * i + 2], in0=kmT[:, 2 * i:2 * i + 2], scalar1=1.0 / 256, scalar2=None, op0=ALU.mult),
                 reads=["kmT"], writes=["kmT"])
            moba_gate(i, qbuf)
            attend(i, 1, list(range(4 * i + 4)), 128, lambda blk: (0.0, []), qbuf, a=1, side=sb_attend(i, qbuf))
    if "dbg_negc" in io:
        S.dma("dbg", lambda e: [e.dma_start(out=io["dbg_negc"], in_=negc[:].rearrange("p b h -> p (b h)")),
                                e.dma_start(out=io["dbg_q"], in_=qT[0][(ntile - 1) % 2][:, :]),
                                e.dma_start(out=io["dbg_k"], in_=kT[0][:, 0:1024]),
                                e.dma_start(out=io["dbg_cc"], in_=cc[:, :])],
              reads=[("negc", i_) for i_ in range(ntile)] + [("qT", 0, (ntile - 1) % 2), ("kT", 0, 0), ("kT", 0, 1), "cc"], n=4)


def _masks():
    s_ = np.arange(128)[:, None, None]
    j_ = np.arange(4)[None, :, None]
    t_ = np.arange(512)[None, None, :]
    mi = ((128 * j_ + s_) <= t_).astype(np.float32)
    ms = ((128 * j_ + s_) < t_).astype(np.float32)
    return mi, ms


REST_W = ("w_out", "wq", "wkv", "wxo", "wup", "wdn", "cw", "cb", "g_xa", "g_mem", "g_ffn")
REST_SHAPES = {"w_out": [128, 8, 1024], "wq": [128, 8, 256], "wkv": [128, 8, 512], "wxo": [64, 4, 1024],
               "wup": [NPAIR, 128, 8 * 256], "wdn": [NPAIR, 2, 128, 512], "cw": [128, 44, 3], "cb": [128, 44],
               "g_xa": [D], "g_mem": [D], "g_ffn": [D]}


def build_fused(phases="AgBhCiD"):
    nc = bass.Bass("TRN2", target_bir_lowering=False, num_devices=NCORES)
    din = lambda name, shape, dt: nc.dram_tensor(name, list(shape), dt, kind="ExternalInput").ap()
    dint = lambda name, shape, dt: nc.dram_tensor(name, list(shape), dt, kind="Internal").ap()
    x = din("x", [SEQ, D], F32)
    hin0 = din("hin0", [NT + 128, D], F32)
    flag = din("flag", [128, 1], F32)
    hidx = din("hidx", [128, 17], I32)
    oidx = din("oidx", [128, 5, 8], I32)
    ident = din("ident", [128, 128], F32)
    maskI = din("maskI", [128, 4, 512], F32)
    maskS = din("maskS", [128, 4, 512], F32)
    pos = din("pos", [SEQ], I32)
    invf = din("invf", [64, 1], F32)
    sgn = din("sgn", [64, 1], F32)
    koh = din("koh", [64, SEQ], F32)
    tri = din("tri", [128, 256], F32)
    mem = din("mem", [MEM, D], F32)
    a_w = din("a_w", [128, 8, 512], F32)
    a_g = din("a_g", [D], F32)
    c_w = din("c_w", [128, 8, 386], F32)
    c_g = din("c_g", [D], F32)
    c_bf = din("c_bf", [2, 1], F32)
    g_fin = din("g_fin", [D], F32)
    rw = [{k: din("r%d_%s" % (L, k), REST_SHAPES[k], F32) for k in REST_W} for L in range(2)]
    out = nc.dram_tensor("out", [NT, D], F32, kind="ExternalOutput").ap()
    o_src = [dint("o_src%d" % L, [NTILE * 128, 512], BF16) for L in range(2)]
    o_all = [dint("o_all%d" % L, [NCORES * NTILE * 128, 512], BF16) for L in range(2)]
    h1_src = dint("h1_src", [NT, D], F32)
    h1_all = dint("h1_all", [SEQ, D], F32)

    def phase(tag, body, waits):
        with nc.cleanup_on_exit():
            with contextlib.ExitStack() as st:
                cx = Ctx(nc, st, tag)
                body(cx)
                cx.S.emit(final_wait_streams=waits)
            nc.all_engine_barrier()

    def gather(tag, src, dst):
        phase(tag, lambda cx: cx.S.cc("ag", lambda e: nc.gpsimd.collective_compute(
            "AllGather", ALU.bypass, replica_groups=[list(range(NCORES))], ins=[src.opt()], outs=[dst.opt()])), ["ag"])

    def scrub(tag):
        def body(cx):
            big = cx.sb("big", [128, 50000], F32)
            pz = [cx.ps("pz%d" % k, [128, 2048], F32) for k in range(2)]
            cx.S.op("dve", lambda e: e.memset(big[:, 0:25000], 0.0), writes=["b0"])
            cx.S.op("pool", lambda e: e.memset(big[:, 25000:50000], 0.0), writes=["b1"])
            for k in range(2):
                cx.S.op("act", lambda e, k=k: e.copy(out=pz[k][:, :], in_=big[:, 0:2048]), reads=["b0"], writes=[("pz", k)])
        phase(tag, body, [])

    def rest_io(L, hin, hout, with_hidx):
        io = dict(rw[L])
        io.update({"hin": hin, "oall": o_all[L], "oidx": oidx, "flag": flag, "mem": mem, "ident": ident, "g_fin": g_fin, "hout": hout})
        if with_hidx:
            io["hidx"] = hidx
        return io

    _phase, _gather = phase, gather
    phase = lambda tag, body, waits: _phase(tag, body, waits) if (tag in phases or tag[0] in "SG") else None
    gather = lambda tag, src, dst: _gather(tag, src, dst) if {"G0": "g", "G1": "h", "G2": "i"}[tag] in phases else None
    phase("A", lambda cx: emit_mixer(cx, {"hfull": x, "g_mix": a_g, "w": a_w, "ident": ident, "maskI": maskI, "maskS": maskS, "pos": pos,
                                          "invf": invf, "sgn": sgn, "koh": koh, "tri": tri,
                                          "oT": o_src[0].rearrange("(i f) t -> i f t", f=128)}, "ab"), ["st_o0", "st_o1"])
    if "X" in phases:
        dbg = nc.dram_tensor("dbg_o", [NTILE * 128, 512], BF16, kind="ExternalOutput").ap()
        _phase("X", lambda cx: cx.S.dma("cp", lambda e: e.dma_start(out=dbg, in_=o_src[0]), writes=["dbg"]), ["cp"])
    gather("G0", o_src[0], o_all[0])
    if "W" in phases:
        dbgw = nc.dram_tensor("dbg_oall", [NCORES * NTILE * 128, 512], BF16, kind="ExternalOutput").ap()
        _phase("W", lambda cx: cx.S.dma("cp", lambda e: e.dma_start(out=dbgw, in_=o_all[0]), writes=["dbg"]), ["cp"])
    if "s" in phases:
        scrub("S1")
    phase("B", lambda cx: emit_rest(cx, rest_io(0, hin0, h1_src, False), False), ["st_h"])
    if "Y" in phases:
        dbg1 = nc.dram_tensor("dbg_h1", [NT, D], F32, kind="ExternalOutput").ap()
        _phase("Y", lambda cx: cx.S.dma("cp", lambda e: e.dma_start(out=dbg1, in_=h1_src), writes=["dbg"]), ["cp"])
    gather("G1", h1_src, h1_all)
    phase("C", lambda cx: emit_mixer(cx, {"hfull": h1_all, "g_mix": c_g, "w": c_w, "ident": ident, "maskI": maskI, "bf": c_bf,
                                          "oT": o_src[1].rearrange("(i f) t -> i f t", f=128)}, "fox"), ["st_o0", "st_o1"])
    if "Z" in phases:
        dbg2 = nc.dram_tensor("dbg_o1", [NTILE * 128, 512], BF16, kind="ExternalOutput").ap()
        _phase("Z", lambda cx: cx.S.dma("cp", lambda e: e.dma_start(out=dbg2, in_=o_src[1]), writes=["dbg"]), ["cp"])
    gather("G2", o_src[1], o_all[1])
    phase("D", lambda cx: emit_rest(cx, rest_io(1, h1_all, out, True), True), ["st_h"])
    return nc


_CACHE = {}


def kernel(x, mem, positions, norm_mix_g, norm_xa_g, norm_mem_g, norm_ffn_g,
           ab_w_in, ab_w_out, fox_w_in, fox_b_f, fox_w_out,
           xa_w_q, xa_w_kv, xa_w_out, ffn_w_up, ffn_conv_w, ffn_conv_b, ffn_w_down,
           final_norm_g):
    a = lambda v: np.asarray(v)
    f32 = np.float32
    c_ = lambda v: np.ascontiguousarray(v, dtype=f32)
    x0 = c_(a(x)[0])
    mem_ = a(mem)
    w_in0, w_in1 = a(ab_w_in)[0], a(fox_w_in)[0]
    mi, ms = _masks()
    inv_freq = (10000.0 ** (-np.arange(32, dtype=f32) / 32)).astype(f32)
    invf = np.concatenate([inv_freq, inv_freq]).reshape(64, 1).astype(f32)
    sgn = np.concatenate([-np.ones(32), np.ones(32)]).reshape(64, 1).astype(f32)
    koh = np.zeros((64, SEQ), f32)
    for n in range(64):
        koh[n, n * 256:(n + 1) * 256] = 30000.0
    jj = np.arange(128)[:, None]
    ss = np.arange(128)[None, :]
    tri = np.concatenate([-(jj > ss).astype(f32), -(jj <= ss).astype(f32)], axis=1)
    perm = np.concatenate([np.arange(32, 64), np.arange(0, 32)])
    w_out0 = a(ab_w_out)[0]
    w_out0p = np.concatenate([np.concatenate([w_out0[r * 64:(r + 1) * 64], w_out0[(8 + r) * 64:(9 + r) * 64]], axis=0) for r in range(8)], axis=0)
    rws = []
    for L, wo in ((0, w_out0p), (1, a(fox_w_out)[0])):
        rws.append(rest_weights(L, wo, a(xa_w_q), a(xa_w_kv), a(xa_w_out), a(ffn_w_up), a(ffn_conv_w), a(ffn_conv_b),
                                a(ffn_w_down), a(norm_xa_g), a(norm_mem_g), a(norm_ffn_g), a(final_norm_g), mem_))
    common = {
        "x": x0, "ident": np.eye(128, dtype=f32), "maskI": c_((mi - 1.0) * 30000.0), "maskS": c_(ms),
        "pos": np.ascontiguousarray(a(positions).reshape(-1), dtype=np.int32), "invf": invf, "sgn": sgn, "koh": koh, "tri": tri,
        "mem": c_(mem_[0]), "a_g": c_(a(norm_mix_g)[0]), "c_g": c_(a(norm_mix_g)[1]), "g_fin": c_(a(final_norm_g)),
    }
    for L in range(2):
        for k in REST_W:
            common["r%d_%s" % (L, k)] = rws[L][k]
    in_maps = []
    p_ = np.arange(128)
    for cid in range(NCORES):
        m = dict(common)
        t0 = cid * NT
        m["hin0"] = np.concatenate([np.zeros((128, D), f32), x0[0:NT]], axis=0) if cid == 0 else c_(x0[t0 - 128:t0 + NT])
        m["flag"] = np.full((128, 1), 0.0 if cid == 0 else 1.0, f32)
        m["hidx"] = np.maximum(t0 - 128 + np.arange(17)[None, :] * 128 + p_[:, None], 0).astype(np.int32)
        tiles = np.array([max(4 * cid - 1, 0)] + [4 * cid + k for k in range(4)])
        m["oidx"] = (np.arange(8)[None, None, :] * (NTILE * 128) + tiles[None, :, None] * 128 + p_[:, None, None]).astype(np.int32)
        hA, hB = cid, 8 + cid
        qB = w_in0[:, hB * 64:(hB + 1) * 64]
        kB = w_in0[:, D + hB * 64:D + (hB + 1) * 64]
        wa = np.concatenate([w_in0[:, hA * 64:(hA + 1) * 64], w_in0[:, D + hA * 64:D + (hA + 1) * 64], qB, kB,
                             w_in0[:, 2 * D + hA * 64:2 * D + (hA + 1) * 64], w_in0[:, 2 * D + hB * 64:2 * D + (hB + 1) * 64],
                             qB[:, perm], kB[:, perm]], axis=1)
        m["a_w"] = c_(wa.reshape(8, 128, -1).transpose(1, 0, 2))
        hA, hB = 2 * cid, 2 * cid + 1
        cols = []
        for hh in (hA, hB):
            cols.append(w_in1[:, hh * 64:(hh + 1) * 64])
            cols.append(w_in1[:, D + hh * 64:D + (hh + 1) * 64])
        cols += [w_in1[:, 2 * D + hA * 64:2 * D + (hA + 1) * 64], w_in1[:, 2 * D + hB * 64:2 * D + (hB + 1) * 64],
                 w_in1[:, 3 * D + hA:3 * D + hA + 1], w_in1[:, 3 * D + hB:3 * D + hB + 1]]
        m["c_w"] = c_(np.concatenate(cols, axis=1).reshape(8, 128, -1).transpose(1, 0, 2))
        m["c_bf"] = c_(a(fox_b_f)[0][[hA, hB]].reshape(2, 1))
        in_maps.append(m)
    if "nc" not in _CACHE:
        _CACHE["nc"] = build_fused()
    res = run_bass_kernel_spmd(_CACHE["nc"], in_maps, core_ids=list(range(NCORES)))
    full = np.concatenate([r["out"] for r in res.results], axis=0)
    return np.ascontiguousarray(full[None].astype(np.float32))
```

```python
import contextlib
import numpy as np
import ml_dtypes
import concourse.bass as bass
import concourse.mybir as mybir
from concourse.bass_utils import run_bass_kernel_spmd

F32 = mybir.dt.float32
BF16 = mybir.dt.bfloat16
I32 = mybir.dt.int32
AF = mybir.ActivationFunctionType
ALU = mybir.AluOpType
AX = mybir.AxisListType

NCORES = 8
D = 1024
SEQ = 16384
NT = SEQ // NCORES
DFF = 2816
NPAIR = DFF // 128
MEM = 256
EPS = 1e-6
ENGS = ("pe", "act", "dve", "pool", "sp")
SEM_EPOCH = 24000
DEBUG = False
LAST = None


class Op:
    __slots__ = ("eng", "fn", "deps", "inc", "cnt", "dma", "ndma", "dcnt")

    def __init__(self, eng, fn, dma=None, ndma=1):
        self.eng = eng
        self.fn = fn
        self.deps = set()
        self.inc = False
        self.cnt = 0
        self.dma = dma
        self.ndma = ndma
        self.dcnt = 0


class Sched:
    def __init__(self, nc, same_engine_sync=("act", "dve", "pool")):
        self.nc = nc
        self.ops = {e: [] for e in ENGS}
        self.last_w = {}
        self.readers = {}
        self.streams = {}
        self.cc_streams = set()
        self.same = set(same_engine_sync)
        self.persist_sems = False
        self.tag = ""

    def _add(self, op, reads, writes):
        for b in reads:
            w = self.last_w.get(b)
            if w is not None:
                op.deps.add(w)
        for b in writes:
            w = self.last_w.get(b)
            if w is not None:
                op.deps.add(w)
            for r in self.readers.get(b, ()):
                op.deps.add(r)
        op.deps.discard(op)
        for b in reads:
            self.readers.setdefault(b, []).append(op)
        for b in writes:
            self.last_w[b] = op
            self.readers[b] = []
        self.ops[op.eng].append(op)
        return op

    def op(self, eng, fn, reads=(), writes=()):
        return self._add(Op(eng, fn), reads, writes)

    def dma(self, stream, fn, reads=(), writes=(), eng="sp", n=1):
        op = Op(eng, fn, dma=stream, ndma=n)
        self.streams.setdefault(stream, []).append(op)
        return self._add(op, reads, writes)

    def cc(self, stream, fn, reads=(), writes=()):
        op = Op("pool", fn, dma=stream, ndma=1)
        self.cc_streams.add(stream)
        self.streams.setdefault(stream, []).append(op)
        return self._add(op, reads, writes)

    def emit(self, final_wait_streams=()):
        nc = self.nc
        for e in ENGS:
            for op in self.ops[e]:
                for d in list(op.deps):
                    if d.dma is not None:
                        continue
                    if d.eng == op.eng and op.dma is None and d.eng not in self.same:
                        op.deps.discard(d)
                        continue
                    d.inc = True
        nep = {}
        for e in ENGS:
            c = 0
            for op in self.ops[e]:
                if op.dma is None and op.inc:
                    c += 1
                op.cnt = c
            nep[e] = max(1, -(-c // SEM_EPOCH))
        total = {}
        for s, lst in self.streams.items():
            c = 0
            for op in lst:
                c += (1 if s in self.cc_streams else 16) * op.ndma
                op.dcnt = c
            total[s] = c
        with contextlib.ExitStack() as st:
            tag = self.tag
            if self.persist_sems:
                esem = {e: [nc.alloc_semaphore("s%s_%s%d" % (tag, e, k)) for k in range(nep[e])] for e in ENGS}
                ssem = {s: nc.alloc_semaphore("d%s_%s" % (tag, s)) for s in self.streams}
            else:
                esem = {e: [st.enter_context(nc.semaphore("s_%s%d" % (e, k))) for k in range(nep[e])] for e in ENGS}
                ssem = {s: st.enter_context(nc.semaphore("d_" + s)) for s in self.streams}
            block = st.enter_context(nc.Block())

            def run(e, eng_obj):
                seen = {}
                for op in self.ops[e]:
                    need = {}
                    for d in op.deps:
                        if d.dma is not None:
                            key, val = ("d", d.dma), d.dcnt
                        else:
                            key, val = ("e", d.eng), d.cnt
                        if val > need.get(key, 0):
                            need[key] = val
                    for key, val in need.items():
                        if seen.get(key, 0) >= val:
                            continue
                        seen[key] = val
                        if key[0] == "d":
                            eng_obj.wait_ge(ssem[key[1]], val)
                        else:
                            k = (val - 1) // SEM_EPOCH
                            eng_obj.wait_ge(esem[key[1]][k], val - k * SEM_EPOCH)
                    ins = op.fn(eng_obj)
                    if op.dma is not None:
                        if not isinstance(ins, (list, tuple)):
                            ins = [ins]
                        assert len(ins) == op.ndma, (len(ins), op.ndma)
                        for i_ in ins:
                            if op.dma in self.cc_streams:
                                i_.then_inc(ssem[op.dma])
                            else:
                                i_.then_inc(ssem[op.dma], 16)
                    elif op.inc:
                        ins.then_inc(esem[e][(op.cnt - 1) // SEM_EPOCH], 1)
                if e == "sp":
                    for s in final_wait_streams:
                        eng_obj.wait_ge(ssem[s], total[s])

            @block.tensor
            def _(eng):
                run("pe", eng)

            @block.scalar
            def _(eng):
                run("act", eng)

            @block.vector
            def _(eng):
                run("dve", eng)

            @block.gpsimd
            def _(eng):
                run("pool", eng)

            @block.sync
            def _(eng):
                run("sp", eng)


class Ctx:
    def __init__(self, nc, st, tag=""):
        self.nc = nc
        self.st = st
        self.S = Sched(nc)
        self.tag = tag
        if tag:
            self.S.persist_sems = True
            self.S.tag = tag

    def sb(self, name, shape, dt):
        return self.st.enter_context(self.nc.sbuf_tensor("sb%s_%s" % (self.tag, name), list(shape), dt))

    def ps(self, name, shape, dt):
        return self.st.enter_context(self.nc.psum_tensor("ps%s_%s" % (self.tag, name), list(shape), dt))

    def din(self, name, shape, dt):
        return self.nc.dram_tensor(name, list(shape), dt, kind="ExternalInput").ap()

    def dout(self, name, shape, dt):
        return self.nc.dram_tensor(name, list(shape), dt, kind="ExternalOutput").ap()

    def dint(self, name, shape, dt):
        return self.nc.dram_tensor("di%s_%s" % (self.tag, name), list(shape), dt, kind="Internal").ap()


def emit_rest(cx, io, final):
    nc, S = cx.nc, cx.S
    sb, ps = cx.sb, cx.ps
    NSUB = NT // 128
    NTT = NT // 512
    ident = sb("ident", [128, 128], BF16)
    ones_f = sb("ones_f", [128, 64], F32)
    gxa = sb("gxa", [128, D], F32)
    gffn = sb("gffn", [128, D], F32)
    gmem = sb("gmem", [128, D], F32)
    gfin = sb("gfin", [128, D], F32) if final else None
    cw = sb("cw", [128, 44, 3], F32)
    cb = sb("cb", [128, 44], F32)
    flag = sb("flag", [128, 1], F32)
    wo_s = sb("wo_s", [128, 8, 1024], BF16)
    wq_s = sb("wq_s", [128, 8, 256], BF16)
    wkv_s = sb("wkv_s", [128, 8, 512], BF16)
    wxo_s = sb("wxo_s", [64, 4, 1024], BF16)
    kxT = sb("kxT", [64, 4, MEM], BF16)
    vxa = sb("vxa", [128, 2, 4, 65], BF16)
    ucarry = sb("ucarry", [128, 44, 2], F32)
    wup_b = cx.dint("wup_b", [NPAIR, 128, 8 * 256], BF16)
    wdn_b = cx.dint("wdn_b", [NPAIR, 2, 128, 512], BF16)
    NUP = 5
    NDN = 8
    wup_r = [sb("wup_r%d" % i, [128, 8, 256], BF16) for i in range(NUP)]
    wdn_r = [sb("wdn_r%d" % i, [128, 512], BF16) for i in range(NDN)]
    hT = [sb("hT%d" % i, [128, 4, D], F32) for i in range(2)]
    oTs = [sb("oTs%d" % i, [128, 8, 512], BF16) for i in range(2)]
    hn_s = [sb("hn_s%d" % i, [128, D], BF16) for i in range(2)]
    junk = sb("junk", [128, D], BF16)
    stat = sb("stat", [128, 8], F32)
    hnT = sb("hnT", [128, 8, 512], BF16)
    qxT = sb("qxT", [64, 4, 512], BF16)
    pxT = [sb("pxT%d" % i, [128, 512], BF16) for i in range(4)]
    nm = sb("nm", [128, 512], F32)
    rd = sb("rd", [128, 512], F32)
    oxT = sb("oxT", [64, 4, 512], BF16)
    Yg = [sb("Yg%d" % i, [128, 512], F32) for i in range(2)]
    Yv = [sb("Yv%d" % i, [128, 512], F32) for i in range(2)]
    gT = sb("gT", [128, NPAIR, 512], BF16)
    trp = [ps("trp%d" % i, [128, 1024], BF16) for i in range(2)]
    acc = [ps("acc%d" % i, [128, 512], F32) for i in range(6)]
    rot = {"acc": 0, "tr": 0, "px": 0}

    def nacc():
        rot["acc"] = (rot["acc"] + 1) % 6
        return rot["acc"]

    def ntr():
        rot["tr"] = (rot["tr"] + 1) % 2
        return rot["tr"]

    def pdma(stream, out, in_, writes, reads=()):
        S.dma(stream, lambda e: nc.gpsimd.dma_start(out=out, in_=in_), reads=reads, writes=writes, eng="pool")

    pdma("c_id", ident[:], io["ident"], ["ident"])
    pdma("c_wo", wo_s[:], io["w_out"], ["wo_s"])
    pdma("c_wq", wq_s[:], io["wq"], ["wq_s"])
    pdma("c_wkv", wkv_s[:], io["wkv"], ["wkv_s"])
    pdma("c_wxo", wxo_s[:], io["wxo"], ["wxo_s"])
    S.dma("c_g", lambda e: [e.dma_start(out=gxa[:], in_=io["g_xa"].partition_broadcast(128)),
                            e.dma_start(out=gffn[:], in_=io["g_ffn"].partition_broadcast(128)),
                            e.dma_start(out=gmem[:], in_=io["g_mem"].partition_broadcast(128)),
                            e.dma_start(out=cw[:], in_=io["cw"]),
                            e.dma_start(out=cb[:], in_=io["cb"]),
                            e.dma_start(out=flag[:], in_=io["flag"])],
          writes=["gxa", "gffn", "gmem", "cw", "cb", "flag"], n=6)
    if final:
        S.dma("c_gf", lambda e: e.dma_start(out=gfin[:], in_=io["g_fin"].partition_broadcast(128)), writes=["gfin"])
    for g in range(4):
        js = list(range(g * 6, min(NPAIR, g * 6 + 6)))
        S.dma("c_up%d" % g, lambda e, js=js: [nc.gpsimd.dma_start(out=wup_b[j], in_=io["wup"][j]) for j in js],
              writes=[("wup_b", j) for j in js], eng="pool", n=len(js))
    for g in range(4):
        js = list(range(g * 6, min(NPAIR, g * 6 + 6)))
        S.dma("c_dn%d" % g, lambda e, js=js: [nc.gpsimd.dma_start(out=wdn_b[j], in_=io["wdn"][j]) for j in js],
              writes=[("wdn_b", j) for j in js], eng="pool", n=len(js))
    S.op("dve", lambda e: e.memset(ones_f[:], 1.0), writes=["ones_f"])
    S.op("dve", lambda e: e.memset(vxa[:], 1.0), writes=["vxa"])
    oidx = sb("oidx", [128, 5, 8], I32)
    hidx = sb("hidx", [128, 17], I32)
    S.dma("c_ix", lambda e: [e.dma_start(out=oidx[:], in_=io["oidx"])] + ([e.dma_start(out=hidx[:], in_=io["hidx"])] if "hidx" in io else []),
          writes=["oidx", "hidx"], n=(2 if "hidx" in io else 1))

    def rmsnorm_to_T(src_ap, src_key, g_tile, g_key, dstT, dst_key, col0, ncol=128, nrows=128):
        b = ntr()
        hb = hn_s[b]
        S.op("act", lambda e: e.activation(out=junk[:nrows, :], in_=src_ap, func=AF.Square, accum_out=stat[:nrows, 0:1]),
             reads=[src_key], writes=["junk", "stat"])
        S.op("act", lambda e: e.activation(out=stat[:nrows, 1:2], in_=stat[:nrows, 0:1], func=AF.Ln, bias=EPS, scale=1.0 / D),
             reads=["stat"], writes=["stat"])
        S.op("act", lambda e: e.activation(out=stat[:nrows, 2:3], in_=stat[:nrows, 1:2], func=AF.Exp, scale=-0.5),
             reads=["stat"], writes=["stat"])
        S.op("dve", lambda e: e.scalar_tensor_tensor(out=hb[:nrows, :], in0=src_ap, scalar=stat[:nrows, 2:3], in1=g_tile[:nrows, :],
                                                     op0=ALU.mult, op1=ALU.mult),
             reads=[src_key, "stat", g_key], writes=[("hn_s", b)])
        for c in range(8):
            S.op("pe", lambda e, c=c: e.transpose(trp[b][:, c * 128:c * 128 + nrows], hb[:nrows, c * 128:(c + 1) * 128], ident[:nrows, :nrows]),
                 reads=[("hn_s", b), "ident"], writes=[("trp", b)])
        S.op("act", lambda e: e.copy(out=dstT[:, :, col0:col0 + nrows],
                                     in_=trp[b][:].rearrange("p (c t) -> p c t", c=8)[:, :, 0:nrows]),
             reads=[("trp", b)], writes=[dst_key])

    memT = hnT
    mem_s = hT[1]
    S.dma("ld_mem", lambda e: e.dma_start(out=mem_s[:, 0:2, :], in_=io["mem"].rearrange("(s p) d -> p s d", p=128)),
          writes=[("hT", 1)])
    for s in range(2):
        rmsnorm_to_T(mem_s[:, s, :], ("hT", 1), gmem, "gmem", memT, "hnT", s * 128)
    for hd in range(4):
        a = nacc()
        for c in range(8):
            S.op("pe", lambda e, a=a, c=c, hd=hd: e.matmul(acc[a][0:64, 0:MEM], lhsT=wkv_s[:, c, hd * 64:(hd + 1) * 64], rhs=memT[:, c, 0:MEM],
                                                            start=(c == 0), stop=(c == 7)),
                 reads=["wkv_s", "hnT"], writes=[("acc", a)])
        S.op("act", lambda e, a=a, hd=hd: e.copy(out=kxT[:, hd, :], in_=acc[a][0:64, 0:MEM]), reads=[("acc", a)], writes=["kxT"])
    for mc in range(2):
        a = nacc()
        for c in range(8):
            S.op("pe", lambda e, a=a, c=c, mc=mc: e.matmul(acc[a][:, 0:256], lhsT=memT[:, c, mc * 128:(mc + 1) * 128], rhs=wkv_s[:, c, 256:512],
                                                            start=(c == 0), stop=(c == 7)),
                 reads=["wkv_s", "hnT"], writes=[("acc", a)])
        S.op("act", lambda e, a=a, mc=mc: e.copy(out=vxa[:, mc, :, 0:64], in_=acc[a][:, 0:256].rearrange("p (h d) -> p h d", h=4)),
             reads=[("acc", a)], writes=["vxa"])

    upq = {"n": 0}
    dnq = {"n": 0}

    def load_up(j):
        slot = upq["n"] % NUP
        upq["n"] += 1
        S.dma("r_up%d" % slot, lambda e: e.dma_start(out=wup_r[slot][:], in_=wup_b[j].rearrange("p (c n) -> p c n", c=8)),
              reads=[("wup_b", j)], writes=[("wup_r", slot)])
        return slot

    def load_dn(j, half):
        slot = dnq["n"] % NDN
        dnq["n"] += 1
        S.dma("r_dn%d" % slot, lambda e: e.dma_start(out=wdn_r[slot][:], in_=wdn_b[j, half]),
              reads=[("wdn_b", j)], writes=[("wdn_r", slot)])
        return slot

    def front(buf, nsub, row0):
        ntok = nsub * 128
        hb = hT[buf]
        k0 = row0 // 128
        t5 = 0 if row0 == 0 else 1 + (row0 - 128) // 512
        oc0 = 384 if row0 == 0 else 0
        if "hidx" in io:
            S.dma("ld_h%d" % buf, lambda e: [nc.gpsimd.indirect_dma_start(out=hb[:, s_, :], out_offset=None, in_=io["hin"],
                                                                        in_offset=bass.IndirectOffsetOnAxis(ap=hidx[:, k0 + s_:k0 + s_ + 1], axis=0))
                                             for s_ in range(nsub)],
                  reads=["hidx"], writes=[("hT", buf)], eng="pool", n=nsub)
        else:
            S.dma("ld_h%d" % buf, lambda e: e.dma_start(out=hb[:, 0:nsub, :], in_=io["hin"][row0:row0 + ntok, :].rearrange("(s p) d -> p s d", p=128)),
                  writes=[("hT", buf)])
        S.dma("ld_o%d" % buf, lambda e: [nc.gpsimd.indirect_dma_start(out=oTs[buf][:, r_, :], out_offset=None, in_=io["oall"],
                                                                    in_offset=bass.IndirectOffsetOnAxis(ap=oidx[:, t5, r_:r_ + 1], axis=0))
                                         for r_ in range(8)],
              reads=["oidx"], writes=[("oTs", buf)], eng="pool", n=8)
        for s in range(nsub):
            for half in range(2):
                a = nacc()
                for c in range(8):
                    S.op("pe", lambda e, a=a, c=c, s=s, half=half: e.matmul(acc[a][:, :], lhsT=oTs[buf][:, c, oc0 + s * 128:oc0 + (s + 1) * 128],
                                                                            rhs=wo_s[:, c, half * 512:(half + 1) * 512], start=(c == 0), stop=(c == 7)),
                         reads=[("oTs", buf), "wo_s"], writes=[("acc", a)])
                S.op("dve", lambda e, a=a, s=s, half=half: e.tensor_tensor(out=hb[:, s, half * 512:(half + 1) * 512], in0=hb[:, s, half * 512:(half + 1) * 512],
                                                                           in1=acc[a][:, :], op=ALU.add),
                     reads=[("acc", a), ("hT", buf)], writes=[("hT", buf)])
        for s in range(nsub):
            rmsnorm_to_T(hb[:, s, :], ("hT", buf), gxa, "gxa", hnT, "hnT", s * 128)
        for hd in range(4):
            a = nacc()
            for c in range(8):
                S.op("pe", lambda e, a=a, c=c, hd=hd: e.matmul(acc[a][0:64, 0:ntok], lhsT=wq_s[:, c, hd * 64:(hd + 1) * 64], rhs=hnT[:, c, 0:ntok],
                                                                start=(c == 0), stop=(c == 7)),
                     reads=["wq_s", "hnT"], writes=[("acc", a)])
            S.op("act", lambda e, a=a, hd=hd: e.copy(out=qxT[:, hd, 0:ntok], in_=acc[a][0:64, 0:ntok]), reads=[("acc", a)], writes=[("qxT", hd)])
        for hd in range(4):
            pk = []
            for mc in range(2):
                a = nacc()
                S.op("pe", lambda e, a=a, mc=mc, hd=hd: e.matmul(acc[a][:, 0:ntok], lhsT=kxT[:, hd, mc * 128:(mc + 1) * 128], rhs=qxT[:, hd, 0:ntok],
                                                                  start=True, stop=True),
                     reads=["kxT", ("qxT", hd)], writes=[("acc", a)])
                p = rot["px"] = (rot["px"] + 1) % 4
                S.op("act", lambda e, a=a, p=p: e.activation(out=pxT[p][:, 0:ntok], in_=acc[a][:, 0:ntok], func=AF.Exp, scale=0.125),
                     reads=[("acc", a)], writes=[("pxT", p)])
                pk.append(p)
            a = nacc()
            for mc in range(2):
                S.op("pe", lambda e, a=a, mc=mc, hd=hd, p=pk[mc]: e.matmul(acc[a][0:65, 0:ntok], lhsT=vxa[:, mc, hd, :], rhs=pxT[p][:, 0:ntok],
                                                                            start=(mc == 0), stop=(mc == 1)),
                     reads=["vxa", ("pxT", pk[mc])], writes=[("acc", a)])
            S.op("dve", lambda e, a=a: e.reciprocal(out=rd[64:65, 0:ntok], in_=acc[a][64:65, 0:ntok]), reads=[("acc", a)], writes=["rd"])
            S.op("act", lambda e, a=a: e.copy(out=nm[0:64, 0:ntok], in_=acc[a][0:64, 0:ntok]), reads=[("acc", a)], writes=["nm"])
            a2 = nacc()
            S.op("pe", lambda e, a2=a2: e.matmul(acc[a2][0:64, 0:ntok], lhsT=ones_f[64:65, 0:64], rhs=rd[64:65, 0:ntok], start=True, stop=True),
                 reads=["ones_f", "rd"], writes=[("acc", a2)])
            S.op("dve", lambda e, a2=a2, hd=hd: e.tensor_tensor(out=oxT[:, hd, 0:ntok], in0=nm[0:64, 0:ntok], in1=acc[a2][0:64, 0:ntok], op=ALU.mult),
                 reads=["nm", ("acc", a2)], writes=[("oxT", hd)])
        for s in range(nsub):
            for half in range(2):
                a = nacc()
                for hd in range(4):
                    S.op("pe", lambda e, a=a, hd=hd, s=s, half=half: e.matmul(acc[a][:, :], lhsT=oxT[:, hd, s * 128:(s + 1) * 128],
                                                                              rhs=wxo_s[:, hd, half * 512:(half + 1) * 512], start=(hd == 0), stop=(hd == 3)),
                         reads=[("oxT", hd), "wxo_s"], writes=[("acc", a)])
                S.op("dve", lambda e, a=a, s=s, half=half: e.tensor_tensor(out=hb[:, s, half * 512:(half + 1) * 512], in0=hb[:, s, half * 512:(half + 1) * 512],
                                                                           in1=acc[a][:, :], op=ALU.add),
                     reads=[("acc", a), ("hT", buf)], writes=[("hT", buf)])
        for s in range(nsub):
            rmsnorm_to_T(hb[:, s, :], ("hT", buf), gffn, "gffn", hnT, "hnT", s * 128)

    front(1, 1, 0)
    for j in range(NPAIR):
        slot = load_up(j)
        for gv in range(2):
            grp = j + gv * NPAIR
            a = nacc()
            for c in range(8):
                S.op("pe", lambda e, a=a, c=c, gv=gv, slot=slot: e.matmul(acc[a][:, 0:2], lhsT=wup_r[slot][:, c, gv * 128:(gv + 1) * 128], rhs=hnT[:, c, 126:128],
                                                                           start=(c == 0), stop=(c == 7)),
                     reads=[("wup_r", slot), "hnT"], writes=[("acc", a)])
            S.op("dve", lambda e, a=a, grp=grp: e.tensor_scalar(out=ucarry[:, grp, :], in0=acc[a][:, 0:2], scalar1=flag[:, 0:1], scalar2=None, op0=ALU.mult),
                 reads=[("acc", a), "flag"], writes=[("ucarry", grp)])

    for tt in range(NTT):
        buf = tt % 2
        hb = hT[buf]
        front(buf, 4, 128 + tt * 512)
        for j in range(NPAIR):
            slot = load_up(j)
            yb = j % 2
            for gv in range(2):
                grp = j + gv * NPAIR
                Y = (Yg if gv == 0 else Yv)[yb]
                ykey = ("Y", gv, yb)
                a = nacc()
                for c in range(8):
                    S.op("pe", lambda e, a=a, c=c, gv=gv, slot=slot: e.matmul(acc[a][:, :], lhsT=wup_r[slot][:, c, gv * 128:(gv + 1) * 128], rhs=hnT[:, c, :],
                                                                               start=(c == 0), stop=(c == 7)),
                         reads=[("wup_r", slot), "hnT"], writes=[("acc", a)])
                S.op("act", lambda e, a=a, grp=grp, Y=Y: e.activation(out=Y[:, :], in_=acc[a][:, :], func=AF.Identity, bias=cb[:, grp:grp + 1], scale=cw[:, grp, 2:3]),
                     reads=[("acc", a), "cw", "cb"], writes=[ykey])
                S.op("dve", lambda e, a=a, grp=grp, Y=Y: e.scalar_tensor_tensor(out=Y[:, 1:512], in0=acc[a][:, 0:511], scalar=cw[:, grp, 1:2], in1=Y[:, 1:512],
                                                                                op0=ALU.mult, op1=ALU.add),
                     reads=[("acc", a), "cw", ykey], writes=[ykey])
                S.op("dve", lambda e, a=a, grp=grp, Y=Y: e.scalar_tensor_tensor(out=Y[:, 2:512], in0=acc[a][:, 0:510], scalar=cw[:, grp, 0:1], in1=Y[:, 2:512],
                                                                                op0=ALU.mult, op1=ALU.add),
                     reads=[("acc", a), "cw", ykey], writes=[ykey])
                S.op("dve", lambda e, grp=grp, Y=Y: e.scalar_tensor_tensor(out=Y[:, 0:1], in0=ucarry[:, grp, 1:2], scalar=cw[:, grp, 1:2], in1=Y[:, 0:1],
                                                                            op0=ALU.mult, op1=ALU.add),
                     reads=[("ucarry", grp), "cw", ykey], writes=[ykey])
                S.op("dve", lambda e, grp=grp, Y=Y: e.scalar_tensor_tensor(out=Y[:, 0:2], in0=ucarry[:, grp, 0:2], scalar=cw[:, grp, 0:1], in1=Y[:, 0:2],
                                                                            op0=ALU.mult, op1=ALU.add),
                     reads=[("ucarry", grp), "cw", ykey], writes=[ykey])
                S.op("dve", lambda e, a=a, grp=grp: e.tensor_copy(out=ucarry[:, grp, :], in_=acc[a][:, 510:512]),
                     reads=[("acc", a)], writes=[("ucarry", grp)])
            S.op("act", lambda e, yb=yb: e.activation(out=Yg[yb][:, :], in_=Yg[yb][:, :], func=AF.Silu),
                 reads=[("Y", 0, yb)], writes=[("Y", 0, yb)])
            S.op("dve", lambda e, yb=yb, j=j: e.tensor_tensor(out=gT[:, j, :], in0=Yg[yb][:, :], in1=Yv[yb][:, :], op=ALU.mult),
                 reads=[("Y", 0, yb), ("Y", 1, yb)], writes=[("gT", j)])
        for half in range(2):
            accs = [nacc() for _ in range(4)]
            for j in range(NPAIR):
                slot = load_dn(j, half)
                for s in range(4):
                    a = accs[s]
                    S.op("pe", lambda e, a=a, j=j, s=s, slot=slot: e.matmul(acc[a][:, :], lhsT=gT[:, j, s * 128:(s + 1) * 128], rhs=wdn_r[slot][:, :],
                                                                             start=(j == 0), stop=(j == NPAIR - 1)),
                         reads=[("gT", j), ("wdn_r", slot)], writes=[("acc", a)])
            for s in range(4):
                a = accs[s]
                S.op("dve", lambda e, a=a, s=s, half=half, hb=hb: e.tensor_tensor(out=hb[:, s, half * 512:(half + 1) * 512], in0=hb[:, s, half * 512:(half + 1) * 512],
                                                                           in1=acc[a][:, :], op=ALU.add),
                     reads=[("acc", a), ("hT", buf)], writes=[("hT", buf)])
        if final:
            for s in range(4):
                S.op("act", lambda e, s=s, hb=hb: e.activation(out=junk[:, :], in_=hb[:, s, :], func=AF.Square, accum_out=stat[:, 4:5]),
                     reads=[("hT", buf)], writes=["junk", "stat2"])
                S.op("act", lambda e: e.activation(out=stat[:, 5:6], in_=stat[:, 4:5], func=AF.Ln, bias=EPS, scale=1.0 / D),
                     reads=["stat2"], writes=["stat2"])
                S.op("act", lambda e: e.activation(out=stat[:, 6:7], in_=stat[:, 5:6], func=AF.Exp, scale=-0.5),
                     reads=["stat2"], writes=["stat2"])
                S.op("dve", lambda e, s=s, hb=hb: e.scalar_tensor_tensor(out=hb[:, s, :], in0=hb[:, s, :], scalar=stat[:, 6:7], in1=gfin[:, :],
                                                                  op0=ALU.mult, op1=ALU.mult),
                     reads=[("hT", buf), "stat2", "gfin"], writes=[("hT", buf)])
        S.dma("st_h", lambda e, tt=tt, hb=hb: e.dma_start(out=io["hout"][tt * 512:(tt + 1) * 512, :].rearrange("(s p) d -> p s d", p=128), in_=hb[:, :, :]),
              reads=[("hT", buf)], writes=[("hout", tt)])


def rest_weights(layer, w_out, xa_w_q, xa_w_kv, xa_w_out, ffn_w_up, ffn_conv_w, ffn_conv_b, ffn_w_down,
                 norm_xa_g, norm_mem_g, norm_ffn_g, final_norm_g, mem):
    f = np.float32
    c = np.ascontiguousarray
    wup = ffn_w_up[layer].reshape(8, 128, 2, NPAIR, 128).transpose(3, 1, 0, 2, 4)
    return {
        "w_out": c(w_out.reshape(8, 128, 1024).transpose(1, 0, 2), dtype=f),
        "wq": c(xa_w_q[layer].reshape(8, 128, 256).transpose(1, 0, 2), dtype=f),
        "wkv": c(xa_w_kv[layer].reshape(8, 128, 512).transpose(1, 0, 2), dtype=f),
        "wxo": c(xa_w_out[layer].reshape(4, 64, 1024).transpose(1, 0, 2), dtype=f),
        "wup": c(wup.reshape(NPAIR, 128, 8 * 256), dtype=f),
        "wdn": c(ffn_w_down[layer].reshape(NPAIR, 128, 2, 512).transpose(0, 2, 1, 3), dtype=f),
        "cw": c(ffn_conv_w[layer].reshape(3, 44, 128).transpose(2, 1, 0), dtype=f),
        "cb": c(ffn_conv_b[layer].reshape(44, 128).T, dtype=f),
        "g_xa": c(norm_xa_g[layer], dtype=f),
        "g_mem": c(norm_mem_g[layer], dtype=f),
        "g_ffn": c(norm_ffn_g[layer], dtype=f),
        "g_fin": c(final_norm_g, dtype=f),
        "mem": c(mem[0], dtype=f),
        "ident": np.eye(128, dtype=f),
    }


NTILE = SEQ // 512
NBLK = SEQ // 128
SB_WIN = 3


def emit_mixer(cx, io, kind, ntile=NTILE):
    nc, S = cx.nc, cx.S
    sb, ps = cx.sb, cx.ps
    fox = kind == "fox"
    NCOL = 386 if fox else 512
    if fox:
        QA, KA, QB, KB, VV, FF = 0, 64, 128, 192, 256, 384
    else:
        QA, KA, QB, KB, VV, QP, KP = 0, 64, 128, 192, 256, 384, 448
    ident = sb("ident", [128, 128], BF16)
    ident_f = sb("ident_f", [128, 128], F32)
    ones_f = sb("ones_f", [128, 64], F32)
    gmix = sb("gmix", [128, D], F32)
    w_s = sb("w_s", [128, 8, NCOL], BF16)
    maskI = sb("maskI", [128, 4, 512], BF16)
    kT = [sb("kT%d" % h, [128, SEQ], BF16) for h in range(2)]
    vX = [sb("vX%d" % h, [128, NBLK, 65], BF16) for h in range(2)]
    qT = [[sb("qT%d_%d" % (h, i), [128, 512], BF16) for i in range(2)] for h in range(2)]
    NXT = 2 if fox else 1
    xt = [sb("xt%d" % i, [128, 4, D], F32) for i in range(NXT)]
    hn_s = [sb("hn_s%d" % i, [128, D], BF16) for i in range(2)]
    junk = sb("junk", [128, D], BF16)
    stat = sb("stat", [128, 8], F32)
    hnT = sb("hnT", [128, 8, 512], BF16)
    NPT = 6 if fox else 4
    pT = [sb("pT%d" % i, [128, 512], BF16) for i in range(6)]
    nm = sb("nm", [128, 512], F32)
    rd = sb("rd", [128, 512], F32)
    oTs = [sb("oTs%d" % i, [64, 512], BF16) for i in range(2)]
    trp = [ps("trp%d" % i, [128, 1024], BF16) for i in range(1)]
    NPACC = 2 if fox else 1
    pacc = [ps("pacc%d" % i, [128, 512], F32) for i in range(NPACC)]
    NSC = 3 if fox else 2
    sc = [ps("sc%d" % i, [128, 512], F32) for i in range(NSC)]
    av = [ps("av%d" % i, [128, 512], F32) for i in range(2)]
    rot = {"pacc": 0, "sc": 0, "pt": 0, "av": 0, "hn": 0, "ot": 0}

    def nxt(k, n):
        rot[k] = (rot[k] + 1) % n
        return rot[k]

    def pdma(stream, out, in_, writes):
        S.dma(stream, lambda e: nc.gpsimd.dma_start(out=out, in_=in_), writes=writes, eng="pool")

    pdma("c_id", ident[:], io["ident"], ["ident"])
    pdma("c_w", w_s[:], io["w"], ["w_s"])
    pdma("c_mi", maskI[:], io["maskI"], ["maskI"])
    S.dma("c_g", lambda e: [e.dma_start(out=gmix[:], in_=io["g_mix"].partition_broadcast(128)),
                            e.dma_start(out=ident_f[:], in_=io["ident"])], writes=["gmix", "ident_f"], n=2)
    S.op("dve", lambda e: e.memset(ones_f[:], 1.0), writes=["ones_f"])
    for h in range(2):
        S.op("pool", lambda e, h=h: e.memset(vX[h][:], 1.0), writes=[("vX", h, i_) for i_ in range(NTILE)])
    if fox:
        nbf = sb("nbf", [2, 1], F32)
        cprev = sb("cprev", [2, 1], F32)
        fE = sb("fE", [2, 512], F32)
        cc = sb("cc", [2, 512], F32)
        rbf = sb("rbf", [2, 512], BF16)
        negc = sb("negc", [128, NBLK, 2], F32)
        S.dma("c_bf", lambda e: e.dma_start(out=nbf[:], in_=io["bf"]), writes=["nbf"])
        S.op("dve", lambda e: e.tensor_scalar(out=nbf[:], in0=nbf[:], scalar1=-1.0, scalar2=None, op0=ALU.mult), reads=["nbf"], writes=["nbf"])
        S.op("dve", lambda e: e.memset(cprev[:], 0.0), writes=["cprev"])
        for h in range(2):
            S.op("pool", lambda e, h=h: e.memset(kT[h][64:128, :], 1.0), writes=[("kT", h, i_) for i_ in range(NTILE)])


    if not fox:
        maskS = sb("maskS", [128, 4, 512], BF16)
        tri = sb("tri", [128, 256], F32)
        invf = sb("invf", [64, 1], F32)
        sgn = sb("sgn", [64, 1], F32)
        pos_i = sb("pos_i", [64, 512], I32)
        ang = sb("ang", [64, 512], F32)
        kint = sb("kint", [64, 512], I32)
        kflt = sb("kflt", [64, 512], F32)
        msk = sb("msk", [64, 512], F32)
        cosF = sb("cosF", [64, 512], F32)
        sinS = sb("sinS", [64, 512], F32)
        rt1 = sb("rt1", [64, 512], F32)
        qrot = sb("qrot", [64, 512], F32)
        krot = sb("krot", [64, 512], F32)
        kmT = sb("kmT", [64, 64], F32)
        G = sb("G", [128, 64], F32)
        top8 = sb("top8", [128, 8], F32)
        MB = sb("MB", [128, 128], BF16)
        spE = [sb("spE%d" % k, [128, 512], F32) for k in range(2)]
        spM = [sb("spM%d" % k, [128, 512], F32) for k in range(2)]
        lgA = [sb("lgA%d" % k, [128, 512], F32) for k in range(2)]
        tailP = ps("tailP", [128, 512], F32)
        scS = ps("scS", [128, 512], F32)
        pdma("c_ms", maskS[:], io["maskS"], ["maskS"])
        S.dma("c_ab", lambda e: [e.dma_start(out=tri[:], in_=io["tri"]), e.dma_start(out=invf[:], in_=io["invf"]),
                                 e.dma_start(out=sgn[:], in_=io["sgn"])], writes=["tri", "invf", "sgn"], n=3)
        S.dma("c_koh", lambda e: nc.gpsimd.dma_start(out=kT[1][64:128, :], in_=io["koh"]), writes=[("kT", 1, i_) for i_ in range(NTILE)], eng="pool")
        S.op("dve", lambda e: e.memset(kmT[:], 0.0), writes=["kmT"])
        S.op("dve", lambda e: e.memset(MB[:], 0.0), writes=["MB"])

    def rope_tables(i):
        PI = float(np.pi)
        S.dma("ld_pos", lambda e: e.dma_start(out=pos_i[:], in_=io["pos"][i * 512:(i + 1) * 512].partition_broadcast(64)), writes=["pos_i"])
        S.op("dve", lambda e: e.tensor_copy(out=ang[:], in_=pos_i[:]), reads=["pos_i"], writes=["ang"])
        S.op("dve", lambda e: e.tensor_scalar(out=ang[:], in0=ang[:], scalar1=invf[:, 0:1], scalar2=None, op0=ALU.mult), reads=["ang", "invf"], writes=["ang"])
        S.op("dve", lambda e: e.tensor_scalar(out=kint[:], in0=ang[:], scalar1=float(1.0 / (2 * np.pi)), scalar2=None, op0=ALU.mult), reads=["ang"], writes=["kint"])
        S.op("dve", lambda e: e.tensor_copy(out=kflt[:], in_=kint[:]), reads=["kint"], writes=["kflt"])
        yield
        S.op("dve", lambda e: e.scalar_tensor_tensor(out=ang[:], in0=kflt[:], scalar=-6.28125, in1=ang[:], op0=ALU.mult, op1=ALU.add), reads=["kflt", "ang"], writes=["ang"])
        S.op("dve", lambda e: e.scalar_tensor_tensor(out=ang[:], in0=kflt[:], scalar=float(-(2 * np.pi - 6.28125)), in1=ang[:], op0=ALU.mult, op1=ALU.add),
             reads=["kflt", "ang"], writes=["ang"])
        S.op("dve", lambda e: e.tensor_scalar(out=msk[:], in0=ang[:], scalar1=PI, scalar2=-2 * PI, op0=ALU.is_gt, op1=ALU.mult), reads=["ang"], writes=["msk"])
        S.op("dve", lambda e: e.tensor_tensor(out=ang[:], in0=ang[:], in1=msk[:], op=ALU.add), reads=["ang", "msk"], writes=["ang"])
        S.op("dve", lambda e: e.tensor_scalar(out=msk[:], in0=ang[:], scalar1=-PI, scalar2=2 * PI, op0=ALU.is_lt, op1=ALU.mult), reads=["ang"], writes=["msk"])
        S.op("dve", lambda e: e.tensor_tensor(out=ang[:], in0=ang[:], in1=msk[:], op=ALU.add), reads=["ang", "msk"], writes=["ang"])
        yield
        S.op("dve", lambda e: e.tensor_scalar(out=rt1[:], in0=ang[:], scalar1=PI / 2, scalar2=None, op0=ALU.add), reads=["ang"], writes=["rt1"])
        S.op("dve", lambda e: e.tensor_scalar(out=msk[:], in0=rt1[:], scalar1=PI, scalar2=-2 * PI, op0=ALU.is_gt, op1=ALU.mult), reads=["rt1"], writes=["msk"])
        S.op("dve", lambda e: e.tensor_tensor(out=rt1[:], in0=rt1[:], in1=msk[:], op=ALU.add), reads=["rt1", "msk"], writes=["rt1"])
        S.op("act", lambda e: e.activation(out=sinS[:], in_=ang[:], func=AF.Sin, scale=sgn[:, 0:1]), reads=["ang", "sgn"], writes=["sinS"])
        yield
        S.op("act", lambda e: e.activation(out=cosF[:], in_=rt1[:], func=AF.Sin), reads=["rt1"], writes=["cosF"])
        yield

    def rope_proj(col_main, col_perm, i, dst, dkey):
        a_main = proj64(col_main, i)
        yield
        yield
        S.op("dve", lambda e: e.tensor_tensor(out=rt1[:], in0=pacc[a_main][0:64, :], in1=cosF[:], op=ALU.mult), reads=[("pacc", a_main), "cosF"], writes=["rt1"])
        yield
        a_perm = proj64(col_perm, i)
        yield
        yield
        S.op("dve", lambda e: e.tensor_tensor(out=dst[:], in0=pacc[a_perm][0:64, :], in1=sinS[:], op=ALU.mult), reads=[("pacc", a_perm), "sinS"], writes=[dkey])
        yield
        S.op("dve", lambda e: e.tensor_tensor(out=dst[:], in0=dst[:], in1=rt1[:], op=ALU.add), reads=[dkey, "rt1"], writes=[dkey])
        yield

    def moba_gate(i, qbuf):
        for s in range(4):
            own = 2 * i + s // 2
            if own > 0:
                a = nxt("pacc", NPACC)
                S.op("pe", lambda e, a=a, s=s: e.matmul(pacc[a][:, 0:64], lhsT=qrot[0:64, s * 128:(s + 1) * 128], rhs=kmT[0:64, 0:64], start=True, stop=True),
                     reads=["qrot", "kmT"], writes=[("pacc", a)])
                yield
                S.op("dve", lambda e: e.memset(G[:], -1e9), writes=["G"])
                S.op("dve", lambda e, a=a, own=own: e.tensor_copy(out=G[:, 0:own], in_=pacc[a][:, 0:own]), reads=[("pacc", a), "G"], writes=["G"])
                yield
                S.op("dve", lambda e: e.max(out=top8[:], in_=G[:]), reads=["G"], writes=["top8"])
                S.op("dve", lambda e: e.tensor_scalar(out=top8[:, 2:3], in0=top8[:, 2:3], scalar1=-1e8, scalar2=None, op0=ALU.max), reads=["top8"], writes=["top8"])
                S.op("dve", lambda e: e.tensor_scalar(out=MB[:, 64:128], in0=G[:], scalar1=top8[:, 2:3], scalar2=-1.0, op0=ALU.is_ge, op1=ALU.add),
                     reads=["G", "top8"], writes=["MB"])
            S.op("dve", lambda e, own=own: e.memset(MB[:, 64 + own:128], 0.0), reads=["MB"], writes=["MB"])
            yield
            S.op("pe", lambda e: e.transpose(trp[0][:, 0:128], MB[:, :], ident[:, :]), reads=["MB", "ident"], writes=["trp"])
            yield
            yield
            S.op("act", lambda e, s=s: e.copy(out=qT[1][qbuf][64:128, s * 128:(s + 1) * 128], in_=trp[0][64:128, 0:128]),
                 reads=["trp"], writes=[("qT", 1, qbuf)])
            yield

    def sb_attend(i, qbuf):
        a = 0
        lo = max(0, 4 * i - SB_WIN)
        blocks = list(range(4 * i + 3, lo - 1, -1))
        for n, blk in enumerate(blocks):
            j = blk - 4 * i
            p_ = 4 + n % 2
            b2 = n % 2
            S.op("pe", lambda e, blk=blk: e.matmul(scS[:, :], lhsT=kT[0][0:64, blk * 128:(blk + 1) * 128], rhs=qT[0][qbuf][0:64, :], start=True, stop=True),
                 reads=[("kT", 0, blk // 4), ("qT", 0, qbuf)], writes=["scS"])
            yield
            S.op("act", lambda e, b2=b2: e.activation(out=spE[b2][:, :], in_=scS[:, :], func=AF.Exp), reads=["scS"], writes=[("spE", b2)])
            yield
            S.op("act", lambda e, b2=b2: e.activation(out=spE[b2][:, :], in_=spE[b2][:, :], func=AF.Ln, bias=1.0, scale=1.0), reads=[("spE", b2)], writes=[("spE", b2)])
            yield
            if j >= 0:
                S.op("pool", lambda e, b2=b2, j=j: e.tensor_tensor(out=spM[b2][:, :], in0=spE[b2][:, :], in1=maskS[:, j, :], op=ALU.mult),
                     reads=[("spE", b2), "maskS"], writes=[("spM", b2)])
                src, skey = spM[b2], ("spM", b2)
            else:
                src, skey = spE[b2], ("spE", b2)
            S.op("dve", lambda e, b2=b2: e.tensor_tensor(out=lgA[b2][:, :], in0=scS[:, :], in1=spE[b2][:, :], op=ALU.subtract),
                 reads=["scS", ("spE", b2)], writes=[("lgA", b2)])
            yield
            S.op("pe", lambda e, src=src, n=n: e.matmul(tailP[:, :], lhsT=tri[:, 0:128], rhs=src[:, :], start=(n == 0), stop=True),
                 reads=["tri", skey], writes=["tailP"])
            yield
            S.op("dve", lambda e, b2=b2: e.tensor_tensor(out=lgA[b2][:, :], in0=lgA[b2][:, :], in1=tailP[:, :], op=ALU.add),
                 reads=[("lgA", b2), "tailP"], writes=[("lgA", b2)])
            yield
            S.op("pe", lambda e, src=src: e.matmul(tailP[:, :], lhsT=tri[:, 128:256], rhs=src[:, :], start=False, stop=True),
                 reads=["tri", skey], writes=["tailP"])
            S.op("act", lambda e, p_=p_, b2=b2: e.activation(out=pT[p_][:, :], in_=lgA[b2][:, :], func=AF.Exp), reads=[("lgA", b2)], writes=[("pT", p_)])
            yield
            if j >= 0:
                S.op("pool", lambda e, p_=p_, j=j: e.tensor_tensor(out=pT[p_][:, :], in0=pT[p_][:, :], in1=maskS[:, j, :], op=ALU.mult),
                     reads=[("pT", p_), "maskS"], writes=[("pT", p_)])
                yield
            S.op("pe", lambda e, p_=p_, blk=blk, n=n: e.matmul(av[a][0:64, :], lhsT=vX[0][:, blk, 0:64], rhs=pT[p_][:, :], start=(n == 0), stop=(n == len(blocks) - 1)),
                 reads=[("vX", 0, blk // 4), ("pT", p_)], writes=[("av", a)])
            yield
        finalize(i, 0, a, False)

    def proj64(col, i, nrows=64):
        a = nxt("pacc", NPACC)
        for c in range(8):
            S.op("pe", lambda e, a=a, c=c: e.matmul(pacc[a][0:nrows, :], lhsT=w_s[:, c, col:col + nrows], rhs=hnT[:, c, :], start=(c == 0), stop=(c == 7)),
                 reads=["w_s", "hnT"], writes=[("pacc", a)])
        return a

    LOOK = 2

    def attend(i, h, blocks, krows, bias_fn, qbuf, a=None, side=None, drain=True):
        if a is None:
            a = nxt("av", 2)
        nb = len(blocks)
        pbuf = [None] * nb
        for n in range(nb + LOOK):
            if n < nb:
                blk = blocks[n]
                s_ = nxt("sc", NSC)
                p_ = nxt("pt", NPT)
                pbuf[n] = p_
                j = blk - 4 * i
                S.op("pe", lambda e, s_=s_, blk=blk, j=j: e.matmul(sc[s_][:, :], lhsT=kT[h][0:krows, blk * 128:(blk + 1) * 128], rhs=qT[h][qbuf][0:krows, :],
                                                                   start=True, stop=(j < 0)),
                     reads=[("kT", h, blk // 4), ("qT", h, qbuf)], writes=[("sc", s_)])
                if j >= 0:
                    S.op("pe", lambda e, s_=s_, j=j: e.matmul(sc[s_][:, :], lhsT=ident[:, :], rhs=maskI[:, j, :], start=False, stop=True),
                         reads=["ident", "maskI"], writes=[("sc", s_)])
                b_ap, b_key = bias_fn(blk)
                S.op("act", lambda e, s_=s_, p_=p_, b_ap=b_ap: e.activation(out=pT[p_][:, :], in_=sc[s_][:, :], func=AF.Exp, bias=b_ap, scale=1.0),
                     reads=[("sc", s_)] + b_key, writes=[("pT", p_)])
            m = n - LOOK
            if m >= 0:
                blk = blocks[m]
                p_ = pbuf[m]
                S.op("pe", lambda e, p_=p_, blk=blk, m=m: e.matmul(av[a][0:65, :], lhsT=vX[h][:, blk, :], rhs=pT[p_][:, :], start=(m == 0), stop=(m == nb - 1)),
                     reads=[("vX", h, blk // 4), ("pT", p_)], writes=[("av", a)])
            if side is not None:
                next(side, None)
        if side is not None and drain:
            for _ in side:
                pass
        finalize(i, h, a, True)

    def finalize(i, h, a, normalize):
        o_ = nxt("ot", 2)
        if normalize:
            S.op("dve", lambda e: e.reciprocal(out=rd[64:65, :], in_=av[a][64:65, :]), reads=[("av", a)], writes=["rd"])
            S.op("act", lambda e: e.copy(out=nm[0:64, :], in_=av[a][0:64, :]), reads=[("av", a)], writes=["nm"])
            a2 = nxt("pacc", NPACC)
            S.op("pe", lambda e: e.matmul(pacc[a2][0:64, :], lhsT=ones_f[64:65, 0:64], rhs=rd[64:65, :], start=True, stop=True),
                 reads=["ones_f", "rd"], writes=[("pacc", a2)])
            S.op("dve", lambda e: e.tensor_tensor(out=oTs[o_][:, :], in0=nm[0:64, :], in1=pacc[a2][0:64, :], op=ALU.mult),
                 reads=["nm", ("pacc", a2)], writes=[("oTs", o_)])
        else:
            S.op("act", lambda e: e.copy(out=oTs[o_][:, :], in_=av[a][0:64, :]), reads=[("av", a)], writes=[("oTs", o_)])
        S.dma("st_o%d" % o_, lambda e: e.dma_start(out=io["oT"][i, h * 64:(h + 1) * 64, :], in_=oTs[o_][:, :]),
              reads=[("oTs", o_)], writes=[("oT", h, i)])

    def setup(i):
        xb = i % NXT
        qbuf = i % 2
        S.dma("ld_x%d" % xb, lambda e, i=i, xb=xb: e.dma_start(out=xt[xb][:, :, :], in_=io["hfull"][i * 512:(i + 1) * 512, :].rearrange("(s p) d -> p s d", p=128)),
              writes=[("xt", xb)])
        yield
        if not fox:
            yield from rope_tables(i)
        for s in range(4):
            b = nxt("hn", 2)
            hb = hn_s[b]
            src = xt[xb][:, s, :]
            S.op("act", lambda e, src=src: e.activation(out=junk[:, :], in_=src, func=AF.Square, accum_out=stat[:, 0:1]),
                 reads=[("xt", xb)], writes=["junk", "stat"])
            yield
            S.op("act", lambda e: e.activation(out=stat[:, 1:2], in_=stat[:, 0:1], func=AF.Ln, bias=EPS, scale=1.0 / D), reads=["stat"], writes=["stat"])
            S.op("act", lambda e: e.activation(out=stat[:, 2:3], in_=stat[:, 1:2], func=AF.Exp, scale=-0.5), reads=["stat"], writes=["stat"])
            yield
            S.op("dve", lambda e, src=src, hb=hb: e.scalar_tensor_tensor(out=hb[:, :], in0=src, scalar=stat[:, 2:3], in1=gmix[:, :], op0=ALU.mult, op1=ALU.mult),
                 reads=[("xt", xb), "stat", "gmix"], writes=[("hn_s", b)])
            yield
            yield
            for c in range(8):
                S.op("pe", lambda e, c=c, hb=hb: e.transpose(trp[0][:, c * 128:(c + 1) * 128], hb[:, c * 128:(c + 1) * 128], ident[:, :]),
                     reads=[("hn_s", b), "ident"], writes=["trp"])
            yield
            yield
            S.op("dve", lambda e, s=s: e.tensor_copy(out=hnT[:, :, s * 128:(s + 1) * 128], in_=trp[0][:].rearrange("p (c t) -> p c t", c=8)),
                 reads=["trp"], writes=["hnT"])
            yield
        for s in range(4):
            a = nxt("pacc", NPACC)
            for c in range(8):
                S.op("pe", lambda e, a=a, c=c, s=s: e.matmul(pacc[a][:, 0:128], lhsT=hnT[:, c, s * 128:(s + 1) * 128], rhs=w_s[:, c, VV:VV + 128], start=(c == 0), stop=(c == 7)),
                     reads=["w_s", "hnT"], writes=[("pacc", a)])
            yield
            yield
            for h in range(2):
                S.op("dve", lambda e, a=a, h=h, s=s, i=i: e.tensor_copy(out=vX[h][:, 4 * i + s, 0:64], in_=pacc[a][:, h * 64:(h + 1) * 64]),
                     reads=[("pacc", a)], writes=[("vX", h, i)])
            yield
        if fox:
            a = nxt("pacc", NPACC)
            for c in range(8):
                S.op("pe", lambda e, a=a, c=c: e.matmul(pacc[a][0:2, :], lhsT=w_s[:, c, FF:FF + 2], rhs=hnT[:, c, :], start=(c == 0), stop=(c == 7)),
                     reads=["w_s", "hnT"], writes=[("pacc", a)])
            yield
            yield
            S.op("act", lambda e, a=a: e.activation(out=fE[:, :], in_=pacc[a][0:2, :], func=AF.Exp, bias=nbf[:, 0:1], scale=-1.0),
                 reads=[("pacc", a), "nbf"], writes=["fE"])
            yield
            S.op("act", lambda e: e.activation(out=fE[:, :], in_=fE[:, :], func=AF.Ln, bias=1.0, scale=1.0), reads=["fE"], writes=["fE"])
            yield
            S.op("dve", lambda e: e.tensor_scalar(out=fE[:, :], in0=fE[:, :], scalar1=-1.0, scalar2=None, op0=ALU.mult), reads=["fE"], writes=["fE"])
            S.op("dve", lambda e: e.tensor_tensor_scan(out=cc[:, :], data0=fE[:, :], data1=fE[:, :], initial=cprev[:, 0:1], op0=ALU.add, op1=ALU.bypass),
                 reads=["fE", "cprev"], writes=["cc"])
            S.op("dve", lambda e: e.tensor_copy(out=cprev[:, :], in_=cc[:, 511:512]), reads=["cc"], writes=["cprev"])
            S.op("dve", lambda e: e.tensor_copy(out=rbf[:, :], in_=cc[:, :]), reads=["cc"], writes=["rbf"])
            yield
            a = nxt("pacc", NPACC)
            for s in range(4):
                S.op("pe", lambda e, a=a, s=s: e.transpose(pacc[a][:, 2 * s:2 * s + 2], cc[0:2, s * 128:(s + 1) * 128], ident_f[0:2, 0:2]),
                     reads=["cc", "ident_f"], writes=[("pacc", a)])
            yield
            yield
            S.op("dve", lambda e, a=a, i=i: e.tensor_scalar(out=negc[:, 4 * i:4 * i + 4, :], in0=pacc[a][:, 0:8].rearrange("p (s h) -> p s h", h=2),
                                                            scalar1=-1.0, scalar2=None, op0=ALU.mult),
                 reads=[("pacc", a)], writes=[("negc", i)])
            yield
            for h in range(2):
                qc, kc = (QA, KA) if h == 0 else (QB, KB)
                a = proj64(qc, i)
                yield
                yield
                S.op("act", lambda e, a=a, h=h, qbuf=qbuf: e.activation(out=qT[h][qbuf][0:64, :], in_=pacc[a][0:64, :], func=AF.Copy, scale=0.125),
                     reads=[("pacc", a)], writes=[("qT", h, qbuf)])
                S.dma("ld_r%d" % h, lambda e, h=h, qbuf=qbuf: e.dma_start(out=qT[h][qbuf][64:65, :], in_=rbf[h:h + 1, :]), reads=["rbf", ("qT", h, qbuf)], writes=[("qT", h, qbuf)])
                yield
                a = proj64(kc, i)
                yield
                yield
                S.op("act", lambda e, a=a, h=h, i=i: e.copy(out=kT[h][0:64, i * 512:(i + 1) * 512], in_=pacc[a][0:64, :]),
                     reads=[("pacc", a)], writes=[("kT", h, i)])
                yield
        else:
            a = proj64(QA, i)
            yield
            yield
            S.op("act", lambda e, a=a, qbuf=qbuf: e.activation(out=qT[0][qbuf][0:64, :], in_=pacc[a][0:64, :], func=AF.Copy, scale=0.125),
                 reads=[("pacc", a)], writes=[("qT", 0, qbuf)])
            yield
            a = proj64(KA, i)
            yield
            yield
            S.op("act", lambda e, a=a, i=i: e.copy(out=kT[0][0:64, i * 512:(i + 1) * 512], in_=pacc[a][0:64, :]), reads=[("pacc", a)], writes=[("kT", 0, i)])
            yield
            yield from rope_proj(QB, QP, i, qrot, "qrot")
            S.op("act", lambda e, qbuf=qbuf: e.activation(out=qT[1][qbuf][0:64, :], in_=qrot[:, :], func=AF.Copy, scale=0.125),
                 reads=["qrot"], writes=[("qT", 1, qbuf)])
            yield
            yield from rope_proj(KB, KP, i, krot, "krot")
            S.op("act", lambda e, i=i: e.copy(out=kT[1][0:64, i * 512:(i + 1) * 512], in_=krot[:, :]), reads=["krot"], writes=[("kT", 1, i)])
            S.op("dve", lambda e, i=i: e.tensor_reduce(out=kmT[:, 2 * i:2 * i + 2], in_=krot[:, :].rearrange("p (n k) -> p n k", n=2), axis=AX.X, op=ALU.add),
                 reads=["krot", "kmT"], writes=["kmT"])
            yield
            S.op("dve", lambda e, i=i: e.tensor_scalar(out=kmT[:, 2 * i:2 * i + 2], in0=kmT[:, 2 * i:2 * i + 2], scalar1=1.0 / 256, scalar2=None, op0=ALU.mult),
                 reads=["kmT"], writes=["kmT"])
            yield
            yield from moba_gate(i, qbuf)

    def roundrobin(*gens):
        gens = [g for g in gens if g is not None]
        while gens:
            for g in list(gens):
                try:
                    next(g)
                except StopIteration:
                    gens.remove(g)
            yield

    for _ in setup(0):
        pass
    for i in range(ntile):
        qbuf = i % 2
        nx = setup(i + 1) if i + 1 < ntile else None
        if fox:
            for h in range(2):
                attend(i, h, list(range(4 * i + 4)), 65, lambda blk, h=h: (negc[:, blk, h:h + 1], [("negc", blk // 4)]), qbuf,
                       side=nx, drain=(h == 1))
        else:
            attend(i, 1, list(range(4 * i + 4)), 128, lambda blk: (0.0, []), qbuf, a=1, side=roundrobin(sb_attend(i, qbuf), nx))
    if "dbg_negc" in io:
        S.dma("dbg", lambda e: [e.dma_start(out=io["dbg_negc"], in_=negc[:].rearrange("p b h -> p (b h)")),
                                e.dma_start(out=io["dbg_q"], in_=qT[0][(ntile - 1) % 2][:, :]),
                                e.dma_start(out=io["dbg_k"], in_=kT[0][:, 0:1024]),
                                e.dma_start(out=io["dbg_cc"], in_=cc[:, :])],
              reads=[("negc", i_) for i_ in range(ntile)] + [("qT", 0, (ntile - 1) % 2), ("kT", 0, 0), ("kT", 0, 1), "cc"], n=4)


def _masks():
    s_ = np.arange(128)[:, None, None]
    j_ = np.arange(4)[None, :, None]
    t_ = np.arange(512)[None, None, :]
    mi = ((128 * j_ + s_) <= t_).astype(np.float32)
    ms = ((128 * j_ + s_) < t_).astype(np.float32)
    return mi, ms


REST_W = ("w_out", "wq", "wkv", "wxo", "wup", "wdn", "cw", "cb", "g_xa", "g_mem", "g_ffn")
REST_SHAPES = {"w_out": [128, 8, 1024], "wq": [128, 8, 256], "wkv": [128, 8, 512], "wxo": [64, 4, 1024],
               "wup": [NPAIR, 128, 8 * 256], "wdn": [NPAIR, 2, 128, 512], "cw": [128, 44, 3], "cb": [128, 44],
               "g_xa": [D], "g_mem": [D], "g_ffn": [D]}


def build_fused(phases="AgBhCiD"):
    nc = bass.Bass("TRN2", target_bir_lowering=False, num_devices=NCORES)
    din = lambda name, shape, dt: nc.dram_tensor(name, list(shape), dt, kind="ExternalInput").ap()
    dint = lambda name, shape, dt: nc.dram_tensor(name, list(shape), dt, kind="Internal").ap()
    x = din("x", [SEQ, D], F32)
    hin0 = din("hin0", [NT + 128, D], F32)
    flag = din("flag", [128, 1], F32)
    hidx = din("hidx", [128, 17], I32)
    oidx = din("oidx", [128, 5, 8], I32)
    ident = din("ident", [128, 128], F32)
    maskI = din("maskI", [128, 4, 512], F32)
    maskS = din("maskS", [128, 4, 512], F32)
    pos = din("pos", [SEQ], I32)
    invf = din("invf", [64, 1], F32)
    sgn = din("sgn", [64, 1], F32)
    koh = din("koh", [64, SEQ], F32)
    tri = din("tri", [128, 256], F32)
    mem = din("mem", [MEM, D], F32)
    a_w = din("a_w", [128, 8, 512], F32)
    a_g = din("a_g", [D], F32)
    c_w = din("c_w", [128, 8, 386], F32)
    c_g = din("c_g", [D], F32)
    c_bf = din("c_bf", [2, 1], F32)
    g_fin = din("g_fin", [D], F32)
    rw = [{k: din("r%d_%s" % (L, k), REST_SHAPES[k], F32) for k in REST_W} for L in range(2)]
    out = nc.dram_tensor("out", [NT, D], F32, kind="ExternalOutput").ap()
    o_src = [dint("o_src%d" % L, [NTILE * 128, 512], BF16) for L in range(2)]
    o_all = [dint("o_all%d" % L, [NCORES * NTILE * 128, 512], BF16) for L in range(2)]
    h1_src = dint("h1_src", [NT, D], F32)
    h1_all = dint("h1_all", [SEQ, D], F32)

    def phase(tag, body, waits):
        with nc.cleanup_on_exit():
            with contextlib.ExitStack() as st:
                cx = Ctx(nc, st, tag)
                body(cx)
                cx.S.emit(final_wait_streams=waits)
            nc.all_engine_barrier()

    def gather(tag, src, dst):
        phase(tag, lambda cx: cx.S.cc("ag", lambda e: nc.gpsimd.collective_compute(
            "AllGather", ALU.bypass, replica_groups=[list(range(NCORES))], ins=[src.opt()], outs=[dst.opt()])), ["ag"])

    def scrub(tag):
        def body(cx):
            big = cx.sb("big", [128, 50000], F32)
            pz = [cx.ps("pz%d" % k, [128, 2048], F32) for k in range(2)]
            cx.S.op("dve", lambda e: e.memset(big[:, 0:25000], 0.0), writes=["b0"])
            cx.S.op("pool", lambda e: e.memset(big[:, 25000:50000], 0.0), writes=["b1"])
            for k in range(2):
                cx.S.op("act", lambda e, k=k: e.copy(out=pz[k][:, :], in_=big[:, 0:2048]), reads=["b0"], writes=[("pz", k)])
        phase(tag, body, [])

    def rest_io(L, hin, hout, with_hidx):
        io = dict(rw[L])
        io.update({"hin": hin, "oall": o_all[L], "oidx": oidx, "flag": flag, "mem": mem, "ident": ident, "g_fin": g_fin, "hout": hout})
        if with_hidx:
            io["hidx"] = hidx
        return io

    _phase, _gather = phase, gather
    phase = lambda tag, body, waits: _phase(tag, body, waits) if (tag in phases or tag[0] in "SG") else None
    gather = lambda tag, src, dst: _gather(tag, src, dst) if {"G0": "g", "G1": "h", "G2": "i"}[tag] in phases else None
    phase("A", lambda cx: emit_mixer(cx, {"hfull": x, "g_mix": a_g, "w": a_w, "ident": ident, "maskI": maskI, "maskS": maskS, "pos": pos,
                                          "invf": invf, "sgn": sgn, "koh": koh, "tri": tri,
                                          "oT": o_src[0].rearrange("(i f) t -> i f t", f=128)}, "ab"), ["st_o0", "st_o1"])
    if "X" in phases:
        dbg = nc.dram_tensor("dbg_o", [NTILE * 128, 512], BF16, kind="ExternalOutput").ap()
        _phase("X", lambda cx: cx.S.dma("cp", lambda e: e.dma_start(out=dbg, in_=o_src[0]), writes=["dbg"]), ["cp"])
    gather("G0", o_src[0], o_all[0])
    if "W" in phases:
        dbgw = nc.dram_tensor("dbg_oall", [NCORES * NTILE * 128, 512], BF16, kind="ExternalOutput").ap()
        _phase("W", lambda cx: cx.S.dma("cp", lambda e: e.dma_start(out=dbgw, in_=o_all[0]), writes=["dbg"]), ["cp"])
    if "s" in phases:
        scrub("S1")
    phase("B", lambda cx: emit_rest(cx, rest_io(0, hin0, h1_src, False), False), ["st_h"])
    if "Y" in phases:
        dbg1 = nc.dram_tensor("dbg_h1", [NT, D], F32, kind="ExternalOutput").ap()
        _phase("Y", lambda cx: cx.S.dma("cp", lambda e: e.dma_start(out=dbg1, in_=h1_src), writes=["dbg"]), ["cp"])
    gather("G1", h1_src, h1_all)
    phase("C", lambda cx: emit_mixer(cx, {"hfull": h1_all, "g_mix": c_g, "w": c_w, "ident": ident, "maskI": maskI, "bf": c_bf,
                                          "oT": o_src[1].rearrange("(i f) t -> i f t", f=128)}, "fox"), ["st_o0", "st_o1"])
    if "Z" in phases:
        dbg2 = nc.dram_tensor("dbg_o1", [NTILE * 128, 512], BF16, kind="ExternalOutput").ap()
        _phase("Z", lambda cx: cx.S.dma("cp", lambda e: e.dma_start(out=dbg2, in_=o_src[1]), writes=["dbg"]), ["cp"])
    gather("G2", o_src[1], o_all[1])
    phase("D", lambda cx: emit_rest(cx, rest_io(1, h1_all, out, True), True), ["st_h"])
    return nc


_CACHE = {}


def kernel(x, mem, positions, norm_mix_g, norm_xa_g, norm_mem_g, norm_ffn_g,
           ab_w_in, ab_w_out, fox_w_in, fox_b_f, fox_w_out,
           xa_w_q, xa_w_kv, xa_w_out, ffn_w_up, ffn_conv_w, ffn_conv_b, ffn_w_down,
           final_norm_g):
    a = lambda v: np.asarray(v)
    f32 = np.float32
    c_ = lambda v: np.ascontiguousarray(v, dtype=f32)
    x0 = c_(a(x)[0])
    mem_ = a(mem)
    w_in0, w_in1 = a(ab_w_in)[0], a(fox_w_in)[0]
    mi, ms = _masks()
    inv_freq = (10000.0 ** (-np.arange(32, dtype=f32) / 32)).astype(f32)
    invf = np.concatenate([inv_freq, inv_freq]).reshape(64, 1).astype(f32)
    sgn = np.concatenate([-np.ones(32), np.ones(32)]).reshape(64, 1).astype(f32)
    koh = np.zeros((64, SEQ), f32)
    for n in range(64):
        koh[n, n * 256:(n + 1) * 256] = 30000.0
    jj = np.arange(128)[:, None]
    ss = np.arange(128)[None, :]
    tri = np.concatenate([-(jj > ss).astype(f32), -(jj <= ss).astype(f32)], axis=1)
    perm = np.concatenate([np.arange(32, 64), np.arange(0, 32)])
    w_out0 = a(ab_w_out)[0]
    w_out0p = np.concatenate([np.concatenate([w_out0[r * 64:(r + 1) * 64], w_out0[(8 + r) * 64:(9 + r) * 64]], axis=0) for r in range(8)], axis=0)
    rws = []
    for L, wo in ((0, w_out0p), (1, a(fox_w_out)[0])):
        rws.append(rest_weights(L, wo, a(xa_w_q), a(xa_w_kv), a(xa_w_out), a(ffn_w_up), a(ffn_conv_w), a(ffn_conv_b),
                                a(ffn_w_down), a(norm_xa_g), a(norm_mem_g), a(norm_ffn_g), a(final_norm_g), mem_))
    common = {
        "x": x0, "ident": np.eye(128, dtype=f32), "maskI": c_((mi - 1.0) * 30000.0), "maskS": c_(ms),
        "pos": np.ascontiguousarray(a(positions).reshape(-1), dtype=np.int32), "invf": invf, "sgn": sgn, "koh": koh, "tri": tri,
        "mem": c_(mem_[0]), "a_g": c_(a(norm_mix_g)[0]), "c_g": c_(a(norm_mix_g)[1]), "g_fin": c_(a(final_norm_g)),
    }
    for L in range(2):
        for k in REST_W:
            common["r%d_%s" % (L, k)] = rws[L][k]
    in_maps = []
    p_ = np.arange(128)
    for cid in range(NCORES):
        m = dict(common)
        t0 = cid * NT
        m["hin0"] = np.concatenate([np.zeros((128, D), f32), x0[0:NT]], axis=0) if cid == 0 else c_(x0[t0 - 128:t0 + NT])
        m["flag"] = np.full((128, 1), 0.0 if cid == 0 else 1.0, f32)
        m["hidx"] = np.maximum(t0 - 128 + np.arange(17)[None, :] * 128 + p_[:, None], 0).astype(np.int32)
        tiles = np.array([max(4 * cid - 1, 0)] + [4 * cid + k for k in range(4)])
        m["oidx"] = (np.arange(8)[None, None, :] * (NTILE * 128) + tiles[None, :, None] * 128 + p_[:, None, None]).astype(np.int32)
        hA, hB = cid, 8 + cid
        qB = w_in0[:, hB * 64:(hB + 1) * 64]
        kB = w_in0[:, D + hB * 64:D + (hB + 1) * 64]
        wa = np.concatenate([w_in0[:, hA * 64:(hA + 1) * 64], w_in0[:, D + hA * 64:D + (hA + 1) * 64], qB, kB,
                             w_in0[:, 2 * D + hA * 64:2 * D + (hA + 1) * 64], w_in0[:, 2 * D + hB * 64:2 * D + (hB + 1) * 64],
                             qB[:, perm], kB[:, perm]], axis=1)
        m["a_w"] = c_(wa.reshape(8, 128, -1).transpose(1, 0, 2))
        hA, hB = 2 * cid, 2 * cid + 1
        cols = []
        for hh in (hA, hB):
            cols.append(w_in1[:, hh * 64:(hh + 1) * 64])
            cols.append(w_in1[:, D + hh * 64:D + (hh + 1) * 64])
        cols += [w_in1[:, 2 * D + hA * 64:2 * D + (hA + 1) * 64], w_in1[:, 2 * D + hB * 64:2 * D + (hB + 1) * 64],
                 w_in1[:, 3 * D + hA:3 * D + hA + 1], w_in1[:, 3 * D + hB:3 * D + hB + 1]]
        m["c_w"] = c_(np.concatenate(cols, axis=1).reshape(8, 128, -1).transpose(1, 0, 2))
        m["c_bf"] = c_(a(fox_b_f)[0][[hA, hB]].reshape(2, 1))
        in_maps.append(m)
    if "nc" not in _CACHE:
        _CACHE["nc"] = build_fused()
    res = run_bass_kernel_spmd(_CACHE["nc"], in_maps, core_ids=list(range(NCORES)))
    full = np.concatenate([r["out"] for r in res.results], axis=0)
    return np.ascontiguousarray(full[None].astype(np.float32))
```

```python
import contextlib
import numpy as np
import ml_dtypes
import concourse.bass as bass
import concourse.mybir as mybir
from concourse.bass_utils import run_bass_kernel_spmd

F32 = mybir.dt.float32
BF16 = mybir.dt.bfloat16
I32 = mybir.dt.int32
AF = mybir.ActivationFunctionType
ALU = mybir.AluOpType
AX = mybir.AxisListType

NCORES = 8
D = 1024
SEQ = 16384
NT = SEQ // NCORES
DFF = 2816
NPAIR = DFF // 128
MEM = 256
EPS = 1e-6
ENGS = ("pe", "act", "dve", "pool", "sp")
SEM_EPOCH = 24000
DEBUG = False
LAST = None


class Op:
    __slots__ = ("eng", "fn", "deps", "inc", "cnt", "dma", "ndma", "dcnt")

    def __init__(self, eng, fn, dma=None, ndma=1):
        self.eng = eng
        self.fn = fn
        self.deps = set()
        self.inc = False
        self.cnt = 0
        self.dma = dma
        self.ndma = ndma
        self.dcnt = 0


class Sched:
    def __init__(self, nc, same_engine_sync=("act", "dve", "pool")):
        self.nc = nc
        self.ops = {e: [] for e in ENGS}
        self.last_w = {}
        self.readers = {}
        self.streams = {}
        self.cc_streams = set()
        self.same = set(same_engine_sync)
        self.persist_sems = False
        self.tag = ""

    def _add(self, op, reads, writes):
        for b in reads:
            w = self.last_w.get(b)
            if w is not None:
                op.deps.add(w)
        for b in writes:
            w = self.last_w.get(b)
            if w is not None:
                op.deps.add(w)
            for r in self.readers.get(b, ()):
                op.deps.add(r)
        op.deps.discard(op)
        for b in reads:
            self.readers.setdefault(b, []).append(op)
        for b in writes:
            self.last_w[b] = op
            self.readers[b] = []
        self.ops[op.eng].append(op)
        return op

    def op(self, eng, fn, reads=(), writes=()):
        return self._add(Op(eng, fn), reads, writes)

    def dma(self, stream, fn, reads=(), writes=(), eng="sp", n=1):
        op = Op(eng, fn, dma=stream, ndma=n)
        self.streams.setdefault(stream, []).append(op)
        return self._add(op, reads, writes)

    def cc(self, stream, fn, reads=(), writes=()):
        op = Op("pool", fn, dma=stream, ndma=1)
        self.cc_streams.add(stream)
        self.streams.setdefault(stream, []).append(op)
        return self._add(op, reads, writes)

    def emit(self, final_wait_streams=()):
        nc = self.nc
        for e in ENGS:
            for op in self.ops[e]:
                for d in list(op.deps):
                    if d.dma is not None:
                        continue
                    if d.eng == op.eng and op.dma is None and d.eng not in self.same:
                        op.deps.discard(d)
                        continue
                    d.inc = True
        nep = {}
        for e in ENGS:
            c = 0
            for op in self.ops[e]:
                if op.dma is None and op.inc:
                    c += 1
                op.cnt = c
            nep[e] = max(1, -(-c // SEM_EPOCH))
        total = {}
        for s, lst in self.streams.items():
            c = 0
            for op in lst:
                c += (1 if s in self.cc_streams else 16) * op.ndma
                op.dcnt = c
            total[s] = c
        with contextlib.ExitStack() as st:
            tag = self.tag
            if self.persist_sems:
                esem = {e: [nc.alloc_semaphore("s%s_%s%d" % (tag, e, k)) for k in range(nep[e])] for e in ENGS}
                ssem = {s: nc.alloc_semaphore("d%s_%s" % (tag, s)) for s in self.streams}
            else:
                esem = {e: [st.enter_context(nc.semaphore("s_%s%d" % (e, k))) for k in range(nep[e])] for e in ENGS}
                ssem = {s: st.enter_context(nc.semaphore("d_" + s)) for s in self.streams}
            block = st.enter_context(nc.Block())

            def run(e, eng_obj):
                seen = {}
                for op in self.ops[e]:
                    need = {}
                    for d in op.deps:
                        if d.dma is not None:
                            key, val = ("d", d.dma), d.dcnt
                        else:
                            key, val = ("e", d.eng), d.cnt
                        if val > need.get(key, 0):
                            need[key] = val
                    for key, val in need.items():
                        if seen.get(key, 0) >= val:
                            continue
                        seen[key] = val
                        if key[0] == "d":
                            eng_obj.wait_ge(ssem[key[1]], val)
                        else:
                            k = (val - 1) // SEM_EPOCH
                            eng_obj.wait_ge(esem[key[1]][k], val - k * SEM_EPOCH)
                    ins = op.fn(eng_obj)
                    if op.dma is not None:
                        if not isinstance(ins, (list, tuple)):
                            ins = [ins]
                        assert len(ins) == op.ndma, (len(ins), op.ndma)
                        for i_ in ins:
                            if op.dma in self.cc_streams:
                                i_.then_inc(ssem[op.dma])
                            else:
                                i_.then_inc(ssem[op.dma], 16)
                    elif op.inc:
                        ins.then_inc(esem[e][(op.cnt - 1) // SEM_EPOCH], 1)
                if e == "sp":
                    for s in final_wait_streams:
                        eng_obj.wait_ge(ssem[s], total[s])

            @block.tensor
            def _(eng):
                run("pe", eng)

            @block.scalar
            def _(eng):
                run("act", eng)

            @block.vector
            def _(eng):
                run("dve", eng)

            @block.gpsimd
            def _(eng):
                run("pool", eng)

            @block.sync
            def _(eng):
                run("sp", eng)


class Ctx:
    def __init__(self, nc, st, tag=""):
        self.nc = nc
        self.st = st
        self.S = Sched(nc)
        self.tag = tag
        if tag:
            self.S.persist_sems = True
            self.S.tag = tag

    def sb(self, name, shape, dt):
        return self.st.enter_context(self.nc.sbuf_tensor("sb%s_%s" % (self.tag, name), list(shape), dt))

    def ps(self, name, shape, dt):
        return self.st.enter_context(self.nc.psum_tensor("ps%s_%s" % (self.tag, name), list(shape), dt))

    def din(self, name, shape, dt):
        return self.nc.dram_tensor(name, list(shape), dt, kind="ExternalInput").ap()

    def dout(self, name, shape, dt):
        return self.nc.dram_tensor(name, list(shape), dt, kind="ExternalOutput").ap()

    def dint(self, name, shape, dt):
        return self.nc.dram_tensor("di%s_%s" % (self.tag, name), list(shape), dt, kind="Internal").ap()


def emit_rest(cx, io, final):
    nc, S = cx.nc, cx.S
    sb, ps = cx.sb, cx.ps
    NSUB = NT // 128
    NTT = NT // 512
    ident = sb("ident", [128, 128], BF16)
    ones_f = sb("ones_f", [128, 64], F32)
    gxa = sb("gxa", [128, D], F32)
    gffn = sb("gffn", [128, D], F32)
    gmem = sb("gmem", [128, D], F32)
    gfin = sb("gfin", [128, D], F32) if final else None
    cw = sb("cw", [128, 44, 3], F32)
    cb = sb("cb", [128, 44], F32)
    flag = sb("flag", [128, 1], F32)
    wo_s = sb("wo_s", [128, 8, 1024], BF16)
    wq_s = sb("wq_s", [128, 8, 256], BF16)
    wkv_s = sb("wkv_s", [128, 8, 512], BF16)
    wxo_s = sb("wxo_s", [64, 4, 1024], BF16)
    kxT = sb("kxT", [64, 4, MEM], BF16)
    vxa = sb("vxa", [128, 2, 4, 65], BF16)
    ucarry = sb("ucarry", [128, 44, 2], F32)
    wup_b = cx.dint("wup_b", [NPAIR, 128, 8 * 256], BF16)
    wdn_b = cx.dint("wdn_b", [NPAIR, 2, 128, 512], BF16)
    NUP = 5
    NDN = 8
    wup_r = [sb("wup_r%d" % i, [128, 8, 256], BF16) for i in range(NUP)]
    wdn_r = [sb("wdn_r%d" % i, [128, 512], BF16) for i in range(NDN)]
    hT = [sb("hT%d" % i, [128, 4, D], F32) for i in range(2)]
    oTs = [sb("oTs%d" % i, [128, 8, 512], BF16) for i in range(2)]
    hn_s = [sb("hn_s%d" % i, [128, D], BF16) for i in range(2)]
    junk = sb("junk", [128, D], BF16)
    stat = sb("stat", [128, 8], F32)
    hnT = sb("hnT", [128, 8, 512], BF16)
    qxT = sb("qxT", [64, 4, 512], BF16)
    pxT = [sb("pxT%d" % i, [128, 512], BF16) for i in range(4)]
    nm = sb("nm", [128, 512], F32)
    rd = sb("rd", [128, 512], F32)
    oxT = sb("oxT", [64, 4, 512], BF16)
    Yg = [sb("Yg%d" % i, [128, 512], F32) for i in range(2)]
    Yv = [sb("Yv%d" % i, [128, 512], F32) for i in range(2)]
    gT = sb("gT", [128, NPAIR, 512], BF16)
    trp = [ps("trp%d" % i, [128, 1024], BF16) for i in range(2)]
    acc = [ps("acc%d" % i, [128, 512], F32) for i in range(6)]
    rot = {"acc": 0, "tr": 0, "px": 0}

    def nacc():
        rot["acc"] = (rot["acc"] + 1) % 6
        return rot["acc"]

    def ntr():
        rot["tr"] = (rot["tr"] + 1) % 2
        return rot["tr"]

    def pdma(stream, out, in_, writes, reads=()):
        S.dma(stream, lambda e: nc.gpsimd.dma_start(out=out, in_=in_), reads=reads, writes=writes, eng="pool")

    pdma("c_id", ident[:], io["ident"], ["ident"])
    pdma("c_wo", wo_s[:], io["w_out"], ["wo_s"])
    pdma("c_wq", wq_s[:], io["wq"], ["wq_s"])
    pdma("c_wkv", wkv_s[:], io["wkv"], ["wkv_s"])
    pdma("c_wxo", wxo_s[:], io["wxo"], ["wxo_s"])
    S.dma("c_g", lambda e: [e.dma_start(out=gxa[:], in_=io["g_xa"].partition_broadcast(128)),
                            e.dma_start(out=gffn[:], in_=io["g_ffn"].partition_broadcast(128)),
                            e.dma_start(out=gmem[:], in_=io["g_mem"].partition_broadcast(128)),
                            e.dma_start(out=cw[:], in_=io["cw"]),
                            e.dma_start(out=cb[:], in_=io["cb"]),
                            e.dma_start(out=flag[:], in_=io["flag"])],
          writes=["gxa", "gffn", "gmem", "cw", "cb", "flag"], n=6)
    if final:
        S.dma("c_gf", lambda e: e.dma_start(out=gfin[:], in_=io["g_fin"].partition_broadcast(128)), writes=["gfin"])
    for g in range(4):
        js = list(range(g * 6, min(NPAIR, g * 6 + 6)))
        S.dma("c_up%d" % g, lambda e, js=js: [nc.gpsimd.dma_start(out=wup_b[j], in_=io["wup"][j]) for j in js],
              writes=[("wup_b", j) for j in js], eng="pool", n=len(js))
    for g in range(4):
        js = list(range(g * 6, min(NPAIR, g * 6 + 6)))
        S.dma("c_dn%d" % g, lambda e, js=js: [nc.gpsimd.dma_start(out=wdn_b[j], in_=io["wdn"][j]) for j in js],
              writes=[("wdn_b", j) for j in js], eng="pool", n=len(js))
    S.op("dve", lambda e: e.memset(ones_f[:], 1.0), writes=["ones_f"])
    S.op("dve", lambda e: e.memset(vxa[:], 1.0), writes=["vxa"])
    oidx = sb("oidx", [128, 5, 8], I32)
    hidx = sb("hidx", [128, 17], I32)
    S.dma("c_ix", lambda e: [e.dma_start(out=oidx[:], in_=io["oidx"])] + ([e.dma_start(out=hidx[:], in_=io["hidx"])] if "hidx" in io else []),
          writes=["oidx", "hidx"], n=(2 if "hidx" in io else 1))

    def rmsnorm_to_T(src_ap, src_key, g_tile, g_key, dstT, dst_key, col0, ncol=128, nrows=128):
        b = ntr()
        hb = hn_s[b]
        S.op("act", lambda e: e.activation(out=junk[:nrows, :], in_=src_ap, func=AF.Square, accum_out=stat[:nrows, 0:1]),
             reads=[src_key], writes=["junk", "stat"])
        S.op("act", lambda e: e.activation(out=stat[:nrows, 1:2], in_=stat[:nrows, 0:1], func=AF.Ln, bias=EPS, scale=1.0 / D),
             reads=["stat"], writes=["stat"])
        S.op("act", lambda e: e.activation(out=stat[:nrows, 2:3], in_=stat[:nrows, 1:2], func=AF.Exp, scale=-0.5),
             reads=["stat"], writes=["stat"])
        S.op("dve", lambda e: e.scalar_tensor_tensor(out=hb[:nrows, :], in0=src_ap, scalar=stat[:nrows, 2:3], in1=g_tile[:nrows, :],
                                                     op0=ALU.mult, op1=ALU.mult),
             reads=[src_key, "stat", g_key], writes=[("hn_s", b)])
        for c in range(8):
            S.op("pe", lambda e, c=c: e.transpose(trp[b][:, c * 128:c * 128 + nrows], hb[:nrows, c * 128:(c + 1) * 128], ident[:nrows, :nrows]),
                 reads=[("hn_s", b), "ident"], writes=[("trp", b)])
        S.op("act", lambda e: e.copy(out=dstT[:, :, col0:col0 + nrows],
                                     in_=trp[b][:].rearrange("p (c t) -> p c t", c=8)[:, :, 0:nrows]),
             reads=[("trp", b)], writes=[dst_key])

    memT = hnT
    mem_s = hT[1]
    S.dma("ld_mem", lambda e: e.dma_start(out=mem_s[:, 0:2, :], in_=io["mem"].rearrange("(s p) d -> p s d", p=128)),
          writes=[("hT", 1)])
    for s in range(2):
        rmsnorm_to_T(mem_s[:, s, :], ("hT", 1), gmem, "gmem", memT, "hnT", s * 128)
    for hd in range(4):
        a = nacc()
        for c in range(8):
            S.op("pe", lambda e, a=a, c=c, hd=hd: e.matmul(acc[a][0:64, 0:MEM], lhsT=wkv_s[:, c, hd * 64:(hd + 1) * 64], rhs=memT[:, c, 0:MEM],
                                                            start=(c == 0), stop=(c == 7)),
                 reads=["wkv_s", "hnT"], writes=[("acc", a)])
        S.op("act", lambda e, a=a, hd=hd: e.copy(out=kxT[:, hd, :], in_=acc[a][0:64, 0:MEM]), reads=[("acc", a)], writes=["kxT"])
    for mc in range(2):
        a = nacc()
        for c in range(8):
            S.op("pe", lambda e, a=a, c=c, mc=mc: e.matmul(acc[a][:, 0:256], lhsT=memT[:, c, mc * 128:(mc + 1) * 128], rhs=wkv_s[:, c, 256:512],
                                                            start=(c == 0), stop=(c == 7)),
                 reads=["wkv_s", "hnT"], writes=[("acc", a)])
        S.op("act", lambda e, a=a, mc=mc: e.copy(out=vxa[:, mc, :, 0:64], in_=acc[a][:, 0:256].rearrange("p (h d) -> p h d", h=4)),
             reads=[("acc", a)], writes=["vxa"])

    upq = {"n": 0}
    dnq = {"n": 0}

    def load_up(j):
        slot = upq["n"] % NUP
        upq["n"] += 1
        S.dma("r_up%d" % slot, lambda e: e.dma_start(out=wup_r[slot][:], in_=wup_b[j].rearrange("p (c n) -> p c n", c=8)),
              reads=[("wup_b", j)], writes=[("wup_r", slot)])
        return slot

    def load_dn(j, half):
        slot = dnq["n"] % NDN
        dnq["n"] += 1
        S.dma("r_dn%d" % slot, lambda e: e.dma_start(out=wdn_r[slot][:], in_=wdn_b[j, half]),
              reads=[("wdn_b", j)], writes=[("wdn_r", slot)])
        return slot

    def front(buf, nsub, row0):
        ntok = nsub * 128
        hb = hT[buf]
        k0 = row0 // 128
        t5 = 0 if row0 == 0 else 1 + (row0 - 128) // 512
        oc0 = 384 if row0 == 0 else 0
        if "hidx" in io:
            S.dma("ld_h%d" % buf, lambda e: [nc.gpsimd.indirect_dma_start(out=hb[:, s_, :], out_offset=None, in_=io["hin"],
                                                                        in_offset=bass.IndirectOffsetOnAxis(ap=hidx[:, k0 + s_:k0 + s_ + 1], axis=0))
                                             for s_ in range(nsub)],
                  reads=["hidx"], writes=[("hT", buf)], eng="pool", n=nsub)
        else:
            S.dma("ld_h%d" % buf, lambda e: e.dma_start(out=hb[:, 0:nsub, :], in_=io["hin"][row0:row0 + ntok, :].rearrange("(s p) d -> p s d", p=128)),
                  writes=[("hT", buf)])
        S.dma("ld_o%d" % buf, lambda e: [nc.gpsimd.indirect_dma_start(out=oTs[buf][:, r_, :], out_offset=None, in_=io["oall"],
                                                                    in_offset=bass.IndirectOffsetOnAxis(ap=oidx[:, t5, r_:r_ + 1], axis=0))
                                         for r_ in range(8)],
              reads=["oidx"], writes=[("oTs", buf)], eng="pool", n=8)
        for s in range(nsub):
            for half in range(2):
                a = nacc()
                for c in range(8):
                    S.op("pe", lambda e, a=a, c=c, s=s, half=half: e.matmul(acc[a][:, :], lhsT=oTs[buf][:, c, oc0 + s * 128:oc0 + (s + 1) * 128],
                                                                            rhs=wo_s[:, c, half * 512:(half + 1) * 512], start=(c == 0), stop=(c == 7)),
                         reads=[("oTs", buf), "wo_s"], writes=[("acc", a)])
                S.op("dve", lambda e, a=a, s=s, half=half: e.tensor_tensor(out=hb[:, s, half * 512:(half + 1) * 512], in0=hb[:, s, half * 512:(half + 1) * 512],
                                                                           in1=acc[a][:, :], op=ALU.add),
                     reads=[("acc", a), ("hT", buf)], writes=[("hT", buf)])
        for s in range(nsub):
            rmsnorm_to_T(hb[:, s, :], ("hT", buf), gxa, "gxa", hnT, "hnT", s * 128)
        for hd in range(4):
            a = nacc()
            for c in range(8):
                S.op("pe", lambda e, a=a, c=c, hd=hd: e.matmul(acc[a][0:64, 0:ntok], lhsT=wq_s[:, c, hd * 64:(hd + 1) * 64], rhs=hnT[:, c, 0:ntok],
                                                                start=(c == 0), stop=(c == 7)),
                     reads=["wq_s", "hnT"], writes=[("acc", a)])
            S.op("act", lambda e, a=a, hd=hd: e.copy(out=qxT[:, hd, 0:ntok], in_=acc[a][0:64, 0:ntok]), reads=[("acc", a)], writes=[("qxT", hd)])
        for hd in range(4):
            pk = []
            for mc in range(2):
                a = nacc()
                S.op("pe", lambda e, a=a, mc=mc, hd=hd: e.matmul(acc[a][:, 0:ntok], lhsT=kxT[:, hd, mc * 128:(mc + 1) * 128], rhs=qxT[:, hd, 0:ntok],
                                                                  start=True, stop=True),
                     reads=["kxT", ("qxT", hd)], writes=[("acc", a)])
                p = rot["px"] = (rot["px"] + 1) % 4
                S.op("act", lambda e, a=a, p=p: e.activation(out=pxT[p][:, 0:ntok], in_=acc[a][:, 0:ntok], func=AF.Exp, scale=0.125),
                     reads=[("acc", a)], writes=[("pxT", p)])
                pk.append(p)
            a = nacc()
            for mc in range(2):
                S.op("pe", lambda e, a=a, mc=mc, hd=hd, p=pk[mc]: e.matmul(acc[a][0:65, 0:ntok], lhsT=vxa[:, mc, hd, :], rhs=pxT[p][:, 0:ntok],
                                                                            start=(mc == 0), stop=(mc == 1)),
                     reads=["vxa", ("pxT", pk[mc])], writes=[("acc", a)])
            S.op("dve", lambda e, a=a: e.reciprocal(out=rd[64:65, 0:ntok], in_=acc[a][64:65, 0:ntok]), reads=[("acc", a)], writes=["rd"])
            S.op("act", lambda e, a=a: e.copy(out=nm[0:64, 0:ntok], in_=acc[a][0:64, 0:ntok]), reads=[("acc", a)], writes=["nm"])
            a2 = nacc()
            S.op("pe", lambda e, a2=a2: e.matmul(acc[a2][0:64, 0:ntok], lhsT=ones_f[64:65, 0:64], rhs=rd[64:65, 0:ntok], start=True, stop=True),
                 reads=["ones_f", "rd"], writes=[("acc", a2)])
            S.op("dve", lambda e, a2=a2, hd=hd: e.tensor_tensor(out=oxT[:, hd, 0:ntok], in0=nm[0:64, 0:ntok], in1=acc[a2][0:64, 0:ntok], op=ALU.mult),
                 reads=["nm", ("acc", a2)], writes=[("oxT", hd)])
        for s in range(nsub):
            for half in range(2):
                a = nacc()
                for hd in range(4):
                    S.op("pe", lambda e, a=a, hd=hd, s=s, half=half: e.matmul(acc[a][:, :], lhsT=oxT[:, hd, s * 128:(s + 1) * 128],
                                                                              rhs=wxo_s[:, hd, half * 512:(half + 1) * 512], start=(hd == 0), stop=(hd == 3)),
                         reads=[("oxT", hd), "wxo_s"], writes=[("acc", a)])
                S.op("dve", lambda e, a=a, s=s, half=half: e.tensor_tensor(out=hb[:, s, half * 512:(half + 1) * 512], in0=hb[:, s, half * 512:(half + 1) * 512],
                                                                           in1=acc[a][:, :], op=ALU.add),
                     reads=[("acc", a), ("hT", buf)], writes=[("hT", buf)])
        for s in range(nsub):
            rmsnorm_to_T(hb[:, s, :], ("hT", buf), gffn, "gffn", hnT, "hnT", s * 128)

    front(1, 1, 0)
    for j in range(NPAIR):
        slot = load_up(j)
        for gv in range(2):
            grp = j + gv * NPAIR
            a = nacc()
            for c in range(8):
                S.op("pe", lambda e, a=a, c=c, gv=gv, slot=slot: e.matmul(acc[a][:, 0:2], lhsT=wup_r[slot][:, c, gv * 128:(gv + 1) * 128], rhs=hnT[:, c, 126:128],
                                                                           start=(c == 0), stop=(c == 7)),
                     reads=[("wup_r", slot), "hnT"], writes=[("acc", a)])
            S.op("dve", lambda e, a=a, grp=grp: e.tensor_scalar(out=ucarry[:, grp, :], in0=acc[a][:, 0:2], scalar1=flag[:, 0:1], scalar2=None, op0=ALU.mult),
                 reads=[("acc", a), "flag"], writes=[("ucarry", grp)])

    for tt in range(NTT):
        buf = tt % 2
        hb = hT[buf]
        front(buf, 4, 128 + tt * 512)
        for j in range(NPAIR):
            slot = load_up(j)
            yb = j % 2
            aa = []
            for gv in range(2):
                a = nacc()
                aa.append(a)
                for c in range(8):
                    S.op("pe", lambda e, a=a, c=c, gv=gv, slot=slot: e.matmul(acc[a][:, :], lhsT=wup_r[slot][:, c, gv * 128:(gv + 1) * 128], rhs=hnT[:, c, :],
                                                                               start=(c == 0), stop=(c == 7)),
                         reads=[("wup_r", slot), "hnT"], writes=[("acc", a)])
            YY = [Yg[yb], Yv[yb]]
            yk = [("Y", 0, yb), ("Y", 1, yb)]
            gp = [j, j + NPAIR]
            for gv in range(2):
                S.op("act", lambda e, a=aa[gv], grp=gp[gv], Y=YY[gv]: e.activation(out=Y[:, :], in_=acc[a][:, :], func=AF.Identity, bias=cb[:, grp:grp + 1], scale=cw[:, grp, 2:3]),
                     reads=[("acc", aa[gv]), "cw", "cb"], writes=[yk[gv]])
            for gv in range(2):
                S.op("dve", lambda e, a=aa[gv], grp=gp[gv], Y=YY[gv]: e.scalar_tensor_tensor(out=Y[:, 1:512], in0=acc[a][:, 0:511], scalar=cw[:, grp, 1:2], in1=Y[:, 1:512],
                                                                                              op0=ALU.mult, op1=ALU.add),
                     reads=[("acc", aa[gv]), "cw", yk[gv]], writes=[yk[gv]])
            for gv in range(2):
                S.op("dve", lambda e, a=aa[gv], grp=gp[gv], Y=YY[gv]: e.scalar_tensor_tensor(out=Y[:, 2:512], in0=acc[a][:, 0:510], scalar=cw[:, grp, 0:1], in1=Y[:, 2:512],
                                                                                              op0=ALU.mult, op1=ALU.add),
                     reads=[("acc", aa[gv]), "cw", yk[gv]], writes=[yk[gv]])
            for gv in range(2):
                S.op("dve", lambda e, grp=gp[gv], Y=YY[gv]: e.scalar_tensor_tensor(out=Y[:, 0:1], in0=ucarry[:, grp, 1:2], scalar=cw[:, grp, 1:2], in1=Y[:, 0:1],
                                                                                   op0=ALU.mult, op1=ALU.add),
                     reads=[("ucarry", gp[gv]), "cw", yk[gv]], writes=[yk[gv]])
            for gv in range(2):
                S.op("dve", lambda e, grp=gp[gv], Y=YY[gv]: e.scalar_tensor_tensor(out=Y[:, 0:2], in0=ucarry[:, grp, 0:2], scalar=cw[:, grp, 0:1], in1=Y[:, 0:2],
                                                                                   op0=ALU.mult, op1=ALU.add),
                     reads=[("ucarry", gp[gv]), "cw", yk[gv]], writes=[yk[gv]])
            for gv in range(2):
                S.op("act", lambda e, a=aa[gv], grp=gp[gv]: e.copy(out=ucarry[:, grp, :], in_=acc[a][:, 510:512]),
                     reads=[("acc", aa[gv])], writes=[("ucarry", gp[gv])])
            S.op("act", lambda e, yb=yb: e.activation(out=Yg[yb][:, :], in_=Yg[yb][:, :], func=AF.Silu),
                 reads=[("Y", 0, yb)], writes=[("Y", 0, yb)])
            S.op("pool", lambda e, yb=yb, j=j: e.tensor_tensor(out=gT[:, j, :], in0=Yg[yb][:, :], in1=Yv[yb][:, :], op=ALU.mult),
                 reads=[("Y", 0, yb), ("Y", 1, yb)], writes=[("gT", j)])
        for half in range(2):
            accs = [nacc() for _ in range(4)]
            for j in range(NPAIR):
                slot = load_dn(j, half)
                for s in range(4):
                    a = accs[s]
                    S.op("pe", lambda e, a=a, j=j, s=s, slot=slot: e.matmul(acc[a][:, :], lhsT=gT[:, j, s * 128:(s + 1) * 128], rhs=wdn_r[slot][:, :],
                                                                             start=(j == 0), stop=(j == NPAIR - 1)),
                         reads=[("gT", j), ("wdn_r", slot)], writes=[("acc", a)])
            for s in range(4):
                a = accs[s]
                S.op("dve", lambda e, a=a, s=s, half=half, hb=hb: e.tensor_tensor(out=hb[:, s, half * 512:(half + 1) * 512], in0=hb[:, s, half * 512:(half + 1) * 512],
                                                                           in1=acc[a][:, :], op=ALU.add),
                     reads=[("acc", a), ("hT", buf)], writes=[("hT", buf)])
        if final:
            for s in range(4):
                S.op("act", lambda e, s=s, hb=hb: e.activation(out=junk[:, :], in_=hb[:, s, :], func=AF.Square, accum_out=stat[:, 4:5]),
                     reads=[("hT", buf)], writes=["junk", "stat2"])
                S.op("act", lambda e: e.activation(out=stat[:, 5:6], in_=stat[:, 4:5], func=AF.Ln, bias=EPS, scale=1.0 / D),
                     reads=["stat2"], writes=["stat2"])
                S.op("act", lambda e: e.activation(out=stat[:, 6:7], in_=stat[:, 5:6], func=AF.Exp, scale=-0.5),
                     reads=["stat2"], writes=["stat2"])
                S.op("dve", lambda e, s=s, hb=hb: e.scalar_tensor_tensor(out=hb[:, s, :], in0=hb[:, s, :], scalar=stat[:, 6:7], in1=gfin[:, :],
                                                                  op0=ALU.mult, op1=ALU.mult),
                     reads=[("hT", buf), "stat2", "gfin"], writes=[("hT", buf)])
        S.dma("st_h", lambda e, tt=tt, hb=hb: e.dma_start(out=io["hout"][tt * 512:(tt + 1) * 512, :].rearrange("(s p) d -> p s d", p=128), in_=hb[:, :, :]),
              reads=[("hT", buf)], writes=[("hout", tt)])


def rest_weights(layer, w_out, xa_w_q, xa_w_kv, xa_w_out, ffn_w_up, ffn_conv_w, ffn_conv_b, ffn_w_down,
                 norm_xa_g, norm_mem_g, norm_ffn_g, final_norm_g, mem):
    f = np.float32
    c = np.ascontiguousarray
    wup = ffn_w_up[layer].reshape(8, 128, 2, NPAIR, 128).transpose(3, 1, 0, 2, 4)
    return {
        "w_out": c(w_out.reshape(8, 128, 1024).transpose(1, 0, 2), dtype=f),
        "wq": c(xa_w_q[layer].reshape(8, 128, 256).transpose(1, 0, 2), dtype=f),
        "wkv": c(xa_w_kv[layer].reshape(8, 128, 512).transpose(1, 0, 2), dtype=f),
        "wxo": c(xa_w_out[layer].reshape(4, 64, 1024).transpose(1, 0, 2), dtype=f),
        "wup": c(wup.reshape(NPAIR, 128, 8 * 256), dtype=f),
        "wdn": c(ffn_w_down[layer].reshape(NPAIR, 128, 2, 512).transpose(0, 2, 1, 3), dtype=f),
        "cw": c(ffn_conv_w[layer].reshape(3, 44, 128).transpose(2, 1, 0), dtype=f),
        "cb": c(ffn_conv_b[layer].reshape(44, 128).T, dtype=f),
        "g_xa": c(norm_xa_g[layer], dtype=f),
        "g_mem": c(norm_mem_g[layer], dtype=f),
        "g_ffn": c(norm_ffn_g[layer], dtype=f),
        "g_fin": c(final_norm_g, dtype=f),
        "mem": c(mem[0], dtype=f),
        "ident": np.eye(128, dtype=f),
    }


NTILE = SEQ // 512
NBLK = SEQ // 128
SB_WIN = 3


def emit_mixer(cx, io, kind, ntile=NTILE):
    nc, S = cx.nc, cx.S
    sb, ps = cx.sb, cx.ps
    fox = kind == "fox"
    NCOL = 386 if fox else 512
    if fox:
        QA, KA, QB, KB, VV, FF = 0, 64, 128, 192, 256, 384
    else:
        QA, KA, QB, KB, VV, QP, KP = 0, 64, 128, 192, 256, 384, 448
    ident = sb("ident", [128, 128], BF16)
    ident_f = sb("ident_f", [128, 128], F32)
    ones_f = sb("ones_f", [128, 64], F32)
    gmix = sb("gmix", [128, D], F32)
    w_s = sb("w_s", [128, 8, NCOL], BF16)
    maskI = sb("maskI", [128, 4, 512], BF16)
    kT = [sb("kT%d" % h, [128, SEQ], BF16) for h in range(2)]
    vX = [sb("vX%d" % h, [128, NBLK, 65], BF16) for h in range(2)]
    qT = [[sb("qT%d_%d" % (h, i), [128, 512], BF16) for i in range(2)] for h in range(2)]
    NXT = 2 if fox else 1
    xt = [sb("xt%d" % i, [128, 4, D], F32) for i in range(NXT)]
    hn_s = [sb("hn_s%d" % i, [128, D], BF16) for i in range(2)]
    junk = sb("junk", [128, D], BF16)
    stat = sb("stat", [128, 8], F32)
    hnT = sb("hnT", [128, 8, 512], BF16)
    NPT = 6 if fox else 4
    pT = [sb("pT%d" % i, [128, 512], BF16) for i in range(6)]
    nm = sb("nm", [128, 512], F32)
    rd = sb("rd", [128, 512], F32)
    oTs = [sb("oTs%d" % i, [64, 512], BF16) for i in range(2)]
    trp = [ps("trp%d" % i, [128, 1024], BF16) for i in range(1)]
    NPACC = 2 if fox else 1
    pacc = [ps("pacc%d" % i, [128, 512], F32) for i in range(NPACC)]
    NSC = 3 if fox else 2
    sc = [ps("sc%d" % i, [128, 512], F32) for i in range(NSC)]
    av = [ps("av%d" % i, [128, 512], F32) for i in range(2)]
    rot = {"pacc": 0, "sc": 0, "pt": 0, "av": 0, "hn": 0, "ot": 0}

    def nxt(k, n):
        rot[k] = (rot[k] + 1) % n
        return rot[k]

    def pdma(stream, out, in_, writes):
        S.dma(stream, lambda e: nc.gpsimd.dma_start(out=out, in_=in_), writes=writes, eng="pool")

    pdma("c_id", ident[:], io["ident"], ["ident"])
    pdma("c_w", w_s[:], io["w"], ["w_s"])
    pdma("c_mi", maskI[:], io["maskI"], ["maskI"])
    S.dma("c_g", lambda e: [e.dma_start(out=gmix[:], in_=io["g_mix"].partition_broadcast(128)),
                            e.dma_start(out=ident_f[:], in_=io["ident"])], writes=["gmix", "ident_f"], n=2)
    S.op("dve", lambda e: e.memset(ones_f[:], 1.0), writes=["ones_f"])
    for h in range(2):
        S.op("pool", lambda e, h=h: e.memset(vX[h][:], 1.0), writes=[("vX", h, i_) for i_ in range(NTILE)])
    if fox:
        nbf = sb("nbf", [2, 1], F32)
        cprev = sb("cprev", [2, 1], F32)
        fE = sb("fE", [2, 512], F32)
        cc = sb("cc", [2, 512], F32)
        rbf = sb("rbf", [2, 512], BF16)
        negc = sb("negc", [128, NBLK, 2], F32)
        S.dma("c_bf", lambda e: e.dma_start(out=nbf[:], in_=io["bf"]), writes=["nbf"])
        S.op("dve", lambda e: e.tensor_scalar(out=nbf[:], in0=nbf[:], scalar1=-1.0, scalar2=None, op0=ALU.mult), reads=["nbf"], writes=["nbf"])
        S.op("dve", lambda e: e.memset(cprev[:], 0.0), writes=["cprev"])
        for h in range(2):
            S.op("pool", lambda e, h=h: e.memset(kT[h][64:128, :], 1.0), writes=[("kT", h, i_) for i_ in range(NTILE)])


    if not fox:
        maskS = sb("maskS", [128, 4, 512], BF16)
        tri = sb("tri", [128, 256], F32)
        invf = sb("invf", [64, 1], F32)
        sgn = sb("sgn", [64, 1], F32)
        pos_i = sb("pos_i", [64, 512], I32)
        ang = sb("ang", [64, 512], F32)
        kint = sb("kint", [64, 512], I32)
        kflt = sb("kflt", [64, 512], F32)
        msk = sb("msk", [64, 512], F32)
        cosF = sb("cosF", [64, 512], F32)
        sinS = sb("sinS", [64, 512], F32)
        rt1 = sb("rt1", [64, 512], F32)
        qrot = sb("qrot", [64, 512], F32)
        krot = sb("krot", [64, 512], F32)
        kmT = sb("kmT", [64, 64], F32)
        G = sb("G", [128, 64], F32)
        top8 = sb("top8", [128, 8], F32)
        MB = sb("MB", [128, 128], BF16)
        spE = [sb("spE%d" % k, [128, 512], F32) for k in range(2)]
        spM = [sb("spM%d" % k, [128, 512], F32) for k in range(2)]
        lgA = [sb("lgA%d" % k, [128, 512], F32) for k in range(2)]
        tailP = ps("tailP", [128, 512], F32)
        scS = ps("scS", [128, 512], F32)
        pdma("c_ms", maskS[:], io["maskS"], ["maskS"])
        S.dma("c_ab", lambda e: [e.dma_start(out=tri[:], in_=io["tri"]), e.dma_start(out=invf[:], in_=io["invf"]),
                                 e.dma_start(out=sgn[:], in_=io["sgn"])], writes=["tri", "invf", "sgn"], n=3)
        S.dma("c_koh", lambda e: nc.gpsimd.dma_start(out=kT[1][64:128, :], in_=io["koh"]), writes=[("kT", 1, i_) for i_ in range(NTILE)], eng="pool")
        S.op("dve", lambda e: e.memset(kmT[:], 0.0), writes=["kmT"])
        S.op("dve", lambda e: e.memset(MB[:], 0.0), writes=["MB"])

    def rope_tables(i):
        PI = float(np.pi)
        S.dma("ld_pos", lambda e: e.dma_start(out=pos_i[:], in_=io["pos"][i * 512:(i + 1) * 512].partition_broadcast(64)), writes=["pos_i"])
        S.op("dve", lambda e: e.tensor_copy(out=ang[:], in_=pos_i[:]), reads=["pos_i"], writes=["ang"])
        S.op("dve", lambda e: e.tensor_scalar(out=ang[:], in0=ang[:], scalar1=invf[:, 0:1], scalar2=None, op0=ALU.mult), reads=["ang", "invf"], writes=["ang"])
        S.op("dve", lambda e: e.tensor_scalar(out=kint[:], in0=ang[:], scalar1=float(1.0 / (2 * np.pi)), scalar2=None, op0=ALU.mult), reads=["ang"], writes=["kint"])
        S.op("dve", lambda e: e.tensor_copy(out=kflt[:], in_=kint[:]), reads=["kint"], writes=["kflt"])
        yield
        S.op("dve", lambda e: e.scalar_tensor_tensor(out=ang[:], in0=kflt[:], scalar=-6.28125, in1=ang[:], op0=ALU.mult, op1=ALU.add), reads=["kflt", "ang"], writes=["ang"])
        S.op("dve", lambda e: e.scalar_tensor_tensor(out=ang[:], in0=kflt[:], scalar=float(-(2 * np.pi - 6.28125)), in1=ang[:], op0=ALU.mult, op1=ALU.add),
             reads=["kflt", "ang"], writes=["ang"])
        S.op("dve", lambda e: e.tensor_scalar(out=msk[:], in0=ang[:], scalar1=PI, scalar2=-2 * PI, op0=ALU.is_gt, op1=ALU.mult), reads=["ang"], writes=["msk"])
        S.op("dve", lambda e: e.tensor_tensor(out=ang[:], in0=ang[:], in1=msk[:], op=ALU.add), reads=["ang", "msk"], writes=["ang"])
        S.op("dve", lambda e: e.tensor_scalar(out=msk[:], in0=ang[:], scalar1=-PI, scalar2=2 * PI, op0=ALU.is_lt, op1=ALU.mult), reads=["ang"], writes=["msk"])
        S.op("dve", lambda e: e.tensor_tensor(out=ang[:], in0=ang[:], in1=msk[:], op=ALU.add), reads=["ang", "msk"], writes=["ang"])
        yield
        S.op("dve", lambda e: e.tensor_scalar(out=rt1[:], in0=ang[:], scalar1=PI / 2, scalar2=None, op0=ALU.add), reads=["ang"], writes=["rt1"])
        S.op("dve", lambda e: e.tensor_scalar(out=msk[:], in0=rt1[:], scalar1=PI, scalar2=-2 * PI, op0=ALU.is_gt, op1=ALU.mult), reads=["rt1"], writes=["msk"])
        S.op("dve", lambda e: e.tensor_tensor(out=rt1[:], in0=rt1[:], in1=msk[:], op=ALU.add), reads=["rt1", "msk"], writes=["rt1"])
        S.op("act", lambda e: e.activation(out=sinS[:], in_=ang[:], func=AF.Sin, scale=sgn[:, 0:1]), reads=["ang", "sgn"], writes=["sinS"])
        yield
        S.op("act", lambda e: e.activation(out=cosF[:], in_=rt1[:], func=AF.Sin), reads=["rt1"], writes=["cosF"])
        yield

    def rope_proj(col_main, col_perm, i, dst, dkey):
        a_main = proj64(col_main, i)
        yield
        yield
        S.op("dve", lambda e: e.tensor_tensor(out=rt1[:], in0=pacc[a_main][0:64, :], in1=cosF[:], op=ALU.mult), reads=[("pacc", a_main), "cosF"], writes=["rt1"])
        yield
        a_perm = proj64(col_perm, i)
        yield
        yield
        S.op("dve", lambda e: e.tensor_tensor(out=dst[:], in0=pacc[a_perm][0:64, :], in1=sinS[:], op=ALU.mult), reads=[("pacc", a_perm), "sinS"], writes=[dkey])
        yield
        S.op("dve", lambda e: e.tensor_tensor(out=dst[:], in0=dst[:], in1=rt1[:], op=ALU.add), reads=[dkey, "rt1"], writes=[dkey])
        yield

    def moba_gate(i, qbuf):
        for s in range(4):
            own = 2 * i + s // 2
            if own > 0:
                a = nxt("pacc", NPACC)
                S.op("pe", lambda e, a=a, s=s: e.matmul(pacc[a][:, 0:64], lhsT=qrot[0:64, s * 128:(s + 1) * 128], rhs=kmT[0:64, 0:64], start=True, stop=True),
                     reads=["qrot", "kmT"], writes=[("pacc", a)])
                yield
                S.op("dve", lambda e: e.memset(G[:], -1e9), writes=["G"])
                S.op("dve", lambda e, a=a, own=own: e.tensor_copy(out=G[:, 0:own], in_=pacc[a][:, 0:own]), reads=[("pacc", a), "G"], writes=["G"])
                yield
                S.op("dve", lambda e: e.max(out=top8[:], in_=G[:]), reads=["G"], writes=["top8"])
                S.op("dve", lambda e: e.tensor_scalar(out=top8[:, 2:3], in0=top8[:, 2:3], scalar1=-1e8, scalar2=None, op0=ALU.max), reads=["top8"], writes=["top8"])
                S.op("dve", lambda e: e.tensor_scalar(out=MB[:, 64:128], in0=G[:], scalar1=top8[:, 2:3], scalar2=-1.0, op0=ALU.is_ge, op1=ALU.add),
                     reads=["G", "top8"], writes=["MB"])
            S.op("dve", lambda e, own=own: e.memset(MB[:, 64 + own:128], 0.0), reads=["MB"], writes=["MB"])
            yield
            S.op("pe", lambda e: e.transpose(trp[0][:, 0:128], MB[:, :], ident[:, :]), reads=["MB", "ident"], writes=["trp"])
            yield
            yield
            S.op("act", lambda e, s=s: e.copy(out=qT[1][qbuf][64:128, s * 128:(s + 1) * 128], in_=trp[0][64:128, 0:128]),
                 reads=["trp"], writes=[("qT", 1, qbuf)])
            yield

    def sb_attend(i, qbuf):
        a = 0
        lo = max(0, 4 * i - SB_WIN)
        blocks = list(range(4 * i + 3, lo - 1, -1))
        for n, blk in enumerate(blocks):
            j = blk - 4 * i
            p_ = 4 + n % 2
            b2 = n % 2
            S.op("pe", lambda e, blk=blk: e.matmul(scS[:, :], lhsT=kT[0][0:64, blk * 128:(blk + 1) * 128], rhs=qT[0][qbuf][0:64, :], start=True, stop=True),
                 reads=[("kT", 0, blk // 4), ("qT", 0, qbuf)], writes=["scS"])
            yield
            S.op("act", lambda e, b2=b2: e.activation(out=spE[b2][:, :], in_=scS[:, :], func=AF.Exp), reads=["scS"], writes=[("spE", b2)])
            yield
            S.op("act", lambda e, b2=b2: e.activation(out=spE[b2][:, :], in_=spE[b2][:, :], func=AF.Ln, bias=1.0, scale=1.0), reads=[("spE", b2)], writes=[("spE", b2)])
            yield
            if j >= 0:
                S.op("pool", lambda e, b2=b2, j=j: e.tensor_tensor(out=spM[b2][:, :], in0=spE[b2][:, :], in1=maskS[:, j, :], op=ALU.mult),
                     reads=[("spE", b2), "maskS"], writes=[("spM", b2)])
                src, skey = spM[b2], ("spM", b2)
            else:
                src, skey = spE[b2], ("spE", b2)
            S.op("dve", lambda e, b2=b2: e.tensor_tensor(out=lgA[b2][:, :], in0=scS[:, :], in1=spE[b2][:, :], op=ALU.subtract),
                 reads=["scS", ("spE", b2)], writes=[("lgA", b2)])
            yield
            S.op("pe", lambda e, src=src, n=n: e.matmul(tailP[:, :], lhsT=tri[:, 0:128], rhs=src[:, :], start=(n == 0), stop=True),
                 reads=["tri", skey], writes=["tailP"])
            yield
            S.op("dve", lambda e, b2=b2: e.tensor_tensor(out=lgA[b2][:, :], in0=lgA[b2][:, :], in1=tailP[:, :], op=ALU.add),
                 reads=[("lgA", b2), "tailP"], writes=[("lgA", b2)])
            yield
            S.op("pe", lambda e, src=src: e.matmul(tailP[:, :], lhsT=tri[:, 128:256], rhs=src[:, :], start=False, stop=True),
                 reads=["tri", skey], writes=["tailP"])
            S.op("act", lambda e, p_=p_, b2=b2: e.activation(out=pT[p_][:, :], in_=lgA[b2][:, :], func=AF.Exp), reads=[("lgA", b2)], writes=[("pT", p_)])
            yield
            if j >= 0:
                S.op("pool", lambda e, p_=p_, j=j: e.tensor_tensor(out=pT[p_][:, :], in0=pT[p_][:, :], in1=maskS[:, j, :], op=ALU.mult),
                     reads=[("pT", p_), "maskS"], writes=[("pT", p_)])
                yield
            S.op("pe", lambda e, p_=p_, blk=blk, n=n: e.matmul(av[a][0:64, :], lhsT=vX[0][:, blk, 0:64], rhs=pT[p_][:, :], start=(n == 0), stop=(n == len(blocks) - 1)),
                 reads=[("vX", 0, blk // 4), ("pT", p_)], writes=[("av", a)])
            yield
        finalize(i, 0, a, False)

    def proj64(col, i, nrows=64):
        a = nxt("pacc", NPACC)
        for c in range(8):
            S.op("pe", lambda e, a=a, c=c: e.matmul(pacc[a][0:nrows, :], lhsT=w_s[:, c, col:col + nrows], rhs=hnT[:, c, :], start=(c == 0), stop=(c == 7)),
                 reads=["w_s", "hnT"], writes=[("pacc", a)])
        return a

    LOOK = 2

    def attend(i, h, blocks, krows, bias_fn, qbuf, a=None, side=None, drain=True):
        if a is None:
            a = nxt("av", 2)
        nb = len(blocks)
        pbuf = [None] * nb
        for n in range(nb + LOOK):
            if n < nb:
                blk = blocks[n]
                s_ = nxt("sc", NSC)
                p_ = nxt("pt", NPT)
                pbuf[n] = p_
                j = blk - 4 * i
                S.op("pe", lambda e, s_=s_, blk=blk, j=j: e.matmul(sc[s_][:, :], lhsT=kT[h][0:krows, blk * 128:(blk + 1) * 128], rhs=qT[h][qbuf][0:krows, :],
                                                                   start=True, stop=(j < 0)),
                     reads=[("kT", h, blk // 4), ("qT", h, qbuf)], writes=[("sc", s_)])
                if j >= 0:
                    S.op("pe", lambda e, s_=s_, j=j: e.matmul(sc[s_][:, :], lhsT=ident[:, :], rhs=maskI[:, j, :], start=False, stop=True),
                         reads=["ident", "maskI"], writes=[("sc", s_)])
                b_ap, b_key = bias_fn(blk)
                S.op("act", lambda e, s_=s_, p_=p_, b_ap=b_ap: e.activation(out=pT[p_][:, :], in_=sc[s_][:, :], func=AF.Exp, bias=b_ap, scale=1.0),
                     reads=[("sc", s_)] + b_key, writes=[("pT", p_)])
            m = n - LOOK
            if m >= 0:
                blk = blocks[m]
                p_ = pbuf[m]
                S.op("pe", lambda e, p_=p_, blk=blk, m=m: e.matmul(av[a][0:65, :], lhsT=vX[h][:, blk, :], rhs=pT[p_][:, :], start=(m == 0), stop=(m == nb - 1)),
                     reads=[("vX", h, blk // 4), ("pT", p_)], writes=[("av", a)])
            if side is not None:
                next(side, None)
        if side is not None and drain:
            for _ in side:
                pass
        finalize(i, h, a, True)

    def finalize(i, h, a, normalize):
        o_ = nxt("ot", 2)
        if normalize:
            S.op("dve", lambda e: e.reciprocal(out=rd[64:65, :], in_=av[a][64:65, :]), reads=[("av", a)], writes=["rd"])
            S.op("act", lambda e: e.copy(out=nm[0:64, :], in_=av[a][0:64, :]), reads=[("av", a)], writes=["nm"])
            a2 = nxt("pacc", NPACC)
            S.op("pe", lambda e: e.matmul(pacc[a2][0:64, :], lhsT=ones_f[64:65, 0:64], rhs=rd[64:65, :], start=True, stop=True),
                 reads=["ones_f", "rd"], writes=[("pacc", a2)])
            S.op("dve", lambda e: e.tensor_tensor(out=oTs[o_][:, :], in0=nm[0:64, :], in1=pacc[a2][0:64, :], op=ALU.mult),
                 reads=["nm", ("pacc", a2)], writes=[("oTs", o_)])
        else:
            S.op("act", lambda e: e.copy(out=oTs[o_][:, :], in_=av[a][0:64, :]), reads=[("av", a)], writes=[("oTs", o_)])
        S.dma("st_o%d" % o_, lambda e: e.dma_start(out=io["oT"][i, h * 64:(h + 1) * 64, :], in_=oTs[o_][:, :]),
              reads=[("oTs", o_)], writes=[("oT", h, i)])

    def setup(i):
        xb = i % NXT
        qbuf = i % 2
        S.dma("ld_x%d" % xb, lambda e, i=i, xb=xb: e.dma_start(out=xt[xb][:, :, :], in_=io["hfull"][i * 512:(i + 1) * 512, :].rearrange("(s p) d -> p s d", p=128)),
              writes=[("xt", xb)])
        yield
        if not fox:
            yield from rope_tables(i)
        for s in range(4):
            b = nxt("hn", 2)
            hb = hn_s[b]
            src = xt[xb][:, s, :]
            S.op("act", lambda e, src=src: e.activation(out=junk[:, :], in_=src, func=AF.Square, accum_out=stat[:, 0:1]),
                 reads=[("xt", xb)], writes=["junk", "stat"])
            yield
            S.op("act", lambda e: e.activation(out=stat[:, 1:2], in_=stat[:, 0:1], func=AF.Ln, bias=EPS, scale=1.0 / D), reads=["stat"], writes=["stat"])
            S.op("act", lambda e: e.activation(out=stat[:, 2:3], in_=stat[:, 1:2], func=AF.Exp, scale=-0.5), reads=["stat"], writes=["stat"])
            yield
            S.op("dve", lambda e, src=src, hb=hb: e.scalar_tensor_tensor(out=hb[:, :], in0=src, scalar=stat[:, 2:3], in1=gmix[:, :], op0=ALU.mult, op1=ALU.mult),
                 reads=[("xt", xb), "stat", "gmix"], writes=[("hn_s", b)])
            yield
            yield
            for c in range(8):
                S.op("pe", lambda e, c=c, hb=hb: e.transpose(trp[0][:, c * 128:(c + 1) * 128], hb[:, c * 128:(c + 1) * 128], ident[:, :]),
                     reads=[("hn_s", b), "ident"], writes=["trp"])
            yield
            yield
            S.op("dve", lambda e, s=s: e.tensor_copy(out=hnT[:, :, s * 128:(s + 1) * 128], in_=trp[0][:].rearrange("p (c t) -> p c t", c=8)),
                 reads=["trp"], writes=["hnT"])
            yield
        for s in range(4):
            a = nxt("pacc", NPACC)
            for c in range(8):
                S.op("pe", lambda e, a=a, c=c, s=s: e.matmul(pacc[a][:, 0:128], lhsT=hnT[:, c, s * 128:(s + 1) * 128], rhs=w_s[:, c, VV:VV + 128], start=(c == 0), stop=(c == 7)),
                     reads=["w_s", "hnT"], writes=[("pacc", a)])
            yield
            yield
            for h in range(2):
                S.op("dve", lambda e, a=a, h=h, s=s, i=i: e.tensor_copy(out=vX[h][:, 4 * i + s, 0:64], in_=pacc[a][:, h * 64:(h + 1) * 64]),
                     reads=[("pacc", a)], writes=[("vX", h, i)])
            yield
        if fox:
            a = nxt("pacc", NPACC)
            for c in range(8):
                S.op("pe", lambda e, a=a, c=c: e.matmul(pacc[a][0:2, :], lhsT=w_s[:, c, FF:FF + 2], rhs=hnT[:, c, :], start=(c == 0), stop=(c == 7)),
                     reads=["w_s", "hnT"], writes=[("pacc", a)])
            yield
            yield
            S.op("act", lambda e, a=a: e.activation(out=fE[:, :], in_=pacc[a][0:2, :], func=AF.Exp, bias=nbf[:, 0:1], scale=-1.0),
                 reads=[("pacc", a), "nbf"], writes=["fE"])
            yield
            S.op("act", lambda e: e.activation(out=fE[:, :], in_=fE[:, :], func=AF.Ln, bias=1.0, scale=1.0), reads=["fE"], writes=["fE"])
            yield
            S.op("dve", lambda e: e.tensor_scalar(out=fE[:, :], in0=fE[:, :], scalar1=-1.0, scalar2=None, op0=ALU.mult), reads=["fE"], writes=["fE"])
            S.op("dve", lambda e: e.tensor_tensor_scan(out=cc[:, :], data0=fE[:, :], data1=fE[:, :], initial=cprev[:, 0:1], op0=ALU.add, op1=ALU.bypass),
                 reads=["fE", "cprev"], writes=["cc"])
            S.op("dve", lambda e: e.tensor_copy(out=cprev[:, :], in_=cc[:, 511:512]), reads=["cc"], writes=["cprev"])
            S.op("dve", lambda e: e.tensor_copy(out=rbf[:, :], in_=cc[:, :]), reads=["cc"], writes=["rbf"])
            yield
            a = nxt("pacc", NPACC)
            for s in range(4):
                S.op("pe", lambda e, a=a, s=s: e.transpose(pacc[a][:, 2 * s:2 * s + 2], cc[0:2, s * 128:(s + 1) * 128], ident_f[0:2, 0:2]),
                     reads=["cc", "ident_f"], writes=[("pacc", a)])
            yield
            yield
            S.op("dve", lambda e, a=a, i=i: e.tensor_scalar(out=negc[:, 4 * i:4 * i + 4, :], in0=pacc[a][:, 0:8].rearrange("p (s h) -> p s h", h=2),
                                                            scalar1=-1.0, scalar2=None, op0=ALU.mult),
                 reads=[("pacc", a)], writes=[("negc", i)])
            yield
            for h in range(2):
                qc, kc = (QA, KA) if h == 0 else (QB, KB)
                a = proj64(qc, i)
                yield
                yield
                S.op("act", lambda e, a=a, h=h, qbuf=qbuf: e.activation(out=qT[h][qbuf][0:64, :], in_=pacc[a][0:64, :], func=AF.Copy, scale=0.125),
                     reads=[("pacc", a)], writes=[("qT", h, qbuf)])
                S.dma("ld_r%d" % h, lambda e, h=h, qbuf=qbuf: e.dma_start(out=qT[h][qbuf][64:65, :], in_=rbf[h:h + 1, :]), reads=["rbf", ("qT", h, qbuf)], writes=[("qT", h, qbuf)])
                yield
                a = proj64(kc, i)
                yield
                yield
                S.op("act", lambda e, a=a, h=h, i=i: e.copy(out=kT[h][0:64, i * 512:(i + 1) * 512], in_=pacc[a][0:64, :]),
                     reads=[("pacc", a)], writes=[("kT", h, i)])
                yield
        else:
            a = proj64(QA, i)
            yield
            yield
            S.op("act", lambda e, a=a, qbuf=qbuf: e.activation(out=qT[0][qbuf][0:64, :], in_=pacc[a][0:64, :], func=AF.Copy, scale=0.125),
                 reads=[("pacc", a)], writes=[("qT", 0, qbuf)])
            yield
            a = proj64(KA, i)
            yield
            yield
            S.op("act", lambda e, a=a, i=i: e.copy(out=kT[0][0:64, i * 512:(i + 1) * 512], in_=pacc[a][0:64, :]), reads=[("pacc", a)], writes=[("kT", 0, i)])
            yield
            yield from rope_proj(QB, QP, i, qrot, "qrot")
            S.op("act", lambda e, qbuf=qbuf: e.activation(out=qT[1][qbuf][0:64, :], in_=qrot[:, :], func=AF.Copy, scale=0.125),
                 reads=["qrot"], writes=[("qT", 1, qbuf)])
            yield
            yield from rope_proj(KB, KP, i, krot, "krot")
            S.op("act", lambda e, i=i: e.copy(out=kT[1][0:64, i * 512:(i + 1) * 512], in_=krot[:, :]), reads=["krot"], writes=[("kT", 1, i)])
            S.op("dve", lambda e, i=i: e.tensor_reduce(out=kmT[:, 2 * i:2 * i + 2], in_=krot[:, :].rearrange("p (n k) -> p n k", n=2), axis=AX.X, op=ALU.add),
                 reads=["krot", "kmT"], writes=["kmT"])
            yield
            S.op("dve", lambda e, i=i: e.tensor_scalar(out=kmT[:, 2 * i:2 * i + 2], in0=kmT[:, 2 * i:2 * i + 2], scalar1=1.0 / 256, scalar2=None, op0=ALU.mult),
                 reads=["kmT"], writes=["kmT"])
            yield
            yield from moba_gate(i, qbuf)

    def roundrobin(*gens):
        gens = [g for g in gens if g is not None]
        while gens:
            for g in list(gens):
                try:
                    next(g)
                except StopIteration:
                    gens.remove(g)
            yield

    for _ in setup(0):
        pass
    for i in range(ntile):
        qbuf = i % 2
        nx = setup(i + 1) if i + 1 < ntile else None
        if fox:
            for h in range(2):
                attend(i, h, list(range(4 * i + 4)), 65, lambda blk, h=h: (negc[:, blk, h:h + 1], [("negc", blk // 4)]), qbuf,
                       side=nx, drain=(h == 1))
        else:
            attend(i, 1, list(range(4 * i + 4)), 128, lambda blk: (0.0, []), qbuf, a=1, side=roundrobin(sb_attend(i, qbuf), nx))
    if "dbg_negc" in io:
        S.dma("dbg", lambda e: [e.dma_start(out=io["dbg_negc"], in_=negc[:].rearrange("p b h -> p (b h)")),
                                e.dma_start(out=io["dbg_q"], in_=qT[0][(ntile - 1) % 2][:, :]),
                                e.dma_start(out=io["dbg_k"], in_=kT[0][:, 0:1024]),
                                e.dma_start(out=io["dbg_cc"], in_=cc[:, :])],
              reads=[("negc", i_) for i_ in range(ntile)] + [("qT", 0, (ntile - 1) % 2), ("kT", 0, 0), ("kT", 0, 1), "cc"], n=4)


def _masks():
    s_ = np.arange(128)[:, None, None]
    j_ = np.arange(4)[None, :, None]
    t_ = np.arange(512)[None, None, :]
    mi = ((128 * j_ + s_) <= t_).astype(np.float32)
    ms = ((128 * j_ + s_) < t_).astype(np.float32)
    return mi, ms


REST_W = ("w_out", "wq", "wkv", "wxo", "wup", "wdn", "cw", "cb", "g_xa", "g_mem", "g_ffn")
REST_SHAPES = {"w_out": [128, 8, 1024], "wq": [128, 8, 256], "wkv": [128, 8, 512], "wxo": [64, 4, 1024],
               "wup": [NPAIR, 128, 8 * 256], "wdn": [NPAIR, 2, 128, 512], "cw": [128, 44, 3], "cb": [128, 44],
               "g_xa": [D], "g_mem": [D], "g_ffn": [D]}


def build_fused(phases="AgBhCiD"):
    nc = bass.Bass("TRN2", target_bir_lowering=False, num_devices=NCORES)
    din = lambda name, shape, dt: nc.dram_tensor(name, list(shape), dt, kind="ExternalInput").ap()
    dint = lambda name, shape, dt: nc.dram_tensor(name, list(shape), dt, kind="Internal").ap()
    x = din("x", [SEQ, D], F32)
    hin0 = din("hin0", [NT + 128, D], F32)
    flag = din("flag", [128, 1], F32)
    hidx = din("hidx", [128, 17], I32)
    oidx = din("oidx", [128, 5, 8], I32)
    ident = din("ident", [128, 128], F32)
    maskI = din("maskI", [128, 4, 512], F32)
    maskS = din("maskS", [128, 4, 512], F32)
    pos = din("pos", [SEQ], I32)
    invf = din("invf", [64, 1], F32)
    sgn = din("sgn", [64, 1], F32)
    koh = din("koh", [64, SEQ], F32)
    tri = din("tri", [128, 256], F32)
    mem = din("mem", [MEM, D], F32)
    a_w = din("a_w", [128, 8, 512], F32)
    a_g = din("a_g", [D], F32)
    c_w = din("c_w", [128, 8, 386], F32)
    c_g = din("c_g", [D], F32)
    c_bf = din("c_bf", [2, 1], F32)
    g_fin = din("g_fin", [D], F32)
    rw = [{k: din("r%d_%s" % (L, k), REST_SHAPES[k], F32) for k in REST_W} for L in range(2)]
    out = nc.dram_tensor("out", [NT, D], F32, kind="ExternalOutput").ap()
    o_src = [dint("o_src%d" % L, [NTILE * 128, 512], BF16) for L in range(2)]
    o_all = [dint("o_all%d" % L, [NCORES * NTILE * 128, 512], BF16) for L in range(2)]
    h1_src = dint("h1_src", [NT, D], F32)
    h1_all = dint("h1_all", [SEQ, D], F32)

    def phase(tag, body, waits):
        with nc.cleanup_on_exit():
            with contextlib.ExitStack() as st:
                cx = Ctx(nc, st, tag)
                body(cx)
                cx.S.emit(final_wait_streams=waits)
            nc.all_engine_barrier()

    def gather(tag, src, dst):
        phase(tag, lambda cx: cx.S.cc("ag", lambda e: nc.gpsimd.collective_compute(
            "AllGather", ALU.bypass, replica_groups=[list(range(NCORES))], ins=[src.opt()], outs=[dst.opt()])), ["ag"])

    def scrub(tag):
        def body(cx):
            big = cx.sb("big", [128, 50000], F32)
            pz = [cx.ps("pz%d" % k, [128, 2048], F32) for k in range(2)]
            cx.S.op("dve", lambda e: e.memset(big[:, 0:25000], 0.0), writes=["b0"])
            cx.S.op("pool", lambda e: e.memset(big[:, 25000:50000], 0.0), writes=["b1"])
            for k in range(2):
                cx.S.op("act", lambda e, k=k: e.copy(out=pz[k][:, :], in_=big[:, 0:2048]), reads=["b0"], writes=[("pz", k)])
        phase(tag, body, [])

    def rest_io(L, hin, hout, with_hidx):
        io = dict(rw[L])
        io.update({"hin": hin, "oall": o_all[L], "oidx": oidx, "flag": flag, "mem": mem, "ident": ident, "g_fin": g_fin, "hout": hout})
        if with_hidx:
            io["hidx"] = hidx
        return io

    _phase, _gather = phase, gather
    phase = lambda tag, body, waits: _phase(tag, body, waits) if (tag in phases or tag[0] in "SG") else None
    gather = lambda tag, src, dst: _gather(tag, src, dst) if {"G0": "g", "G1": "h", "G2": "i"}[tag] in phases else None
    phase("A", lambda cx: emit_mixer(cx, {"hfull": x, "g_mix": a_g, "w": a_w, "ident": ident, "maskI": maskI, "maskS": maskS, "pos": pos,
                                          "invf": invf, "sgn": sgn, "koh": koh, "tri": tri,
                                          "oT": o_src[0].rearrange("(i f) t -> i f t", f=128)}, "ab"), ["st_o0", "st_o1"])
    if "X" in phases:
        dbg = nc.dram_tensor("dbg_o", [NTILE * 128, 512], BF16, kind="ExternalOutput").ap()
        _phase("X", lambda cx: cx.S.dma("cp", lambda e: e.dma_start(out=dbg, in_=o_src[0]), writes=["dbg"]), ["cp"])
    gather("G0", o_src[0], o_all[0])
    if "W" in phases:
        dbgw = nc.dram_tensor("dbg_oall", [NCORES * NTILE * 128, 512], BF16, kind="ExternalOutput").ap()
        _phase("W", lambda cx: cx.S.dma("cp", lambda e: e.dma_start(out=dbgw, in_=o_all[0]), writes=["dbg"]), ["cp"])
    if "s" in phases:
        scrub("S1")
    phase("B", lambda cx: emit_rest(cx, rest_io(0, hin0, h1_src, False), False), ["st_h"])
    if "Y" in phases:
        dbg1 = nc.dram_tensor("dbg_h1", [NT, D], F32, kind="ExternalOutput").ap()
        _phase("Y", lambda cx: cx.S.dma("cp", lambda e: e.dma_start(out=dbg1, in_=h1_src), writes=["dbg"]), ["cp"])
    gather("G1", h1_src, h1_all)
    phase("C", lambda cx: emit_mixer(cx, {"hfull": h1_all, "g_mix": c_g, "w": c_w, "ident": ident, "maskI": maskI, "bf": c_bf,
                                          "oT": o_src[1].rearrange("(i f) t -> i f t", f=128)}, "fox"), ["st_o0", "st_o1"])
    if "Z" in phases:
        dbg2 = nc.dram_tensor("dbg_o1", [NTILE * 128, 512], BF16, kind="ExternalOutput").ap()
        _phase("Z", lambda cx: cx.S.dma("cp", lambda e: e.dma_start(out=dbg2, in_=o_src[1]), writes=["dbg"]), ["cp"])
    gather("G2", o_src[1], o_all[1])
    phase("D", lambda cx: emit_rest(cx, rest_io(1, h1_all, out, True), True), ["st_h"])
    return nc


_CACHE = {}


def kernel(x, mem, positions, norm_mix_g, norm_xa_g, norm_mem_g, norm_ffn_g,
           ab_w_in, ab_w_out, fox_w_in, fox_b_f, fox_w_out,
           xa_w_q, xa_w_kv, xa_w_out, ffn_w_up, ffn_conv_w, ffn_conv_b, ffn_w_down,
           final_norm_g):
    a = lambda v: np.asarray(v)
    f32 = np.float32
    c_ = lambda v: np.ascontiguousarray(v, dtype=f32)
    x0 = c_(a(x)[0])
    mem_ = a(mem)
    w_in0, w_in1 = a(ab_w_in)[0], a(fox_w_in)[0]
    mi, ms = _masks()
    inv_freq = (10000.0 ** (-np.arange(32, dtype=f32) / 32)).astype(f32)
    invf = np.concatenate([inv_freq, inv_freq]).reshape(64, 1).astype(f32)
    sgn = np.concatenate([-np.ones(32), np.ones(32)]).reshape(64, 1).astype(f32)
    koh = np.zeros((64, SEQ), f32)
    for n in range(64):
        koh[n, n * 256:(n + 1) * 256] = 30000.0
    jj = np.arange(128)[:, None]
    ss = np.arange(128)[None, :]
    tri = np.concatenate([-(jj > ss).astype(f32), -(jj <= ss).astype(f32)], axis=1)
    perm = np.concatenate([np.arange(32, 64), np.arange(0, 32)])
    w_out0 = a(ab_w_out)[0]
    w_out0p = np.concatenate([np.concatenate([w_out0[r * 64:(r + 1) * 64], w_out0[(8 + r) * 64:(9 + r) * 64]], axis=0) for r in range(8)], axis=0)
    rws = []
    for L, wo in ((0, w_out0p), (1, a(fox_w_out)[0])):
        rws.append(rest_weights(L, wo, a(xa_w_q), a(xa_w_kv), a(xa_w_out), a(ffn_w_up), a(ffn_conv_w), a(ffn_conv_b),
                                a(ffn_w_down), a(norm_xa_g), a(norm_mem_g), a(norm_ffn_g), a(final_norm_g), mem_))
    common = {
        "x": x0, "ident": np.eye(128, dtype=f32), "maskI": c_((mi - 1.0) * 30000.0), "maskS": c_(ms),
        "pos": np.ascontiguousarray(a(positions).reshape(-1), dtype=np.int32), "invf": invf, "sgn": sgn, "koh": koh, "tri": tri,
        "mem": c_(mem_[0]), "a_g": c_(a(norm_mix_g)[0]), "c_g": c_(a(norm_mix_g)[1]), "g_fin": c_(a(final_norm_g)),
    }
    for L in range(2):
        for k in REST_W:
            common["r%d_%s" % (L, k)] = rws[L][k]
    in_maps = []
    p_ = np.arange(128)
    for cid in range(NCORES):
        m = dict(common)
        t0 = cid * NT
        m["hin0"] = np.concatenate([np.zeros((128, D), f32), x0[0:NT]], axis=0) if cid == 0 else c_(x0[t0 - 128:t0 + NT])
        m["flag"] = np.full((128, 1), 0.0 if cid == 0 else 1.0, f32)
        m["hidx"] = np.maximum(t0 - 128 + np.arange(17)[None, :] * 128 + p_[:, None], 0).astype(np.int32)
        tiles = np.array([max(4 * cid - 1, 0)] + [4 * cid + k for k in range(4)])
        m["oidx"] = (np.arange(8)[None, None, :] * (NTILE * 128) + tiles[None, :, None] * 128 + p_[:, None, None]).astype(np.int32)
        hA, hB = cid, 8 + cid
        qB = w_in0[:, hB * 64:(hB + 1) * 64]
        kB = w_in0[:, D + hB * 64:D + (hB + 1) * 64]
        wa = np.concatenate([w_in0[:, hA * 64:(hA + 1) * 64], w_in0[:, D + hA * 64:D + (hA + 1) * 64], qB, kB,
                             w_in0[:, 2 * D + hA * 64:2 * D + (hA + 1) * 64], w_in0[:, 2 * D + hB * 64:2 * D + (hB + 1) * 64],
                             qB[:, perm], kB[:, perm]], axis=1)
        m["a_w"] = c_(wa.reshape(8, 128, -1).transpose(1, 0, 2))
        hA, hB = 2 * cid, 2 * cid + 1
        cols = []
        for hh in (hA, hB):
            cols.append(w_in1[:, hh * 64:(hh + 1) * 64])
            cols.append(w_in1[:, D + hh * 64:D + (hh + 1) * 64])
        cols += [w_in1[:, 2 * D + hA * 64:2 * D + (hA + 1) * 64], w_in1[:, 2 * D + hB * 64:2 * D + (hB + 1) * 64],
                 w_in1[:, 3 * D + hA:3 * D + hA + 1], w_in1[:, 3 * D + hB:3 * D + hB + 1]]
        m["c_w"] = c_(np.concatenate(cols, axis=1).reshape(8, 128, -1).transpose(1, 0, 2))
        m["c_bf"] = c_(a(fox_b_f)[0][[hA, hB]].reshape(2, 1))
        in_maps.append(m)
    if "nc" not in _CACHE:
        _CACHE["nc"] = build_fused()
    res = run_bass_kernel_spmd(_CACHE["nc"], in_maps, core_ids=list(range(NCORES)))
    full = np.concatenate([r["out"] for r in res.results], axis=0)
    return np.ascontiguousarray(full[None].astype(np.float32))
```

```python
import contextlib
import numpy as np
import ml_dtypes
import concourse.bass as bass
import concourse.mybir as mybir
from concourse.bass_utils import run_bass_kernel_spmd

F32 = mybir.dt.float32
BF16 = mybir.dt.bfloat16
I32 = mybir.dt.int32
AF = mybir.ActivationFunctionType
ALU = mybir.AluOpType
AX = mybir.AxisListType

NCORES = 8
D = 1024
SEQ = 16384
NT = SEQ // NCORES
DFF = 2816
NPAIR = DFF // 128
MEM = 256
EPS = 1e-6
ENGS = ("pe", "act", "dve", "pool", "sp")
SEM_EPOCH = 24000
DEBUG = False
LAST = None


class Op:
    __slots__ = ("eng", "fn", "deps", "inc", "cnt", "dma", "ndma", "dcnt")

    def __init__(self, eng, fn, dma=None, ndma=1):
        self.eng = eng
        self.fn = fn
        self.deps = set()
        self.inc = False
        self.cnt = 0
        self.dma = dma
        self.ndma = ndma
        self.dcnt = 0


class Sched:
    def __init__(self, nc, same_engine_sync=("act", "dve", "pool")):
        self.nc = nc
        self.ops = {e: [] for e in ENGS}
        self.last_w = {}
        self.readers = {}
        self.streams = {}
        self.cc_streams = set()
        self.same = set(same_engine_sync)
        self.persist_sems = False
        self.tag = ""

    def _add(self, op, reads, writes):
        for b in reads:
            w = self.last_w.get(b)
            if w is not None:
                op.deps.add(w)
        for b in writes:
            w = self.last_w.get(b)
            if w is not None:
                op.deps.add(w)
            for r in self.readers.get(b, ()):
                op.deps.add(r)
        op.deps.discard(op)
        for b in reads:
            self.readers.setdefault(b, []).append(op)
        for b in writes:
            self.last_w[b] = op
            self.readers[b] = []
        self.ops[op.eng].append(op)
        return op

    def op(self, eng, fn, reads=(), writes=()):
        return self._add(Op(eng, fn), reads, writes)

    def dma(self, stream, fn, reads=(), writes=(), eng="sp", n=1):
        op = Op(eng, fn, dma=stream, ndma=n)
        self.streams.setdefault(stream, []).append(op)
        return self._add(op, reads, writes)

    def cc(self, stream, fn, reads=(), writes=()):
        op = Op("pool", fn, dma=stream, ndma=1)
        self.cc_streams.add(stream)
        self.streams.setdefault(stream, []).append(op)
        return self._add(op, reads, writes)

    def emit(self, final_wait_streams=()):
        nc = self.nc
        for e in ENGS:
            for op in self.ops[e]:
                for d in list(op.deps):
                    if d.dma is not None:
                        continue
                    if d.eng == op.eng and op.dma is None and d.eng not in self.same:
                        op.deps.discard(d)
                        continue
                    d.inc = True
        nep = {}
        for e in ENGS:
            c = 0
            for op in self.ops[e]:
                if op.dma is None and op.inc:
                    c += 1
                op.cnt = c
            nep[e] = max(1, -(-c // SEM_EPOCH))
        total = {}
        for s, lst in self.streams.items():
            c = 0
            for op in lst:
                c += (1 if s in self.cc_streams else 16) * op.ndma
                op.dcnt = c
            total[s] = c
        with contextlib.ExitStack() as st:
            tag = self.tag
            if self.persist_sems:
                esem = {e: [nc.alloc_semaphore("s%s_%s%d" % (tag, e, k)) for k in range(nep[e])] for e in ENGS}
                ssem = {s: nc.alloc_semaphore("d%s_%s" % (tag, s)) for s in self.streams}
            else:
                esem = {e: [st.enter_context(nc.semaphore("s_%s%d" % (e, k))) for k in range(nep[e])] for e in ENGS}
                ssem = {s: st.enter_context(nc.semaphore("d_" + s)) for s in self.streams}
            block = st.enter_context(nc.Block())

            def run(e, eng_obj):
                seen = {}
                for op in self.ops[e]:
                    need = {}
                    for d in op.deps:
                        if d.dma is not None:
                            key, val = ("d", d.dma), d.dcnt
                        else:
                            key, val = ("e", d.eng), d.cnt
                        if val > need.get(key, 0):
                            need[key] = val
                    for key, val in need.items():
                        if seen.get(key, 0) >= val:
                            continue
                        seen[key] = val
                        if key[0] == "d":
                            eng_obj.wait_ge(ssem[key[1]], val)
                        else:
                            k = (val - 1) // SEM_EPOCH
                            eng_obj.wait_ge(esem[key[1]][k], val - k * SEM_EPOCH)
                    ins = op.fn(eng_obj)
                    if op.dma is not None:
                        if not isinstance(ins, (list, tuple)):
                            ins = [ins]
                        assert len(ins) == op.ndma, (len(ins), op.ndma)
                        for i_ in ins:
                            if op.dma in self.cc_streams:
                                i_.then_inc(ssem[op.dma])
                            else:
                                i_.then_inc(ssem[op.dma], 16)
                    elif op.inc:
                        ins.then_inc(esem[e][(op.cnt - 1) // SEM_EPOCH], 1)
                if e == "sp":
                    for s in final_wait_streams:
                        eng_obj.wait_ge(ssem[s], total[s])

            @block.tensor
            def _(eng):
                run("pe", eng)

            @block.scalar
            def _(eng):
                run("act", eng)

            @block.vector
            def _(eng):
                run("dve", eng)

            @block.gpsimd
            def _(eng):
                run("pool", eng)

            @block.sync
            def _(eng):
                run("sp", eng)


class Ctx:
    def __init__(self, nc, st, tag=""):
        self.nc = nc
        self.st = st
        self.S = Sched(nc)
        self.tag = tag
        if tag:
            self.S.persist_sems = True
            self.S.tag = tag

    def sb(self, name, shape, dt):
        return self.st.enter_context(self.nc.sbuf_tensor("sb%s_%s" % (self.tag, name), list(shape), dt))

    def ps(self, name, shape, dt):
        return self.st.enter_context(self.nc.psum_tensor("ps%s_%s" % (self.tag, name), list(shape), dt))

    def din(self, name, shape, dt):
        return self.nc.dram_tensor(name, list(shape), dt, kind="ExternalInput").ap()

    def dout(self, name, shape, dt):
        return self.nc.dram_tensor(name, list(shape), dt, kind="ExternalOutput").ap()

    def dint(self, name, shape, dt):
        return self.nc.dram_tensor("di%s_%s" % (self.tag, name), list(shape), dt, kind="Internal").ap()


def emit_rest(cx, io, final):
    nc, S = cx.nc, cx.S
    sb, ps = cx.sb, cx.ps
    NSUB = NT // 128
    NTT = NT // 512
    ident = sb("ident", [128, 128], BF16)
    ones_f = sb("ones_f", [128, 64], F32)
    gxa = sb("gxa", [128, D], F32)
    gffn = sb("gffn", [128, D], F32)
    gmem = sb("gmem", [128, D], F32)
    gfin = sb("gfin", [128, D], F32) if final else None
    cw = sb("cw", [128, 44, 3], F32)
    cb = sb("cb", [128, 44], F32)
    flag = sb("flag", [128, 1], F32)
    wo_s = sb("wo_s", [128, 8, 1024], BF16)
    wq_s = sb("wq_s", [128, 8, 256], BF16)
    wkv_s = sb("wkv_s", [128, 8, 512], BF16)
    wxo_s = sb("wxo_s", [64, 4, 1024], BF16)
    kxT = sb("kxT", [64, 4, MEM], BF16)
    vxa = sb("vxa", [128, 2, 4, 65], BF16)
    ucarry = sb("ucarry", [128, 44, 2], F32)
    wup_b = cx.dint("wup_b", [NPAIR, 128, 8 * 256], BF16)
    wdn_b = cx.dint("wdn_b", [NPAIR, 2, 128, 512], BF16)
    NUP = 5
    NDN = 8
    wup_r = [sb("wup_r%d" % i, [128, 8, 256], BF16) for i in range(NUP)]
    wdn_r = [sb("wdn_r%d" % i, [128, 512], BF16) for i in range(NDN)]
    hT = [sb("hT%d" % i, [128, 4, D], F32) for i in range(2)]
    oTs = [sb("oTs%d" % i, [128, 8, 512], BF16) for i in range(2)]
    hn_s = [sb("hn_s%d" % i, [128, D], BF16) for i in range(2)]
    junk = sb("junk", [128, D], BF16)
    stat = sb("stat", [128, 8], F32)
    hnT = sb("hnT", [128, 8, 512], BF16)
    qxT = sb("qxT", [64, 4, 512], BF16)
    pxT = [sb("pxT%d" % i, [128, 512], BF16) for i in range(4)]
    nm = sb("nm", [128, 512], F32)
    rd = sb("rd", [128, 512], F32)
    oxT = sb("oxT", [64, 4, 512], BF16)
    Yg = [sb("Yg%d" % i, [128, 512], F32) for i in range(2)]
    Yv = [sb("Yv%d" % i, [128, 512], F32) for i in range(2)]
    gT = sb("gT", [128, NPAIR, 512], BF16)
    trp = [ps("trp%d" % i, [128, 1024], BF16) for i in range(2)]
    acc = [ps("acc%d" % i, [128, 512], F32) for i in range(6)]
    rot = {"acc": 0, "tr": 0, "px": 0}

    def nacc():
        rot["acc"] = (rot["acc"] + 1) % 6
        return rot["acc"]

    def ntr():
        rot["tr"] = (rot["tr"] + 1) % 2
        return rot["tr"]

    def pdma(stream, out, in_, writes, reads=()):
        S.dma(stream, lambda e: nc.gpsimd.dma_start(out=out, in_=in_), reads=reads, writes=writes, eng="pool")

    pdma("c_id", ident[:], io["ident"], ["ident"])
    pdma("c_wo", wo_s[:], io["w_out"], ["wo_s"])
    pdma("c_wq", wq_s[:], io["wq"], ["wq_s"])
    pdma("c_wkv", wkv_s[:], io["wkv"], ["wkv_s"])
    pdma("c_wxo", wxo_s[:], io["wxo"], ["wxo_s"])
    S.dma("c_g", lambda e: [e.dma_start(out=gxa[:], in_=io["g_xa"].partition_broadcast(128)),
                            e.dma_start(out=gffn[:], in_=io["g_ffn"].partition_broadcast(128)),
                            e.dma_start(out=gmem[:], in_=io["g_mem"].partition_broadcast(128)),
                            e.dma_start(out=cw[:], in_=io["cw"]),
                            e.dma_start(out=cb[:], in_=io["cb"]),
                            e.dma_start(out=flag[:], in_=io["flag"])],
          writes=["gxa", "gffn", "gmem", "cw", "cb", "flag"], n=6)
    if final:
        S.dma("c_gf", lambda e: e.dma_start(out=gfin[:], in_=io["g_fin"].partition_broadcast(128)), writes=["gfin"])
    for g in range(4):
        js = list(range(g * 6, min(NPAIR, g * 6 + 6)))
        S.dma("c_up%d" % g, lambda e, js=js: [nc.gpsimd.dma_start(out=wup_b[j], in_=io["wup"][j]) for j in js],
              writes=[("wup_b", j) for j in js], eng="pool", n=len(js))
    for g in range(4):
        js = list(range(g * 6, min(NPAIR, g * 6 + 6)))
        S.dma("c_dn%d" % g, lambda e, js=js: [nc.gpsimd.dma_start(out=wdn_b[j], in_=io["wdn"][j]) for j in js],
              writes=[("wdn_b", j) for j in js], eng="pool", n=len(js))
    S.op("dve", lambda e: e.memset(ones_f[:], 1.0), writes=["ones_f"])
    S.op("dve", lambda e: e.memset(vxa[:], 1.0), writes=["vxa"])
    oidx = sb("oidx", [128, 5, 8], I32)
    hidx = sb("hidx", [128, 17], I32)
    S.dma("c_ix", lambda e: [e.dma_start(out=oidx[:], in_=io["oidx"])] + ([e.dma_start(out=hidx[:], in_=io["hidx"])] if "hidx" in io else []),
          writes=["oidx", "hidx"], n=(2 if "hidx" in io else 1))

    def rmsnorm_to_T(src_ap, src_key, g_tile, g_key, dstT, dst_key, col0, ncol=128, nrows=128):
        b = ntr()
        hb = hn_s[b]
        S.op("act", lambda e: e.activation(out=junk[:nrows, :], in_=src_ap, func=AF.Square, accum_out=stat[:nrows, 0:1]),
             reads=[src_key], writes=["junk", "stat"])
        S.op("act", lambda e: e.activation(out=stat[:nrows, 1:2], in_=stat[:nrows, 0:1], func=AF.Ln, bias=EPS, scale=1.0 / D),
             reads=["stat"], writes=["stat"])
        S.op("act", lambda e: e.activation(out=stat[:nrows, 2:3], in_=stat[:nrows, 1:2], func=AF.Exp, scale=-0.5),
             reads=["stat"], writes=["stat"])
        S.op("dve", lambda e: e.scalar_tensor_tensor(out=hb[:nrows, :], in0=src_ap, scalar=stat[:nrows, 2:3], in1=g_tile[:nrows, :],
                                                     op0=ALU.mult, op1=ALU.mult),
             reads=[src_key, "stat", g_key], writes=[("hn_s", b)])
        for c in range(8):
            S.op("pe", lambda e, c=c: e.transpose(trp[b][:, c * 128:c * 128 + nrows], hb[:nrows, c * 128:(c + 1) * 128], ident[:nrows, :nrows]),
                 reads=[("hn_s", b), "ident"], writes=[("trp", b)])
        S.op("act", lambda e: e.copy(out=dstT[:, :, col0:col0 + nrows],
                                     in_=trp[b][:].rearrange("p (c t) -> p c t", c=8)[:, :, 0:nrows]),
             reads=[("trp", b)], writes=[dst_key])

    memT = hnT
    mem_s = hT[1]
    S.dma("ld_mem", lambda e: e.dma_start(out=mem_s[:, 0:2, :], in_=io["mem"].rearrange("(s p) d -> p s d", p=128)),
          writes=[("hT", 1)])
    for s in range(2):
        rmsnorm_to_T(mem_s[:, s, :], ("hT", 1), gmem, "gmem", memT, "hnT", s * 128)
    for hd in range(4):
        a = nacc()
        for c in range(8):
            S.op("pe", lambda e, a=a, c=c, hd=hd: e.matmul(acc[a][0:64, 0:MEM], lhsT=wkv_s[:, c, hd * 64:(hd + 1) * 64], rhs=memT[:, c, 0:MEM],
                                                            start=(c == 0), stop=(c == 7)),
                 reads=["wkv_s", "hnT"], writes=[("acc", a)])
        S.op("act", lambda e, a=a, hd=hd: e.copy(out=kxT[:, hd, :], in_=acc[a][0:64, 0:MEM]), reads=[("acc", a)], writes=["kxT"])
    for mc in range(2):
        a = nacc()
        for c in range(8):
            S.op("pe", lambda e, a=a, c=c, mc=mc: e.matmul(acc[a][:, 0:256], lhsT=memT[:, c, mc * 128:(mc + 1) * 128], rhs=wkv_s[:, c, 256:512],
                                                            start=(c == 0), stop=(c == 7)),
                 reads=["wkv_s", "hnT"], writes=[("acc", a)])
        S.op("act", lambda e, a=a, mc=mc: e.copy(out=vxa[:, mc, :, 0:64], in_=acc[a][:, 0:256].rearrange("p (h d) -> p h d", h=4)),
             reads=[("acc", a)], writes=["vxa"])

    upq = {"n": 0}
    dnq = {"n": 0}

    def load_up(j):
        slot = upq["n"] % NUP
        upq["n"] += 1
        S.dma("r_up%d" % slot, lambda e: e.dma_start(out=wup_r[slot][:], in_=wup_b[j].rearrange("p (c n) -> p c n", c=8)),
              reads=[("wup_b", j)], writes=[("wup_r", slot)])
        return slot

    def load_dn(j, half):
        slot = dnq["n"] % NDN
        dnq["n"] += 1
        S.dma("r_dn%d" % slot, lambda e: e.dma_start(out=wdn_r[slot][:], in_=wdn_b[j, half]),
              reads=[("wdn_b", j)], writes=[("wdn_r", slot)])
        return slot

    def front(buf, nsub, row0):
        ntok = nsub * 128
        hb = hT[buf]
        k0 = row0 // 128
        t5 = 0 if row0 == 0 else 1 + (row0 - 128) // 512
        oc0 = 384 if row0 == 0 else 0
        if "hidx" in io:
            S.dma("ld_h%d" % buf, lambda e: [nc.gpsimd.indirect_dma_start(out=hb[:, s_, :], out_offset=None, in_=io["hin"],
                                                                        in_offset=bass.IndirectOffsetOnAxis(ap=hidx[:, k0 + s_:k0 + s_ + 1], axis=0))
                                             for s_ in range(nsub)],
                  reads=["hidx"], writes=[("hT", buf)], eng="pool", n=nsub)
        else:
            S.dma("ld_h%d" % buf, lambda e: e.dma_start(out=hb[:, 0:nsub, :], in_=io["hin"][row0:row0 + ntok, :].rearrange("(s p) d -> p s d", p=128)),
                  writes=[("hT", buf)])
        S.dma("ld_o%d" % buf, lambda e: [nc.gpsimd.indirect_dma_start(out=oTs[buf][:, r_, :], out_offset=None, in_=io["oall"],
                                                                    in_offset=bass.IndirectOffsetOnAxis(ap=oidx[:, t5, r_:r_ + 1], axis=0))
                                         for r_ in range(8)],
              reads=["oidx"], writes=[("oTs", buf)], eng="pool", n=8)
        for s in range(nsub):
            for half in range(2):
                a = nacc()
                for c in range(8):
                    S.op("pe", lambda e, a=a, c=c, s=s, half=half: e.matmul(acc[a][:, :], lhsT=oTs[buf][:, c, oc0 + s * 128:oc0 + (s + 1) * 128],
                                                                            rhs=wo_s[:, c, half * 512:(half + 1) * 512], start=(c == 0), stop=(c == 7)),
                         reads=[("oTs", buf), "wo_s"], writes=[("acc", a)])
                S.op("dve", lambda e, a=a, s=s, half=half: e.tensor_tensor(out=hb[:, s, half * 512:(half + 1) * 512], in0=hb[:, s, half * 512:(half + 1) * 512],
                                                                           in1=acc[a][:, :], op=ALU.add),
                     reads=[("acc", a), ("hT", buf)], writes=[("hT", buf)])
        for s in range(nsub):
            rmsnorm_to_T(hb[:, s, :], ("hT", buf), gxa, "gxa", hnT, "hnT", s * 128)
        for hd in range(4):
            a = nacc()
            for c in range(8):
                S.op("pe", lambda e, a=a, c=c, hd=hd: e.matmul(acc[a][0:64, 0:ntok], lhsT=wq_s[:, c, hd * 64:(hd + 1) * 64], rhs=hnT[:, c, 0:ntok],
                                                                start=(c == 0), stop=(c == 7)),
                     reads=["wq_s", "hnT"], writes=[("acc", a)])
            S.op("act", lambda e, a=a, hd=hd: e.copy(out=qxT[:, hd, 0:ntok], in_=acc[a][0:64, 0:ntok]), reads=[("acc", a)], writes=[("qxT", hd)])
        for hd in range(4):
            pk = []
            for mc in range(2):
                a = nacc()
                S.op("pe", lambda e, a=a, mc=mc, hd=hd: e.matmul(acc[a][:, 0:ntok], lhsT=kxT[:, hd, mc * 128:(mc + 1) * 128], rhs=qxT[:, hd, 0:ntok],
                                                                  start=True, stop=True),
                     reads=["kxT", ("qxT", hd)], writes=[("acc", a)])
                p = rot["px"] = (rot["px"] + 1) % 4
                S.op("act", lambda e, a=a, p=p: e.activation(out=pxT[p][:, 0:ntok], in_=acc[a][:, 0:ntok], func=AF.Exp, scale=0.125),
                     reads=[("acc", a)], writes=[("pxT", p)])
                pk.append(p)
            a = nacc()
            for mc in range(2):
                S.op("pe", lambda e, a=a, mc=mc, hd=hd, p=pk[mc]: e.matmul(acc[a][0:65, 0:ntok], lhsT=vxa[:, mc, hd, :], rhs=pxT[p][:, 0:ntok],
                                                                            start=(mc == 0), stop=(mc == 1)),
                     reads=["vxa", ("pxT", pk[mc])], writes=[("acc", a)])
            S.op("dve", lambda e, a=a: e.reciprocal(out=rd[64:65, 0:ntok], in_=acc[a][64:65, 0:ntok]), reads=[("acc", a)], writes=["rd"])
            S.op("act", lambda e, a=a: e.copy(out=nm[0:64, 0:ntok], in_=acc[a][0:64, 0:ntok]), reads=[("acc", a)], writes=["nm"])
            a2 = nacc()
            S.op("pe", lambda e, a2=a2: e.matmul(acc[a2][0:64, 0:ntok], lhsT=ones_f[64:65, 0:64], rhs=rd[64:65, 0:ntok], start=True, stop=True),
                 reads=["ones_f", "rd"], writes=[("acc", a2)])
            S.op("dve", lambda e, a2=a2, hd=hd: e.tensor_tensor(out=oxT[:, hd, 0:ntok], in0=nm[0:64, 0:ntok], in1=acc[a2][0:64, 0:ntok], op=ALU.mult),
                 reads=["nm", ("acc", a2)], writes=[("oxT", hd)])
        for s in range(nsub):
            for half in range(2):
                a = nacc()
                for hd in range(4):
                    S.op("pe", lambda e, a=a, hd=hd, s=s, half=half: e.matmul(acc[a][:, :], lhsT=oxT[:, hd, s * 128:(s + 1) * 128],
                                                                              rhs=wxo_s[:, hd, half * 512:(half + 1) * 512], start=(hd == 0), stop=(hd == 3)),
                         reads=[("oxT", hd), "wxo_s"], writes=[("acc", a)])
                S.op("dve", lambda e, a=a, s=s, half=half: e.tensor_tensor(out=hb[:, s, half * 512:(half + 1) * 512], in0=hb[:, s, half * 512:(half + 1) * 512],
                                                                           in1=acc[a][:, :], op=ALU.add),
                     reads=[("acc", a), ("hT", buf)], writes=[("hT", buf)])
        for s in range(nsub):
            rmsnorm_to_T(hb[:, s, :], ("hT", buf), gffn, "gffn", hnT, "hnT", s * 128)

    front(1, 1, 0)
    for j in range(NPAIR):
        slot = load_up(j)
        for gv in range(2):
            grp = j + gv * NPAIR
            a = nacc()
            for c in range(8):
                S.op("pe", lambda e, a=a, c=c, gv=gv, slot=slot: e.matmul(acc[a][:, 0:2], lhsT=wup_r[slot][:, c, gv * 128:(gv + 1) * 128], rhs=hnT[:, c, 126:128],
                                                                           start=(c == 0), stop=(c == 7)),
                     reads=[("wup_r", slot), "hnT"], writes=[("acc", a)])
            S.op("dve", lambda e, a=a, grp=grp: e.tensor_scalar(out=ucarry[:, grp, :], in0=acc[a][:, 0:2], scalar1=flag[:, 0:1], scalar2=None, op0=ALU.mult),
                 reads=[("acc", a), "flag"], writes=[("ucarry", grp)])

    for tt in range(NTT):
        buf = tt % 2
        hb = hT[buf]
        front(buf, 4, 128 + tt * 512)
        for j in range(NPAIR):
            slot = load_up(j)
            yb = j % 2
            aa = []
            for gv in range(2):
                a = nacc()
                aa.append(a)
                for c in range(8):
                    S.op("pe", lambda e, a=a, c=c, gv=gv, slot=slot: e.matmul(acc[a][:, :], lhsT=wup_r[slot][:, c, gv * 128:(gv + 1) * 128], rhs=hnT[:, c, :],
                                                                               start=(c == 0), stop=(c == 7)),
                         reads=[("wup_r", slot), "hnT"], writes=[("acc", a)])
            YY = [Yg[yb], Yv[yb]]
            yk = [("Y", 0, yb), ("Y", 1, yb)]
            gp = [j, j + NPAIR]
            for gv in range(2):
                S.op("act", lambda e, a=aa[gv], grp=gp[gv], Y=YY[gv]: e.activation(out=Y[:, :], in_=acc[a][:, :], func=AF.Identity, bias=cb[:, grp:grp + 1], scale=cw[:, grp, 2:3]),
                     reads=[("acc", aa[gv]), "cw", "cb"], writes=[yk[gv]])
            for gv in range(2):
                S.op("dve", lambda e, a=aa[gv], grp=gp[gv], Y=YY[gv]: e.scalar_tensor_tensor(out=Y[:, 1:512], in0=acc[a][:, 0:511], scalar=cw[:, grp, 1:2], in1=Y[:, 1:512],
                                                                                              op0=ALU.mult, op1=ALU.add),
                     reads=[("acc", aa[gv]), "cw", yk[gv]], writes=[yk[gv]])
            for gv in range(2):
                S.op("dve", lambda e, a=aa[gv], grp=gp[gv], Y=YY[gv]: e.scalar_tensor_tensor(out=Y[:, 2:512], in0=acc[a][:, 0:510], scalar=cw[:, grp, 0:1], in1=Y[:, 2:512],
                                                                                              op0=ALU.mult, op1=ALU.add),
                     reads=[("acc", aa[gv]), "cw", yk[gv]], writes=[yk[gv]])
            for gv in range(2):
                S.op("dve", lambda e, grp=gp[gv], Y=YY[gv]: e.scalar_tensor_tensor(out=Y[:, 0:1], in0=ucarry[:, grp, 1:2], scalar=cw[:, grp, 1:2], in1=Y[:, 0:1],
                                                                                   op0=ALU.mult, op1=ALU.add),
                     reads=[("ucarry", gp[gv]), "cw", yk[gv]], writes=[yk[gv]])
            for gv in range(2):
                S.op("dve", lambda e, grp=gp[gv], Y=YY[gv]: e.scalar_tensor_tensor(out=Y[:, 0:2], in0=ucarry[:, grp, 0:2], scalar=cw[:, grp, 0:1], in1=Y[:, 0:2],
                                                                                   op0=ALU.mult, op1=ALU.add),
                     reads=[("ucarry", gp[gv]), "cw", yk[gv]], writes=[yk[gv]])
            for gv in range(2):
                S.op("act", lambda e, a=aa[gv], grp=gp[gv]: e.copy(out=ucarry[:, grp, :], in_=acc[a][:, 510:512]),
                     reads=[("acc", aa[gv])], writes=[("ucarry", gp[gv])])
            S.op("act", lambda e, yb=yb: e.activation(out=Yg[yb][:, :], in_=Yg[yb][:, :], func=AF.Silu),
                 reads=[("Y", 0, yb)], writes=[("Y", 0, yb)])
            S.op("dve", lambda e, yb=yb, j=j: e.tensor_tensor(out=gT[:, j, :], in0=Yg[yb][:, :], in1=Yv[yb][:, :], op=ALU.mult),
                 reads=[("Y", 0, yb), ("Y", 1, yb)], writes=[("gT", j)])
        for half in range(2):
            accs = [nacc() for _ in range(4)]
            for j in range(NPAIR):
                slot = load_dn(j, half)
                for s in range(4):
                    a = accs[s]
                    S.op("pe", lambda e, a=a, j=j, s=s, slot=slot: e.matmul(acc[a][:, :], lhsT=gT[:, j, s * 128:(s + 1) * 128], rhs=wdn_r[slot][:, :],
                                                                             start=(j == 0), stop=(j == NPAIR - 1)),
                         reads=[("gT", j), ("wdn_r", slot)], writes=[("acc", a)])
            for s in range(4):
                a = accs[s]
                S.op("dve", lambda e, a=a, s=s, half=half, hb=hb: e.tensor_tensor(out=hb[:, s, half * 512:(half + 1) * 512], in0=hb[:, s, half * 512:(half + 1) * 512],
                                                                           in1=acc[a][:, :], op=ALU.add),
                     reads=[("acc", a), ("hT", buf)], writes=[("hT", buf)])
        if final:
            for s in range(4):
                S.op("act", lambda e, s=s, hb=hb: e.activation(out=junk[:, :], in_=hb[:, s, :], func=AF.Square, accum_out=stat[:, 4:5]),
                     reads=[("hT", buf)], writes=["junk", "stat2"])
                S.op("act", lambda e: e.activation(out=stat[:, 5:6], in_=stat[:, 4:5], func=AF.Ln, bias=EPS, scale=1.0 / D),
                     reads=["stat2"], writes=["stat2"])
                S.op("act", lambda e: e.activation(out=stat[:, 6:7], in_=stat[:, 5:6], func=AF.Exp, scale=-0.5),
                     reads=["stat2"], writes=["stat2"])
                S.op("dve", lambda e, s=s, hb=hb: e.scalar_tensor_tensor(out=hb[:, s, :], in0=hb[:, s, :], scalar=stat[:, 6:7], in1=gfin[:, :],
                                                                  op0=ALU.mult, op1=ALU.mult),
                     reads=[("hT", buf), "stat2", "gfin"], writes=[("hT", buf)])
        S.dma("st_h", lambda e, tt=tt, hb=hb: e.dma_start(out=io["hout"][tt * 512:(tt + 1) * 512, :].rearrange("(s p) d -> p s d", p=128), in_=hb[:, :, :]),
              reads=[("hT", buf)], writes=[("hout", tt)])


def rest_weights(layer, w_out, xa_w_q, xa_w_kv, xa_w_out, ffn_w_up, ffn_conv_w, ffn_conv_b, ffn_w_down,
                 norm_xa_g, norm_mem_g, norm_ffn_g, final_norm_g, mem):
    f = np.float32
    c = np.ascontiguousarray
    wup = ffn_w_up[layer].reshape(8, 128, 2, NPAIR, 128).transpose(3, 1, 0, 2, 4)
    return {
        "w_out": c(w_out.reshape(8, 128, 1024).transpose(1, 0, 2), dtype=f),
        "wq": c(xa_w_q[layer].reshape(8, 128, 256).transpose(1, 0, 2), dtype=f),
        "wkv": c(xa_w_kv[layer].reshape(8, 128, 512).transpose(1, 0, 2), dtype=f),
        "wxo": c(xa_w_out[layer].reshape(4, 64, 1024).transpose(1, 0, 2), dtype=f),
        "wup": c(wup.reshape(NPAIR, 128, 8 * 256), dtype=f),
        "wdn": c(ffn_w_down[layer].reshape(NPAIR, 128, 2, 512).transpose(0, 2, 1, 3), dtype=f),
        "cw": c(ffn_conv_w[layer].reshape(3, 44, 128).transpose(2, 1, 0), dtype=f),
        "cb": c(ffn_conv_b[layer].reshape(44, 128).T, dtype=f),
        "g_xa": c(norm_xa_g[layer], dtype=f),
        "g_mem": c(norm_mem_g[layer], dtype=f),
        "g_ffn": c(norm_ffn_g[layer], dtype=f),
        "g_fin": c(final_norm_g, dtype=f),
        "mem": c(mem[0], dtype=f),
        "ident": np.eye(128, dtype=f),
    }


NTILE = SEQ // 512
NBLK = SEQ // 128
SB_WIN = 3


def emit_mixer(cx, io, kind, ntile=NTILE):
    nc, S = cx.nc, cx.S
    sb, ps = cx.sb, cx.ps
    fox = kind == "fox"
    NCOL = 386 if fox else 512
    if fox:
        QA, KA, QB, KB, VV, FF = 0, 64, 128, 192, 256, 384
    else:
        QA, KA, QB, KB, VV, QP, KP = 0, 64, 128, 192, 256, 384, 448
    ident = sb("ident", [128, 128], BF16)
    ident_f = sb("ident_f", [128, 128], F32)
    ones_f = sb("ones_f", [128, 64], F32)
    gmix = sb("gmix", [128, D], F32)
    w_s = sb("w_s", [128, 8, NCOL], BF16)
    maskI = sb("maskI", [128, 4, 512], BF16)
    kT = [sb("kT%d" % h, [128, SEQ], BF16) for h in range(2)]
    vX = [sb("vX%d" % h, [128, NBLK, 65], BF16) for h in range(2)]
    qT = [[sb("qT%d_%d" % (h, i), [128, 512], BF16) for i in range(2)] for h in range(2)]
    NXT = 2 if fox else 1
    xt = [sb("xt%d" % i, [128, 4, D], F32) for i in range(NXT)]
    hn_s = [sb("hn_s%d" % i, [128, D], BF16) for i in range(2)]
    junk = sb("junk", [128, D], BF16)
    stat = sb("stat", [128, 8], F32)
    hnT = sb("hnT", [128, 8, 512], BF16)
    NPT = 6 if fox else 4
    pT = [sb("pT%d" % i, [128, 512], BF16) for i in range(6)]
    nm = sb("nm", [128, 512], F32)
    rd = sb("rd", [128, 512], F32)
    oTs = [sb("oTs%d" % i, [64, 512], BF16) for i in range(2)]
    trp = [ps("trp%d" % i, [128, 1024], BF16) for i in range(1)]
    NPACC = 2 if fox else 1
    pacc = [ps("pacc%d" % i, [128, 512], F32) for i in range(NPACC)]
    NSC = 3 if fox else 2
    sc = [ps("sc%d" % i, [128, 512], F32) for i in range(NSC)]
    av = [ps("av%d" % i, [128, 512], F32) for i in range(2)]
    rot = {"pacc": 0, "sc": 0, "pt": 0, "av": 0, "hn": 0, "ot": 0}

    def nxt(k, n):
        rot[k] = (rot[k] + 1) % n
        return rot[k]

    def pdma(stream, out, in_, writes):
        S.dma(stream, lambda e: nc.gpsimd.dma_start(out=out, in_=in_), writes=writes, eng="pool")

    pdma("c_id", ident[:], io["ident"], ["ident"])
    pdma("c_w", w_s[:], io["w"], ["w_s"])
    pdma("c_mi", maskI[:], io["maskI"], ["maskI"])
    S.dma("c_g", lambda e: [e.dma_start(out=gmix[:], in_=io["g_mix"].partition_broadcast(128)),
                            e.dma_start(out=ident_f[:], in_=io["ident"])], writes=["gmix", "ident_f"], n=2)
    S.op("dve", lambda e: e.memset(ones_f[:], 1.0), writes=["ones_f"])
    for h in range(2):
        S.op("pool", lambda e, h=h: e.memset(vX[h][:], 1.0), writes=[("vX", h, i_) for i_ in range(NTILE)])
    if fox:
        nbf = sb("nbf", [2, 1], F32)
        cprev = sb("cprev", [2, 1], F32)
        fE = sb("fE", [2, 512], F32)
        cc = sb("cc", [2, 512], F32)
        rbf = sb("rbf", [2, 512], BF16)
        negc = sb("negc", [128, NBLK, 2], F32)
        S.dma("c_bf", lambda e: e.dma_start(out=nbf[:], in_=io["bf"]), writes=["nbf"])
        S.op("dve", lambda e: e.tensor_scalar(out=nbf[:], in0=nbf[:], scalar1=-1.0, scalar2=None, op0=ALU.mult), reads=["nbf"], writes=["nbf"])
        S.op("dve", lambda e: e.memset(cprev[:], 0.0), writes=["cprev"])
        for h in range(2):
            S.op("pool", lambda e, h=h: e.memset(kT[h][64:128, :], 1.0), writes=[("kT", h, i_) for i_ in range(NTILE)])


    if not fox:
        maskS = sb("maskS", [128, 4, 512], BF16)
        tri = sb("tri", [128, 256], F32)
        invf = sb("invf", [64, 1], F32)
        sgn = sb("sgn", [64, 1], F32)
        pos_i = sb("pos_i", [64, 512], I32)
        ang = sb("ang", [64, 512], F32)
        kint = sb("kint", [64, 512], I32)
        kflt = sb("kflt", [64, 512], F32)
        msk = sb("msk", [64, 512], F32)
        cosF = sb("cosF", [64, 512], F32)
        sinS = sb("sinS", [64, 512], F32)
        rt1 = sb("rt1", [64, 512], F32)
        qrot = sb("qrot", [64, 512], F32)
        krot = sb("krot", [64, 512], F32)
        kmT = sb("kmT", [64, 64], F32)
        G = sb("G", [128, 64], F32)
        top8 = sb("top8", [128, 8], F32)
        MB = sb("MB", [128, 128], BF16)
        spE = [sb("spE%d" % k, [128, 512], F32) for k in range(2)]
        spM = [sb("spM%d" % k, [128, 512], F32) for k in range(2)]
        lgA = [sb("lgA%d" % k, [128, 512], F32) for k in range(2)]
        tailP = ps("tailP", [128, 512], F32)
        scS = ps("scS", [128, 512], F32)
        pdma("c_ms", maskS[:], io["maskS"], ["maskS"])
        S.dma("c_ab", lambda e: [e.dma_start(out=tri[:], in_=io["tri"]), e.dma_start(out=invf[:], in_=io["invf"]),
                                 e.dma_start(out=sgn[:], in_=io["sgn"])], writes=["tri", "invf", "sgn"], n=3)
        S.dma("c_koh", lambda e: nc.gpsimd.dma_start(out=kT[1][64:128, :], in_=io["koh"]), writes=[("kT", 1, i_) for i_ in range(NTILE)], eng="pool")
        S.op("dve", lambda e: e.memset(kmT[:], 0.0), writes=["kmT"])
        S.op("dve", lambda e: e.memset(MB[:], 0.0), writes=["MB"])

    def rope_tables(i):
        PI = float(np.pi)
        S.dma("ld_pos", lambda e: e.dma_start(out=pos_i[:], in_=io["pos"][i * 512:(i + 1) * 512].partition_broadcast(64)), writes=["pos_i"])
        S.op("dve", lambda e: e.tensor_copy(out=ang[:], in_=pos_i[:]), reads=["pos_i"], writes=["ang"])
        S.op("dve", lambda e: e.tensor_scalar(out=ang[:], in0=ang[:], scalar1=invf[:, 0:1], scalar2=None, op0=ALU.mult), reads=["ang", "invf"], writes=["ang"])
        S.op("dve", lambda e: e.tensor_scalar(out=kint[:], in0=ang[:], scalar1=float(1.0 / (2 * np.pi)), scalar2=None, op0=ALU.mult), reads=["ang"], writes=["kint"])
        S.op("dve", lambda e: e.tensor_copy(out=kflt[:], in_=kint[:]), reads=["kint"], writes=["kflt"])
        yield
        S.op("dve", lambda e: e.scalar_tensor_tensor(out=ang[:], in0=kflt[:], scalar=-6.28125, in1=ang[:], op0=ALU.mult, op1=ALU.add), reads=["kflt", "ang"], writes=["ang"])
        S.op("dve", lambda e: e.scalar_tensor_tensor(out=ang[:], in0=kflt[:], scalar=float(-(2 * np.pi - 6.28125)), in1=ang[:], op0=ALU.mult, op1=ALU.add),
             reads=["kflt", "ang"], writes=["ang"])
        S.op("dve", lambda e: e.tensor_scalar(out=msk[:], in0=ang[:], scalar1=PI, scalar2=-2 * PI, op0=ALU.is_gt, op1=ALU.mult), reads=["ang"], writes=["msk"])
        S.op("dve", lambda e: e.tensor_tensor(out=ang[:], in0=ang[:], in1=msk[:], op=ALU.add), reads=["ang", "msk"], writes=["ang"])
        S.op("dve", lambda e: e.tensor_scalar(out=msk[:], in0=ang[:], scalar1=-PI, scalar2=2 * PI, op0=ALU.is_lt, op1=ALU.mult), reads=["ang"], writes=["msk"])
        S.op("dve", lambda e: e.tensor_tensor(out=ang[:], in0=ang[:], in1=msk[:], op=ALU.add), reads=["ang", "msk"], writes=["ang"])
        yield
        S.op("dve", lambda e: e.tensor_scalar(out=rt1[:], in0=ang[:], scalar1=PI / 2, scalar2=None, op0=ALU.add), reads=["ang"], writes=["rt1"])
        S.op("dve", lambda e: e.tensor_scalar(out=msk[:], in0=rt1[:], scalar1=PI, scalar2=-2 * PI, op0=ALU.is_gt, op1=ALU.mult), reads=["rt1"], writes=["msk"])
        S.op("dve", lambda e: e.tensor_tensor(out=rt1[:], in0=rt1[:], in1=msk[:], op=ALU.add), reads=["rt1", "msk"], writes=["rt1"])
        S.op("act", lambda e: e.activation(out=sinS[:], in_=ang[:], func=AF.Sin, scale=sgn[:, 0:1]), reads=["ang", "sgn"], writes=["sinS"])
        yield
        S.op("act", lambda e: e.activation(out=cosF[:], in_=rt1[:], func=AF.Sin), reads=["rt1"], writes=["cosF"])
        yield

    def rope_proj(col_main, col_perm, i, dst, dkey):
        a_main = proj64(col_main, i)
        yield
        yield
        S.op("dve", lambda e: e.tensor_tensor(out=rt1[:], in0=pacc[a_main][0:64, :], in1=cosF[:], op=ALU.mult), reads=[("pacc", a_main), "cosF"], writes=["rt1"])
        yield
        a_perm = proj64(col_perm, i)
        yield
        yield
        S.op("dve", lambda e: e.tensor_tensor(out=dst[:], in0=pacc[a_perm][0:64, :], in1=sinS[:], op=ALU.mult), reads=[("pacc", a_perm), "sinS"], writes=[dkey])
        yield
        S.op("dve", lambda e: e.tensor_tensor(out=dst[:], in0=dst[:], in1=rt1[:], op=ALU.add), reads=[dkey, "rt1"], writes=[dkey])
        yield

    def moba_gate(i, qbuf):
        for s in range(4):
            own = 2 * i + s // 2
            if own > 0:
                a = nxt("pacc", NPACC)
                S.op("pe", lambda e, a=a, s=s: e.matmul(pacc[a][:, 0:64], lhsT=qrot[0:64, s * 128:(s + 1) * 128], rhs=kmT[0:64, 0:64], start=True, stop=True),
                     reads=["qrot", "kmT"], writes=[("pacc", a)])
                yield
                S.op("dve", lambda e: e.memset(G[:], -1e9), writes=["G"])
                S.op("dve", lambda e, a=a, own=own: e.tensor_copy(out=G[:, 0:own], in_=pacc[a][:, 0:own]), reads=[("pacc", a), "G"], writes=["G"])
                yield
                S.op("dve", lambda e: e.max(out=top8[:], in_=G[:]), reads=["G"], writes=["top8"])
                S.op("dve", lambda e: e.tensor_scalar(out=top8[:, 2:3], in0=top8[:, 2:3], scalar1=-1e8, scalar2=None, op0=ALU.max), reads=["top8"], writes=["top8"])
                S.op("dve", lambda e: e.tensor_scalar(out=MB[:, 64:128], in0=G[:], scalar1=top8[:, 2:3], scalar2=-1.0, op0=ALU.is_ge, op1=ALU.add),
                     reads=["G", "top8"], writes=["MB"])
            S.op("dve", lambda e, own=own: e.memset(MB[:, 64 + own:128], 0.0), reads=["MB"], writes=["MB"])
            yield
            S.op("pe", lambda e: e.transpose(trp[0][:, 0:128], MB[:, :], ident[:, :]), reads=["MB", "ident"], writes=["trp"])
            yield
            yield
            S.op("act", lambda e, s=s: e.copy(out=qT[1][qbuf][64:128, s * 128:(s + 1) * 128], in_=trp[0][64:128, 0:128]),
                 reads=["trp"], writes=[("qT", 1, qbuf)])
            yield

    def sb_attend(i, qbuf):
        a = 0
        lo = max(0, 4 * i - SB_WIN)
        blocks = list(range(4 * i + 3, lo - 1, -1))
        for n, blk in enumerate(blocks):
            j = blk - 4 * i
            p_ = 4 + n % 2
            b2 = n % 2
            S.op("pe", lambda e, blk=blk: e.matmul(scS[:, :], lhsT=kT[0][0:64, blk * 128:(blk + 1) * 128], rhs=qT[0][qbuf][0:64, :], start=True, stop=True),
                 reads=[("kT", 0, blk // 4), ("qT", 0, qbuf)], writes=["scS"])
            yield
            S.op("act", lambda e, b2=b2: e.activation(out=spE[b2][:, :], in_=scS[:, :], func=AF.Exp), reads=["scS"], writes=[("spE", b2)])
            yield
            S.op("act", lambda e, b2=b2: e.activation(out=spE[b2][:, :], in_=spE[b2][:, :], func=AF.Ln, bias=1.0, scale=1.0), reads=[("spE", b2)], writes=[("spE", b2)])
            yield
            if j >= 0:
                S.op("dve", lambda e, b2=b2, j=j: e.tensor_tensor(out=spM[b2][:, :], in0=spE[b2][:, :], in1=maskS[:, j, :], op=ALU.mult),
                     reads=[("spE", b2), "maskS"], writes=[("spM", b2)])
                src, skey = spM[b2], ("spM", b2)
            else:
                src, skey = spE[b2], ("spE", b2)
            S.op("dve", lambda e, b2=b2: e.tensor_tensor(out=lgA[b2][:, :], in0=scS[:, :], in1=spE[b2][:, :], op=ALU.subtract),
                 reads=["scS", ("spE", b2)], writes=[("lgA", b2)])
            yield
            S.op("pe", lambda e, src=src, n=n: e.matmul(tailP[:, :], lhsT=tri[:, 0:128], rhs=src[:, :], start=(n == 0), stop=True),
                 reads=["tri", skey], writes=["tailP"])
            yield
            S.op("dve", lambda e, b2=b2: e.tensor_tensor(out=lgA[b2][:, :], in0=lgA[b2][:, :], in1=tailP[:, :], op=ALU.add),
                 reads=[("lgA", b2), "tailP"], writes=[("lgA", b2)])
            yield
            S.op("pe", lambda e, src=src: e.matmul(tailP[:, :], lhsT=tri[:, 128:256], rhs=src[:, :], start=False, stop=True),
                 reads=["tri", skey], writes=["tailP"])
            S.op("act", lambda e, p_=p_, b2=b2: e.activation(out=pT[p_][:, :], in_=lgA[b2][:, :], func=AF.Exp), reads=[("lgA", b2)], writes=[("pT", p_)])
            yield
            if j >= 0:
                S.op("dve", lambda e, p_=p_, j=j: e.tensor_tensor(out=pT[p_][:, :], in0=pT[p_][:, :], in1=maskS[:, j, :], op=ALU.mult),
                     reads=[("pT", p_), "maskS"], writes=[("pT", p_)])
                yield
            S.op("pe", lambda e, p_=p_, blk=blk, n=n: e.matmul(av[a][0:64, :], lhsT=vX[0][:, blk, 0:64], rhs=pT[p_][:, :], start=(n == 0), stop=(n == len(blocks) - 1)),
                 reads=[("vX", 0, blk // 4), ("pT", p_)], writes=[("av", a)])
            yield
        finalize(i, 0, a, False)

    def proj64(col, i, nrows=64):
        a = nxt("pacc", NPACC)
        for c in range(8):
            S.op("pe", lambda e, a=a, c=c: e.matmul(pacc[a][0:nrows, :], lhsT=w_s[:, c, col:col + nrows], rhs=hnT[:, c, :], start=(c == 0), stop=(c == 7)),
                 reads=["w_s", "hnT"], writes=[("pacc", a)])
        return a

    LOOK = 2

    def attend(i, h, blocks, krows, bias_fn, qbuf, a=None, side=None, drain=True):
        if a is None:
            a = nxt("av", 2)
        nb = len(blocks)
        pbuf = [None] * nb
        for n in range(nb + LOOK):
            if n < nb:
                blk = blocks[n]
                s_ = nxt("sc", NSC)
                p_ = nxt("pt", NPT)
                pbuf[n] = p_
                j = blk - 4 * i
                S.op("pe", lambda e, s_=s_, blk=blk, j=j: e.matmul(sc[s_][:, :], lhsT=kT[h][0:krows, blk * 128:(blk + 1) * 128], rhs=qT[h][qbuf][0:krows, :],
                                                                   start=True, stop=(j < 0)),
                     reads=[("kT", h, blk // 4), ("qT", h, qbuf)], writes=[("sc", s_)])
                if j >= 0:
                    S.op("pe", lambda e, s_=s_, j=j: e.matmul(sc[s_][:, :], lhsT=ident[:, :], rhs=maskI[:, j, :], start=False, stop=True),
                         reads=["ident", "maskI"], writes=[("sc", s_)])
                b_ap, b_key = bias_fn(blk)
                S.op("act", lambda e, s_=s_, p_=p_, b_ap=b_ap: e.activation(out=pT[p_][:, :], in_=sc[s_][:, :], func=AF.Exp, bias=b_ap, scale=1.0),
                     reads=[("sc", s_)] + b_key, writes=[("pT", p_)])
            m = n - LOOK
            if m >= 0:
                blk = blocks[m]
                p_ = pbuf[m]
                S.op("pe", lambda e, p_=p_, blk=blk, m=m: e.matmul(av[a][0:65, :], lhsT=vX[h][:, blk, :], rhs=pT[p_][:, :], start=(m == 0), stop=(m == nb - 1)),
                     reads=[("vX", h, blk // 4), ("pT", p_)], writes=[("av", a)])
            if side is not None:
                next(side, None)
        if side is not None and drain:
            for _ in side:
                pass
        finalize(i, h, a, True)

    def finalize(i, h, a, normalize):
        o_ = nxt("ot", 2)
        if normalize:
            S.op("dve", lambda e: e.reciprocal(out=rd[64:65, :], in_=av[a][64:65, :]), reads=[("av", a)], writes=["rd"])
            S.op("act", lambda e: e.copy(out=nm[0:64, :], in_=av[a][0:64, :]), reads=[("av", a)], writes=["nm"])
            a2 = nxt("pacc", NPACC)
            S.op("pe", lambda e: e.matmul(pacc[a2][0:64, :], lhsT=ones_f[64:65, 0:64], rhs=rd[64:65, :], start=True, stop=True),
                 reads=["ones_f", "rd"], writes=[("pacc", a2)])
            S.op("dve", lambda e: e.tensor_tensor(out=oTs[o_][:, :], in0=nm[0:64, :], in1=pacc[a2][0:64, :], op=ALU.mult),
                 reads=["nm", ("pacc", a2)], writes=[("oTs", o_)])
        else:
            S.op("act", lambda e: e.copy(out=oTs[o_][:, :], in_=av[a][0:64, :]), reads=[("av", a)], writes=[("oTs", o_)])
        S.dma("st_o%d" % o_, lambda e: e.dma_start(out=io["oT"][i, h * 64:(h + 1) * 64, :], in_=oTs[o_][:, :]),
              reads=[("oTs", o_)], writes=[("oT", h, i)])

    def setup(i):
        xb = i % NXT
        qbuf = i % 2
        S.dma("ld_x%d" % xb, lambda e, i=i, xb=xb: e.dma_start(out=xt[xb][:, :, :], in_=io["hfull"][i * 512:(i + 1) * 512, :].rearrange("(s p) d -> p s d", p=128)),
              writes=[("xt", xb)])
        yield
        if not fox:
            yield from rope_tables(i)
        for s in range(4):
            b = nxt("hn", 2)
            hb = hn_s[b]
            src = xt[xb][:, s, :]
            S.op("act", lambda e, src=src: e.activation(out=junk[:, :], in_=src, func=AF.Square, accum_out=stat[:, 0:1]),
                 reads=[("xt", xb)], writes=["junk", "stat"])
            yield
            S.op("act", lambda e: e.activation(out=stat[:, 1:2], in_=stat[:, 0:1], func=AF.Ln, bias=EPS, scale=1.0 / D), reads=["stat"], writes=["stat"])
            S.op("act", lambda e: e.activation(out=stat[:, 2:3], in_=stat[:, 1:2], func=AF.Exp, scale=-0.5), reads=["stat"], writes=["stat"])
            yield
            S.op("dve", lambda e, src=src, hb=hb: e.scalar_tensor_tensor(out=hb[:, :], in0=src, scalar=stat[:, 2:3], in1=gmix[:, :], op0=ALU.mult, op1=ALU.mult),
                 reads=[("xt", xb), "stat", "gmix"], writes=[("hn_s", b)])
            yield
            yield
            for c in range(8):
                S.op("pe", lambda e, c=c, hb=hb: e.transpose(trp[0][:, c * 128:(c + 1) * 128], hb[:, c * 128:(c + 1) * 128], ident[:, :]),
                     reads=[("hn_s", b), "ident"], writes=["trp"])
            yield
            yield
            S.op("dve", lambda e, s=s: e.tensor_copy(out=hnT[:, :, s * 128:(s + 1) * 128], in_=trp[0][:].rearrange("p (c t) -> p c t", c=8)),
                 reads=["trp"], writes=["hnT"])
            yield
        for s in range(4):
            a = nxt("pacc", NPACC)
            for c in range(8):
                S.op("pe", lambda e, a=a, c=c, s=s: e.matmul(pacc[a][:, 0:128], lhsT=hnT[:, c, s * 128:(s + 1) * 128], rhs=w_s[:, c, VV:VV + 128], start=(c == 0), stop=(c == 7)),
                     reads=["w_s", "hnT"], writes=[("pacc", a)])
            yield
            yield
            for h in range(2):
                S.op("dve", lambda e, a=a, h=h, s=s, i=i: e.tensor_copy(out=vX[h][:, 4 * i + s, 0:64], in_=pacc[a][:, h * 64:(h + 1) * 64]),
                     reads=[("pacc", a)], writes=[("vX", h, i)])
            yield
        if fox:
            a = nxt("pacc", NPACC)
            for c in range(8):
                S.op("pe", lambda e, a=a, c=c: e.matmul(pacc[a][0:2, :], lhsT=w_s[:, c, FF:FF + 2], rhs=hnT[:, c, :], start=(c == 0), stop=(c == 7)),
                     reads=["w_s", "hnT"], writes=[("pacc", a)])
            yield
            yield
            S.op("act", lambda e, a=a: e.activation(out=fE[:, :], in_=pacc[a][0:2, :], func=AF.Exp, bias=nbf[:, 0:1], scale=-1.0),
                 reads=[("pacc", a), "nbf"], writes=["fE"])
            yield
            S.op("act", lambda e: e.activation(out=fE[:, :], in_=fE[:, :], func=AF.Ln, bias=1.0, scale=1.0), reads=["fE"], writes=["fE"])
            yield
            S.op("dve", lambda e: e.tensor_scalar(out=fE[:, :], in0=fE[:, :], scalar1=-1.0, scalar2=None, op0=ALU.mult), reads=["fE"], writes=["fE"])
            S.op("dve", lambda e: e.tensor_tensor_scan(out=cc[:, :], data0=fE[:, :], data1=fE[:, :], initial=cprev[:, 0:1], op0=ALU.add, op1=ALU.bypass),
                 reads=["fE", "cprev"], writes=["cc"])
            S.op("dve", lambda e: e.tensor_copy(out=cprev[:, :], in_=cc[:, 511:512]), reads=["cc"], writes=["cprev"])
            S.op("dve", lambda e: e.tensor_copy(out=rbf[:, :], in_=cc[:, :]), reads=["cc"], writes=["rbf"])
            yield
            a = nxt("pacc", NPACC)
            for s in range(4):
                S.op("pe", lambda e, a=a, s=s: e.transpose(pacc[a][:, 2 * s:2 * s + 2], cc[0:2, s * 128:(s + 1) * 128], ident_f[0:2, 0:2]),
                     reads=["cc", "ident_f"], writes=[("pacc", a)])
            yield
            yield
            S.op("dve", lambda e, a=a, i=i: e.tensor_scalar(out=negc[:, 4 * i:4 * i + 4, :], in0=pacc[a][:, 0:8].rearrange("p (s h) -> p s h", h=2),
                                                            scalar1=-1.0, scalar2=None, op0=ALU.mult),
                 reads=[("pacc", a)], writes=[("negc", i)])
            yield
            for h in range(2):
                qc, kc = (QA, KA) if h == 0 else (QB, KB)
                a = proj64(qc, i)
                yield
                yield
                S.op("act", lambda e, a=a, h=h, qbuf=qbuf: e.activation(out=qT[h][qbuf][0:64, :], in_=pacc[a][0:64, :], func=AF.Copy, scale=0.125),
                     reads=[("pacc", a)], writes=[("qT", h, qbuf)])
                S.dma("ld_r%d" % h, lambda e, h=h, qbuf=qbuf: e.dma_start(out=qT[h][qbuf][64:65, :], in_=rbf[h:h + 1, :]), reads=["rbf", ("qT", h, qbuf)], writes=[("qT", h, qbuf)])
                yield
                a = proj64(kc, i)
                yield
                yield
                S.op("act", lambda e, a=a, h=h, i=i: e.copy(out=kT[h][0:64, i * 512:(i + 1) * 512], in_=pacc[a][0:64, :]),
                     reads=[("pacc", a)], writes=[("kT", h, i)])
                yield
        else:
            a = proj64(QA, i)
            yield
            yield
            S.op("act", lambda e, a=a, qbuf=qbuf: e.activation(out=qT[0][qbuf][0:64, :], in_=pacc[a][0:64, :], func=AF.Copy, scale=0.125),
                 reads=[("pacc", a)], writes=[("qT", 0, qbuf)])
            yield
            a = proj64(KA, i)
            yield
            yield
            S.op("act", lambda e, a=a, i=i: e.copy(out=kT[0][0:64, i * 512:(i + 1) * 512], in_=pacc[a][0:64, :]), reads=[("pacc", a)], writes=[("kT", 0, i)])
            yield
            yield from rope_proj(QB, QP, i, qrot, "qrot")
            S.op("act", lambda e, qbuf=qbuf: e.activation(out=qT[1][qbuf][0:64, :], in_=qrot[:, :], func=AF.Copy, scale=0.125),
                 reads=["qrot"], writes=[("qT", 1, qbuf)])
            yield
            yield from rope_proj(KB, KP, i, krot, "krot")
            S.op("act", lambda e, i=i: e.copy(out=kT[1][0:64, i * 512:(i + 1) * 512], in_=krot[:, :]), reads=["krot"], writes=[("kT", 1, i)])
            S.op("dve", lambda e, i=i: e.tensor_reduce(out=kmT[:, 2 * i:2 * i + 2], in_=krot[:, :].rearrange("p (n k) -> p n k", n=2), axis=AX.X, op=ALU.add),
                 reads=["krot", "kmT"], writes=["kmT"])
            yield
            S.op("dve", lambda e, i=i: e.tensor_scalar(out=kmT[:, 2 * i:2 * i + 2], in0=kmT[:, 2 * i:2 * i + 2], scalar1=1.0 / 256, scalar2=None, op0=ALU.mult),
                 reads=["kmT"], writes=["kmT"])
            yield
            yield from moba_gate(i, qbuf)

    def roundrobin(*gens):
        gens = [g for g in gens if g is not None]
        while gens:
            for g in list(gens):
                try:
                    next(g)
                except StopIteration:
                    gens.remove(g)
            yield

    for _ in setup(0):
        pass
    for i in range(ntile):
        qbuf = i % 2
        nx = setup(i + 1) if i + 1 < ntile else None
        if fox:
            for h in range(2):
                attend(i, h, list(range(4 * i + 4)), 65, lambda blk, h=h: (negc[:, blk, h:h + 1], [("negc", blk // 4)]), qbuf,
                       side=nx, drain=(h == 1))
        else:
            attend(i, 1, list(range(4 * i + 4)), 128, lambda blk: (0.0, []), qbuf, a=1, side=roundrobin(sb_attend(i, qbuf), nx))
    if "dbg_negc" in io:
        S.dma("dbg", lambda e: [e.dma_start(out=io["dbg_negc"], in_=negc[:].rearrange("p b h -> p (b h)")),
                                e.dma_start(out=io["dbg_q"], in_=qT[0][(ntile - 1) % 2][:, :]),
                                e.dma_start(out=io["dbg_k"], in_=kT[0][:, 0:1024]),
                                e.dma_start(out=io["dbg_cc"], in_=cc[:, :])],
              reads=[("negc", i_) for i_ in range(ntile)] + [("qT", 0, (ntile - 1) % 2), ("kT", 0, 0), ("kT", 0, 1), "cc"], n=4)


def _masks():
    s_ = np.arange(128)[:, None, None]
    j_ = np.arange(4)[None, :, None]
    t_ = np.arange(512)[None, None, :]
    mi = ((128 * j_ + s_) <= t_).astype(np.float32)
    ms = ((128 * j_ + s_) < t_).astype(np.float32)
    return mi, ms


REST_W = ("w_out", "wq", "wkv", "wxo", "wup", "wdn", "cw", "cb", "g_xa", "g_mem", "g_ffn")
REST_SHAPES = {"w_out": [128, 8, 1024], "wq": [128, 8, 256], "wkv": [128, 8, 512], "wxo": [64, 4, 1024],
               "wup": [NPAIR, 128, 8 * 256], "wdn": [NPAIR, 2, 128, 512], "cw": [128, 44, 3], "cb": [128, 44],
               "g_xa": [D], "g_mem": [D], "g_ffn": [D]}


def build_fused(phases="AgBhCiD"):
    nc = bass.Bass("TRN2", target_bir_lowering=False, num_devices=NCORES)
    din = lambda name, shape, dt: nc.dram_tensor(name, list(shape), dt, kind="ExternalInput").ap()
    dint = lambda name, shape, dt: nc.dram_tensor(name, list(shape), dt, kind="Internal").ap()
    x = din("x", [SEQ, D], F32)
    hin0 = din("hin0", [NT + 128, D], F32)
    flag = din("flag", [128, 1], F32)
    hidx = din("hidx", [128, 17], I32)
    oidx = din("oidx", [128, 5, 8], I32)
    ident = din("ident", [128, 128], F32)
    maskI = din("maskI", [128, 4, 512], F32)
    maskS = din("maskS", [128, 4, 512], F32)
    pos = din("pos", [SEQ], I32)
    invf = din("invf", [64, 1], F32)
    sgn = din("sgn", [64, 1], F32)
    koh = din("koh", [64, SEQ], F32)
    tri = din("tri", [128, 256], F32)
    mem = din("mem", [MEM, D], F32)
    a_w = din("a_w", [128, 8, 512], F32)
    a_g = din("a_g", [D], F32)
    c_w = din("c_w", [128, 8, 386], F32)
    c_g = din("c_g", [D], F32)
    c_bf = din("c_bf", [2, 1], F32)
    g_fin = din("g_fin", [D], F32)
    rw = [{k: din("r%d_%s" % (L, k), REST_SHAPES[k], F32) for k in REST_W} for L in range(2)]
    out = nc.dram_tensor("out", [NT, D], F32, kind="ExternalOutput").ap()
    o_src = [dint("o_src%d" % L, [NTILE * 128, 512], BF16) for L in range(2)]
    o_all = [dint("o_all%d" % L, [NCORES * NTILE * 128, 512], BF16) for L in range(2)]
    h1_src = dint("h1_src", [NT, D], F32)
    h1_all = dint("h1_all", [SEQ, D], F32)

    def phase(tag, body, waits):
        with nc.cleanup_on_exit():
            with contextlib.ExitStack() as st:
                cx = Ctx(nc, st, tag)
                body(cx)
                cx.S.emit(final_wait_streams=waits)
            nc.all_engine_barrier()

    def gather(tag, src, dst):
        phase(tag, lambda cx: cx.S.cc("ag", lambda e: nc.gpsimd.collective_compute(
            "AllGather", ALU.bypass, replica_groups=[list(range(NCORES))], ins=[src.opt()], outs=[dst.opt()])), ["ag"])

    def scrub(tag):
        def body(cx):
            big = cx.sb("big", [128, 50000], F32)
            pz = [cx.ps("pz%d" % k, [128, 2048], F32) for k in range(2)]
            cx.S.op("dve", lambda e: e.memset(big[:, 0:25000], 0.0), writes=["b0"])
            cx.S.op("pool", lambda e: e.memset(big[:, 25000:50000], 0.0), writes=["b1"])
            for k in range(2):
                cx.S.op("act", lambda e, k=k: e.copy(out=pz[k][:, :], in_=big[:, 0:2048]), reads=["b0"], writes=[("pz", k)])
        phase(tag, body, [])

    def rest_io(L, hin, hout, with_hidx):
        io = dict(rw[L])
        io.update({"hin": hin, "oall": o_all[L], "oidx": oidx, "flag": flag, "mem": mem, "ident": ident, "g_fin": g_fin, "hout": hout})
        if with_hidx:
            io["hidx"] = hidx
        return io

    _phase, _gather = phase, gather
    phase = lambda tag, body, waits: _phase(tag, body, waits) if (tag in phases or tag[0] in "SG") else None
    gather = lambda tag, src, dst: _gather(tag, src, dst) if {"G0": "g", "G1": "h", "G2": "i"}[tag] in phases else None
    phase("A", lambda cx: emit_mixer(cx, {"hfull": x, "g_mix": a_g, "w": a_w, "ident": ident, "maskI": maskI, "maskS": maskS, "pos": pos,
                                          "invf": invf, "sgn": sgn, "koh": koh, "tri": tri,
                                          "oT": o_src[0].rearrange("(i f) t -> i f t", f=128)}, "ab"), ["st_o0", "st_o1"])
    if "X" in phases:
        dbg = nc.dram_tensor("dbg_o", [NTILE * 128, 512], BF16, kind="ExternalOutput").ap()
        _phase("X", lambda cx: cx.S.dma("cp", lambda e: e.dma_start(out=dbg, in_=o_src[0]), writes=["dbg"]), ["cp"])
    gather("G0", o_src[0], o_all[0])
    if "W" in phases:
        dbgw = nc.dram_tensor("dbg_oall", [NCORES * NTILE * 128, 512], BF16, kind="ExternalOutput").ap()
        _phase("W", lambda cx: cx.S.dma("cp", lambda e: e.dma_start(out=dbgw, in_=o_all[0]), writes=["dbg"]), ["cp"])
    if "s" in phases:
        scrub("S1")
    phase("B", lambda cx: emit_rest(cx, rest_io(0, hin0, h1_src, False), False), ["st_h"])
    if "Y" in phases:
        dbg1 = nc.dram_tensor("dbg_h1", [NT, D], F32, kind="ExternalOutput").ap()
        _phase("Y", lambda cx: cx.S.dma("cp", lambda e: e.dma_start(out=dbg1, in_=h1_src), writes=["dbg"]), ["cp"])
    gather("G1", h1_src, h1_all)
    phase("C", lambda cx: emit_mixer(cx, {"hfull": h1_all, "g_mix": c_g, "w": c_w, "ident": ident, "maskI": maskI, "bf": c_bf,
                                          "oT": o_src[1].rearrange("(i f) t -> i f t", f=128)}, "fox"), ["st_o0", "st_o1"])
    if "Z" in phases:
        dbg2 = nc.dram_tensor("dbg_o1", [NTILE * 128, 512], BF16, kind="ExternalOutput").ap()
        _phase("Z", lambda cx: cx.S.dma("cp", lambda e: e.dma_start(out=dbg2, in_=o_src[1]), writes=["dbg"]), ["cp"])
    gather("G2", o_src[1], o_all[1])
    phase("D", lambda cx: emit_rest(cx, rest_io(1, h1_all, out, True), True), ["st_h"])
    return nc


_CACHE = {}


def kernel(x, mem, positions, norm_mix_g, norm_xa_g, norm_mem_g, norm_ffn_g,
           ab_w_in, ab_w_out, fox_w_in, fox_b_f, fox_w_out,
           xa_w_q, xa_w_kv, xa_w_out, ffn_w_up, ffn_conv_w, ffn_conv_b, ffn_w_down,
           final_norm_g):
    a = lambda v: np.asarray(v)
    f32 = np.float32
    c_ = lambda v: np.ascontiguousarray(v, dtype=f32)
    x0 = c_(a(x)[0])
    mem_ = a(mem)
    w_in0, w_in1 = a(ab_w_in)[0], a(fox_w_in)[0]
    mi, ms = _masks()
    inv_freq = (10000.0 ** (-np.arange(32, dtype=f32) / 32)).astype(f32)
    invf = np.concatenate([inv_freq, inv_freq]).reshape(64, 1).astype(f32)
    sgn = np.concatenate([-np.ones(32), np.ones(32)]).reshape(64, 1).astype(f32)
    koh = np.zeros((64, SEQ), f32)
    for n in range(64):
        koh[n, n * 256:(n + 1) * 256] = 30000.0
    jj = np.arange(128)[:, None]
    ss = np.arange(128)[None, :]
    tri = np.concatenate([-(jj > ss).astype(f32), -(jj <= ss).astype(f32)], axis=1)
    perm = np.concatenate([np.arange(32, 64), np.arange(0, 32)])
    w_out0 = a(ab_w_out)[0]
    w_out0p = np.concatenate([np.concatenate([w_out0[r * 64:(r + 1) * 64], w_out0[(8 + r) * 64:(9 + r) * 64]], axis=0) for r in range(8)], axis=0)
    rws = []
    for L, wo in ((0, w_out0p), (1, a(fox_w_out)[0])):
        rws.append(rest_weights(L, wo, a(xa_w_q), a(xa_w_kv), a(xa_w_out), a(ffn_w_up), a(ffn_conv_w), a(ffn_conv_b),
                                a(ffn_w_down), a(norm_xa_g), a(norm_mem_g), a(norm_ffn_g), a(final_norm_g), mem_))
    common = {
        "x": x0, "ident": np.eye(128, dtype=f32), "maskI": c_((mi - 1.0) * 30000.0), "maskS": c_(ms),
        "pos": np.ascontiguousarray(a(positions).reshape(-1), dtype=np.int32), "invf": invf, "sgn": sgn, "koh": koh, "tri": tri,
        "mem": c_(mem_[0]), "a_g": c_(a(norm_mix_g)[0]), "c_g": c_(a(norm_mix_g)[1]), "g_fin": c_(a(final_norm_g)),
    }
    for L in range(2):
        for k in REST_W:
            common["r%d_%s" % (L, k)] = rws[L][k]
    in_maps = []
    p_ = np.arange(128)
    for cid in range(NCORES):
        m = dict(common)
        t0 = cid * NT
        m["hin0"] = np.concatenate([np.zeros((128, D), f32), x0[0:NT]], axis=0) if cid == 0 else c_(x0[t0 - 128:t0 + NT])
        m["flag"] = np.full((128, 1), 0.0 if cid == 0 else 1.0, f32)
        m["hidx"] = np.maximum(t0 - 128 + np.arange(17)[None, :] * 128 + p_[:, None], 0).astype(np.int32)
        tiles = np.array([max(4 * cid - 1, 0)] + [4 * cid + k for k in range(4)])
        m["oidx"] = (np.arange(8)[None, None, :] * (NTILE * 128) + tiles[None, :, None] * 128 + p_[:, None, None]).astype(np.int32)
        hA, hB = cid, 8 + cid
        qB = w_in0[:, hB * 64:(hB + 1) * 64]
        kB = w_in0[:, D + hB * 64:D + (hB + 1) * 64]
        wa = np.concatenate([w_in0[:, hA * 64:(hA + 1) * 64], w_in0[:, D + hA * 64:D + (hA + 1) * 64], qB, kB,
                             w_in0[:, 2 * D + hA * 64:2 * D + (hA + 1) * 64], w_in0[:, 2 * D + hB * 64:2 * D + (hB + 1) * 64],
                             qB[:, perm], kB[:, perm]], axis=1)
        m["a_w"] = c_(wa.reshape(8, 128, -1).transpose(1, 0, 2))
        hA, hB = 2 * cid, 2 * cid + 1
        cols = []
        for hh in (hA, hB):
            cols.append(w_in1[:, hh * 64:(hh + 1) * 64])
            cols.append(w_in1[:, D + hh * 64:D + (hh + 1) * 64])
        cols += [w_in1[:, 2 * D + hA * 64:2 * D + (hA + 1) * 64], w_in1[:, 2 * D + hB * 64:2 * D + (hB + 1) * 64],
                 w_in1[:, 3 * D + hA:3 * D + hA + 1], w_in1[:, 3 * D + hB:3 * D + hB + 1]]
        m["c_w"] = c_(np.concatenate(cols, axis=1).reshape(8, 128, -1).transpose(1, 0, 2))
        m["c_bf"] = c_(a(fox_b_f)[0][[hA, hB]].reshape(2, 1))
        in_maps.append(m)
    if "nc" not in _CACHE:
        _CACHE["nc"] = build_fused()
    res = run_bass_kernel_spmd(_CACHE["nc"], in_maps, core_ids=list(range(NCORES)))
    full = np.concatenate([r["out"] for r in res.results], axis=0)
    return np.ascontiguousarray(full[None].astype(np.float32))
```
